# Optimizing a Trainium2 kernel written in Bass

```python
import math
import jax, jax.numpy as jnp
from jax import lax
import numpy as np

D_MODEL = 2048
BATCH = 4
SEQ = 2048
DEPTH = 1
DEC_BATCH = 16
DEC_SEQ = 32
PAST_LEN = 2048

CHUNK = 64
HEAD_DIM = 128
N_HEADS_GDN = 8
N_HEADS_SB = 8
GDN_WIDTH = N_HEADS_GDN * HEAD_DIM
SB_WIDTH = N_HEADS_SB * HEAD_DIM
MIX_WIDTH = GDN_WIDTH + SB_WIDTH
CONV_W = 4
D_FF = 4 * D_MODEL
SB_BLOCK = 128
DEEPNORM_ALPHA = (2 * DEPTH) ** 0.25
DEEPNORM_BETA = (8 * DEPTH) ** -0.25
LN_EPS = 1e-5
RMS_EPS = 1e-6
L2_EPS = 1e-6

OFF_GDN_Z = 3 * GDN_WIDTH
OFF_GDN_B = 4 * GDN_WIDTH
OFF_GDN_A = OFF_GDN_B + N_HEADS_GDN
OFF_SB = OFF_GDN_A + N_HEADS_GDN
PROJ_WIDTH = OFF_SB + 3 * SB_WIDTH

kernel_name = "hybrid_gdn_stickbreaking_stream_step"


def layer_norm(x, g, b):
    xf = x.astype(jnp.float32)
    mu = jnp.mean(xf, axis=-1, keepdims=True)
    var = jnp.mean(jnp.square(xf - mu), axis=-1, keepdims=True)
    y = (xf - mu) * lax.rsqrt(var + LN_EPS) * g.astype(jnp.float32) + b.astype(jnp.float32)
    return y.astype(x.dtype)


def l2_normalize(t):
    return t * lax.rsqrt(jnp.sum(jnp.square(t), axis=-1, keepdims=True) + L2_EPS)


def causal_short_conv(u, buf, w):
    T = u.shape[1]
    up = jnp.concatenate([buf.astype(u.dtype), u], axis=1)
    out = up[:, 0:T] * w[0]
    for i in range(1, CONV_W):
        out = out + up[:, i:i + T] * w[i]
    return jax.nn.silu(out), up[:, T:]


def gated_delta_rule(q, k, v, beta, g, S0):
    B, T, H, dk = q.shape
    dv = v.shape[-1]
    C = CHUNK if T % CHUNK == 0 else T
    N = T // C

    def blocks(t):
        t = t.reshape((B, N, C, H) + t.shape[3:])
        return jnp.moveaxis(t, 3, 1)

    q, k, v, beta, g = blocks(q), blocks(k), blocks(v), blocks(beta), blocks(g)
    g_cum = jnp.cumsum(g, axis=-1)
    idx = jnp.arange(C)
    incl = idx[:, None] >= idx[None, :]
    strict = idx[:, None] > idx[None, :]
    decay = jnp.exp(jnp.where(incl, g_cum[..., :, None] - g_cum[..., None, :], -jnp.inf))
    k_beta = k * beta[..., None]
    m = jnp.where(strict, jnp.einsum('bhnid,bhnjd->bhnij', k_beta, k) * decay, 0.0)
    eye = jnp.eye(C, dtype=m.dtype)
    t_inv = lax.linalg.triangular_solve(eye + m, jnp.broadcast_to(eye, m.shape),
                                        left_side=True, lower=True, unit_diagonal=True)
    u = jnp.einsum('bhnij,bhnjd->bhnid', t_inv, v * beta[..., None])
    w = jnp.einsum('bhnij,bhnjd->bhnid', t_inv, k_beta * jnp.exp(g_cum)[..., None])
    attn = jnp.einsum('bhnid,bhnjd->bhnij', q, k) * decay
    q_dec = q * jnp.exp(g_cum)[..., None]
    k_tail = k * jnp.exp(g_cum[..., -1:] - g_cum)[..., None]
    g_tot = jnp.exp(g_cum[..., -1])
    xs = tuple(jnp.moveaxis(t, 2, 0) for t in (u, w, q_dec, attn, k_tail, g_tot))

    def step(S, inp):
        u_n, w_n, qd_n, at_n, kt_n, gt_n = inp
        v_new = u_n - jnp.einsum('bhcd,bhde->bhce', w_n, S)
        o = jnp.einsum('bhcd,bhde->bhce', qd_n, S) + jnp.einsum('bhij,bhje->bhie', at_n, v_new)
        S = S * gt_n[..., None, None] + jnp.einsum('bhcd,bhce->bhde', kt_n, v_new)
        return S, o

    S_fin, o = lax.scan(step, S0, xs)
    o = jnp.transpose(o, (1, 0, 3, 2, 4)).reshape(B, T, H, dv)
    return o, S_fin


def stick_breaking(q, k, v, q_pos, k_pos):
    z = jnp.einsum('bqhd,bkhd->bhqk', q.astype(jnp.float32), k.astype(jnp.float32)) * (HEAD_DIM ** -0.5)
    causal = k_pos[None, :] < q_pos[:, None]
    log_1m = jnp.where(causal, jax.nn.log_sigmoid(-z), 0.0)
    rest = lax.cumsum(log_1m, axis=3, reverse=True) - log_1m
    wts = jnp.where(causal, jnp.exp(jax.nn.log_sigmoid(z) + rest), 0.0)
    return jnp.einsum('bhqk,bkhd->bqhd', wts, v.astype(jnp.float32))


def stick_breaking_prompt(q, k, v):
    B, T, H, d = q.shape
    nb = T // SB_BLOCK
    pos = jnp.arange(T)
    qb = jnp.moveaxis(q.reshape(B, nb, SB_BLOCK, H, d), 1, 0)
    qpos = pos.reshape(nb, SB_BLOCK)
    out = lax.map(lambda blk: stick_breaking(blk[0], k, v, blk[1], pos), (qb, qpos))
    return jnp.moveaxis(out, 0, 1).reshape(B, T, H, d)


def hybrid_layer(x, conv_buf, S0, k_past, v_past, w_in, conv_w, a_log, dt_bias, gdn_norm_w,
                 w_out, ln1_g, ln1_b, w_up, w_down, ln2_g, ln2_b):
    B, T, _ = x.shape
    proj = x @ w_in
    qkv, conv_new = causal_short_conv(proj[..., :OFF_GDN_Z], conv_buf, conv_w)
    qkv = qkv.astype(jnp.float32).reshape(B, T, 3, N_HEADS_GDN, HEAD_DIM)
    q_a = l2_normalize(qkv[:, :, 0]) * (HEAD_DIM ** -0.5)
    k_a = l2_normalize(qkv[:, :, 1])
    v_a = qkv[:, :, 2]
    z = proj[..., OFF_GDN_Z:OFF_GDN_B].astype(jnp.float32).reshape(B, T, N_HEADS_GDN, HEAD_DIM)
    beta = jax.nn.sigmoid(proj[..., OFF_GDN_B:OFF_GDN_A].astype(jnp.float32))
    g = -jnp.exp(a_log.astype(jnp.float32)) * jax.nn.softplus(
        proj[..., OFF_GDN_A:OFF_SB].astype(jnp.float32) + dt_bias.astype(jnp.float32))
    o_a, S_new = gated_delta_rule(q_a, k_a, v_a, beta, g, S0.astype(jnp.float32))
    o_a = (o_a * lax.rsqrt(jnp.mean(jnp.square(o_a), axis=-1, keepdims=True) + RMS_EPS)
           * gdn_norm_w.astype(jnp.float32) * jax.nn.silu(z))
    sb = proj[..., OFF_SB:].reshape(B, T, 3, N_HEADS_SB, HEAD_DIM)
    q_b, k_b, v_b = sb[:, :, 0], sb[:, :, 1], sb[:, :, 2]
    if k_past is None:
        o_b = stick_breaking_prompt(q_b, k_b, v_b)
    else:
        P = k_past.shape[1]
        k_all = jnp.concatenate([k_past.astype(k_b.dtype), k_b], axis=1)
        v_all = jnp.concatenate([v_past.astype(v_b.dtype), v_b], axis=1)
        o_b = stick_breaking(q_b, k_all, v_all, P + jnp.arange(T), jnp.arange(P + T))
    mixed = jnp.concatenate([o_a.reshape(B, T, GDN_WIDTH), o_b.reshape(B, T, SB_WIDTH)],
                            axis=-1).astype(x.dtype)
    x = layer_norm(DEEPNORM_ALPHA * x + mixed @ w_out, ln1_g, ln1_b)
    h = jnp.square(jax.nn.relu(x @ w_up))
    x = layer_norm(DEEPNORM_ALPHA * x + h @ w_down, ln2_g, ln2_b)
    return x, conv_new, S_new, k_b, v_b


def setup_inputs(seed: int = 0) -> dict:
    key = jax.random.key(seed)
    ks = jax.random.split(key, 18)
    f32 = jnp.float32
    col_scale = jnp.ones((PROJ_WIDTH,), f32)
    col_scale = col_scale.at[2 * GDN_WIDTH:3 * GDN_WIDTH].set(DEEPNORM_BETA)
    col_scale = col_scale.at[OFF_SB + 2 * SB_WIDTH:].set(DEEPNORM_BETA)
    w_in = jax.random.normal(ks[6], (DEPTH, D_MODEL, PROJ_WIDTH), f32) * (D_MODEL ** -0.5) * col_scale
    log_lo, log_hi = math.log(1e-3), math.log(1e-1)
    dt = jnp.exp(jax.random.uniform(ks[9], (DEPTH, N_HEADS_GDN), f32) * (log_hi - log_lo) + log_lo)
    dt_bias = dt + jnp.log(-jnp.expm1(-dt))
    return {
        "x_prompt": jax.random.normal(ks[0], (BATCH, SEQ, D_MODEL), f32),
        "x_sample": jax.random.normal(ks[1], (DEC_BATCH, DEC_SEQ, D_MODEL), f32),
        "state_gdn_conv": jax.random.normal(ks[2], (DEPTH, DEC_BATCH, CONV_W - 1, 3 * GDN_WIDTH), f32),
        "state_gdn_S": 0.5 * jax.random.normal(ks[3], (DEPTH, DEC_BATCH, N_HEADS_GDN, HEAD_DIM, HEAD_DIM), f32),
        "cache_sb_k": jax.random.normal(ks[4], (DEPTH, DEC_BATCH, PAST_LEN, N_HEADS_SB, HEAD_DIM), f32),
        "cache_sb_v": DEEPNORM_BETA * jax.random.normal(ks[5], (DEPTH, DEC_BATCH, PAST_LEN, N_HEADS_SB, HEAD_DIM), f32),
        "w_in": w_in,
        "conv_w": 0.5 * jax.random.normal(ks[7], (DEPTH, CONV_W, 3 * GDN_WIDTH), f32),
        "a_log": jnp.log(jax.random.uniform(ks[8], (DEPTH, N_HEADS_GDN), f32, 1.0, 16.0)),
        "dt_bias": dt_bias,
        "gdn_norm_w": 1.0 + 0.02 * jax.random.normal(ks[10], (DEPTH, HEAD_DIM), f32),
        "w_out": jax.random.normal(ks[11], (DEPTH, MIX_WIDTH, D_MODEL), f32) * (MIX_WIDTH ** -0.5) * DEEPNORM_BETA,
        "ln1_g": 1.0 + 0.02 * jax.random.normal(ks[12], (DEPTH, D_MODEL), f32),
        "ln1_b": 0.02 * jax.random.normal(ks[13], (DEPTH, D_MODEL), f32),
        "w_up": jax.random.normal(ks[14], (DEPTH, D_MODEL, D_FF), f32) * (D_MODEL ** -0.5),
        "w_down": jax.random.normal(ks[15], (DEPTH, D_FF, D_MODEL), f32) * (D_FF ** -0.5) * DEEPNORM_BETA,
        "ln2_g": 1.0 + 0.02 * jax.random.normal(ks[16], (DEPTH, D_MODEL), f32),
        "ln2_b": 0.02 * jax.random.normal(ks[17], (DEPTH, D_MODEL), f32),
    }


def reference(x_prompt, x_sample, state_gdn_conv, state_gdn_S, cache_sb_k, cache_sb_v,
              w_in, conv_w, a_log, dt_bias, gdn_norm_w, w_out, ln1_g, ln1_b,
              w_up, w_down, ln2_g, ln2_b):
    yp, ys = x_prompt, x_sample
    conv_p, S_p, k_p, v_p = [], [], [], []
    conv_s, S_s, k_s, v_s = [], [], [], []
    for l in range(DEPTH):
        wl = (w_in[l], conv_w[l], a_log[l], dt_bias[l], gdn_norm_w[l], w_out[l],
              ln1_g[l], ln1_b[l], w_up[l], w_down[l], ln2_g[l], ln2_b[l])
        b_p = yp.shape[0]
        zero_conv = jnp.zeros((b_p, CONV_W - 1, 3 * GDN_WIDTH), yp.dtype)
        zero_S = jnp.zeros((b_p, N_HEADS_GDN, HEAD_DIM, HEAD_DIM), jnp.float32)
        yp, c1, s1, kk1, vv1 = hybrid_layer(yp, zero_conv, zero_S, None, None, *wl)
        ys, c2, s2, kk2, vv2 = hybrid_layer(ys, state_gdn_conv[l], state_gdn_S[l],
                                            cache_sb_k[l], cache_sb_v[l], *wl)
        conv_p.append(c1); S_p.append(s1); k_p.append(kk1); v_p.append(vv1)
        conv_s.append(c2); S_s.append(s2); k_s.append(kk2); v_s.append(vv2)
    new_conv_p = jnp.stack(conv_p)
    new_S_p = jnp.stack(S_p)
    new_k_p = jnp.stack(k_p)
    new_v_p = jnp.stack(v_p)
    new_conv_s = jnp.stack(conv_s)
    new_S_s = jnp.stack(S_s)
    new_k_s = jnp.stack(k_s)
    new_v_s = jnp.stack(v_s)
    return (yp, ys, new_conv_p, new_S_p, new_k_p, new_v_p, new_conv_s, new_S_s, new_k_s, new_v_s)
```

```python
import numpy as np
import concourse.bass as bass
import concourse.mybir as mybir
from concourse.bass_utils import run_bass_kernel_spmd

F32 = mybir.dt.float32
BF16 = mybir.dt.bfloat16
AF = mybir.ActivationFunctionType
ALU = mybir.AluOpType

D = 2048
KC = 16
NOWN = 1088
NPRE = 1024
NALL = NPRE + NOWN
PROJ_W = 7184
OFF_Z = 3072
OFF_B = 4096
OFF_A = 4104
OFF_SB = 4112
DFF = 8192
ALPHA = float(2 ** 0.25)
LN_EPS = 1e-5
NEG = -30000.0
EPOCH = 30000
import os as _os
SAME_ENGINE_INORDER = bool(_os.environ.get('SEI'))
NEU_SINGLE = _os.environ.get('NEU_SINGLE', '0') == '1'


def _rect(ap):
    t = ap.tensor
    dims = list(ap.ap)
    esz = mybir.dt.size(ap.dtype)
    tsz = mybir.dt.size(t.dtype)
    row = 1
    for s in list(t.shape)[1:]:
        row *= s
    rowb = row * tsz
    offb = ap.offset * esz
    pcnt = dims[0][1]
    p_lo = offb // rowb
    f_lo = offb - p_lo * rowb
    ext = 0
    for st, c in dims[1:]:
        ext += abs(st) * (c - 1)
    f_hi = f_lo + (ext + 1) * esz
    p_hi = p_lo + pcnt
    if t.name.startswith("ps"):
        f_lo, f_hi = 0, 2048
        p_lo = (p_lo // 32) * 32
        p_hi = ((p_hi + 31) // 32) * 32
    return (t.name, p_lo, p_hi, f_lo, f_hi)


class Sched:
    def __init__(self, nc, n_dma_sems=48):
        self.nc = nc
        self.E = {"pe": nc.tensor, "dve": nc.vector, "act": nc.scalar, "pool": nc.gpsimd, "sp": nc.sync}
        self.csem = {}
        self.ccnt = {}
        self.nep = {}
        for e in ("pe", "dve", "act", "pool"):
            self.csem[e] = nc.alloc_semaphore(f"c_{e}_0")
            self.ccnt[e] = 0
            self.nep[e] = 0
        self.dsems = [nc.alloc_semaphore(f"d_{i}") for i in range(n_dma_sems)]
        self.dcnt = [0] * n_dma_sems
        self.dnext = {"sw": 0, "hw": n_dma_sems // 2}
        self.drange = {"sw": (0, n_dma_sems // 2), "hw": (n_dma_sems // 2, n_dma_sems)}
        self.known = {e: {} for e in self.E}
        self.recs = {}
        self.n_inst = 0
        self.n_wait = 0
        self.out_tokens = []

    def _need(self, eng, tok, waits):
        sem, val, name = tok
        if self.known[eng].get(name, 0) >= val:
            return
        cur = waits.get(name)
        if cur is None or cur[1] < val:
            waits[name] = (sem, val)

    @staticmethod
    def _ov(a, b):
        return a[1] < b[2] and b[1] < a[2] and a[3] < b[4] and b[3] < a[4]

    @staticmethod
    def _covers(a, b):
        return a[1] <= b[1] and a[2] >= b[2] and a[3] <= b[3] and a[4] >= b[4]

    def _deps(self, eng, reads, writes, waits):
        for r in reads:
            psum = r[0].startswith("ps")
            for rec in self.recs.get(r[0], ()):
                if rec[1] and self._ov(rec[0], r):
                    if rec[3] == eng and (eng == "pe" or (SAME_ENGINE_INORDER and eng in ("act", "dve"))):
                        continue
                    self._need(eng, rec[2], waits)
                elif psum and (not rec[1]) and rec[3] != eng:
                    self._need(eng, rec[2], waits)
        for w in writes:
            for rec in self.recs.get(w[0], ()):
                if self._ov(rec[0], w):
                    if rec[3] == eng and (eng == "pe" or (SAME_ENGINE_INORDER and eng in ("act", "dve"))):
                        continue
                    self._need(eng, rec[2], waits)

    def _record(self, eng, tok, reads, writes):
        for w in writes:
            lst = self.recs.setdefault(w[0], [])
            lst[:] = [rec for rec in lst if not self._covers(w, rec[0])]
            lst.append([w, True, tok, eng])
        for r in reads:
            lst = self.recs.setdefault(r[0], [])
            done = False
            if eng != "dma":
                for rec in lst:
                    if (not rec[1]) and rec[3] == eng and rec[0] == r:
                        rec[2] = tok
                        done = True
                        break
            if not done:
                lst.append([r, False, tok, eng])

    def _emit_waits(self, eng, waits):
        e = self.E[eng]
        for name, (sem, val) in waits.items():
            e.wait_ge(sem, val)
            self.known[eng][name] = val
            self.n_wait += 1

    def op(self, eng, fn, reads=(), writes=()):
        rr = [_rect(a) for a in reads]
        ww = [_rect(a) for a in writes]
        waits = {}
        self._deps(eng, rr, ww, waits)
        if self.ccnt[eng] >= EPOCH:
            self.nep[eng] += 1
            self.csem[eng] = self.nc.alloc_semaphore(f"c_{eng}_{self.nep[eng]}")
            self.ccnt[eng] = 0
        self._emit_waits(eng, waits)
        ins = fn(self.E[eng])
        self.ccnt[eng] += 1
        sem = self.csem[eng]
        ins.then_inc(sem, 1)
        tok = (sem, self.ccnt[eng], f"c_{eng}_{self.nep[eng]}")
        self._record(eng, tok, rr, ww)
        self.n_inst += 1
        return tok

    def dma(self, q, out, in_, reads=(), writes=(), is_output=False, after=(), **kw):
        rr = [_rect(a) for a in reads]
        ww = [_rect(a) for a in writes]
        waits = {}
        self._deps(q, rr, ww, waits)
        for tok in after:
            self._need(q, tok, waits)
        kind = "sw" if q == "pool" else "hw"
        i = self.dnext[kind]
        lo, hi = self.drange[kind]
        self.dnext[kind] = lo + (i + 1 - lo) % (hi - lo)
        sem = self.dsems[i]
        name = f"d_{i}"
        if self.dcnt[i] > 0:
            self._need(q, (sem, self.dcnt[i] * 16, name), waits)
        self._emit_waits(q, waits)
        ins = self.E[q].dma_start(out=out, in_=in_, **kw)
        self.dcnt[i] += 1
        ins.then_inc(sem, 16)
        tok = (sem, self.dcnt[i] * 16, name)
        self._record("dma", tok, rr, ww)
        self.n_inst += 1
        if is_output:
            self.out_tokens.append(tok)
        return tok

    def finish(self):
        waits = {}
        for tok in self.out_tokens:
            self._need("sp", tok, waits)
        for i, sem in enumerate(self.dsems):
            if self.dcnt[i]:
                self._need("sp", (sem, self.dcnt[i] * 16, f"d_{i}"), waits)
        self._emit_waits("sp", waits)


class Arena:
    def __init__(self, nc, nbytes):
        self.t = nc.alloc_sbuf_tensor("arena", [128, nbytes // 2], BF16)
        self.cap = nbytes
        self.top = 0

    def alloc(self, shape, dtype, parts=128, at=None):
        if isinstance(shape, int):
            shape = (shape,)
        n = 1
        for s in shape:
            n *= s
        nb = n * mybir.dt.size(dtype)
        if at is not None:
            off = at
            assert off % 64 == 0 and off + nb <= self.cap
        else:
            off = self.top
            self.top += (nb + 63) // 64 * 64
            assert self.top <= self.cap, f"arena overflow {self.top} > {self.cap}"
        self.last_off = off
        v = self.t[0:parts, off // 2:(off + nb) // 2]
        if dtype != BF16:
            v = v.bitcast(dtype)
        if len(shape) == 2:
            v = v.rearrange("p (a b) -> p a b", a=shape[0])
        elif len(shape) == 3:
            v = v.rearrange("p (a b c) -> p a b c", a=shape[0], b=shape[1])
        return v

    def mark(self):
        return self.top

    def release(self, m):
        self.top = m


OWN_TILES = [(i * 128, 128) for i in range(8)] + [(1024, 64)]
PRE_TILES = [(i * 128, 128) for i in range(8)]


def tok_groups(n, g=512):
    out = []
    t = 0
    while t < n:
        out.append((t, min(g, n - t)))
        t += g
    return out


class Prog:
    def __init__(self, stop_after=None, dbg=()):
        self.stop_after = stop_after
        self.dbg_names = dbg
        nc = bass.Bass("TRN2", target_bir_lowering=False)
        self.nc = nc
        self.S = Sched(nc)
        self.I = {}
        self.O = {}
        self.dbg = {}

        def inp(name, shape):
            self.I[name] = nc.dram_tensor(name, list(shape), F32, kind="ExternalInput").ap()

        def outp(name, shape):
            self.O[name] = nc.dram_tensor(name, list(shape), F32, kind="ExternalOutput").ap()

        inp("x_own", (NOWN, D))
        inp("x_pre", (NPRE, D))
        inp("conv_s", (2, 3, 3072))
        inp("S0_s", (2, 8, 128, 128))
        inp("ck", (2, 2048, 1024))
        inp("cv", (2, 2048, 1024))
        inp("w_in", (D, PROJ_W))
        inp("conv_w", (4, 3072))
        inp("a_log", (8,))
        inp("dt_bias", (8,))
        inp("gdn_norm_w", (128,))
        inp("w_out", (D, D))
        inp("ln1_g", (D,))
        inp("ln1_b", (D,))
        inp("w_up", (D, DFF))
        inp("w_down", (DFF, D))
        inp("ln2_g", (D,))
        inp("ln2_b", (D,))
        inp("pre_bias", (128, 1))
        outp("y", (NOWN, D))
        outp("kb", (NOWN, 1024))
        outp("vb", (NOWN, 1024))
        outp("conv_p_o", (3, 3072))
        outp("conv_s_o", (2, 3, 3072))
        outp("S_p_o", (8, 128, 128))
        outp("S_s_o", (2, 8, 128, 128))
        self.x1s = nc.dram_tensor("x1_scratch", [NOWN, D], F32, kind="Internal").ap()
        self.y2s = nc.dram_tensor("y2_scratch", [NOWN, D], F32, kind="Internal").ap()

        self.ar = Arena(nc, 212480)
        self.ps = [nc.alloc_psum_tensor(f"ps{i}", [128, 512], F32) for i in range(8)]
        self.build()

    def psf(self, i):
        return self.ps[i][:]

    def psb(self, i):
        return self.ps[i][:].bitcast(BF16)

    def tap(self, name, ap, shape):
        if name not in self.dbg_names:
            return
        t = self.nc.dram_tensor("dbg_" + name, list(shape), ap.dtype, kind="ExternalOutput").ap()
        self.dbg[name] = t
        self.S.dma("sp", t, ap, reads=[ap], is_output=True)

    def mm(self, out, lhsT, rhs, start=True, stop=True, skip=False):
        if skip:
            self.S.op("pe", lambda e: e.matmul(out, lhsT, rhs, start=start, stop=stop, skip_group_check=True),
                      reads=[lhsT, rhs], writes=[out])
        else:
            self.S.op("pe", lambda e: e.matmul(out, lhsT, rhs, start=start, stop=stop),
                      reads=[lhsT, rhs], writes=[out])

    def tr(self, out, in_, ident):
        self.S.op("pe", lambda e: e.transpose(out, in_, ident), reads=[in_, ident], writes=[out])

    def copy(self, eng, out, in_):
        if eng == "act":
            self.S.op("act", lambda e: e.copy(out, in_), reads=[in_], writes=[out])
        else:
            self.S.op(eng, lambda e: e.tensor_copy(out, in_), reads=[in_], writes=[out])

    def act(self, out, in_, func, bias=0.0, scale=1.0, accum_out=None, extra_reads=()):
        rd = [in_] + list(extra_reads)
        wr = [out] + ([accum_out] if accum_out is not None else [])
        if accum_out is not None:
            self.S.op("act", lambda e: e.activation(out=out, in_=in_, func=func, bias=bias, scale=scale,
                                                    accum_out=accum_out), reads=rd, writes=wr)
        else:
            self.S.op("act", lambda e: e.activation(out=out, in_=in_, func=func, bias=bias, scale=scale),
                      reads=rd, writes=wr)

    def tt(self, eng, out, in0, in1, op):
        self.S.op(eng, lambda e: e.tensor_tensor(out=out, in0=in0, in1=in1, op=op), reads=[in0, in1], writes=[out])

    def ts(self, eng, out, in0, s1, op0, s2=None, op1=None, extra_reads=()):
        rd = [in0] + list(extra_reads)
        if op1 is None:
            self.S.op(eng, lambda e: e.tensor_scalar(out=out, in0=in0, scalar1=s1, scalar2=None, op0=op0),
                      reads=rd, writes=[out])
        else:
            self.S.op(eng, lambda e: e.tensor_scalar(out=out, in0=in0, scalar1=s1, scalar2=s2, op0=op0, op1=op1),
                      reads=rd, writes=[out])

    def stt(self, eng, out, in0, scalar, in1, op0, op1, extra_reads=()):
        rd = [in0, in1] + list(extra_reads)
        self.S.op(eng, lambda e: e.scalar_tensor_tensor(out=out, in0=in0, scalar=scalar, in1=in1, op0=op0, op1=op1),
                  reads=rd, writes=[out])

    def memset(self, eng, ap, val):
        self.S.op(eng, lambda e: e.memset(ap, val), writes=[ap])

    def asel(self, out, in_, pattern, cmp, fill, base, cm):
        self.S.op("pool", lambda e: e.affine_select(out=out, in_=in_, pattern=pattern, compare_op=cmp, fill=fill,
                                                    base=base, channel_multiplier=cm), reads=[in_], writes=[out])

    def build(self):
        self.consts()
        self.phase_x()
        if self.stop_after == "x":
            return self.S.finish()
        self.phase_attn()
        if self.stop_after == "attn":
            return self.S.finish()
        self.phase_wout()
        if self.stop_after == "wout":
            return self.S.finish()
        self.phase_ffn()
        self.S.finish()

    def consts(self):
        ar = self.ar
        self.ident_f = ar.alloc((128,), F32)
        self.ident_b = ar.alloc((128,), BF16)
        self.zeros_f = ar.alloc((128,), F32)
        self.memset("pool", self.zeros_f, 0.0)
        self.memset("pool", self.ident_f, 0.0)
        self.asel(self.ident_f, self.ident_f, [[-1, 128]], ALU.not_equal, 1.0, 0, 1)
        self.copy("pool", self.ident_b, self.ident_f)
        self.pre_bias = self.nc.alloc_sbuf_tensor("pre_bias_t", [128, 1], F32)[:]
        import os
        if os.environ.get("PB_MEMSET"):
            self.memset("pool", self.pre_bias, 1.0)
        else:
            self.S.dma("sp", self.pre_bias, self.I["pre_bias"], writes=[self.pre_bias])

    def phase_x(self):
        ar, S = self.ar, self.S
        self.xT_own = ar.alloc((KC, NOWN), BF16)
        self.xT_own_off = ar.last_off
        self.xT_pre = ar.alloc((KC, NPRE), BF16)
        self.xT_pre_off = ar.last_off
        m = ar.mark()
        xb = [ar.alloc((D,), BF16) for _ in range(2)]
        k = 0
        for src, tiles, dst in ((self.I["x_pre"], PRE_TILES, self.xT_pre), (self.I["x_own"], OWN_TILES, self.xT_own)):
            for (t0, n) in tiles:
                b = xb[k % 2]
                S.dma("pool", b[0:n, :], src[t0:t0 + n, :], writes=[b[0:n, :]])
                for half in range(2):
                    pb = self.psb(half)
                    for c in range(8):
                        cc = half * 8 + c
                        self.tr(pb[:, c * 128:c * 128 + n], b[0:n, cc * 128:(cc + 1) * 128], self.ident_b[0:n, 0:n])
                    src_ps = pb[:, 0:1024].rearrange("p (c t) -> p c t", c=8)[:, :, 0:n]
                    self.copy("dve" if half == 0 else "act", dst[:, half * 8:(half + 1) * 8, t0:t0 + n], src_ps)
                k += 1
        ar.release(m)
        self.tap("xT_own", self.xT_own, (128, KC, NOWN))

    def phase_attn(self):
        ar = self.ar
        self.mixedT = ar.alloc((KC, NOWN), BF16)
        self.memset("pool", self.mixedT, 0.0)
        m = ar.mark()
        import os
        if not os.environ.get("NO_GDN"):
            self.phase_gdn()
        ar.release(m)
        if not os.environ.get("NO_SB"):
            self.phase_sb()
        ar.release(m)

    def phase_gdn(self):
        ar, S, I = self.ar, self.S, self.I
        NT = 17
        tiles = [("pre", t0, n, t0) for (t0, n) in PRE_TILES] + [("own", t0, n, NPRE + t0) for (t0, n) in OWN_TILES]
        zb = ar.alloc((128,), BF16)
        ones_b = ar.alloc((128,), BF16)
        ones_f = ar.alloc((128,), F32)
        nones_f = ar.alloc((128,), F32)
        self.memset("pool", zb, 0.0)
        self.memset("pool", ones_b, 1.0)
        self.memset("pool", ones_f, 1.0)
        self.memset("pool", nones_f, -1.0)
        offd = ar.alloc((128,), BF16)
        self.memset("pool", offd, 1.0)
        self.asel(offd, offd, [[-1, 128]], ALU.not_equal, 0.0, 0, 1)
        ident4 = ar.alloc((4, 128), BF16)
        for j in range(4):
            self.copy("pool", ident4[:, j, :], self.ident_b)

        def block_mask(kind, c, rows):
            if kind in ("Ms", "Mi"):
                m = ar.alloc((128,), BF16)
                self.memset("pool", m, NEG)
            else:
                m = ar.alloc((128,), F32)
                self.memset("pool", m, 0.0)
            for r0 in range(0, rows, c):
                blk = slice(r0, r0 + c)
                if kind == "Ms":
                    self.asel(m[blk, blk], zb[blk, blk], [[-1, c]], ALU.is_ge, NEG, -1, 1)
                elif kind == "Mi":
                    self.asel(m[blk, blk], zb[blk, blk], [[-1, c]], ALU.is_ge, NEG, 0, 1)
                elif kind == "tri":
                    self.asel(m[blk, blk], ones_f[blk, blk], [[1, c]], ALU.is_ge, 0.0, 0, -1)
                elif kind == "last":
                    self.asel(m[blk, blk], ones_f[blk, blk], [[0, c]], ALU.is_equal, 0.0, -(c - 1), 1)
            return m
        maskMs = {64: block_mask("Ms", 64, 128), 32: block_mask("Ms", 32, 64)}
        maskMi = {64: block_mask("Mi", 64, 128), 32: block_mask("Mi", 32, 64)}
        trich = {64: block_mask("tri", 64, 128), 32: block_mask("tri", 32, 64)}
        sellast = {64: block_mask("last", 64, 128), 32: block_mask("last", 32, 64)}
        lastsel = {}
        for (c, rows) in ((64, 128), (32, 64)):
            for nch in range(2):
                m = ar.alloc((128,), F32)
                self.memset("pool", m, 0.0)
                self.asel(m[0:rows, :], ones_f[0:rows, :], [[0, 128]], ALU.is_equal, 0.0, -((nch + 1) * c - 1), 1)
                lastsel[(c, nch)] = m
        nsel = ar.alloc((8, 128), F32, parts=8)
        self.memset("pool", nsel, 0.0)
        for h in range(8):
            self.asel(nsel[:, h, :], nones_f[0:8, :], [[0, 128]], ALU.is_equal, 0.0, -h, 1)
        normw = ar.alloc((128,), F32)
        S.dma("sp", normw, I["gdn_norm_w"].partition_broadcast(128), writes=[normw])
        dtb = ar.alloc((8,), F32)
        S.dma("sp", dtb, I["dt_bias"].partition_broadcast(128), writes=[dtb])
        negA = ar.alloc((8,), F32)
        S.dma("sp", negA, I["a_log"].partition_broadcast(128), writes=[negA])
        self.act(negA, negA, AF.Exp)
        self.ts("dve", negA, negA, -1.0, ALU.mult)
        eps6 = self.eps_tile(1e-6)
        one_c = self.eps_tile(1.0)
        lnqs = self.eps_tile(float(np.log(128 ** -0.5)))
        cw = ar.alloc((24, 4), F32)
        hist = ar.alloc((24, 6), F32)
        mtmp = ar.mark()
        cwt = ar.alloc((3072,), F32, parts=4)
        S.dma("sp", cwt, I["conv_w"], writes=[cwt])
        hst = ar.alloc((3072,), F32, parts=6)
        S.dma("sp", hst, I["conv_s"].rearrange("s r c -> (s r) c"), writes=[hst])
        pc = self.psf(0)
        for ct in range(24):
            self.mm(pc[:, ct * 4:ct * 4 + 4], cwt[:, ct * 128:(ct + 1) * 128], self.ident_f[0:4, 0:4])
        self.copy("dve", cw, pc[:, 0:96].rearrange("p (a b) -> p a b", b=4))
        pc = self.psf(1)
        for ct in range(24):
            self.mm(pc[:, ct * 6:ct * 6 + 6], hst[:, ct * 128:(ct + 1) * 128], self.ident_f[0:6, 0:6])
        self.copy("dve", hist, pc[:, 0:144].rearrange("p (a b) -> p a b", b=6))
        ar.release(mtmp)
        w16 = ar.alloc((KC, 16), BF16)
        S.dma("pool", w16, I["w_in"][:, OFF_B:OFF_B + 16].rearrange("(c p) n -> p c n", p=128), writes=[w16])
        P16 = ar.alloc((NT, 16), F32)
        ps = self.psf(0)
        for ti, (src, t0, n, c0) in enumerate(tiles):
            xT = self.xT_pre if src == "pre" else self.xT_own
            for c in range(KC):
                self.mm(ps[0:n, ti * 16:(ti + 1) * 16], xT[:, c, t0:t0 + n], w16[:, c, :], start=(c == 0), stop=(c == KC - 1))
        self.memset("pool", P16, 0.0)
        self.copy("dve", P16[:, 0:16, :], ps[:, 0:256].rearrange("p (a b) -> p a b", b=16))
        self.copy("dve", P16[0:64, 16, :], ps[0:64, 256:272])
        TMq = {}
        for nm in ("g", "G", "lb", "beta", "sk", "eg", "st", "tmp"):
            TMq[nm] = ar.alloc((NT, 8), F32)
        bl = P16[:, :, 0:8]
        al = P16[:, :, 8:16]
        bc17 = lambda t: t.unsqueeze(1).broadcast_to([128, NT, 8])
        self.tt("dve", TMq["tmp"], al, bc17(dtb), ALU.add)
        self.act(TMq["tmp"], TMq["tmp"], AF.Exp)
        self.act(TMq["tmp"], TMq["tmp"], AF.Ln, bias=one_c, extra_reads=[one_c])
        self.tt("dve", TMq["g"], TMq["tmp"], bc17(negA), ALU.mult)
        self.act(TMq["lb"], bl, AF.Exp, scale=-1.0)
        self.act(TMq["lb"], TMq["lb"], AF.Ln, bias=one_c, extra_reads=[one_c])
        self.act(TMq["beta"], TMq["lb"], AF.Exp, scale=-1.0)
        ps = self.psf(1)
        for ti, (src, t0, n, c0) in enumerate(tiles):
            c = 64 if n == 128 else 32
            self.mm(ps[0:n, ti * 8:(ti + 1) * 8], trich[c][0:n, 0:n], TMq["g"][0:n, ti, :])
        self.memset("pool", TMq["G"], 0.0)
        self.copy("dve", TMq["G"][:, 0:16, :], ps[:, 0:128].rearrange("p (a b) -> p a b", b=8))
        self.copy("dve", TMq["G"][0:64, 16, :], ps[0:64, 128:136])
        self.act(TMq["eg"], TMq["G"], AF.Exp)
        self.tt("dve", TMq["tmp"], TMq["G"], TMq["lb"], ALU.subtract)
        self.act(TMq["sk"], TMq["tmp"], AF.Exp)
        ps = self.psf(0)
        for ti, (src, t0, n, c0) in enumerate(tiles):
            c = 64 if n == 128 else 32
            self.mm(ps[0:n, ti * 8:(ti + 1) * 8], sellast[c][0:n, 0:n], TMq["G"][0:n, ti, :])
        self.memset("pool", TMq["tmp"], 0.0)
        self.tt("dve", TMq["tmp"][:, 0:16, :], ps[:, 0:128].rearrange("p (a b) -> p a b", b=8), TMq["G"][:, 0:16, :], ALU.subtract)
        self.tt("dve", TMq["tmp"][0:64, 16, :], ps[0:64, 128:136], TMq["G"][0:64, 16, :], ALU.subtract)
        self.act(TMq["st"], TMq["tmp"], AF.Exp)
        GT = ar.alloc((NT * 2, 8), F32)
        ps = self.psf(1)
        for ti, (src, t0, n, c0) in enumerate(tiles):
            c = 64 if n == 128 else 32
            for nch in range(2):
                j = ti * 2 + nch
                self.mm(ps[:, j * 8:(j + 1) * 8], lastsel[(c, nch)][0:n, :], TMq["eg"][0:n, ti, :])
        self.copy("dve", GT, ps[:, 0:NT * 16].rearrange("p (a b) -> p a b", b=8))
        Grow = ar.alloc((NALL,), F32, parts=8)
        for q0 in range(0, NT, 4):
            ps = self.psf(q0 // 4 % 2)
            for ti in range(q0, min(NT, q0 + 4)):
                (src, t0, n, c0) = tiles[ti]
                self.mm(ps[0:8, (ti - q0) * 128:(ti - q0) * 128 + n], TMq["G"][0:n, ti, :], self.ident_f[0:n, 0:n])
            nc_ = sum(tiles[ti][2] for ti in range(q0, min(NT, q0 + 4)))
            self.copy("dve", Grow[:, tiles[q0][3]:tiles[q0][3] + nc_], ps[0:8, 0:nc_])
        self.tap("Grow", Grow, (8, NALL))
        self.tap("TMst", TMq["st"], (128, NT, 8))
        self.tap("TMsk", TMq["sk"], (128, NT, 8))
        self.tap("GT", GT, (128, NT * 2, 8))
        wb = ar.alloc((KC, 512), BF16)
        NKV = 2121
        pcb = ar.alloc((NKV,), F32)
        cvb = ar.alloc((NKV,), F32)
        knT = ar.alloc((NALL,), BF16)
        vT = ar.alloc((NALL,), BF16)
        qnT = ar.alloc((NOWN,), BF16)
        sqb = ar.alloc((512,), BF16)
        lnb_ = ar.alloc((512,), F32)
        zg = ar.alloc((9, 128), BF16)
        zs = ar.alloc((128,), F32)
        cst = lnb_[0:3, 0:384]
        opq = []
        for i in range(2):
            opq.append({
                "kbg": ar.alloc((4, 128), BF16), "kt": ar.alloc((4, 128), BF16), "vb": ar.alloc((4, 128), BF16),
                "wtok": ar.alloc((4, 128), BF16), "attnT": ar.alloc((4, 128), BF16), "ub": ar.alloc((4, 128), BF16),
                "nW2": ar.alloc((4, 2, 128), BF16), "nAWT": ar.alloc((4, 128), BF16),
            })
        Mb = [ar.alloc((4, 128), BF16) for _ in range(2)]
        Ab = [ar.alloc((4, 128), BF16) for _ in range(2)]
        Yb = [ar.alloc((4, 128), BF16) for _ in range(2)]
        Mp = ar.alloc((4, 128), BF16) if NEU_SINGLE else None
        DMi = ar.alloc((4, 128), BF16)
        DMs = ar.alloc((4, 128), BF16)
        attn = DMs
        Sf = ar.alloc((128,), F32)
        Sbf = ar.alloc((128,), BF16)
        t1 = ar.alloc((128,), F32)
        otok = ar.alloc((128,), F32)
        osq = t1
        ogb = ar.alloc((128,), BF16)
        ssq = ar.alloc((1,), F32)
        rs = ar.alloc((1,), F32)

        def kvcol(tok):
            if tok < 2048:
                return 3 + tok
            if tok < 2080:
                return 2054 + (tok - 2048)
            return 2089 + (tok - 2080)

        def qcol(t):
            if t < 1024:
                return 3 + t
            if t < 1056:
                return 1030 + (t - 1024)
            return 1065 + (t - 1056)
        kv_segs = [(3, 2048, 0), (2054, 32, 2048), (2089, 32, 2080)]
        q_segs = [(3, 1024, 0), (1030, 32, 1024), (1065, 32, 1056)]

        def conv_silu(ncols, ct):
            n = ncols - 3
            self.ts("dve", cvb[:, 3:ncols], pcb[:, 0:n], cw[:, ct, 0:1], ALU.mult, extra_reads=[cw[:, ct, 0:1]])
            for i in range(1, 4):
                self.stt("dve", cvb[:, 3:ncols], pcb[:, i:i + n], cw[:, ct, i:i + 1], cvb[:, 3:ncols], ALU.mult, ALU.add,
                         extra_reads=[cw[:, ct, i:i + 1]])

        def conv_out(which, h, cols3):
            ct = which * 8 + h
            pz = self.psf(1)
            for sgi, c0 in enumerate(cols3):
                self.mm(pz[0:3, sgi * 128:(sgi + 1) * 128], pcb[:, c0:c0 + 3], self.ident_f)
            self.copy("dve", cst, pz[0:3, 0:384])
            S.dma("sp", self.O["conv_p_o"][:, ct * 128:(ct + 1) * 128], cst[:, 0:128], reads=[cst[:, 0:128]], is_output=True)
            for j in range(2):
                S.dma("sp", self.O["conv_s_o"][j, :, ct * 128:(ct + 1) * 128], cst[:, (j + 1) * 128:(j + 2) * 128],
                      reads=[cst[:, (j + 1) * 128:(j + 2) * 128]], is_output=True)

        def normalize(segs, dst, extra_bias):
            for (c0, ln, k0) in segs:
                for o in range(0, ln, 512):
                    n = min(512, ln - o)
                    src = cvb[:, c0 + o:c0 + o + n]
                    self.act(sqb[:, 0:n], src, AF.Square)
                    pz = self.psf(1)
                    self.mm(pz[:, 0:n], ones_b, sqb[:, 0:n])
                    self.act(lnb_[:, 0:n], pz[:, 0:n], AF.Ln, bias=eps6, extra_reads=[eps6])
                    if extra_bias is None:
                        self.act(lnb_[:, 0:n], lnb_[:, 0:n], AF.Exp, scale=-0.5)
                    else:
                        self.act(lnb_[:, 0:n], lnb_[:, 0:n], AF.Exp, scale=-0.5, bias=extra_bias, extra_reads=[extra_bias])
                    self.tt("dve", dst[:, k0 + o:k0 + o + n], src, lnb_[:, 0:n], ALU.mult)

        quads = [list(range(0, 4)), list(range(4, 8)), list(range(8, 12)), list(range(12, 16)), [16]]

        for h in range(8):
            def wload_head(hh):
                for j, coff in enumerate((hh * 128, 1024 + hh * 128, 2048 + hh * 128, OFF_Z + hh * 128)):
                    src = I["w_in"][:, coff:coff + 128].rearrange("(c p) n -> p c n", p=128)
                    S.dma("pool", wb[:, :, j * 128:(j + 1) * 128], src, writes=[wb[:, :, j * 128:(j + 1) * 128]])
            if h == 0:
                wload_head(0)

            def proj_fm(wj, xT, t0, n, dst):
                self._pfm = getattr(self, "_pfm", 0) + 1
                po = self.psf(self._pfm % 2)[:, 0:n]
                for c in range(KC):
                    self.mm(po, wb[:, c, wj * 128:(wj + 1) * 128], xT[:, c, t0:t0 + n], start=(c == 0), stop=(c == KC - 1))
                self.copy("act", dst, po)
            def proj_q():
                proj_fm(0, self.xT_pre, 1021, 3, pcb[:, 0:3])
                for (t0, n) in ((0, 512), (512, 512)):
                    proj_fm(0, self.xT_own, t0, n, pcb[:, 3 + t0:3 + t0 + n])
                proj_fm(0, self.xT_own, 1024, 32, pcb[:, 1030:1062])
                proj_fm(0, self.xT_own, 1056, 32, pcb[:, 1065:1097])
                for j in range(2):
                    self.copy("pool", pcb[:, 1027 + 35 * j:1030 + 35 * j], hist[:, h, 3 * j:3 * j + 3])

            def proj_kv(which):
                self.memset("pool", pcb[:, 0:3], 0.0)
                for (t0, n) in ((0, 512), (512, 512)):
                    proj_fm(which, self.xT_pre, t0, n, pcb[:, 3 + t0:3 + t0 + n])
                for (t0, n) in ((0, 512), (512, 512)):
                    proj_fm(which, self.xT_own, t0, n, pcb[:, 3 + 1024 + t0:3 + 1024 + t0 + n])
                proj_fm(which, self.xT_own, 1024, 32, pcb[:, 2054:2086])
                proj_fm(which, self.xT_own, 1056, 32, pcb[:, 2089:2121])
                for j in range(2):
                    self.copy("pool", pcb[:, 2051 + 35 * j:2054 + 35 * j], hist[:, which * 8 + h, 3 * j:3 * j + 3])

            def conv_stage(which):
                if which == 0:
                    conv_out(0, h, (1024, 1059, 1094))
                    conv_silu(1097, h)
                    self.act(cvb[:, 3:1097], cvb[:, 3:1097], AF.Silu)
                else:
                    conv_out(which, h, (2048, 2083, 2118))
                    conv_silu(NKV, which * 8 + h)
                    self.act(cvb[:, 3:NKV], cvb[:, 3:NKV], AF.Silu)
            proj_q()
            conv_stage(0)
            proj_kv(1)
            normalize(q_segs, qnT, lnqs)
            conv_stage(1)
            proj_kv(2)
            normalize(kv_segs, knT, None)
            conv_stage(2)
            for (c0, ln, k0) in kv_segs:
                self.copy("pool", vT[:, k0:k0 + ln], cvb[:, c0:c0 + ln])
            for ti, (t0, n) in enumerate(OWN_TILES):
                pz = self.psf(0)[0:n, 0:128]
                for c in range(KC):
                    self.mm(pz, self.xT_own[:, c, t0:t0 + n], wb[:, c, 384:512], start=(c == 0), stop=(c == KC - 1))
                self.act(zs[0:n, :], pz, AF.Silu)
                self.tt("dve", zg[0:n, ti, :], zs[0:n, :], normw[0:n, :], ALU.mult)
            if h == 0:
                self.tap("knT0", knT, (128, NALL))
                self.tap("qnT0", qnT, (128, NOWN))
                self.tap("vT0", vT, (128, NALL))
            if h + 1 < 8:
                wload_head(h + 1)
            self.memset("pool", Sf, 0.0)
            self.memset("pool", Sbf, 0.0)
            def gen_L(qi, h=h):
                quad = quads[qi]
                ops = opq[qi % 2]
                own_quad = tiles[quad[0]][0] == "own"
                nq = len(quad)
                nn = tiles[quad[0]][2]
                v3 = lambda p: p[0:nn, 0:nq * 128].rearrange("p (a b) -> p a b", b=128)[:, :, 0:nn]
                pb = self.psb(2)
                for j, ti in enumerate(quad):
                    (src, t0, n, c0) = tiles[ti]
                    self.tr(pb[0:n, j * 256:j * 256 + 128], knT[:, c0:c0 + n], self.ident_b)
                    self.tr(pb[0:n, j * 256 + 128:j * 256 + 256], vT[:, c0:c0 + n], self.ident_b)
                yield
                for j, ti in enumerate(quad):
                    (src, t0, n, c0) = tiles[ti]
                    kps = pb[0:n, j * 256:j * 256 + 128]
                    vps = pb[0:n, j * 256 + 128:j * 256 + 256]
                    sc = lambda nm: TMq[nm][0:n, ti, h:h + 1]
                    self.ts("dve", ops["kbg"][0:n, j, :], kps, sc("sk"), ALU.mult, extra_reads=[sc("sk")])
                    self.ts("dve", ops["kt"][0:n, j, :], kps, sc("st"), ALU.mult, extra_reads=[sc("st")])
                    self.ts("dve", ops["vb"][0:n, j, :], vps, sc("beta"), ALU.mult, extra_reads=[sc("beta")])
                pk = self.psf(3)
                pe_ = self.psf(4)
                for j, ti in enumerate(quad):
                    (src, t0, n, c0) = tiles[ti]
                    c = 64 if n == 128 else 32
                    self.mm(pk[0:n, j * 128:j * 128 + n], knT[:, c0:c0 + n], knT[:, c0:c0 + n])
                    self.mm(pe_[0:n, j * 128:j * 128 + n], nsel[:, h, 0:n], Grow[:, c0:c0 + n], start=True, stop=False)
                    msk = maskMi[c] if own_quad else maskMs[c]
                    self.mm(pe_[0:n, j * 128:j * 128 + n], self.ident_b[0:n, 0:n], msk[0:n, 0:n], start=False, stop=True)
                yield
                for j, ti in enumerate(quad):
                    (src, t0, n, c0) = tiles[ti]
                    gb_ = TMq["G"][0:n, ti, h:h + 1]
                    dst = DMi if own_quad else DMs
                    self.act(dst[0:n, j, 0:n], pe_[0:n, j * 128:j * 128 + n], AF.Exp, bias=gb_, extra_reads=[gb_])
                    if own_quad:
                        self.tt("pool", DMs[0:n, j, 0:n], DMi[0:n, j, 0:n], offd[0:n, 0:n], ALU.mult)
                M0, A0, Y0 = Mb[0], Ab[0], Yb[0]
                for j, ti in enumerate(quad):
                    (src, t0, n, c0) = tiles[ti]
                    bt_ = TMq["beta"][0:n, ti, h:h + 1]
                    self.stt("dve", M0[0:n, j, 0:n], pk[0:n, j * 128:j * 128 + n], bt_, DMs[0:n, j, 0:n], ALU.mult, ALU.mult,
                             extra_reads=[bt_])
                yield
                pt = self.psb(2)
                for j, ti in enumerate(quad):
                    n = tiles[ti][2]
                    self.tr(pt[0:n, j * 128:j * 128 + n], M0[0:n, j, 0:n], self.ident_b[0:n, 0:n])
                ptv = pt[0:nn, 0:nq * 128].rearrange("p (a b) -> p a b", b=128)[:, :, 0:nn]
                self.copy("act", A0[0:nn, 0:nq, 0:nn], ptv)
                self.stt("dve", Y0[0:nn, 0:nq, 0:nn], ptv, -1.0, ident4[0:nn, 0:nq, 0:nn], ALU.mult, ALU.add)
                yield
                cur = 0
                ycur = 0
                pend = None
                p5, p6, p7 = self.psf(5), self.psf(6), self.psf(7)

                def y_mm(Mlhs, Ysrc):
                    for j, ti in enumerate(quad):
                        n = tiles[ti][2]
                        self.mm(p7[0:n, j * 128:j * 128 + n], self.ident_b[0:n, 0:n], Ysrc[0:n, j, 0:n], start=True, stop=False)
                        self.mm(p7[0:n, j * 128:j * 128 + n], Mlhs[0:n, j, 0:n], Ysrc[0:n, j, 0:n], start=False, stop=True)
                for it in range(5):
                    Mc, Ac = Mb[cur], Ab[cur]
                    Mn, An = Mb[1 - cur], Ab[1 - cur]
                    for j, ti in enumerate(quad):
                        n = tiles[ti][2]
                        self.mm(p5[0:n, j * 128:j * 128 + n], Ac[0:n, j, 0:n], Mc[0:n, j, 0:n])
                        if it < 4:
                            self.mm(p6[0:n, j * 128:j * 128 + n], Mc[0:n, j, 0:n], Ac[0:n, j, 0:n])
                    if pend is not None:
                        y_mm(Mc, Yb[ycur])
                    yield
                    self.copy("act", Mn[0:nn, 0:nq, 0:nn], v3(p5))
                    if it < 4:
                        self.copy("dve", An[0:nn, 0:nq, 0:nn], v3(p6))
                    if pend is not None:
                        self.copy("dve", Yb[1 - ycur][0:nn, 0:nq, 0:nn], v3(p7))
                        ycur = 1 - ycur
                    pend = True
                    cur = 1 - cur
                    yield
                y_mm(Mb[cur], Yb[ycur])
                yield
                self.copy("dve", Yb[1 - ycur][0:nn, 0:nq, 0:nn], v3(p7))
                cur = 1 - ycur
                Yf = Yb[cur]
                pu, pw = self.psf(3), self.psf(4)
                for j, ti in enumerate(quad):
                    n = tiles[ti][2]
                    self.mm(pu[0:n, j * 128:(j + 1) * 128], Yf[0:n, j, 0:n], ops["vb"][0:n, j, :])
                    self.mm(pw[0:n, j * 128:(j + 1) * 128], Yf[0:n, j, 0:n], ops["kbg"][0:n, j, :])
                yield
                self.copy("act", ops["ub"][0:nn, 0:nq, :], pu[0:nn, 0:nq * 128].rearrange("p (a b) -> p a b", b=128))
                self.copy("dve", ops["wtok"][0:nn, 0:nq, :], pw[0:nn, 0:nq * 128].rearrange("p (a b) -> p a b", b=128))
                p56 = (self.psf(5), self.psf(6))
                cc = 64 if nn == 128 else 32
                for j, ti in enumerate(quad):
                    for nch in range(2):
                        r = slice(nch * cc, (nch + 1) * cc)
                        self.mm(p56[nch][:, j * 128:(j + 1) * 128], ops["wtok"][r, j, :], ops["kt"][r, j, :])
                yield
                self.ts("dve", ops["nW2"][:, 0:nq, 0, :], p56[0][:, 0:nq * 128].rearrange("p (a b) -> p a b", b=128), -1.0, ALU.mult)
                self.act(ops["nW2"][:, 0:nq, 1, :], p56[1][:, 0:nq * 128].rearrange("p (a b) -> p a b", b=128), AF.Copy, scale=-1.0)
                if own_quad:
                    pq = self.psf(3)
                    for j, ti in enumerate(quad):
                        (src, t0, n, c0) = tiles[ti]
                        self.mm(pq[0:n, j * 128:j * 128 + n], qnT[:, t0:t0 + n], knT[:, c0:c0 + n])
                    yield
                    self.tt("dve", attn[0:nn, 0:nq, 0:nn], v3(pq), DMi[0:nn, 0:nq, 0:nn], ALU.mult)
                    pt = self.psb(2)
                    for j, ti in enumerate(quad):
                        n = tiles[ti][2]
                        self.tr(pt[0:n, j * 128:j * 128 + n], attn[0:n, j, 0:n], self.ident_b[0:n, 0:n])
                    yield
                    self.copy("act", ops["attnT"][0:nn, 0:nq, 0:nn],
                              pt[0:nn, 0:nq * 128].rearrange("p (a b) -> p a b", b=128)[:, :, 0:nn])
                    p7 = self.psf(7)
                    for j, ti in enumerate(quad):
                        n = tiles[ti][2]
                        self.mm(p7[:, j * 128:j * 128 + n], ops["wtok"][0:n, j, :], ops["attnT"][0:n, j, 0:n])
                    yield
                    self.ts("dve", ops["nAWT"][:, 0:nq, 0:nn],
                            p7[:, 0:nq * 128].rearrange("p (a b) -> p a b", b=128)[:, :, 0:nn], -1.0, ALU.mult)
                if h == 0 and qi == 0:
                    self.tap("Yf0", Yf, (128, 4, 128))
                    self.tap("M00", Mb[0], (128, 4, 128))

            def gen_S(qi, h=h):
                quad = quads[qi]
                ops = opq[qi % 2]
                for j, ti in enumerate(quad):
                    (src, t0, n, c0) = tiles[ti]
                    c = 64 if n == 128 else 32
                    pS = self.psf(1)
                    for nch in range(n // c):
                        r = slice(nch * c, (nch + 1) * c)
                        if n == 64:
                            S.dma("sp", Sf, I["S0_s"][nch, h], writes=[Sf])
                            self.copy("act", Sbf, Sf)
                        self.mm(pS[:, 128:256], ops["nW2"][:, j, nch, :], Sbf, start=True, stop=False)
                        self.mm(pS[:, 128:256], ops["kt"][r, j, :], ops["ub"][r, j, :], start=False, stop=True)
                        if src == "own":
                            self.mm(pS[r, 256:384], qnT[:, t0 + nch * c:t0 + (nch + 1) * c], Sbf)
                            self.mm(pS[r, 384:512], ops["nAWT"][:, j, r], Sbf, start=True, stop=False)
                            self.mm(pS[r, 384:512], ops["attnT"][r, j, r], ops["ub"][r, j, :], start=False, stop=True)
                        yield
                        gt_ = GT[:, ti * 2 + nch, h:h + 1]
                        self.stt("dve", Sbf, Sf, gt_, pS[:, 128:256], ALU.mult, ALU.add, extra_reads=[gt_])
                        self.stt("dve", Sf, Sf, gt_, pS[:, 128:256], ALU.mult, ALU.add, extra_reads=[gt_])
                        if src == "own":
                            eg_ = TMq["eg"][r, ti, h:h + 1]
                            self.act(t1[r, :], pS[r, 256:384], AF.Copy, scale=eg_, extra_reads=[eg_])
                            self.tt("dve", otok[r, :], t1[r, :], pS[r, 384:512], ALU.add)
                        if n == 64:
                            S.dma("sp", self.O["S_s_o"][nch, h], Sf, reads=[Sf], is_output=True)
                        yield
                    if src == "own" and t0 == 896:
                        S.dma("sp", self.O["S_p_o"][h], Sf, reads=[Sf], is_output=True)
                    if src == "own":
                        oti = t0 // 128
                        self.act(osq[0:n, :], otok[0:n, :], AF.Square, accum_out=ssq[0:n, :])
                        self.act(rs[0:n, :], ssq[0:n, :], AF.Ln, scale=1.0 / 128.0, bias=eps6[0:n, :], extra_reads=[eps6[0:n, :]])
                        self.act(rs[0:n, :], rs[0:n, :], AF.Exp, scale=-0.5)
                        self.stt("dve", ogb[0:n, :], otok[0:n, :], rs[0:n, :], zg[0:n, oti, :], ALU.mult, ALU.mult,
                                 extra_reads=[rs[0:n, :]])
                        pg = self.psb(0)
                        self.tr(pg[:, 0:n], ogb[0:n, :], self.ident_b[0:n, 0:n])
                        yield
                        self.copy("act", self.mixedT[:, h, t0:t0 + n], pg[:, 0:n])

            def run_gens(gens):
                gens = list(gens)
                while gens:
                    for g_ in list(gens):
                        try:
                            next(g_)
                        except StopIteration:
                            gens.remove(g_)
            run_gens([gen_L(0)])
            for qi in range(len(quads)):
                gl = [gen_S(qi)]
                if qi + 1 < len(quads):
                    gl.append(gen_L(qi + 1))
                run_gens(gl)
        self.tap("mixedT_g", self.mixedT, (128, KC, NOWN))


    def wload(self, buf, w_ap, c0, ncols, kc=KC):
        src = w_ap[:, c0:c0 + ncols].rearrange("(c p) n -> p c n", p=128)
        dst = buf[:, :, 0:ncols]
        self.S.dma("pool", dst, src, writes=[dst])

    def phase_sb(self):
        ar, S = self.ar, self.S
        wb = [ar.alloc((KC, 512), BF16) for _ in range(2)]
        stage = [ar.alloc((512,), F32) for _ in range(2)]
        KT = ar.alloc((4, NALL), BF16)
        VT = ar.alloc((17, 512), BF16)
        QT2 = [ar.alloc((NOWN,), BF16) for _ in range(2)]
        Eb = [ar.alloc((512,), F32) for _ in range(2)]
        SPb = [ar.alloc((512,), BF16) for _ in range(2)]
        Wb = [ar.alloc((512,), BF16) for _ in range(2)]
        Xb = [ar.alloc((512,), F32) for _ in range(2)] + [ar.alloc((64,), F32)]
        Eb.append(ar.alloc((64,), F32))
        SPb.append(ar.alloc((64,), BF16))
        Wb.append(ar.alloc((64,), BF16))
        KTn = ar.alloc((4, 64), BF16)
        Vn = ar.alloc((512,), BF16)
        ntri_i = ar.alloc((128,), BF16)
        ntri_c = ar.alloc((128,), BF16)
        zb = ar.alloc((512,), BF16)
        KTc = ar.alloc((2, 2048), BF16)
        Vc = ar.alloc((2, 16, 128), BF16)
        m64 = ar.alloc((64,), BF16, parts=64)
        ones64 = ar.alloc((64,), BF16, parts=64)
        one_c = self.eps_tile(1.0)
        self.memset("pool", m64, 0.0)
        self.memset("pool", ones64, 1.0)
        for s_ in range(2):
            sq = slice(32 * s_, 32 * s_ + 32)
            self.asel(m64[sq, sq], ones64[sq, sq], [[1, 32]], ALU.is_ge, 0.0, -1, -1)
        self.memset("pool", zb, 0.0)
        self.memset("pool", ntri_i, -1.0)
        self.asel(ntri_i, ntri_i, [[-1, 128]], ALU.is_ge, 0.0, 0, 1)
        self.memset("pool", ntri_c, -1.0)
        self.asel(ntri_c, ntri_c, [[1, 128]], ALU.is_gt, 0.0, 0, -1)
        all_tiles = [("pre", t0, n) for (t0, n) in PRE_TILES] + [("own", t0, n) for (t0, n) in OWN_TILES]
        import os
        sbstop = int(os.environ.get("SB_STOP", "99"))
        if sbstop <= 1:
            return
        nw = 0
        ev = 0
        for g in range(2):
            for which in ("k", "v"):
                coff = OFF_SB + (1024 if which == "k" else 2048) + g * 512
                dst = self.O["kb"] if which == "k" else self.O["vb"]
                w = wb[nw % 2]
                nw += 1
                self.wload(w, self.I["w_in"], coff, 512)

                def emit_ktr(ti, kpos, n):
                    pt = self.psf(4 + (ti % 2))
                    stf = stage[ti % 2]
                    for h in range(4):
                        self.tr(pt[:, h * 128:(h + 1) * 128], stf[:, h * 128:(h + 1) * 128], self.ident_f)
                    src_ps = pt[:, 0:512].rearrange("p (h t) -> p h t", h=4)[:, :, 0:n]
                    self.copy("act", KT[:, :, kpos:kpos + n], src_ps)
                pend_k = None
                for ti, (src, t0, n) in enumerate(all_tiles):
                    xT = self.xT_pre if src == "pre" else self.xT_own
                    kpos = t0 if src == "pre" else NPRE + t0
                    po = self.psf(6 + (ti % 2))[0:n, :]
                    for c in range(KC):
                        self.mm(po, xT[:, c, t0:t0 + n], w[:, c, :], start=(c == 0), stop=(c == KC - 1))
                    st = stage[ti % 2][0:n, :]
                    if which == "k":
                        self.copy("dve", st, po)
                        if src == "own":
                            S.dma("sp", dst[t0:t0 + n, g * 512:(g + 1) * 512], st, reads=[st], is_output=True)
                        if pend_k is not None:
                            emit_ktr(*pend_k)
                        pend_k = (ti, kpos, n)
                    else:
                        if src == "own":
                            self.copy("dve", st, po)
                            S.dma("sp", dst[t0:t0 + n, g * 512:(g + 1) * 512], st, reads=[st], is_output=True)
                            self.copy("act", VT[0:n, ti, :], st)
                        else:
                            self.copy("act", VT[0:n, ti, :], po)
                if which == "k" and pend_k is not None:
                    emit_ktr(*pend_k)
            self.copy("dve", KTn, KT[:, :, NPRE + 1024:NPRE + 1088])
            self.copy("dve", Vn[0:64, :], VT[0:64, 16, :])
            wq = wb[nw % 2]
            nw += 1
            self.wload(wq, self.I["w_in"], OFF_SB + g * 512, 512)
            def q_proj(hh, gi, wq=wq, g=g):
                (t0, n) = tok_groups(NOWN)[gi]
                po = self.psf(6)[:, 0:n]
                for c in range(KC):
                    self.mm(po, wq[:, c, hh * 128:(hh + 1) * 128], self.xT_own[:, c, t0:t0 + n],
                            start=(c == 0), stop=(c == KC - 1))
                self.act(QT2[(g * 4 + hh) % 2][:, t0:t0 + n], po, AF.Copy, scale=float(128 ** -0.5))
            for h in range(4):
                hg = g * 4 + h
                QT = QT2[hg % 2]
                if h == 0:
                    for gi in range(3):
                        q_proj(0, gi)
                def prompt_stream(sb, h=h, hg=hg, QT=QT):
                    nonlocal ev
                    blocks = []
                    for kb in range(4 * sb + 3, -1, -1):
                        cs = max(0, (kb - 4 * sb)) * 128
                        blocks.append(("own", kb, cs, kb >= 4 * sb))
                    for kb in range(7, -1, -1):
                        blocks.append(("pre", kb, 0, False))
                    A = self.psf(2 + sb)
                    OT = self.psf(4 + sb)
                    q0 = sb * 512
                    self.mm(A, zb[:, 0:128], zb[:, 0:512], start=True, stop=False, skip=True)
                    self.mm(OT, zb[:, 0:128], zb[:, 0:512], start=True, stop=False, skip=True)
                    yield
                    for (src, kb, cs, diag) in blocks:
                        kpos = kb * 128 if src == "pre" else NPRE + kb * 128
                        vt = kb if src == "pre" else 8 + kb
                        zt = self.psf(ev % 2)
                        E, SP, W, X = Eb[sb], SPb[sb], Wb[sb], Xb[sb]
                        ev += 1
                        kt = KT[:, h, kpos:kpos + 128]
                        self.mm(zt[:, cs:512], kt, QT[:, q0 + cs:q0 + 512])
                        self.act(E[:, cs:512], zt[:, cs:512], AF.Exp)
                        if src == "pre":
                            self.act(SP[:, cs:512], E[:, cs:512], AF.Ln, bias=one_c, scale=self.pre_bias,
                                     extra_reads=[one_c, self.pre_bias])
                        else:
                            self.act(SP[:, cs:512], E[:, cs:512], AF.Ln, bias=one_c, extra_reads=[one_c])
                        if diag:
                            self.asel(SP[:, cs:cs + 128], SP[:, cs:cs + 128], [[1, 128]], ALU.is_ge, 0.0, -1, -1)
                        yield
                        self.mm(A[:, cs:512], ntri_i, SP[:, cs:512], start=False, stop=False, skip=True)
                        self.act(X[:, cs:512], A[:, cs:512], AF.Exp)
                        self.tt("dve", W[:, cs:512], E[:, cs:512], X[:, cs:512], ALU.mult)
                        if diag:
                            self.asel(W[:, cs:cs + 128], W[:, cs:cs + 128], [[1, 128]], ALU.is_ge, 0.0, -1, -1)
                        yield
                        self.mm(A[:, cs:512], ntri_c, SP[:, cs:512], start=False, stop=False, skip=True)
                        self.mm(OT[:, cs:512], VT[:, vt, h * 128:(h + 1) * 128], W[:, cs:512], start=False, stop=False, skip=True)
                        yield
                    self.copy("dve", self.mixedT[:, 8 + hg, sb * 512:(sb + 1) * 512], OT)

                def sample_stream(h=h, hg=hg, QT=QT):
                    nonlocal ev
                    par = hg % 2
                    wfree = wb[nw % 2]
                    for s_ in range(2):
                        kst = wfree[:, 8 * par + 4 * s_:8 * par + 4 * s_ + 4, :].rearrange("p a (b d) -> p (a b) d", d=128)
                        S.dma("pool", kst, self.I["ck"][s_, :, hg * 128:(hg + 1) * 128].rearrange("(b p) d -> p b d", p=128),
                              writes=[kst])
                        S.dma("pool", Vc[:, s_, :, :],
                              self.I["cv"][s_, :, hg * 128:(hg + 1) * 128].rearrange("(b p) d -> p b d", p=128),
                              writes=[Vc[:, s_, :, :]])
                    yield
                    for s_ in range(2):
                        kst = wfree[:, 8 * par + 4 * s_:8 * par + 4 * s_ + 4, :].rearrange("p a (b d) -> p (a b) d", d=128)
                        for half in range(2):
                            pb = self.psb(6)
                            for c in range(8):
                                self.tr(pb[:, c * 128:(c + 1) * 128], kst[:, half * 8 + c, :], self.ident_b)
                            self.copy("dve", KTc[:, s_, half * 1024:(half + 1) * 1024], pb[:, 0:1024])
                            yield
                    A = self.psf(7)[:, 0:64]
                    OT = self.psf(7)[:, 128:192]
                    self.mm(self.psf(7)[:, 0:192], zb[:, 0:128], zb[:, 0:192], start=True, stop=False, skip=True)
                    qs = QT[:, 1024:1088]
                    for blk in [-1] + list(range(15, -1, -1)):
                        zt = self.psf(ev % 2)
                        E, SP, W, X = Eb[2], SPb[2], Wb[2], Xb[2]
                        ev += 1
                        if blk < 0:
                            kn = KTn[:, h, :]
                            self.mm(zt[0:64, 0:64], kn, qs)
                            self.act(E[0:64, 0:64], zt[0:64, 0:64], AF.Exp)
                            self.act(SP[0:64, 0:64], E[0:64, 0:64], AF.Ln, bias=one_c[0:64, :], extra_reads=[one_c[0:64, :]])
                            self.tt("pool", SP[0:64, 0:64], SP[0:64, 0:64], m64, ALU.mult)
                            yield
                            self.mm(A[0:64, :], ntri_i[0:64, 0:64], SP[0:64, 0:64], start=False, stop=False, skip=True)
                            self.act(X[0:64, 0:64], A[0:64, :], AF.Exp)
                            self.tt("dve", W[0:64, 0:64], E[0:64, 0:64], X[0:64, 0:64], ALU.mult)
                            self.tt("pool", W[0:64, 0:64], W[0:64, 0:64], m64, ALU.mult)
                            yield
                            self.mm(A, ntri_c[0:64, :], SP[0:64, 0:64], start=False, stop=False, skip=True)
                            self.mm(OT, Vn[0:64, h * 128:(h + 1) * 128], W[0:64, 0:64], start=False, stop=False, skip=True)
                            yield
                        else:
                            ks = [KTc[:, s_, blk * 128:(blk + 1) * 128] for s_ in range(2)]
                            cs2 = [slice(32 * s_, 32 * s_ + 32) for s_ in range(2)]
                            for s_ in range(2):
                                self.mm(zt[:, cs2[s_]], ks[s_], qs[:, cs2[s_]])
                            self.act(E[:, 0:64], zt[:, 0:64], AF.Exp)
                            self.act(SP[:, 0:64], E[:, 0:64], AF.Ln, bias=one_c, extra_reads=[one_c])
                            yield
                            self.mm(A, ntri_i, SP[:, 0:64], start=False, stop=False, skip=True)
                            self.act(X[:, 0:64], A, AF.Exp)
                            self.tt("dve", W[:, 0:64], E[:, 0:64], X[:, 0:64], ALU.mult)
                            yield
                            self.mm(A, ntri_c, SP[:, 0:64], start=False, stop=False, skip=True)
                            for s_ in range(2):
                                self.mm(OT[:, cs2[s_]], Vc[:, s_, blk, :], W[:, cs2[s_]], start=False, stop=False, skip=True)
                            if h < 3 and 13 <= blk <= 15:
                                q_proj(h + 1, 15 - blk)
                            yield
                    self.copy("dve", self.mixedT[:, 8 + hg, 1024:1088], OT)

                gens = [prompt_stream(0), prompt_stream(1), sample_stream()]
                while gens:
                    for g_ in list(gens):
                        try:
                            next(g_)
                        except StopIteration:
                            gens.remove(g_)
        self.tap("mixedT", self.mixedT, (128, KC, NOWN))

    def phase_wout(self):
        ar, S = self.ar, self.S
        m0 = ar.mark()
        wo = ar.alloc((KC, D), BF16)
        for g in range(4):
            src = self.I["w_out"][:, g * 512:(g + 1) * 512].rearrange("(c p) n -> p c n", p=128)
            S.dma("pool", wo[:, :, g * 512:(g + 1) * 512], src, writes=[wo[:, :, g * 512:(g + 1) * 512]])
        po_ = self.xT_pre_off
        gb = ar.alloc((D,), F32, at=po_)
        bb = ar.alloc((D,), F32, at=po_ + 8192)
        S.dma("sp", gb, self.I["ln1_g"].partition_broadcast(128), writes=[gb])
        S.dma("sp", bb, self.I["ln1_b"].partition_broadcast(128), writes=[bb])
        self.x1T = self.xT_own
        xs = [ar.alloc((D,), F32, at=po_ + 16384 + i * 8192) for i in range(2)]
        ys = [ar.alloc((D,), F32) for _ in range(2)]
        yb = [ar.alloc((D,), BF16) for _ in range(2)]
        stats = ar.alloc((4, 6), F32)
        mv = ar.alloc((2,), F32)
        rstd = ar.alloc((1,), F32)
        self.x1_tokens = []
        def emit_tr(ybf, t0, n):
            for half in range(2):
                pb = self.psb(4 + half)
                for c in range(8):
                    cc = half * 8 + c
                    self.tr(pb[:, c * 128:c * 128 + n], ybf[:, cc * 128:(cc + 1) * 128], self.ident_b[0:n, 0:n])
                src_ps = pb[:, 0:1024].rearrange("p (c t) -> p c t", c=8)[:, :, 0:n]
                self.copy("dve" if half == 0 else "act", self.x1T[:, half * 8:(half + 1) * 8, t0:t0 + n], src_ps)
        pend_tr = None
        for ti, (t0, n) in enumerate(OWN_TILES):
            x = xs[ti % 2][0:n, :]
            y = ys[ti % 2][0:n, :]
            S.dma("sp", x, self.I["x_own"][t0:t0 + n, :], writes=[x])
            for g in range(4):
                po = self.psf(g)[0:n, :]
                for c in range(KC):
                    self.mm(po, self.mixedT[:, c, t0:t0 + n], wo[:, c, g * 512:(g + 1) * 512],
                            start=(c == 0), stop=(c == KC - 1))
                self.stt("dve", y[:, g * 512:(g + 1) * 512], x[:, g * 512:(g + 1) * 512], ALPHA, po, ALU.mult, ALU.add)
            if pend_tr is not None:
                emit_tr(*pend_tr)
            self.layernorm(y, n, gb, bb, stats, mv, rstd)
            tok = S.dma("sp", self.x1s[t0:t0 + n, :], y, reads=[y])
            self.x1_tokens.append(tok)
            ybf = yb[ti % 2][0:n, :]
            self.copy("act", ybf, y)
            pend_tr = (ybf, t0, n)
        emit_tr(*pend_tr)
        ar.release(m0)

    def layernorm(self, y, n, gb, bb, stats, mv, rstd):
        S = self.S
        for g in range(4):
            S.op("dve", lambda e: e.bn_stats(stats[0:n, g, :], y[:, g * 512:(g + 1) * 512]),
                 reads=[y[:, g * 512:(g + 1) * 512]], writes=[stats[0:n, g, :]])
        S.op("dve", lambda e: e.bn_aggr(mv[0:n, :], stats[0:n, :, :].rearrange("p a b -> p (a b)")),
             reads=[stats[0:n, :, :]], writes=[mv[0:n, :]])
        self.act(rstd[0:n, :], mv[0:n, 1:2], AF.Ln, bias=self.eps_tile(LN_EPS)[0:n, :], scale=1.0,
                 extra_reads=[self.eps_tile(LN_EPS)[0:n, :]])
        self.act(rstd[0:n, :], rstd[0:n, :], AF.Exp, bias=0.0, scale=-0.5)
        self.stt("dve", y, y, mv[0:n, 0:1], gb[0:n, :], ALU.subtract, ALU.mult, extra_reads=[mv[0:n, 0:1]])
        self.stt("dve", y, y, rstd[0:n, :], bb[0:n, :], ALU.mult, ALU.add, extra_reads=[rstd[0:n, :]])

    def eps_tile(self, val):
        if not hasattr(self, "_eps"):
            self._eps = {}
        if val not in self._eps:
            t = self.nc.alloc_sbuf_tensor(f"eps_{len(self._eps)}", [128, 1], F32)
            self.memset("pool", t[:], val)
            self._eps[val] = t[:]
        return self._eps[val]

    def phase_ffn(self):
        ar, S = self.ar, self.S
        ar.release(self.xT_pre_off)
        m0 = ar.mark()
        hT = ar.alloc((64, NOWN), BF16)
        groups = tok_groups(NOWN)
        m1 = ar.mark()
        wu = [ar.alloc((KC, 256), BF16) for _ in range(2)]
        rl = [ar.alloc((512,), F32) for _ in range(2)]
        self.wload(wu[0], self.I["w_up"], 0, 256)
        k = 0
        for s in range(DFF // 256):
            if s + 1 < DFF // 256:
                self.wload(wu[(s + 1) % 2], self.I["w_up"], (s + 1) * 256, 256)
            w = wu[s % 2]
            for j in range(2):
                ft = s * 2 + j
                for gi, (t0, n) in enumerate(groups):
                    bank = k % 4
                    po = self.psf(bank)[:, 0:n]
                    for c in range(KC):
                        self.mm(po, w[:, c, j * 128:(j + 1) * 128], self.x1T[:, c, t0:t0 + n],
                                start=(c == 0), stop=(c == KC - 1))
                    r = rl[k % 2][:, 0:n]
                    self.act(r, po, AF.Relu)
                    self.tt("dve" if k % 2 == 0 else "pool", hT[:, ft, t0:t0 + n], r, r, ALU.mult)
                    k += 1
        ar.release(m1)
        wd = [ar.alloc((64, 128), BF16, at=self.xT_own_off + i * 16384) for i in range(2)]
        oT = [ar.alloc((NOWN,), F32) for _ in range(2)]
        tk = [ar.alloc((128,), F32) for _ in range(2)]
        y2_tokens = []

        def wdload(i):
            src = self.I["w_down"][:, i * 128:(i + 1) * 128].rearrange("(c p) n -> p c n", p=128)
            S.dma("pool", wd[i % 2], src, writes=[wd[i % 2]])
        wdload(0)
        kkc = [0]

        def emit_dn(o, ct):
            for ti, (t0, n) in enumerate(OWN_TILES):
                kk = kkc[0]
                bank = 4 + (kk % 4)
                pt = self.psf(bank)[0:n, 0:128]
                self.tr(pt, o[:, t0:t0 + n], self.ident_f)
                st = tk[kk % 2][0:n, :]
                self.copy("dve" if kk % 2 == 0 else "act", st, pt)
                tok = S.dma("sp", self.y2s[t0:t0 + n, ct * 128:(ct + 1) * 128], st, reads=[st])
                y2_tokens.append(tok)
                kkc[0] += 1
        pend_dn = None
        for ct in range(16):
            if ct + 1 < 16:
                wdload(ct + 1)
            w = wd[ct % 2]
            o = oT[ct % 2]
            for gi, (t0, n) in enumerate(groups):
                bank = gi
                po = self.psf(bank)[:, 0:n]
                for c in range(64):
                    self.mm(po, w[:, c, :], hT[:, c, t0:t0 + n], start=(c == 0), stop=(c == 63))
                self.copy("act" if gi % 2 == 0 else "dve", o[:, t0:t0 + n], po)
            if pend_dn is not None:
                emit_dn(*pend_dn)
            pend_dn = (o, ct)
        emit_dn(*pend_dn)
        if True:
            pass
        ar.release(m0)
        gb = ar.alloc((D,), F32)
        bb = ar.alloc((D,), F32)
        S.dma("sp", gb, self.I["ln2_g"].partition_broadcast(128), writes=[gb])
        S.dma("sp", bb, self.I["ln2_b"].partition_broadcast(128), writes=[bb])
        xs = [ar.alloc((D,), F32) for _ in range(2)]
        ys = [ar.alloc((D,), F32) for _ in range(2)]
        stats = ar.alloc((4, 6), F32)
        mv = ar.alloc((2,), F32)
        rstd = ar.alloc((1,), F32)
        for ti, (t0, n) in enumerate(OWN_TILES):
            x = xs[ti % 2][0:n, :]
            y = ys[ti % 2][0:n, :]
            S.dma("sp", x, self.x1s[t0:t0 + n, :], writes=[x], after=self.x1_tokens)
            S.dma("sp", y, self.y2s[t0:t0 + n, :], writes=[y], after=y2_tokens)
            self.stt("dve", y, x, ALPHA, y, ALU.mult, ALU.add)
            self.layernorm(y, n, gb, bb, stats, mv, rstd)
            S.dma("sp", self.O["y"][t0:t0 + n, :], y, reads=[y], is_output=True)
        ar.release(m0)


_PROG = {}


def get_prog(stop_after=None, dbg=()):
    key = (stop_after, tuple(dbg))
    if key not in _PROG:
        _PROG[key] = Prog(stop_after, dbg)
    return _PROG[key]


def core_inputs(c, inp):
    b, h = c // 2, c % 2
    f = np.float32
    xp = inp["x_prompt"][b]
    x_own = np.concatenate([xp[h * 1024:(h + 1) * 1024], inp["x_sample"][2 * c], inp["x_sample"][2 * c + 1]], 0)
    x_pre = xp[0:1024] if h == 1 else np.zeros((1024, D), f)
    pre_bias = np.full((128, 1), 1.0 if h == 1 else 0.0, f)
    m = {
        "x_own": x_own, "x_pre": x_pre,
        "conv_s": inp["state_gdn_conv"][0, 2 * c:2 * c + 2],
        "S0_s": inp["state_gdn_S"][0, 2 * c:2 * c + 2],
        "ck": inp["cache_sb_k"][0, 2 * c:2 * c + 2].reshape(2, 2048, 1024),
        "cv": inp["cache_sb_v"][0, 2 * c:2 * c + 2].reshape(2, 2048, 1024),
        "w_in": inp["w_in"][0], "conv_w": inp["conv_w"][0], "a_log": inp["a_log"][0],
        "dt_bias": inp["dt_bias"][0], "gdn_norm_w": inp["gdn_norm_w"][0], "w_out": inp["w_out"][0],
        "ln1_g": inp["ln1_g"][0], "ln1_b": inp["ln1_b"][0], "w_up": inp["w_up"][0],
        "w_down": inp["w_down"][0], "ln2_g": inp["ln2_g"][0], "ln2_b": inp["ln2_b"][0],
        "pre_bias": pre_bias,
    }
    return {k: np.ascontiguousarray(v, dtype=f) for k, v in m.items()}


def kernel(**inputs):
    inp = {k: np.asarray(v) for k, v in inputs.items()}
    prog = get_prog()
    in_maps = [core_inputs(c, inp) for c in range(8)]
    res = run_bass_kernel_spmd(prog.nc, in_maps, core_ids=list(range(8)))
    R = res.results
    f = np.float32
    y_p = np.zeros((4, 2048, D), f)
    y_s = np.zeros((16, 32, D), f)
    conv_p = np.zeros((1, 4, 3, 3072), f)
    S_p = np.zeros((1, 4, 8, 128, 128), f)
    k_p = np.zeros((1, 4, 2048, 8, 128), f)
    v_p = np.zeros((1, 4, 2048, 8, 128), f)
    conv_s = np.zeros((1, 16, 3, 3072), f)
    S_s = np.zeros((1, 16, 8, 128, 128), f)
    k_s = np.zeros((1, 16, 32, 8, 128), f)
    v_s = np.zeros((1, 16, 32, 8, 128), f)
    for c in range(8):
        b, h = c // 2, c % 2
        r = R[c]
        y = np.asarray(r["y"])
        kb = np.asarray(r["kb"]).reshape(NOWN, 8, 128)
        vb = np.asarray(r["vb"]).reshape(NOWN, 8, 128)
        sl = slice(h * 1024, (h + 1) * 1024)
        y_p[b, sl] = y[0:1024]
        k_p[0, b, sl] = kb[0:1024]
        v_p[0, b, sl] = vb[0:1024]
        for j in range(2):
            s = 2 * c + j
            y_s[s] = y[1024 + 32 * j:1056 + 32 * j]
            k_s[0, s] = kb[1024 + 32 * j:1056 + 32 * j]
            v_s[0, s] = vb[1024 + 32 * j:1056 + 32 * j]
            conv_s[0, s] = np.asarray(r["conv_s_o"])[j]
            S_s[0, s] = np.asarray(r["S_s_o"])[j]
        if h == 1:
            conv_p[0, b] = np.asarray(r["conv_p_o"])
            S_p[0, b] = np.asarray(r["S_p_o"])
    return (y_p, y_s, conv_p, S_p, k_p, v_p, conv_s, S_s, k_s, v_s)
```

```python
import numpy as np
import concourse.bass as bass
import concourse.mybir as mybir
from concourse.bass_utils import run_bass_kernel_spmd

F32 = mybir.dt.float32
BF16 = mybir.dt.bfloat16
AF = mybir.ActivationFunctionType
ALU = mybir.AluOpType

D = 2048
KC = 16
NOWN = 1088
NPRE = 1024
NALL = NPRE + NOWN
PROJ_W = 7184
OFF_Z = 3072
OFF_B = 4096
OFF_A = 4104
OFF_SB = 4112
DFF = 8192
ALPHA = float(2 ** 0.25)
LN_EPS = 1e-5
NEG = -30000.0
EPOCH = 30000
import os as _os
SAME_ENGINE_INORDER = bool(_os.environ.get('SEI'))
NEU_SINGLE = _os.environ.get('NEU_SINGLE', '0') == '1'


def _rect(ap):
    t = ap.tensor
    dims = list(ap.ap)
    esz = mybir.dt.size(ap.dtype)
    tsz = mybir.dt.size(t.dtype)
    row = 1
    for s in list(t.shape)[1:]:
        row *= s
    rowb = row * tsz
    offb = ap.offset * esz
    pcnt = dims[0][1]
    p_lo = offb // rowb
    f_lo = offb - p_lo * rowb
    ext = 0
    for st, c in dims[1:]:
        ext += abs(st) * (c - 1)
    f_hi = f_lo + (ext + 1) * esz
    p_hi = p_lo + pcnt
    if t.name.startswith("ps"):
        f_lo, f_hi = 0, 2048
        p_lo = (p_lo // 32) * 32
        p_hi = ((p_hi + 31) // 32) * 32
    return (t.name, p_lo, p_hi, f_lo, f_hi)


class Sched:
    def __init__(self, nc, n_dma_sems=48):
        self.nc = nc
        self.E = {"pe": nc.tensor, "dve": nc.vector, "act": nc.scalar, "pool": nc.gpsimd, "sp": nc.sync}
        self.csem = {}
        self.ccnt = {}
        self.nep = {}
        for e in ("pe", "dve", "act", "pool"):
            self.csem[e] = nc.alloc_semaphore(f"c_{e}_0")
            self.ccnt[e] = 0
            self.nep[e] = 0
        self.dsems = [nc.alloc_semaphore(f"d_{i}") for i in range(n_dma_sems)]
        self.dcnt = [0] * n_dma_sems
        self.dnext = {"sw": 0, "hw": n_dma_sems // 2}
        self.drange = {"sw": (0, n_dma_sems // 2), "hw": (n_dma_sems // 2, n_dma_sems)}
        self.known = {e: {} for e in self.E}
        self.recs = {}
        self.n_inst = 0
        self.n_wait = 0
        self.out_tokens = []

    def _need(self, eng, tok, waits):
        sem, val, name = tok
        if self.known[eng].get(name, 0) >= val:
            return
        cur = waits.get(name)
        if cur is None or cur[1] < val:
            waits[name] = (sem, val)

    @staticmethod
    def _ov(a, b):
        return a[1] < b[2] and b[1] < a[2] and a[3] < b[4] and b[3] < a[4]

    @staticmethod
    def _covers(a, b):
        return a[1] <= b[1] and a[2] >= b[2] and a[3] <= b[3] and a[4] >= b[4]

    def _deps(self, eng, reads, writes, waits):
        for r in reads:
            psum = r[0].startswith("ps")
            for rec in self.recs.get(r[0], ()):
                if rec[1] and self._ov(rec[0], r):
                    if rec[3] == eng and (eng == "pe" or (SAME_ENGINE_INORDER and eng in ("act", "dve"))):
                        continue
                    self._need(eng, rec[2], waits)
                elif psum and (not rec[1]) and rec[3] != eng:
                    self._need(eng, rec[2], waits)
        for w in writes:
            for rec in self.recs.get(w[0], ()):
                if self._ov(rec[0], w):
                    if rec[3] == eng and (eng == "pe" or (SAME_ENGINE_INORDER and eng in ("act", "dve"))):
                        continue
                    self._need(eng, rec[2], waits)

    def _record(self, eng, tok, reads, writes):
        for w in writes:
            lst = self.recs.setdefault(w[0], [])
            lst[:] = [rec for rec in lst if not self._covers(w, rec[0])]
            lst.append([w, True, tok, eng])
        for r in reads:
            lst = self.recs.setdefault(r[0], [])
            done = False
            if eng != "dma":
                for rec in lst:
                    if (not rec[1]) and rec[3] == eng and rec[0] == r:
                        rec[2] = tok
                        done = True
                        break
            if not done:
                lst.append([r, False, tok, eng])

    def _emit_waits(self, eng, waits):
        e = self.E[eng]
        for name, (sem, val) in waits.items():
            e.wait_ge(sem, val)
            self.known[eng][name] = val
            self.n_wait += 1

    def op(self, eng, fn, reads=(), writes=()):
        rr = [_rect(a) for a in reads]
        ww = [_rect(a) for a in writes]
        waits = {}
        self._deps(eng, rr, ww, waits)
        if self.ccnt[eng] >= EPOCH:
            self.nep[eng] += 1
            self.csem[eng] = self.nc.alloc_semaphore(f"c_{eng}_{self.nep[eng]}")
            self.ccnt[eng] = 0
        self._emit_waits(eng, waits)
        ins = fn(self.E[eng])
        self.ccnt[eng] += 1
        sem = self.csem[eng]
        ins.then_inc(sem, 1)
        tok = (sem, self.ccnt[eng], f"c_{eng}_{self.nep[eng]}")
        self._record(eng, tok, rr, ww)
        self.n_inst += 1
        return tok

    def dma(self, q, out, in_, reads=(), writes=(), is_output=False, after=(), **kw):
        rr = [_rect(a) for a in reads]
        ww = [_rect(a) for a in writes]
        waits = {}
        self._deps(q, rr, ww, waits)
        for tok in after:
            self._need(q, tok, waits)
        kind = "sw" if q == "pool" else "hw"
        i = self.dnext[kind]
        lo, hi = self.drange[kind]
        self.dnext[kind] = lo + (i + 1 - lo) % (hi - lo)
        sem = self.dsems[i]
        name = f"d_{i}"
        if self.dcnt[i] > 0:
            self._need(q, (sem, self.dcnt[i] * 16, name), waits)
        self._emit_waits(q, waits)
        ins = self.E[q].dma_start(out=out, in_=in_, **kw)
        self.dcnt[i] += 1
        ins.then_inc(sem, 16)
        tok = (sem, self.dcnt[i] * 16, name)
        self._record("dma", tok, rr, ww)
        self.n_inst += 1
        if is_output:
            self.out_tokens.append(tok)
        return tok

    def finish(self):
        waits = {}
        for tok in self.out_tokens:
            self._need("sp", tok, waits)
        for i, sem in enumerate(self.dsems):
            if self.dcnt[i]:
                self._need("sp", (sem, self.dcnt[i] * 16, f"d_{i}"), waits)
        self._emit_waits("sp", waits)


class Arena:
    def __init__(self, nc, nbytes):
        self.t = nc.alloc_sbuf_tensor("arena", [128, nbytes // 2], BF16)
        self.cap = nbytes
        self.top = 0

    def alloc(self, shape, dtype, parts=128, at=None):
        if isinstance(shape, int):
            shape = (shape,)
        n = 1
        for s in shape:
            n *= s
        nb = n * mybir.dt.size(dtype)
        if at is not None:
            off = at
            assert off % 64 == 0 and off + nb <= self.cap
        else:
            off = self.top
            self.top += (nb + 63) // 64 * 64
            assert self.top <= self.cap, f"arena overflow {self.top} > {self.cap}"
        self.last_off = off
        v = self.t[0:parts, off // 2:(off + nb) // 2]
        if dtype != BF16:
            v = v.bitcast(dtype)
        if len(shape) == 2:
            v = v.rearrange("p (a b) -> p a b", a=shape[0])
        elif len(shape) == 3:
            v = v.rearrange("p (a b c) -> p a b c", a=shape[0], b=shape[1])
        return v

    def mark(self):
        return self.top

    def release(self, m):
        self.top = m


OWN_TILES = [(i * 128, 128) for i in range(8)] + [(1024, 64)]
PRE_TILES = [(i * 128, 128) for i in range(8)]


def tok_groups(n, g=512):
    out = []
    t = 0
    while t < n:
        out.append((t, min(g, n - t)))
        t += g
    return out


class Prog:
    def __init__(self, stop_after=None, dbg=()):
        self.stop_after = stop_after
        self.dbg_names = dbg
        nc = bass.Bass("TRN2", target_bir_lowering=False)
        self.nc = nc
        self.S = Sched(nc)
        self.I = {}
        self.O = {}
        self.dbg = {}

        def inp(name, shape):
            self.I[name] = nc.dram_tensor(name, list(shape), F32, kind="ExternalInput").ap()

        def outp(name, shape):
            self.O[name] = nc.dram_tensor(name, list(shape), F32, kind="ExternalOutput").ap()

        inp("x_own", (NOWN, D))
        inp("x_pre", (NPRE, D))
        inp("conv_s", (2, 3, 3072))
        inp("S0_s", (2, 8, 128, 128))
        inp("ck", (2, 2048, 1024))
        inp("cv", (2, 2048, 1024))
        inp("w_in", (D, PROJ_W))
        inp("conv_w", (4, 3072))
        inp("a_log", (8,))
        inp("dt_bias", (8,))
        inp("gdn_norm_w", (128,))
        inp("w_out", (D, D))
        inp("ln1_g", (D,))
        inp("ln1_b", (D,))
        inp("w_up", (D, DFF))
        inp("w_down", (DFF, D))
        inp("ln2_g", (D,))
        inp("ln2_b", (D,))
        inp("pre_bias", (128, 1))
        outp("y", (NOWN, D))
        outp("kb", (NOWN, 1024))
        outp("vb", (NOWN, 1024))
        outp("conv_p_o", (3, 3072))
        outp("conv_s_o", (2, 3, 3072))
        outp("S_p_o", (8, 128, 128))
        outp("S_s_o", (2, 8, 128, 128))
        self.x1s = nc.dram_tensor("x1_scratch", [NOWN, D], F32, kind="Internal").ap()
        self.y2s = nc.dram_tensor("y2_scratch", [NOWN, D], F32, kind="Internal").ap()

        self.ar = Arena(nc, 212480)
        self.ps = [nc.alloc_psum_tensor(f"ps{i}", [128, 512], F32) for i in range(8)]
        self.build()

    def psf(self, i):
        return self.ps[i][:]

    def psb(self, i):
        return self.ps[i][:].bitcast(BF16)

    def tap(self, name, ap, shape):
        if name not in self.dbg_names:
            return
        t = self.nc.dram_tensor("dbg_" + name, list(shape), ap.dtype, kind="ExternalOutput").ap()
        self.dbg[name] = t
        self.S.dma("sp", t, ap, reads=[ap], is_output=True)

    def mm(self, out, lhsT, rhs, start=True, stop=True, skip=False):
        if skip:
            self.S.op("pe", lambda e: e.matmul(out, lhsT, rhs, start=start, stop=stop, skip_group_check=True),
                      reads=[lhsT, rhs], writes=[out])
        else:
            self.S.op("pe", lambda e: e.matmul(out, lhsT, rhs, start=start, stop=stop),
                      reads=[lhsT, rhs], writes=[out])

    def tr(self, out, in_, ident):
        self.S.op("pe", lambda e: e.transpose(out, in_, ident), reads=[in_, ident], writes=[out])

    def copy(self, eng, out, in_):
        if eng == "act":
            self.S.op("act", lambda e: e.copy(out, in_), reads=[in_], writes=[out])
        else:
            self.S.op(eng, lambda e: e.tensor_copy(out, in_), reads=[in_], writes=[out])

    def act(self, out, in_, func, bias=0.0, scale=1.0, accum_out=None, extra_reads=()):
        rd = [in_] + list(extra_reads)
        wr = [out] + ([accum_out] if accum_out is not None else [])
        if accum_out is not None:
            self.S.op("act", lambda e: e.activation(out=out, in_=in_, func=func, bias=bias, scale=scale,
                                                    accum_out=accum_out), reads=rd, writes=wr)
        else:
            self.S.op("act", lambda e: e.activation(out=out, in_=in_, func=func, bias=bias, scale=scale),
                      reads=rd, writes=wr)

    def tt(self, eng, out, in0, in1, op):
        self.S.op(eng, lambda e: e.tensor_tensor(out=out, in0=in0, in1=in1, op=op), reads=[in0, in1], writes=[out])

    def ts(self, eng, out, in0, s1, op0, s2=None, op1=None, extra_reads=()):
        rd = [in0] + list(extra_reads)
        if op1 is None:
            self.S.op(eng, lambda e: e.tensor_scalar(out=out, in0=in0, scalar1=s1, scalar2=None, op0=op0),
                      reads=rd, writes=[out])
        else:
            self.S.op(eng, lambda e: e.tensor_scalar(out=out, in0=in0, scalar1=s1, scalar2=s2, op0=op0, op1=op1),
                      reads=rd, writes=[out])

    def stt(self, eng, out, in0, scalar, in1, op0, op1, extra_reads=()):
        rd = [in0, in1] + list(extra_reads)
        self.S.op(eng, lambda e: e.scalar_tensor_tensor(out=out, in0=in0, scalar=scalar, in1=in1, op0=op0, op1=op1),
                  reads=rd, writes=[out])

    def memset(self, eng, ap, val):
        self.S.op(eng, lambda e: e.memset(ap, val), writes=[ap])

    def asel(self, out, in_, pattern, cmp, fill, base, cm):
        self.S.op("pool", lambda e: e.affine_select(out=out, in_=in_, pattern=pattern, compare_op=cmp, fill=fill,
                                                    base=base, channel_multiplier=cm), reads=[in_], writes=[out])

    def build(self):
        self.consts()
        self.phase_x()
        if self.stop_after == "x":
            return self.S.finish()
        self.phase_attn()
        if self.stop_after == "attn":
            return self.S.finish()
        self.phase_wout()
        if self.stop_after == "wout":
            return self.S.finish()
        self.phase_ffn()
        self.S.finish()

    def consts(self):
        ar = self.ar
        self.ident_f = ar.alloc((128,), F32)
        self.ident_b = ar.alloc((128,), BF16)
        self.zeros_f = ar.alloc((128,), F32)
        self.memset("pool", self.zeros_f, 0.0)
        self.memset("pool", self.ident_f, 0.0)
        self.asel(self.ident_f, self.ident_f, [[-1, 128]], ALU.not_equal, 1.0, 0, 1)
        self.copy("pool", self.ident_b, self.ident_f)
        self.pre_bias = self.nc.alloc_sbuf_tensor("pre_bias_t", [128, 1], F32)[:]
        import os
        if os.environ.get("PB_MEMSET"):
            self.memset("pool", self.pre_bias, 1.0)
        else:
            self.S.dma("sp", self.pre_bias, self.I["pre_bias"], writes=[self.pre_bias])

    def phase_x(self):
        ar, S = self.ar, self.S
        self.xT_own = ar.alloc((KC, NOWN), BF16)
        self.xT_own_off = ar.last_off
        self.xT_pre = ar.alloc((KC, NPRE), BF16)
        self.xT_pre_off = ar.last_off
        m = ar.mark()
        xb = [ar.alloc((D,), BF16) for _ in range(2)]
        k = 0
        for src, tiles, dst in ((self.I["x_pre"], PRE_TILES, self.xT_pre), (self.I["x_own"], OWN_TILES, self.xT_own)):
            for (t0, n) in tiles:
                b = xb[k % 2]
                S.dma("pool", b[0:n, :], src[t0:t0 + n, :], writes=[b[0:n, :]])
                for half in range(2):
                    pb = self.psb(half)
                    for c in range(8):
                        cc = half * 8 + c
                        self.tr(pb[:, c * 128:c * 128 + n], b[0:n, cc * 128:(cc + 1) * 128], self.ident_b[0:n, 0:n])
                    src_ps = pb[:, 0:1024].rearrange("p (c t) -> p c t", c=8)[:, :, 0:n]
                    self.copy("dve" if half == 0 else "act", dst[:, half * 8:(half + 1) * 8, t0:t0 + n], src_ps)
                k += 1
        ar.release(m)
        self.tap("xT_own", self.xT_own, (128, KC, NOWN))

    def phase_attn(self):
        ar = self.ar
        self.mixedT = ar.alloc((KC, NOWN), BF16)
        self.memset("pool", self.mixedT, 0.0)
        m = ar.mark()
        import os
        if not os.environ.get("NO_GDN"):
            self.phase_gdn()
        ar.release(m)
        if not os.environ.get("NO_SB"):
            self.phase_sb()
        ar.release(m)

    def phase_gdn(self):
        ar, S, I = self.ar, self.S, self.I
        NT = 17
        tiles = [("pre", t0, n, t0) for (t0, n) in PRE_TILES] + [("own", t0, n, NPRE + t0) for (t0, n) in OWN_TILES]
        zb = ar.alloc((128,), BF16)
        ones_b = ar.alloc((128,), BF16)
        ones_f = ar.alloc((128,), F32)
        nones_f = ar.alloc((128,), F32)
        self.memset("pool", zb, 0.0)
        self.memset("pool", ones_b, 1.0)
        self.memset("pool", ones_f, 1.0)
        self.memset("pool", nones_f, -1.0)
        offd = ar.alloc((128,), BF16)
        self.memset("pool", offd, 1.0)
        self.asel(offd, offd, [[-1, 128]], ALU.not_equal, 0.0, 0, 1)
        ident4 = ar.alloc((4, 128), BF16)
        for j in range(4):
            self.copy("pool", ident4[:, j, :], self.ident_b)

        def block_mask(kind, c, rows):
            if kind in ("Ms", "Mi"):
                m = ar.alloc((128,), BF16)
                self.memset("pool", m, NEG)
            else:
                m = ar.alloc((128,), F32)
                self.memset("pool", m, 0.0)
            for r0 in range(0, rows, c):
                blk = slice(r0, r0 + c)
                if kind == "Ms":
                    self.asel(m[blk, blk], zb[blk, blk], [[-1, c]], ALU.is_ge, NEG, -1, 1)
                elif kind == "Mi":
                    self.asel(m[blk, blk], zb[blk, blk], [[-1, c]], ALU.is_ge, NEG, 0, 1)
                elif kind == "tri":
                    self.asel(m[blk, blk], ones_f[blk, blk], [[1, c]], ALU.is_ge, 0.0, 0, -1)
                elif kind == "last":
                    self.asel(m[blk, blk], ones_f[blk, blk], [[0, c]], ALU.is_equal, 0.0, -(c - 1), 1)
            return m
        maskMs = {64: block_mask("Ms", 64, 128), 32: block_mask("Ms", 32, 64)}
        maskMi = {64: block_mask("Mi", 64, 128), 32: block_mask("Mi", 32, 64)}
        trich = {64: block_mask("tri", 64, 128), 32: block_mask("tri", 32, 64)}
        sellast = {64: block_mask("last", 64, 128), 32: block_mask("last", 32, 64)}
        lastsel = {}
        for (c, rows) in ((64, 128), (32, 64)):
            for nch in range(2):
                m = ar.alloc((128,), F32)
                self.memset("pool", m, 0.0)
                self.asel(m[0:rows, :], ones_f[0:rows, :], [[0, 128]], ALU.is_equal, 0.0, -((nch + 1) * c - 1), 1)
                lastsel[(c, nch)] = m
        nsel = ar.alloc((8, 128), F32, parts=8)
        self.memset("pool", nsel, 0.0)
        for h in range(8):
            self.asel(nsel[:, h, :], nones_f[0:8, :], [[0, 128]], ALU.is_equal, 0.0, -h, 1)
        normw = ar.alloc((128,), F32)
        S.dma("sp", normw, I["gdn_norm_w"].partition_broadcast(128), writes=[normw])
        dtb = ar.alloc((8,), F32)
        S.dma("sp", dtb, I["dt_bias"].partition_broadcast(128), writes=[dtb])
        negA = ar.alloc((8,), F32)
        S.dma("sp", negA, I["a_log"].partition_broadcast(128), writes=[negA])
        self.act(negA, negA, AF.Exp)
        self.ts("dve", negA, negA, -1.0, ALU.mult)
        eps6 = self.eps_tile(1e-6)
        one_c = self.eps_tile(1.0)
        lnqs = self.eps_tile(float(np.log(128 ** -0.5)))
        cw = ar.alloc((24, 4), F32)
        hist = ar.alloc((24, 6), F32)
        mtmp = ar.mark()
        cwt = ar.alloc((3072,), F32, parts=4)
        S.dma("sp", cwt, I["conv_w"], writes=[cwt])
        hst = ar.alloc((3072,), F32, parts=6)
        S.dma("sp", hst, I["conv_s"].rearrange("s r c -> (s r) c"), writes=[hst])
        pc = self.psf(0)
        for ct in range(24):
            self.mm(pc[:, ct * 4:ct * 4 + 4], cwt[:, ct * 128:(ct + 1) * 128], self.ident_f[0:4, 0:4])
        self.copy("dve", cw, pc[:, 0:96].rearrange("p (a b) -> p a b", b=4))
        pc = self.psf(1)
        for ct in range(24):
            self.mm(pc[:, ct * 6:ct * 6 + 6], hst[:, ct * 128:(ct + 1) * 128], self.ident_f[0:6, 0:6])
        self.copy("dve", hist, pc[:, 0:144].rearrange("p (a b) -> p a b", b=6))
        ar.release(mtmp)
        w16 = ar.alloc((KC, 16), BF16)
        S.dma("pool", w16, I["w_in"][:, OFF_B:OFF_B + 16].rearrange("(c p) n -> p c n", p=128), writes=[w16])
        P16 = ar.alloc((NT, 16), F32)
        ps = self.psf(0)
        for ti, (src, t0, n, c0) in enumerate(tiles):
            xT = self.xT_pre if src == "pre" else self.xT_own
            for c in range(KC):
                self.mm(ps[0:n, ti * 16:(ti + 1) * 16], xT[:, c, t0:t0 + n], w16[:, c, :], start=(c == 0), stop=(c == KC - 1))
        self.memset("pool", P16, 0.0)
        self.copy("dve", P16[:, 0:16, :], ps[:, 0:256].rearrange("p (a b) -> p a b", b=16))
        self.copy("dve", P16[0:64, 16, :], ps[0:64, 256:272])
        TMq = {}
        for nm in ("g", "G", "lb", "beta", "sk", "eg", "st", "tmp"):
            TMq[nm] = ar.alloc((NT, 8), F32)
        bl = P16[:, :, 0:8]
        al = P16[:, :, 8:16]
        bc17 = lambda t: t.unsqueeze(1).broadcast_to([128, NT, 8])
        self.tt("dve", TMq["tmp"], al, bc17(dtb), ALU.add)
        self.act(TMq["tmp"], TMq["tmp"], AF.Exp)
        self.act(TMq["tmp"], TMq["tmp"], AF.Ln, bias=one_c, extra_reads=[one_c])
        self.tt("dve", TMq["g"], TMq["tmp"], bc17(negA), ALU.mult)
        self.act(TMq["lb"], bl, AF.Exp, scale=-1.0)
        self.act(TMq["lb"], TMq["lb"], AF.Ln, bias=one_c, extra_reads=[one_c])
        self.act(TMq["beta"], TMq["lb"], AF.Exp, scale=-1.0)
        ps = self.psf(1)
        for ti, (src, t0, n, c0) in enumerate(tiles):
            c = 64 if n == 128 else 32
            self.mm(ps[0:n, ti * 8:(ti + 1) * 8], trich[c][0:n, 0:n], TMq["g"][0:n, ti, :])
        self.memset("pool", TMq["G"], 0.0)
        self.copy("dve", TMq["G"][:, 0:16, :], ps[:, 0:128].rearrange("p (a b) -> p a b", b=8))
        self.copy("dve", TMq["G"][0:64, 16, :], ps[0:64, 128:136])
        self.act(TMq["eg"], TMq["G"], AF.Exp)
        self.tt("dve", TMq["tmp"], TMq["G"], TMq["lb"], ALU.subtract)
        self.act(TMq["sk"], TMq["tmp"], AF.Exp)
        ps = self.psf(0)
        for ti, (src, t0, n, c0) in enumerate(tiles):
            c = 64 if n == 128 else 32
            self.mm(ps[0:n, ti * 8:(ti + 1) * 8], sellast[c][0:n, 0:n], TMq["G"][0:n, ti, :])
        self.memset("pool", TMq["tmp"], 0.0)
        self.tt("dve", TMq["tmp"][:, 0:16, :], ps[:, 0:128].rearrange("p (a b) -> p a b", b=8), TMq["G"][:, 0:16, :], ALU.subtract)
        self.tt("dve", TMq["tmp"][0:64, 16, :], ps[0:64, 128:136], TMq["G"][0:64, 16, :], ALU.subtract)
        self.act(TMq["st"], TMq["tmp"], AF.Exp)
        GT = ar.alloc((NT * 2, 8), F32)
        ps = self.psf(1)
        for ti, (src, t0, n, c0) in enumerate(tiles):
            c = 64 if n == 128 else 32
            for nch in range(2):
                j = ti * 2 + nch
                self.mm(ps[:, j * 8:(j + 1) * 8], lastsel[(c, nch)][0:n, :], TMq["eg"][0:n, ti, :])
        self.copy("dve", GT, ps[:, 0:NT * 16].rearrange("p (a b) -> p a b", b=8))
        Grow = ar.alloc((NALL,), F32, parts=8)
        for q0 in range(0, NT, 4):
            ps = self.psf(q0 // 4 % 2)
            for ti in range(q0, min(NT, q0 + 4)):
                (src, t0, n, c0) = tiles[ti]
                self.mm(ps[0:8, (ti - q0) * 128:(ti - q0) * 128 + n], TMq["G"][0:n, ti, :], self.ident_f[0:n, 0:n])
            nc_ = sum(tiles[ti][2] for ti in range(q0, min(NT, q0 + 4)))
            self.copy("dve", Grow[:, tiles[q0][3]:tiles[q0][3] + nc_], ps[0:8, 0:nc_])
        self.tap("Grow", Grow, (8, NALL))
        self.tap("TMst", TMq["st"], (128, NT, 8))
        self.tap("TMsk", TMq["sk"], (128, NT, 8))
        self.tap("GT", GT, (128, NT * 2, 8))
        wb = ar.alloc((KC, 512), BF16)
        NKV = 2121
        pcb = ar.alloc((NKV,), F32)
        cvb = ar.alloc((NKV,), F32)
        knT = ar.alloc((NALL,), BF16)
        vT = ar.alloc((NALL,), BF16)
        qnT = ar.alloc((NOWN,), BF16)
        sqb = ar.alloc((512,), BF16)
        lnb_ = ar.alloc((512,), F32)
        zg = ar.alloc((9, 128), BF16)
        zs = ar.alloc((128,), F32)
        cst = lnb_[0:3, 0:384]
        opq = []
        for i in range(2):
            opq.append({
                "kbg": ar.alloc((4, 128), BF16), "kt": ar.alloc((4, 128), BF16), "vb": ar.alloc((4, 128), BF16),
                "wtok": ar.alloc((4, 128), BF16), "attnT": ar.alloc((4, 128), BF16), "ub": ar.alloc((4, 128), BF16),
                "nW2": ar.alloc((4, 2, 128), BF16), "nAWT": ar.alloc((4, 128), BF16),
            })
        Mb = [ar.alloc((4, 128), BF16) for _ in range(2)]
        Ab = [ar.alloc((4, 128), BF16) for _ in range(2)]
        Yb = [ar.alloc((4, 128), BF16) for _ in range(2)]
        Mp = ar.alloc((4, 128), BF16) if NEU_SINGLE else None
        DMi = ar.alloc((4, 128), BF16)
        DMs = ar.alloc((4, 128), BF16)
        attn = DMs
        Sf = ar.alloc((128,), F32)
        Sbf = ar.alloc((128,), BF16)
        t1 = ar.alloc((128,), F32)
        otok = ar.alloc((128,), F32)
        osq = t1
        ogb = ar.alloc((128,), BF16)
        ssq = ar.alloc((1,), F32)
        rs = ar.alloc((1,), F32)

        def kvcol(tok):
            if tok < 2048:
                return 3 + tok
            if tok < 2080:
                return 2054 + (tok - 2048)
            return 2089 + (tok - 2080)

        def qcol(t):
            if t < 1024:
                return 3 + t
            if t < 1056:
                return 1030 + (t - 1024)
            return 1065 + (t - 1056)
        kv_segs = [(3, 2048, 0), (2054, 32, 2048), (2089, 32, 2080)]
        q_segs = [(3, 1024, 0), (1030, 32, 1024), (1065, 32, 1056)]

        def conv_silu(ncols, ct):
            n = ncols - 3
            self.ts("dve", cvb[:, 3:ncols], pcb[:, 0:n], cw[:, ct, 0:1], ALU.mult, extra_reads=[cw[:, ct, 0:1]])
            for i in range(1, 4):
                self.stt("dve", cvb[:, 3:ncols], pcb[:, i:i + n], cw[:, ct, i:i + 1], cvb[:, 3:ncols], ALU.mult, ALU.add,
                         extra_reads=[cw[:, ct, i:i + 1]])

        def conv_out(which, h, cols3):
            ct = which * 8 + h
            pz = self.psf(1)
            for sgi, c0 in enumerate(cols3):
                self.mm(pz[0:3, sgi * 128:(sgi + 1) * 128], pcb[:, c0:c0 + 3], self.ident_f)
            self.copy("dve", cst, pz[0:3, 0:384])
            S.dma("sp", self.O["conv_p_o"][:, ct * 128:(ct + 1) * 128], cst[:, 0:128], reads=[cst[:, 0:128]], is_output=True)
            for j in range(2):
                S.dma("sp", self.O["conv_s_o"][j, :, ct * 128:(ct + 1) * 128], cst[:, (j + 1) * 128:(j + 2) * 128],
                      reads=[cst[:, (j + 1) * 128:(j + 2) * 128]], is_output=True)

        def normalize(segs, dst, extra_bias):
            for (c0, ln, k0) in segs:
                for o in range(0, ln, 512):
                    n = min(512, ln - o)
                    src = cvb[:, c0 + o:c0 + o + n]
                    self.act(sqb[:, 0:n], src, AF.Square)
                    pz = self.psf(1)
                    self.mm(pz[:, 0:n], ones_b, sqb[:, 0:n])
                    self.act(lnb_[:, 0:n], pz[:, 0:n], AF.Ln, bias=eps6, extra_reads=[eps6])
                    if extra_bias is None:
                        self.act(lnb_[:, 0:n], lnb_[:, 0:n], AF.Exp, scale=-0.5)
                    else:
                        self.act(lnb_[:, 0:n], lnb_[:, 0:n], AF.Exp, scale=-0.5, bias=extra_bias, extra_reads=[extra_bias])
                    self.tt("dve", dst[:, k0 + o:k0 + o + n], src, lnb_[:, 0:n], ALU.mult)

        quads = [list(range(0, 4)), list(range(4, 8)), list(range(8, 12)), list(range(12, 16)), [16]]

        for h in range(8):
            def wload_head(hh):
                for j, coff in enumerate((hh * 128, 1024 + hh * 128, 2048 + hh * 128, OFF_Z + hh * 128)):
                    src = I["w_in"][:, coff:coff + 128].rearrange("(c p) n -> p c n", p=128)
                    S.dma("pool", wb[:, :, j * 128:(j + 1) * 128], src, writes=[wb[:, :, j * 128:(j + 1) * 128]])
            if h == 0:
                wload_head(0)

            def proj_fm(wj, xT, t0, n, dst):
                self._pfm = getattr(self, "_pfm", 0) + 1
                po = self.psf(self._pfm % 2)[:, 0:n]
                for c in range(KC):
                    self.mm(po, wb[:, c, wj * 128:(wj + 1) * 128], xT[:, c, t0:t0 + n], start=(c == 0), stop=(c == KC - 1))
                self.copy("act", dst, po)
            def proj_q():
                proj_fm(0, self.xT_pre, 1021, 3, pcb[:, 0:3])
                for (t0, n) in ((0, 512), (512, 512)):
                    proj_fm(0, self.xT_own, t0, n, pcb[:, 3 + t0:3 + t0 + n])
                proj_fm(0, self.xT_own, 1024, 32, pcb[:, 1030:1062])
                proj_fm(0, self.xT_own, 1056, 32, pcb[:, 1065:1097])
                for j in range(2):
                    self.copy("pool", pcb[:, 1027 + 35 * j:1030 + 35 * j], hist[:, h, 3 * j:3 * j + 3])

            def proj_kv(which):
                self.memset("pool", pcb[:, 0:3], 0.0)
                for (t0, n) in ((0, 512), (512, 512)):
                    proj_fm(which, self.xT_pre, t0, n, pcb[:, 3 + t0:3 + t0 + n])
                for (t0, n) in ((0, 512), (512, 512)):
                    proj_fm(which, self.xT_own, t0, n, pcb[:, 3 + 1024 + t0:3 + 1024 + t0 + n])
                proj_fm(which, self.xT_own, 1024, 32, pcb[:, 2054:2086])
                proj_fm(which, self.xT_own, 1056, 32, pcb[:, 2089:2121])
                for j in range(2):
                    self.copy("pool", pcb[:, 2051 + 35 * j:2054 + 35 * j], hist[:, which * 8 + h, 3 * j:3 * j + 3])

            def conv_stage(which):
                if which == 0:
                    conv_out(0, h, (1024, 1059, 1094))
                    conv_silu(1097, h)
                    self.act(cvb[:, 3:1097], cvb[:, 3:1097], AF.Silu)
                else:
                    conv_out(which, h, (2048, 2083, 2118))
                    conv_silu(NKV, which * 8 + h)
                    self.act(cvb[:, 3:NKV], cvb[:, 3:NKV], AF.Silu)
            proj_q()
            conv_stage(0)
            proj_kv(1)
            normalize(q_segs, qnT, lnqs)
            conv_stage(1)
            proj_kv(2)
            normalize(kv_segs, knT, None)
            conv_stage(2)
            for (c0, ln, k0) in kv_segs:
                self.copy("pool", vT[:, k0:k0 + ln], cvb[:, c0:c0 + ln])
            for ti, (t0, n) in enumerate(OWN_TILES):
                pz = self.psf(0)[0:n, 0:128]
                for c in range(KC):
                    self.mm(pz, self.xT_own[:, c, t0:t0 + n], wb[:, c, 384:512], start=(c == 0), stop=(c == KC - 1))
                self.act(zs[0:n, :], pz, AF.Silu)
                self.tt("dve", zg[0:n, ti, :], zs[0:n, :], normw[0:n, :], ALU.mult)
            if h == 0:
                self.tap("knT0", knT, (128, NALL))
                self.tap("qnT0", qnT, (128, NOWN))
                self.tap("vT0", vT, (128, NALL))
            if h + 1 < 8:
                wload_head(h + 1)
            self.memset("pool", Sf, 0.0)
            self.memset("pool", Sbf, 0.0)
            def gen_L(qi, h=h):
                quad = quads[qi]
                ops = opq[qi % 2]
                own_quad = tiles[quad[0]][0] == "own"
                nq = len(quad)
                nn = tiles[quad[0]][2]
                v3 = lambda p: p[0:nn, 0:nq * 128].rearrange("p (a b) -> p a b", b=128)[:, :, 0:nn]
                pb = self.psb(2)
                for j, ti in enumerate(quad):
                    (src, t0, n, c0) = tiles[ti]
                    self.tr(pb[0:n, j * 256:j * 256 + 128], knT[:, c0:c0 + n], self.ident_b)
                    self.tr(pb[0:n, j * 256 + 128:j * 256 + 256], vT[:, c0:c0 + n], self.ident_b)
                yield
                for j, ti in enumerate(quad):
                    (src, t0, n, c0) = tiles[ti]
                    kps = pb[0:n, j * 256:j * 256 + 128]
                    vps = pb[0:n, j * 256 + 128:j * 256 + 256]
                    sc = lambda nm: TMq[nm][0:n, ti, h:h + 1]
                    self.ts("dve", ops["kbg"][0:n, j, :], kps, sc("sk"), ALU.mult, extra_reads=[sc("sk")])
                    self.ts("dve", ops["kt"][0:n, j, :], kps, sc("st"), ALU.mult, extra_reads=[sc("st")])
                    self.ts("dve", ops["vb"][0:n, j, :], vps, sc("beta"), ALU.mult, extra_reads=[sc("beta")])
                pk = self.psf(3)
                pe_ = self.psf(4)
                for j, ti in enumerate(quad):
                    (src, t0, n, c0) = tiles[ti]
                    c = 64 if n == 128 else 32
                    self.mm(pk[0:n, j * 128:j * 128 + n], knT[:, c0:c0 + n], knT[:, c0:c0 + n])
                    self.mm(pe_[0:n, j * 128:j * 128 + n], nsel[:, h, 0:n], Grow[:, c0:c0 + n], start=True, stop=False)
                    msk = maskMi[c] if own_quad else maskMs[c]
                    self.mm(pe_[0:n, j * 128:j * 128 + n], self.ident_b[0:n, 0:n], msk[0:n, 0:n], start=False, stop=True)
                yield
                for j, ti in enumerate(quad):
                    (src, t0, n, c0) = tiles[ti]
                    gb_ = TMq["G"][0:n, ti, h:h + 1]
                    dst = DMi if own_quad else DMs
                    self.act(dst[0:n, j, 0:n], pe_[0:n, j * 128:j * 128 + n], AF.Exp, bias=gb_, extra_reads=[gb_])
                    if own_quad:
                        self.tt("pool", DMs[0:n, j, 0:n], DMi[0:n, j, 0:n], offd[0:n, 0:n], ALU.mult)
                M0, A0, Y0 = Mb[0], Ab[0], Yb[0]
                for j, ti in enumerate(quad):
                    (src, t0, n, c0) = tiles[ti]
                    bt_ = TMq["beta"][0:n, ti, h:h + 1]
                    self.stt("dve", M0[0:n, j, 0:n], pk[0:n, j * 128:j * 128 + n], bt_, DMs[0:n, j, 0:n], ALU.mult, ALU.mult,
                             extra_reads=[bt_])
                yield
                pt = self.psb(2)
                for j, ti in enumerate(quad):
                    n = tiles[ti][2]
                    self.tr(pt[0:n, j * 128:j * 128 + n], M0[0:n, j, 0:n], self.ident_b[0:n, 0:n])
                ptv = pt[0:nn, 0:nq * 128].rearrange("p (a b) -> p a b", b=128)[:, :, 0:nn]
                self.copy("act", A0[0:nn, 0:nq, 0:nn], ptv)
                self.stt("dve", Y0[0:nn, 0:nq, 0:nn], ptv, -1.0, ident4[0:nn, 0:nq, 0:nn], ALU.mult, ALU.add)
                yield
                cur = 0
                ycur = 0
                pend = None
                p5, p6, p7 = self.psf(5), self.psf(6), self.psf(7)

                def y_mm(Mlhs, Ysrc):
                    for j, ti in enumerate(quad):
                        n = tiles[ti][2]
                        self.mm(p7[0:n, j * 128:j * 128 + n], self.ident_b[0:n, 0:n], Ysrc[0:n, j, 0:n], start=True, stop=False)
                        self.mm(p7[0:n, j * 128:j * 128 + n], Mlhs[0:n, j, 0:n], Ysrc[0:n, j, 0:n], start=False, stop=True)
                for it in range(5):
                    Mc, Ac = Mb[cur], Ab[cur]
                    Mn, An = Mb[1 - cur], Ab[1 - cur]
                    for j, ti in enumerate(quad):
                        n = tiles[ti][2]
                        self.mm(p5[0:n, j * 128:j * 128 + n], Ac[0:n, j, 0:n], Mc[0:n, j, 0:n])
                        if it < 4:
                            self.mm(p6[0:n, j * 128:j * 128 + n], Mc[0:n, j, 0:n], Ac[0:n, j, 0:n])
                    if pend is not None:
                        y_mm(Mc, Yb[ycur])
                    yield
                    self.copy("act", Mn[0:nn, 0:nq, 0:nn], v3(p5))
                    if it < 4:
                        self.copy("dve", An[0:nn, 0:nq, 0:nn], v3(p6))
                    if pend is not None:
                        self.copy("dve", Yb[1 - ycur][0:nn, 0:nq, 0:nn], v3(p7))
                        ycur = 1 - ycur
                    pend = True
                    cur = 1 - cur
                    yield
                y_mm(Mb[cur], Yb[ycur])
                yield
                self.copy("dve", Yb[1 - ycur][0:nn, 0:nq, 0:nn], v3(p7))
                cur = 1 - ycur
                Yf = Yb[cur]
                pu, pw = self.psf(3), self.psf(4)
                for j, ti in enumerate(quad):
                    n = tiles[ti][2]
                    self.mm(pu[0:n, j * 128:(j + 1) * 128], Yf[0:n, j, 0:n], ops["vb"][0:n, j, :])
                    self.mm(pw[0:n, j * 128:(j + 1) * 128], Yf[0:n, j, 0:n], ops["kbg"][0:n, j, :])
                yield
                self.copy("act", ops["ub"][0:nn, 0:nq, :], pu[0:nn, 0:nq * 128].rearrange("p (a b) -> p a b", b=128))
                self.copy("dve", ops["wtok"][0:nn, 0:nq, :], pw[0:nn, 0:nq * 128].rearrange("p (a b) -> p a b", b=128))
                p56 = (self.psf(5), self.psf(6))
                cc = 64 if nn == 128 else 32
                for j, ti in enumerate(quad):
                    for nch in range(2):
                        r = slice(nch * cc, (nch + 1) * cc)
                        self.mm(p56[nch][:, j * 128:(j + 1) * 128], ops["wtok"][r, j, :], ops["kt"][r, j, :])
                yield
                self.ts("dve", ops["nW2"][:, 0:nq, 0, :], p56[0][:, 0:nq * 128].rearrange("p (a b) -> p a b", b=128), -1.0, ALU.mult)
                self.act(ops["nW2"][:, 0:nq, 1, :], p56[1][:, 0:nq * 128].rearrange("p (a b) -> p a b", b=128), AF.Copy, scale=-1.0)
                if own_quad:
                    pq = self.psf(3)
                    for j, ti in enumerate(quad):
                        (src, t0, n, c0) = tiles[ti]
                        self.mm(pq[0:n, j * 128:j * 128 + n], qnT[:, t0:t0 + n], knT[:, c0:c0 + n])
                    yield
                    self.tt("dve", attn[0:nn, 0:nq, 0:nn], v3(pq), DMi[0:nn, 0:nq, 0:nn], ALU.mult)
                    pt = self.psb(2)
                    for j, ti in enumerate(quad):
                        n = tiles[ti][2]
                        self.tr(pt[0:n, j * 128:j * 128 + n], attn[0:n, j, 0:n], self.ident_b[0:n, 0:n])
                    yield
                    self.copy("act", ops["attnT"][0:nn, 0:nq, 0:nn],
                              pt[0:nn, 0:nq * 128].rearrange("p (a b) -> p a b", b=128)[:, :, 0:nn])
                    p7 = self.psf(7)
                    for j, ti in enumerate(quad):
                        n = tiles[ti][2]
                        self.mm(p7[:, j * 128:j * 128 + n], ops["wtok"][0:n, j, :], ops["attnT"][0:n, j, 0:n])
                    yield
                    self.ts("dve", ops["nAWT"][:, 0:nq, 0:nn],
                            p7[:, 0:nq * 128].rearrange("p (a b) -> p a b", b=128)[:, :, 0:nn], -1.0, ALU.mult)
                if h == 0 and qi == 0:
                    self.tap("Yf0", Yf, (128, 4, 128))
                    self.tap("M00", Mb[0], (128, 4, 128))

            def gen_S(qi, h=h):
                quad = quads[qi]
                ops = opq[qi % 2]
                for j, ti in enumerate(quad):
                    (src, t0, n, c0) = tiles[ti]
                    c = 64 if n == 128 else 32
                    pS = self.psf(1)
                    for nch in range(n // c):
                        r = slice(nch * c, (nch + 1) * c)
                        if n == 64:
                            S.dma("sp", Sf, I["S0_s"][nch, h], writes=[Sf])
                            self.copy("act", Sbf, Sf)
                        self.mm(pS[:, 128:256], ops["nW2"][:, j, nch, :], Sbf, start=True, stop=False)
                        self.mm(pS[:, 128:256], ops["kt"][r, j, :], ops["ub"][r, j, :], start=False, stop=True)
                        if src == "own":
                            self.mm(pS[r, 256:384], qnT[:, t0 + nch * c:t0 + (nch + 1) * c], Sbf)
                            self.mm(pS[r, 384:512], ops["nAWT"][:, j, r], Sbf, start=True, stop=False)
                            self.mm(pS[r, 384:512], ops["attnT"][r, j, r], ops["ub"][r, j, :], start=False, stop=True)
                        yield
                        gt_ = GT[:, ti * 2 + nch, h:h + 1]
                        self.stt("dve", Sbf, Sf, gt_, pS[:, 128:256], ALU.mult, ALU.add, extra_reads=[gt_])
                        self.stt("dve", Sf, Sf, gt_, pS[:, 128:256], ALU.mult, ALU.add, extra_reads=[gt_])
                        if src == "own":
                            eg_ = TMq["eg"][r, ti, h:h + 1]
                            self.act(t1[r, :], pS[r, 256:384], AF.Copy, scale=eg_, extra_reads=[eg_])
                            self.tt("dve", otok[r, :], t1[r, :], pS[r, 384:512], ALU.add)
                        if n == 64:
                            S.dma("sp", self.O["S_s_o"][nch, h], Sf, reads=[Sf], is_output=True)
                        yield
                    if src == "own" and t0 == 896:
                        S.dma("sp", self.O["S_p_o"][h], Sf, reads=[Sf], is_output=True)
                    if src == "own":
                        oti = t0 // 128
                        self.act(osq[0:n, :], otok[0:n, :], AF.Square, accum_out=ssq[0:n, :])
                        self.act(rs[0:n, :], ssq[0:n, :], AF.Ln, scale=1.0 / 128.0, bias=eps6[0:n, :], extra_reads=[eps6[0:n, :]])
                        self.act(rs[0:n, :], rs[0:n, :], AF.Exp, scale=-0.5)
                        self.stt("dve", ogb[0:n, :], otok[0:n, :], rs[0:n, :], zg[0:n, oti, :], ALU.mult, ALU.mult,
                                 extra_reads=[rs[0:n, :]])
                        pg = self.psb(0)
                        self.tr(pg[:, 0:n], ogb[0:n, :], self.ident_b[0:n, 0:n])
                        yield
                        self.copy("act", self.mixedT[:, h, t0:t0 + n], pg[:, 0:n])

            def run_gens(gens):
                gens = list(gens)
                while gens:
                    for g_ in list(gens):
                        try:
                            next(g_)
                        except StopIteration:
                            gens.remove(g_)
            run_gens([gen_L(0)])
            for qi in range(len(quads)):
                gl = [gen_S(qi)]
                if qi + 1 < len(quads):
                    gl.append(gen_L(qi + 1))
                run_gens(gl)
        self.tap("mixedT_g", self.mixedT, (128, KC, NOWN))


    def wload(self, buf, w_ap, c0, ncols, kc=KC):
        src = w_ap[:, c0:c0 + ncols].rearrange("(c p) n -> p c n", p=128)
        dst = buf[:, :, 0:ncols]
        self.S.dma("pool", dst, src, writes=[dst])

    def phase_sb(self):
        ar, S = self.ar, self.S
        wb = [ar.alloc((KC, 512), BF16) for _ in range(2)]
        stage = [ar.alloc((512,), F32) for _ in range(2)]
        KT = ar.alloc((4, NALL), BF16)
        VT = ar.alloc((17, 512), BF16)
        QT2 = [ar.alloc((NOWN,), BF16) for _ in range(2)]
        Eb = [ar.alloc((512,), F32) for _ in range(2)]
        SPb = [ar.alloc((512,), BF16) for _ in range(2)]
        Wb = [ar.alloc((512,), BF16) for _ in range(2)]
        Xb = [ar.alloc((512,), F32) for _ in range(2)] + [ar.alloc((64,), F32)]
        Eb.append(ar.alloc((64,), F32))
        SPb.append(ar.alloc((64,), BF16))
        Wb.append(ar.alloc((64,), BF16))
        KTn = ar.alloc((4, 64), BF16)
        Vn = ar.alloc((512,), BF16)
        ntri_i = ar.alloc((128,), BF16)
        ntri_c = ar.alloc((128,), BF16)
        zb = ar.alloc((512,), BF16)
        KTc = ar.alloc((2, 2048), BF16)
        Vc = ar.alloc((2, 16, 128), BF16)
        m64 = ar.alloc((64,), BF16, parts=64)
        ones64 = ar.alloc((64,), BF16, parts=64)
        one_c = self.eps_tile(1.0)
        self.memset("pool", m64, 0.0)
        self.memset("pool", ones64, 1.0)
        for s_ in range(2):
            sq = slice(32 * s_, 32 * s_ + 32)
            self.asel(m64[sq, sq], ones64[sq, sq], [[1, 32]], ALU.is_ge, 0.0, -1, -1)
        self.memset("pool", zb, 0.0)
        self.memset("pool", ntri_i, -1.0)
        self.asel(ntri_i, ntri_i, [[-1, 128]], ALU.is_ge, 0.0, 0, 1)
        self.memset("pool", ntri_c, -1.0)
        self.asel(ntri_c, ntri_c, [[1, 128]], ALU.is_gt, 0.0, 0, -1)
        all_tiles = [("pre", t0, n) for (t0, n) in PRE_TILES] + [("own", t0, n) for (t0, n) in OWN_TILES]
        import os
        sbstop = int(os.environ.get("SB_STOP", "99"))
        if sbstop <= 1:
            return
        nw = 0
        ev = 0
        for g in range(2):
            for which in ("k", "v"):
                coff = OFF_SB + (1024 if which == "k" else 2048) + g * 512
                dst = self.O["kb"] if which == "k" else self.O["vb"]
                w = wb[nw % 2]
                nw += 1
                self.wload(w, self.I["w_in"], coff, 512)

                def emit_ktr(ti, kpos, n):
                    pt = self.psf(4 + (ti % 2))
                    stf = stage[ti % 2]
                    for h in range(4):
                        self.tr(pt[:, h * 128:(h + 1) * 128], stf[:, h * 128:(h + 1) * 128], self.ident_f)
                    src_ps = pt[:, 0:512].rearrange("p (h t) -> p h t", h=4)[:, :, 0:n]
                    self.copy("act", KT[:, :, kpos:kpos + n], src_ps)
                pend_k = None
                for ti, (src, t0, n) in enumerate(all_tiles):
                    xT = self.xT_pre if src == "pre" else self.xT_own
                    kpos = t0 if src == "pre" else NPRE + t0
                    po = self.psf(6 + (ti % 2))[0:n, :]
                    for c in range(KC):
                        self.mm(po, xT[:, c, t0:t0 + n], w[:, c, :], start=(c == 0), stop=(c == KC - 1))
                    st = stage[ti % 2][0:n, :]
                    if which == "k":
                        self.copy("dve", st, po)
                        if src == "own":
                            S.dma("sp", dst[t0:t0 + n, g * 512:(g + 1) * 512], st, reads=[st], is_output=True)
                        if pend_k is not None:
                            emit_ktr(*pend_k)
                        pend_k = (ti, kpos, n)
                    else:
                        if src == "own":
                            self.copy("dve", st, po)
                            S.dma("sp", dst[t0:t0 + n, g * 512:(g + 1) * 512], st, reads=[st], is_output=True)
                            self.copy("act", VT[0:n, ti, :], st)
                        else:
                            self.copy("act", VT[0:n, ti, :], po)
                if which == "k" and pend_k is not None:
                    emit_ktr(*pend_k)
            self.copy("dve", KTn, KT[:, :, NPRE + 1024:NPRE + 1088])
            self.copy("dve", Vn[0:64, :], VT[0:64, 16, :])
            wq = wb[nw % 2]
            nw += 1
            self.wload(wq, self.I["w_in"], OFF_SB + g * 512, 512)
            def q_proj(hh, gi, wq=wq, g=g):
                (t0, n) = tok_groups(NOWN)[gi]
                po = self.psf(6)[:, 0:n]
                for c in range(KC):
                    self.mm(po, wq[:, c, hh * 128:(hh + 1) * 128], self.xT_own[:, c, t0:t0 + n],
                            start=(c == 0), stop=(c == KC - 1))
                self.act(QT2[(g * 4 + hh) % 2][:, t0:t0 + n], po, AF.Copy, scale=float(128 ** -0.5))
            for h in range(4):
                hg = g * 4 + h
                QT = QT2[hg % 2]
                if h == 0:
                    for gi in range(3):
                        q_proj(0, gi)
                def prompt_stream(sb, h=h, hg=hg, QT=QT):
                    nonlocal ev
                    blocks = []
                    for kb in range(4 * sb + 3, -1, -1):
                        cs = max(0, (kb - 4 * sb)) * 128
                        blocks.append(("own", kb, cs, kb >= 4 * sb))
                    for kb in range(7, -1, -1):
                        blocks.append(("pre", kb, 0, False))
                    A = self.psf(2 + sb)
                    OT = self.psf(4 + sb)
                    q0 = sb * 512
                    self.mm(A, zb[:, 0:128], zb[:, 0:512], start=True, stop=False, skip=True)
                    self.mm(OT, zb[:, 0:128], zb[:, 0:512], start=True, stop=False, skip=True)
                    yield
                    for (src, kb, cs, diag) in blocks:
                        kpos = kb * 128 if src == "pre" else NPRE + kb * 128
                        vt = kb if src == "pre" else 8 + kb
                        zt = self.psf(ev % 2)
                        E, SP, W, X = Eb[sb], SPb[sb], Wb[sb], Xb[sb]
                        ev += 1
                        kt = KT[:, h, kpos:kpos + 128]
                        self.mm(zt[:, cs:512], kt, QT[:, q0 + cs:q0 + 512])
                        self.act(E[:, cs:512], zt[:, cs:512], AF.Exp)
                        if src == "pre":
                            self.act(SP[:, cs:512], E[:, cs:512], AF.Ln, bias=one_c, scale=self.pre_bias,
                                     extra_reads=[one_c, self.pre_bias])
                        else:
                            self.act(SP[:, cs:512], E[:, cs:512], AF.Ln, bias=one_c, extra_reads=[one_c])
                        if diag:
                            self.asel(SP[:, cs:cs + 128], SP[:, cs:cs + 128], [[1, 128]], ALU.is_ge, 0.0, -1, -1)
                        yield
                        self.mm(A[:, cs:512], ntri_i, SP[:, cs:512], start=False, stop=False, skip=True)
                        self.act(X[:, cs:512], A[:, cs:512], AF.Exp)
                        self.tt("dve", W[:, cs:512], E[:, cs:512], X[:, cs:512], ALU.mult)
                        if diag:
                            self.asel(W[:, cs:cs + 128], W[:, cs:cs + 128], [[1, 128]], ALU.is_ge, 0.0, -1, -1)
                        yield
                        self.mm(A[:, cs:512], ntri_c, SP[:, cs:512], start=False, stop=False, skip=True)
                        self.mm(OT[:, cs:512], VT[:, vt, h * 128:(h + 1) * 128], W[:, cs:512], start=False, stop=False, skip=True)
                        yield
                    self.copy("dve", self.mixedT[:, 8 + hg, sb * 512:(sb + 1) * 512], OT)

                def sample_stream(h=h, hg=hg, QT=QT):
                    nonlocal ev
                    par = hg % 2
                    wfree = wb[nw % 2]
                    for s_ in range(2):
                        kst = wfree[:, 8 * par + 4 * s_:8 * par + 4 * s_ + 4, :].rearrange("p a (b d) -> p (a b) d", d=128)
                        S.dma("pool", kst, self.I["ck"][s_, :, hg * 128:(hg + 1) * 128].rearrange("(b p) d -> p b d", p=128),
                              writes=[kst])
                        S.dma("pool", Vc[:, s_, :, :],
                              self.I["cv"][s_, :, hg * 128:(hg + 1) * 128].rearrange("(b p) d -> p b d", p=128),
                              writes=[Vc[:, s_, :, :]])
                    yield
                    for s_ in range(2):
                        kst = wfree[:, 8 * par + 4 * s_:8 * par + 4 * s_ + 4, :].rearrange("p a (b d) -> p (a b) d", d=128)
                        for half in range(2):
                            pb = self.psb(6)
                            for c in range(8):
                                self.tr(pb[:, c * 128:(c + 1) * 128], kst[:, half * 8 + c, :], self.ident_b)
                            self.copy("dve", KTc[:, s_, half * 1024:(half + 1) * 1024], pb[:, 0:1024])
                            yield
                    A = self.psf(7)[:, 0:64]
                    OT = self.psf(7)[:, 128:192]
                    self.mm(self.psf(7)[:, 0:192], zb[:, 0:128], zb[:, 0:192], start=True, stop=False, skip=True)
                    qs = QT[:, 1024:1088]
                    for blk in [-1] + list(range(15, -1, -1)):
                        zt = self.psf(ev % 2)
                        E, SP, W, X = Eb[2], SPb[2], Wb[2], Xb[2]
                        ev += 1
                        if blk < 0:
                            kn = KTn[:, h, :]
                            self.mm(zt[0:64, 0:64], kn, qs)
                            self.act(E[0:64, 0:64], zt[0:64, 0:64], AF.Exp)
                            self.act(SP[0:64, 0:64], E[0:64, 0:64], AF.Ln, bias=one_c[0:64, :], extra_reads=[one_c[0:64, :]])
                            self.tt("pool", SP[0:64, 0:64], SP[0:64, 0:64], m64, ALU.mult)
                            yield
                            self.mm(A[0:64, :], ntri_i[0:64, 0:64], SP[0:64, 0:64], start=False, stop=False, skip=True)
                            self.act(X[0:64, 0:64], A[0:64, :], AF.Exp)
                            self.tt("dve", W[0:64, 0:64], E[0:64, 0:64], X[0:64, 0:64], ALU.mult)
                            self.tt("pool", W[0:64, 0:64], W[0:64, 0:64], m64, ALU.mult)
                            yield
                            self.mm(A, ntri_c[0:64, :], SP[0:64, 0:64], start=False, stop=False, skip=True)
                            self.mm(OT, Vn[0:64, h * 128:(h + 1) * 128], W[0:64, 0:64], start=False, stop=False, skip=True)
                            yield
                        else:
                            ks = [KTc[:, s_, blk * 128:(blk + 1) * 128] for s_ in range(2)]
                            cs2 = [slice(32 * s_, 32 * s_ + 32) for s_ in range(2)]
                            for s_ in range(2):
                                self.mm(zt[:, cs2[s_]], ks[s_], qs[:, cs2[s_]])
                            self.act(E[:, 0:64], zt[:, 0:64], AF.Exp)
                            self.act(SP[:, 0:64], E[:, 0:64], AF.Ln, bias=one_c, extra_reads=[one_c])
                            yield
                            self.mm(A, ntri_i, SP[:, 0:64], start=False, stop=False, skip=True)
                            self.act(X[:, 0:64], A, AF.Exp)
                            self.tt("dve", W[:, 0:64], E[:, 0:64], X[:, 0:64], ALU.mult)
                            yield
                            self.mm(A, ntri_c, SP[:, 0:64], start=False, stop=False, skip=True)
                            for s_ in range(2):
                                self.mm(OT[:, cs2[s_]], Vc[:, s_, blk, :], W[:, cs2[s_]], start=False, stop=False, skip=True)
                            if h < 3 and 13 <= blk <= 15:
                                q_proj(h + 1, 15 - blk)
                            yield
                    self.copy("dve", self.mixedT[:, 8 + hg, 1024:1088], OT)

                gens = [prompt_stream(0), prompt_stream(1), sample_stream()]
                while gens:
                    for g_ in list(gens):
                        try:
                            next(g_)
                        except StopIteration:
                            gens.remove(g_)
        self.tap("mixedT", self.mixedT, (128, KC, NOWN))

    def phase_wout(self):
        ar, S = self.ar, self.S
        m0 = ar.mark()
        wo = ar.alloc((4, KC, 512), BF16)
        for g in range(4):
            src = self.I["w_out"][:, g * 512:(g + 1) * 512].rearrange("(c p) n -> p c n", p=128)
            S.dma("pool", wo[:, g, :, :], src, writes=[wo[:, g, :, :]])
        po_ = self.xT_pre_off
        gb = ar.alloc((D,), F32, at=po_)
        bb = ar.alloc((D,), F32, at=po_ + 8192)
        S.dma("sp", gb, self.I["ln1_g"].partition_broadcast(128), writes=[gb])
        S.dma("sp", bb, self.I["ln1_b"].partition_broadcast(128), writes=[bb])
        self.x1T = self.xT_own
        xs = [ar.alloc((D,), F32, at=po_ + 16384 + i * 8192) for i in range(2)]
        ys = [ar.alloc((D,), F32) for _ in range(2)]
        yb = [ar.alloc((D,), BF16) for _ in range(2)]
        stats = ar.alloc((4, 6), F32)
        mv = ar.alloc((2,), F32)
        rstd = ar.alloc((1,), F32)
        self.x1_tokens = []
        def emit_tr(ybf, t0, n):
            for half in range(2):
                pb = self.psb(4 + half)
                for c in range(8):
                    cc = half * 8 + c
                    self.tr(pb[:, c * 128:c * 128 + n], ybf[:, cc * 128:(cc + 1) * 128], self.ident_b[0:n, 0:n])
                src_ps = pb[:, 0:1024].rearrange("p (c t) -> p c t", c=8)[:, :, 0:n]
                self.copy("dve" if half == 0 else "act", self.x1T[:, half * 8:(half + 1) * 8, t0:t0 + n], src_ps)
        pend_tr = None
        for ti, (t0, n) in enumerate(OWN_TILES):
            x = xs[ti % 2][0:n, :]
            y = ys[ti % 2][0:n, :]
            S.dma("sp", x, self.I["x_own"][t0:t0 + n, :], writes=[x])
            for g in range(4):
                po = self.psf(g)[0:n, :]
                for c in range(KC):
                    self.mm(po, self.mixedT[:, c, t0:t0 + n], wo[:, g, c, :],
                            start=(c == 0), stop=(c == KC - 1))
                self.stt("dve", y[:, g * 512:(g + 1) * 512], x[:, g * 512:(g + 1) * 512], ALPHA, po, ALU.mult, ALU.add)
            if pend_tr is not None:
                emit_tr(*pend_tr)
            self.layernorm(y, n, gb, bb, stats, mv, rstd)
            tok = S.dma("sp", self.x1s[t0:t0 + n, :], y, reads=[y])
            self.x1_tokens.append(tok)
            ybf = yb[ti % 2][0:n, :]
            self.copy("act", ybf, y)
            pend_tr = (ybf, t0, n)
        emit_tr(*pend_tr)
        ar.release(m0)

    def layernorm(self, y, n, gb, bb, stats, mv, rstd):
        S = self.S
        for g in range(4):
            S.op("dve", lambda e: e.bn_stats(stats[0:n, g, :], y[:, g * 512:(g + 1) * 512]),
                 reads=[y[:, g * 512:(g + 1) * 512]], writes=[stats[0:n, g, :]])
        S.op("dve", lambda e: e.bn_aggr(mv[0:n, :], stats[0:n, :, :].rearrange("p a b -> p (a b)")),
             reads=[stats[0:n, :, :]], writes=[mv[0:n, :]])
        self.act(rstd[0:n, :], mv[0:n, 1:2], AF.Ln, bias=self.eps_tile(LN_EPS)[0:n, :], scale=1.0,
                 extra_reads=[self.eps_tile(LN_EPS)[0:n, :]])
        self.act(rstd[0:n, :], rstd[0:n, :], AF.Exp, bias=0.0, scale=-0.5)
        self.stt("dve", y, y, mv[0:n, 0:1], gb[0:n, :], ALU.subtract, ALU.mult, extra_reads=[mv[0:n, 0:1]])
        self.stt("dve", y, y, rstd[0:n, :], bb[0:n, :], ALU.mult, ALU.add, extra_reads=[rstd[0:n, :]])

    def eps_tile(self, val):
        if not hasattr(self, "_eps"):
            self._eps = {}
        if val not in self._eps:
            t = self.nc.alloc_sbuf_tensor(f"eps_{len(self._eps)}", [128, 1], F32)
            self.memset("pool", t[:], val)
            self._eps[val] = t[:]
        return self._eps[val]

    def phase_ffn(self):
        ar, S = self.ar, self.S
        ar.release(self.xT_pre_off)
        m0 = ar.mark()
        hT = ar.alloc((64, NOWN), BF16)
        groups = tok_groups(NOWN)
        m1 = ar.mark()
        wu = [ar.alloc((KC, 256), BF16) for _ in range(2)]
        rl = [ar.alloc((512,), F32) for _ in range(2)]
        self.wload(wu[0], self.I["w_up"], 0, 256)
        k = 0
        for s in range(DFF // 256):
            if s + 1 < DFF // 256:
                self.wload(wu[(s + 1) % 2], self.I["w_up"], (s + 1) * 256, 256)
            w = wu[s % 2]
            for j in range(2):
                ft = s * 2 + j
                for gi, (t0, n) in enumerate(groups):
                    bank = k % 4
                    po = self.psf(bank)[:, 0:n]
                    for c in range(KC):
                        self.mm(po, w[:, c, j * 128:(j + 1) * 128], self.x1T[:, c, t0:t0 + n],
                                start=(c == 0), stop=(c == KC - 1))
                    r = rl[k % 2][:, 0:n]
                    self.act(r, po, AF.Relu)
                    self.tt("dve", hT[:, ft, t0:t0 + n], r, r, ALU.mult)
                    k += 1
        ar.release(m1)
        wd = [ar.alloc((64, 128), BF16, at=self.xT_own_off + i * 16384) for i in range(2)]
        oT = [ar.alloc((NOWN,), F32) for _ in range(2)]
        tk = [ar.alloc((128,), F32) for _ in range(2)]
        y2_tokens = []

        def wdload(i):
            src = self.I["w_down"][:, i * 128:(i + 1) * 128].rearrange("(c p) n -> p c n", p=128)
            S.dma("pool", wd[i % 2], src, writes=[wd[i % 2]])
        wdload(0)
        kkc = [0]

        def emit_dn(o, ct):
            for ti, (t0, n) in enumerate(OWN_TILES):
                kk = kkc[0]
                bank = 4 + (kk % 4)
                pt = self.psf(bank)[0:n, 0:128]
                self.tr(pt, o[:, t0:t0 + n], self.ident_f)
                st = tk[kk % 2][0:n, :]
                self.copy("dve" if kk % 2 == 0 else "act", st, pt)
                tok = S.dma("sp", self.y2s[t0:t0 + n, ct * 128:(ct + 1) * 128], st, reads=[st])
                y2_tokens.append(tok)
                kkc[0] += 1
        pend_dn = None
        for ct in range(16):
            if ct + 1 < 16:
                wdload(ct + 1)
            w = wd[ct % 2]
            o = oT[ct % 2]
            for gi, (t0, n) in enumerate(groups):
                bank = gi
                po = self.psf(bank)[:, 0:n]
                for c in range(64):
                    self.mm(po, w[:, c, :], hT[:, c, t0:t0 + n], start=(c == 0), stop=(c == 63))
                self.copy("act" if gi % 2 == 0 else "dve", o[:, t0:t0 + n], po)
            if pend_dn is not None:
                emit_dn(*pend_dn)
            pend_dn = (o, ct)
        emit_dn(*pend_dn)
        if True:
            pass
        ar.release(m0)
        gb = ar.alloc((D,), F32)
        bb = ar.alloc((D,), F32)
        S.dma("sp", gb, self.I["ln2_g"].partition_broadcast(128), writes=[gb])
        S.dma("sp", bb, self.I["ln2_b"].partition_broadcast(128), writes=[bb])
        xs = [ar.alloc((D,), F32) for _ in range(2)]
        ys = [ar.alloc((D,), F32) for _ in range(2)]
        stats = ar.alloc((4, 6), F32)
        mv = ar.alloc((2,), F32)
        rstd = ar.alloc((1,), F32)
        for ti, (t0, n) in enumerate(OWN_TILES):
            x = xs[ti % 2][0:n, :]
            y = ys[ti % 2][0:n, :]
            S.dma("sp", x, self.x1s[t0:t0 + n, :], writes=[x], after=self.x1_tokens)
            S.dma("sp", y, self.y2s[t0:t0 + n, :], writes=[y], after=y2_tokens)
            self.stt("dve", y, x, ALPHA, y, ALU.mult, ALU.add)
            self.layernorm(y, n, gb, bb, stats, mv, rstd)
            S.dma("sp", self.O["y"][t0:t0 + n, :], y, reads=[y], is_output=True)
        ar.release(m0)


_PROG = {}


def get_prog(stop_after=None, dbg=()):
    key = (stop_after, tuple(dbg))
    if key not in _PROG:
        _PROG[key] = Prog(stop_after, dbg)
    return _PROG[key]


def core_inputs(c, inp):
    b, h = c // 2, c % 2
    f = np.float32
    xp = inp["x_prompt"][b]
    x_own = np.concatenate([xp[h * 1024:(h + 1) * 1024], inp["x_sample"][2 * c], inp["x_sample"][2 * c + 1]], 0)
    x_pre = xp[0:1024] if h == 1 else np.zeros((1024, D), f)
    pre_bias = np.full((128, 1), 1.0 if h == 1 else 0.0, f)
    m = {
        "x_own": x_own, "x_pre": x_pre,
        "conv_s": inp["state_gdn_conv"][0, 2 * c:2 * c + 2],
        "S0_s": inp["state_gdn_S"][0, 2 * c:2 * c + 2],
        "ck": inp["cache_sb_k"][0, 2 * c:2 * c + 2].reshape(2, 2048, 1024),
        "cv": inp["cache_sb_v"][0, 2 * c:2 * c + 2].reshape(2, 2048, 1024),
        "w_in": inp["w_in"][0], "conv_w": inp["conv_w"][0], "a_log": inp["a_log"][0],
        "dt_bias": inp["dt_bias"][0], "gdn_norm_w": inp["gdn_norm_w"][0], "w_out": inp["w_out"][0],
        "ln1_g": inp["ln1_g"][0], "ln1_b": inp["ln1_b"][0], "w_up": inp["w_up"][0],
        "w_down": inp["w_down"][0], "ln2_g": inp["ln2_g"][0], "ln2_b": inp["ln2_b"][0],
        "pre_bias": pre_bias,
    }
    return {k: np.ascontiguousarray(v, dtype=f) for k, v in m.items()}


def kernel(**inputs):
    inp = {k: np.asarray(v) for k, v in inputs.items()}
    prog = get_prog()
    in_maps = [core_inputs(c, inp) for c in range(8)]
    res = run_bass_kernel_spmd(prog.nc, in_maps, core_ids=list(range(8)))
    R = res.results
    f = np.float32
    y_p = np.zeros((4, 2048, D), f)
    y_s = np.zeros((16, 32, D), f)
    conv_p = np.zeros((1, 4, 3, 3072), f)
    S_p = np.zeros((1, 4, 8, 128, 128), f)
    k_p = np.zeros((1, 4, 2048, 8, 128), f)
    v_p = np.zeros((1, 4, 2048, 8, 128), f)
    conv_s = np.zeros((1, 16, 3, 3072), f)
    S_s = np.zeros((1, 16, 8, 128, 128), f)
    k_s = np.zeros((1, 16, 32, 8, 128), f)
    v_s = np.zeros((1, 16, 32, 8, 128), f)
    for c in range(8):
        b, h = c // 2, c % 2
        r = R[c]
        y = np.asarray(r["y"])
        kb = np.asarray(r["kb"]).reshape(NOWN, 8, 128)
        vb = np.asarray(r["vb"]).reshape(NOWN, 8, 128)
        sl = slice(h * 1024, (h + 1) * 1024)
        y_p[b, sl] = y[0:1024]
        k_p[0, b, sl] = kb[0:1024]
        v_p[0, b, sl] = vb[0:1024]
        for j in range(2):
            s = 2 * c + j
            y_s[s] = y[1024 + 32 * j:1056 + 32 * j]
            k_s[0, s] = kb[1024 + 32 * j:1056 + 32 * j]
            v_s[0, s] = vb[1024 + 32 * j:1056 + 32 * j]
            conv_s[0, s] = np.asarray(r["conv_s_o"])[j]
            S_s[0, s] = np.asarray(r["S_s_o"])[j]
        if h == 1:
            conv_p[0, b] = np.asarray(r["conv_p_o"])
            S_p[0, b] = np.asarray(r["S_p_o"])
    return (y_p, y_s, conv_p, S_p, k_p, v_p, conv_s, S_s, k_s, v_s)
```

```python
import numpy as np
import concourse.bass as bass
import concourse.mybir as mybir
from concourse.bass_utils import run_bass_kernel_spmd

F32 = mybir.dt.float32
BF16 = mybir.dt.bfloat16
AF = mybir.ActivationFunctionType
ALU = mybir.AluOpType

D = 2048
KC = 16
NOWN = 1088
NPRE = 1024
NALL = NPRE + NOWN
PROJ_W = 7184
OFF_Z = 3072
OFF_B = 4096
OFF_A = 4104
OFF_SB = 4112
DFF = 8192
ALPHA = float(2 ** 0.25)
LN_EPS = 1e-5
NEG = -30000.0
EPOCH = 30000
import os as _os
SAME_ENGINE_INORDER = bool(_os.environ.get('SEI'))
NEU_SINGLE = _os.environ.get('NEU_SINGLE', '0') == '1'


def _rect(ap):
    t = ap.tensor
    dims = list(ap.ap)
    esz = mybir.dt.size(ap.dtype)
    tsz = mybir.dt.size(t.dtype)
    row = 1
    for s in list(t.shape)[1:]:
        row *= s
    rowb = row * tsz
    offb = ap.offset * esz
    pcnt = dims[0][1]
    p_lo = offb // rowb
    f_lo = offb - p_lo * rowb
    ext = 0
    for st, c in dims[1:]:
        ext += abs(st) * (c - 1)
    f_hi = f_lo + (ext + 1) * esz
    p_hi = p_lo + pcnt
    if t.name.startswith("ps"):
        f_lo, f_hi = 0, 2048
        p_lo = (p_lo // 32) * 32
        p_hi = ((p_hi + 31) // 32) * 32
    return (t.name, p_lo, p_hi, f_lo, f_hi)


class Sched:
    def __init__(self, nc, n_dma_sems=48):
        self.nc = nc
        self.E = {"pe": nc.tensor, "dve": nc.vector, "act": nc.scalar, "pool": nc.gpsimd, "sp": nc.sync}
        self.csem = {}
        self.ccnt = {}
        self.nep = {}
        for e in ("pe", "dve", "act", "pool"):
            self.csem[e] = nc.alloc_semaphore(f"c_{e}_0")
            self.ccnt[e] = 0
            self.nep[e] = 0
        self.dsems = [nc.alloc_semaphore(f"d_{i}") for i in range(n_dma_sems)]
        self.dcnt = [0] * n_dma_sems
        self.dnext = {"sw": 0, "hw": n_dma_sems // 2}
        self.drange = {"sw": (0, n_dma_sems // 2), "hw": (n_dma_sems // 2, n_dma_sems)}
        self.known = {e: {} for e in self.E}
        self.recs = {}
        self.n_inst = 0
        self.n_wait = 0
        self.out_tokens = []

    def _need(self, eng, tok, waits):
        sem, val, name = tok
        if self.known[eng].get(name, 0) >= val:
            return
        cur = waits.get(name)
        if cur is None or cur[1] < val:
            waits[name] = (sem, val)

    @staticmethod
    def _ov(a, b):
        return a[1] < b[2] and b[1] < a[2] and a[3] < b[4] and b[3] < a[4]

    @staticmethod
    def _covers(a, b):
        return a[1] <= b[1] and a[2] >= b[2] and a[3] <= b[3] and a[4] >= b[4]

    def _deps(self, eng, reads, writes, waits):
        for r in reads:
            psum = r[0].startswith("ps")
            for rec in self.recs.get(r[0], ()):
                if rec[1] and self._ov(rec[0], r):
                    if rec[3] == eng and (eng == "pe" or (SAME_ENGINE_INORDER and eng in ("act", "dve"))):
                        continue
                    self._need(eng, rec[2], waits)
                elif psum and (not rec[1]) and rec[3] != eng:
                    self._need(eng, rec[2], waits)
        for w in writes:
            for rec in self.recs.get(w[0], ()):
                if self._ov(rec[0], w):
                    if rec[3] == eng and (eng == "pe" or (SAME_ENGINE_INORDER and eng in ("act", "dve"))):
                        continue
                    self._need(eng, rec[2], waits)

    def _record(self, eng, tok, reads, writes):
        for w in writes:
            lst = self.recs.setdefault(w[0], [])
            lst[:] = [rec for rec in lst if not self._covers(w, rec[0])]
            lst.append([w, True, tok, eng])
        for r in reads:
            lst = self.recs.setdefault(r[0], [])
            done = False
            if eng != "dma":
                for rec in lst:
                    if (not rec[1]) and rec[3] == eng and rec[0] == r:
                        rec[2] = tok
                        done = True
                        break
            if not done:
                lst.append([r, False, tok, eng])

    def _emit_waits(self, eng, waits):
        e = self.E[eng]
        for name, (sem, val) in waits.items():
            e.wait_ge(sem, val)
            self.known[eng][name] = val
            self.n_wait += 1

    def op(self, eng, fn, reads=(), writes=()):
        rr = [_rect(a) for a in reads]
        ww = [_rect(a) for a in writes]
        waits = {}
        self._deps(eng, rr, ww, waits)
        if self.ccnt[eng] >= EPOCH:
            self.nep[eng] += 1
            self.csem[eng] = self.nc.alloc_semaphore(f"c_{eng}_{self.nep[eng]}")
            self.ccnt[eng] = 0
        self._emit_waits(eng, waits)
        ins = fn(self.E[eng])
        self.ccnt[eng] += 1
        sem = self.csem[eng]
        ins.then_inc(sem, 1)
        tok = (sem, self.ccnt[eng], f"c_{eng}_{self.nep[eng]}")
        self._record(eng, tok, rr, ww)
        self.n_inst += 1
        return tok

    def dma(self, q, out, in_, reads=(), writes=(), is_output=False, after=(), **kw):
        rr = [_rect(a) for a in reads]
        ww = [_rect(a) for a in writes]
        waits = {}
        self._deps(q, rr, ww, waits)
        for tok in after:
            self._need(q, tok, waits)
        kind = "sw" if q == "pool" else "hw"
        i = self.dnext[kind]
        lo, hi = self.drange[kind]
        self.dnext[kind] = lo + (i + 1 - lo) % (hi - lo)
        sem = self.dsems[i]
        name = f"d_{i}"
        if self.dcnt[i] > 0:
            self._need(q, (sem, self.dcnt[i] * 16, name), waits)
        self._emit_waits(q, waits)
        ins = self.E[q].dma_start(out=out, in_=in_, **kw)
        self.dcnt[i] += 1
        ins.then_inc(sem, 16)
        tok = (sem, self.dcnt[i] * 16, name)
        self._record("dma", tok, rr, ww)
        self.n_inst += 1
        if is_output:
            self.out_tokens.append(tok)
        return tok

    def finish(self):
        waits = {}
        for tok in self.out_tokens:
            self._need("sp", tok, waits)
        for i, sem in enumerate(self.dsems):
            if self.dcnt[i]:
                self._need("sp", (sem, self.dcnt[i] * 16, f"d_{i}"), waits)
        self._emit_waits("sp", waits)


class Arena:
    def __init__(self, nc, nbytes):
        self.t = nc.alloc_sbuf_tensor("arena", [128, nbytes // 2], BF16)
        self.cap = nbytes
        self.top = 0

    def alloc(self, shape, dtype, parts=128, at=None):
        if isinstance(shape, int):
            shape = (shape,)
        n = 1
        for s in shape:
            n *= s
        nb = n * mybir.dt.size(dtype)
        if at is not None:
            off = at
            assert off % 64 == 0 and off + nb <= self.cap
        else:
            off = self.top
            self.top += (nb + 63) // 64 * 64
            assert self.top <= self.cap, f"arena overflow {self.top} > {self.cap}"
        self.last_off = off
        v = self.t[0:parts, off // 2:(off + nb) // 2]
        if dtype != BF16:
            v = v.bitcast(dtype)
        if len(shape) == 2:
            v = v.rearrange("p (a b) -> p a b", a=shape[0])
        elif len(shape) == 3:
            v = v.rearrange("p (a b c) -> p a b c", a=shape[0], b=shape[1])
        return v

    def mark(self):
        return self.top

    def release(self, m):
        self.top = m


OWN_TILES = [(i * 128, 128) for i in range(8)] + [(1024, 64)]
PRE_TILES = [(i * 128, 128) for i in range(8)]


def tok_groups(n, g=512):
    out = []
    t = 0
    while t < n:
        out.append((t, min(g, n - t)))
        t += g
    return out


class Prog:
    def __init__(self, stop_after=None, dbg=()):
        self.stop_after = stop_after
        self.dbg_names = dbg
        nc = bass.Bass("TRN2", target_bir_lowering=False)
        self.nc = nc
        self.S = Sched(nc)
        self.I = {}
        self.O = {}
        self.dbg = {}

        def inp(name, shape):
            self.I[name] = nc.dram_tensor(name, list(shape), F32, kind="ExternalInput").ap()

        def outp(name, shape):
            self.O[name] = nc.dram_tensor(name, list(shape), F32, kind="ExternalOutput").ap()

        inp("x_own", (NOWN, D))
        inp("x_pre", (NPRE, D))
        inp("conv_s", (2, 3, 3072))
        inp("S0_s", (2, 8, 128, 128))
        inp("ck", (2, 2048, 1024))
        inp("cv", (2, 2048, 1024))
        inp("w_in", (D, PROJ_W))
        inp("conv_w", (4, 3072))
        inp("a_log", (8,))
        inp("dt_bias", (8,))
        inp("gdn_norm_w", (128,))
        inp("w_out", (D, D))
        inp("ln1_g", (D,))
        inp("ln1_b", (D,))
        inp("w_up", (D, DFF))
        inp("w_down", (DFF, D))
        inp("ln2_g", (D,))
        inp("ln2_b", (D,))
        inp("pre_bias", (128, 1))
        outp("y", (NOWN, D))
        outp("kb", (NOWN, 1024))
        outp("vb", (NOWN, 1024))
        outp("conv_p_o", (3, 3072))
        outp("conv_s_o", (2, 3, 3072))
        outp("S_p_o", (8, 128, 128))
        outp("S_s_o", (2, 8, 128, 128))
        self.x1s = nc.dram_tensor("x1_scratch", [NOWN, D], F32, kind="Internal").ap()
        self.y2s = nc.dram_tensor("y2_scratch", [NOWN, D], F32, kind="Internal").ap()

        self.ar = Arena(nc, 212480)
        self.ps = [nc.alloc_psum_tensor(f"ps{i}", [128, 512], F32) for i in range(8)]
        self.build()

    def psf(self, i):
        return self.ps[i][:]

    def psb(self, i):
        return self.ps[i][:].bitcast(BF16)

    def tap(self, name, ap, shape):
        if name not in self.dbg_names:
            return
        t = self.nc.dram_tensor("dbg_" + name, list(shape), ap.dtype, kind="ExternalOutput").ap()
        self.dbg[name] = t
        self.S.dma("sp", t, ap, reads=[ap], is_output=True)

    def mm(self, out, lhsT, rhs, start=True, stop=True, skip=False):
        if skip:
            self.S.op("pe", lambda e: e.matmul(out, lhsT, rhs, start=start, stop=stop, skip_group_check=True),
                      reads=[lhsT, rhs], writes=[out])
        else:
            self.S.op("pe", lambda e: e.matmul(out, lhsT, rhs, start=start, stop=stop),
                      reads=[lhsT, rhs], writes=[out])

    def tr(self, out, in_, ident):
        self.S.op("pe", lambda e: e.transpose(out, in_, ident), reads=[in_, ident], writes=[out])

    def copy(self, eng, out, in_):
        if eng == "act":
            self.S.op("act", lambda e: e.copy(out, in_), reads=[in_], writes=[out])
        else:
            self.S.op(eng, lambda e: e.tensor_copy(out, in_), reads=[in_], writes=[out])

    def act(self, out, in_, func, bias=0.0, scale=1.0, accum_out=None, extra_reads=()):
        rd = [in_] + list(extra_reads)
        wr = [out] + ([accum_out] if accum_out is not None else [])
        if accum_out is not None:
            self.S.op("act", lambda e: e.activation(out=out, in_=in_, func=func, bias=bias, scale=scale,
                                                    accum_out=accum_out), reads=rd, writes=wr)
        else:
            self.S.op("act", lambda e: e.activation(out=out, in_=in_, func=func, bias=bias, scale=scale),
                      reads=rd, writes=wr)

    def tt(self, eng, out, in0, in1, op):
        self.S.op(eng, lambda e: e.tensor_tensor(out=out, in0=in0, in1=in1, op=op), reads=[in0, in1], writes=[out])

    def ts(self, eng, out, in0, s1, op0, s2=None, op1=None, extra_reads=()):
        rd = [in0] + list(extra_reads)
        if op1 is None:
            self.S.op(eng, lambda e: e.tensor_scalar(out=out, in0=in0, scalar1=s1, scalar2=None, op0=op0),
                      reads=rd, writes=[out])
        else:
            self.S.op(eng, lambda e: e.tensor_scalar(out=out, in0=in0, scalar1=s1, scalar2=s2, op0=op0, op1=op1),
                      reads=rd, writes=[out])

    def stt(self, eng, out, in0, scalar, in1, op0, op1, extra_reads=()):
        rd = [in0, in1] + list(extra_reads)
        self.S.op(eng, lambda e: e.scalar_tensor_tensor(out=out, in0=in0, scalar=scalar, in1=in1, op0=op0, op1=op1),
                  reads=rd, writes=[out])

    def memset(self, eng, ap, val):
        self.S.op(eng, lambda e: e.memset(ap, val), writes=[ap])

    def asel(self, out, in_, pattern, cmp, fill, base, cm):
        self.S.op("pool", lambda e: e.affine_select(out=out, in_=in_, pattern=pattern, compare_op=cmp, fill=fill,
                                                    base=base, channel_multiplier=cm), reads=[in_], writes=[out])

    def build(self):
        self.consts()
        self.phase_x()
        if self.stop_after == "x":
            return self.S.finish()
        self.phase_attn()
        if self.stop_after == "attn":
            return self.S.finish()
        self.phase_wout()
        if self.stop_after == "wout":
            return self.S.finish()
        self.phase_ffn()
        self.S.finish()

    def consts(self):
        ar = self.ar
        self.ident_f = ar.alloc((128,), F32)
        self.ident_b = ar.alloc((128,), BF16)
        self.zeros_f = ar.alloc((128,), F32)
        self.memset("pool", self.zeros_f, 0.0)
        self.memset("pool", self.ident_f, 0.0)
        self.asel(self.ident_f, self.ident_f, [[-1, 128]], ALU.not_equal, 1.0, 0, 1)
        self.copy("pool", self.ident_b, self.ident_f)
        self.pre_bias = self.nc.alloc_sbuf_tensor("pre_bias_t", [128, 1], F32)[:]
        import os
        if os.environ.get("PB_MEMSET"):
            self.memset("pool", self.pre_bias, 1.0)
        else:
            self.S.dma("sp", self.pre_bias, self.I["pre_bias"], writes=[self.pre_bias])

    def phase_x(self):
        ar, S = self.ar, self.S
        self.xT_own = ar.alloc((KC, NOWN), BF16)
        self.xT_own_off = ar.last_off
        self.xT_pre = ar.alloc((KC, NPRE), BF16)
        self.xT_pre_off = ar.last_off
        m = ar.mark()
        xb = [ar.alloc((D,), BF16) for _ in range(2)]
        k = 0
        for src, tiles, dst in ((self.I["x_pre"], PRE_TILES, self.xT_pre), (self.I["x_own"], OWN_TILES, self.xT_own)):
            for (t0, n) in tiles:
                b = xb[k % 2]
                S.dma("pool", b[0:n, :], src[t0:t0 + n, :], writes=[b[0:n, :]])
                for half in range(2):
                    pb = self.psb(half)
                    for c in range(8):
                        cc = half * 8 + c
                        self.tr(pb[:, c * 128:c * 128 + n], b[0:n, cc * 128:(cc + 1) * 128], self.ident_b[0:n, 0:n])
                    src_ps = pb[:, 0:1024].rearrange("p (c t) -> p c t", c=8)[:, :, 0:n]
                    self.copy("dve" if half == 0 else "act", dst[:, half * 8:(half + 1) * 8, t0:t0 + n], src_ps)
                k += 1
        ar.release(m)
        self.tap("xT_own", self.xT_own, (128, KC, NOWN))

    def phase_attn(self):
        ar = self.ar
        self.mixedT = ar.alloc((KC, NOWN), BF16)
        self.memset("pool", self.mixedT, 0.0)
        m = ar.mark()
        import os
        if not os.environ.get("NO_GDN"):
            self.phase_gdn()
        ar.release(m)
        if not os.environ.get("NO_SB"):
            self.phase_sb()
        ar.release(m)

    def phase_gdn(self):
        ar, S, I = self.ar, self.S, self.I
        NT = 17
        tiles = [("pre", t0, n, t0) for (t0, n) in PRE_TILES] + [("own", t0, n, NPRE + t0) for (t0, n) in OWN_TILES]
        zb = ar.alloc((128,), BF16)
        ones_b = ar.alloc((128,), BF16)
        ones_f = ar.alloc((128,), F32)
        nones_f = ar.alloc((128,), F32)
        self.memset("pool", zb, 0.0)
        self.memset("pool", ones_b, 1.0)
        self.memset("pool", ones_f, 1.0)
        self.memset("pool", nones_f, -1.0)
        offd = ar.alloc((128,), BF16)
        self.memset("pool", offd, 1.0)
        self.asel(offd, offd, [[-1, 128]], ALU.not_equal, 0.0, 0, 1)
        ident4 = ar.alloc((4, 128), BF16)
        for j in range(4):
            self.copy("pool", ident4[:, j, :], self.ident_b)

        def block_mask(kind, c, rows):
            if kind in ("Ms", "Mi"):
                m = ar.alloc((128,), BF16)
                self.memset("pool", m, NEG)
            else:
                m = ar.alloc((128,), F32)
                self.memset("pool", m, 0.0)
            for r0 in range(0, rows, c):
                blk = slice(r0, r0 + c)
                if kind == "Ms":
                    self.asel(m[blk, blk], zb[blk, blk], [[-1, c]], ALU.is_ge, NEG, -1, 1)
                elif kind == "Mi":
                    self.asel(m[blk, blk], zb[blk, blk], [[-1, c]], ALU.is_ge, NEG, 0, 1)
                elif kind == "tri":
                    self.asel(m[blk, blk], ones_f[blk, blk], [[1, c]], ALU.is_ge, 0.0, 0, -1)
                elif kind == "last":
                    self.asel(m[blk, blk], ones_f[blk, blk], [[0, c]], ALU.is_equal, 0.0, -(c - 1), 1)
            return m
        maskMs = {64: block_mask("Ms", 64, 128), 32: block_mask("Ms", 32, 64)}
        maskMi = {64: block_mask("Mi", 64, 128), 32: block_mask("Mi", 32, 64)}
        trich = {64: block_mask("tri", 64, 128), 32: block_mask("tri", 32, 64)}
        sellast = {64: block_mask("last", 64, 128), 32: block_mask("last", 32, 64)}
        lastsel = {}
        for (c, rows) in ((64, 128), (32, 64)):
            for nch in range(2):
                m = ar.alloc((128,), F32)
                self.memset("pool", m, 0.0)
                self.asel(m[0:rows, :], ones_f[0:rows, :], [[0, 128]], ALU.is_equal, 0.0, -((nch + 1) * c - 1), 1)
                lastsel[(c, nch)] = m
        nsel = ar.alloc((8, 128), F32, parts=8)
        self.memset("pool", nsel, 0.0)
        for h in range(8):
            self.asel(nsel[:, h, :], nones_f[0:8, :], [[0, 128]], ALU.is_equal, 0.0, -h, 1)
        normw = ar.alloc((1,), F32)
        S.dma("sp", normw, I["gdn_norm_w"].rearrange("(p o) -> p o", o=1), writes=[normw])
        dtb = ar.alloc((8,), F32)
        S.dma("sp", dtb, I["dt_bias"].partition_broadcast(128), writes=[dtb])
        negA = ar.alloc((8,), F32)
        S.dma("sp", negA, I["a_log"].partition_broadcast(128), writes=[negA])
        self.act(negA, negA, AF.Exp)
        self.ts("dve", negA, negA, -1.0, ALU.mult)
        eps6 = self.eps_tile(1e-6)
        one_c = self.eps_tile(1.0)
        lnqs = self.eps_tile(float(np.log(128 ** -0.5)))
        cw = ar.alloc((24, 4), F32)
        hist = ar.alloc((24, 6), F32)
        mtmp = ar.mark()
        cwt = ar.alloc((3072,), F32, parts=4)
        S.dma("sp", cwt, I["conv_w"], writes=[cwt])
        hst = ar.alloc((3072,), F32, parts=6)
        S.dma("sp", hst, I["conv_s"].rearrange("s r c -> (s r) c"), writes=[hst])
        pc = self.psf(0)
        for ct in range(24):
            self.mm(pc[:, ct * 4:ct * 4 + 4], cwt[:, ct * 128:(ct + 1) * 128], self.ident_f[0:4, 0:4])
        self.copy("dve", cw, pc[:, 0:96].rearrange("p (a b) -> p a b", b=4))
        pc = self.psf(1)
        for ct in range(24):
            self.mm(pc[:, ct * 6:ct * 6 + 6], hst[:, ct * 128:(ct + 1) * 128], self.ident_f[0:6, 0:6])
        self.copy("dve", hist, pc[:, 0:144].rearrange("p (a b) -> p a b", b=6))
        ar.release(mtmp)
        w16 = ar.alloc((KC, 16), BF16)
        S.dma("pool", w16, I["w_in"][:, OFF_B:OFF_B + 16].rearrange("(c p) n -> p c n", p=128), writes=[w16])
        P16 = ar.alloc((NT, 16), F32)
        ps = self.psf(0)
        for ti, (src, t0, n, c0) in enumerate(tiles):
            xT = self.xT_pre if src == "pre" else self.xT_own
            for c in range(KC):
                self.mm(ps[0:n, ti * 16:(ti + 1) * 16], xT[:, c, t0:t0 + n], w16[:, c, :], start=(c == 0), stop=(c == KC - 1))
        self.memset("pool", P16, 0.0)
        self.copy("dve", P16[:, 0:16, :], ps[:, 0:256].rearrange("p (a b) -> p a b", b=16))
        self.copy("dve", P16[0:64, 16, :], ps[0:64, 256:272])
        TMq = {}
        for nm in ("g", "G", "lb", "beta", "sk", "eg", "st", "tmp"):
            TMq[nm] = ar.alloc((NT, 8), F32)
        bl = P16[:, :, 0:8]
        al = P16[:, :, 8:16]
        bc17 = lambda t: t.unsqueeze(1).broadcast_to([128, NT, 8])
        self.tt("dve", TMq["tmp"], al, bc17(dtb), ALU.add)
        self.act(TMq["tmp"], TMq["tmp"], AF.Exp)
        self.act(TMq["tmp"], TMq["tmp"], AF.Ln, bias=one_c, extra_reads=[one_c])
        self.tt("dve", TMq["g"], TMq["tmp"], bc17(negA), ALU.mult)
        self.act(TMq["lb"], bl, AF.Exp, scale=-1.0)
        self.act(TMq["lb"], TMq["lb"], AF.Ln, bias=one_c, extra_reads=[one_c])
        self.act(TMq["beta"], TMq["lb"], AF.Exp, scale=-1.0)
        ps = self.psf(1)
        for ti, (src, t0, n, c0) in enumerate(tiles):
            c = 64 if n == 128 else 32
            self.mm(ps[0:n, ti * 8:(ti + 1) * 8], trich[c][0:n, 0:n], TMq["g"][0:n, ti, :])
        self.memset("pool", TMq["G"], 0.0)
        self.copy("dve", TMq["G"][:, 0:16, :], ps[:, 0:128].rearrange("p (a b) -> p a b", b=8))
        self.copy("dve", TMq["G"][0:64, 16, :], ps[0:64, 128:136])
        self.act(TMq["eg"], TMq["G"], AF.Exp)
        self.tt("dve", TMq["tmp"], TMq["G"], TMq["lb"], ALU.subtract)
        self.act(TMq["sk"], TMq["tmp"], AF.Exp)
        ps = self.psf(0)
        for ti, (src, t0, n, c0) in enumerate(tiles):
            c = 64 if n == 128 else 32
            self.mm(ps[0:n, ti * 8:(ti + 1) * 8], sellast[c][0:n, 0:n], TMq["G"][0:n, ti, :])
        self.memset("pool", TMq["tmp"], 0.0)
        self.tt("dve", TMq["tmp"][:, 0:16, :], ps[:, 0:128].rearrange("p (a b) -> p a b", b=8), TMq["G"][:, 0:16, :], ALU.subtract)
        self.tt("dve", TMq["tmp"][0:64, 16, :], ps[0:64, 128:136], TMq["G"][0:64, 16, :], ALU.subtract)
        self.act(TMq["st"], TMq["tmp"], AF.Exp)
        GT = ar.alloc((NT * 2, 8), F32)
        ps = self.psf(1)
        for ti, (src, t0, n, c0) in enumerate(tiles):
            c = 64 if n == 128 else 32
            for nch in range(2):
                j = ti * 2 + nch
                self.mm(ps[:, j * 8:(j + 1) * 8], lastsel[(c, nch)][0:n, :], TMq["eg"][0:n, ti, :])
        self.copy("dve", GT, ps[:, 0:NT * 16].rearrange("p (a b) -> p a b", b=8))
        Grow = ar.alloc((NALL,), F32, parts=8)
        for q0 in range(0, NT, 4):
            ps = self.psf(q0 // 4 % 2)
            for ti in range(q0, min(NT, q0 + 4)):
                (src, t0, n, c0) = tiles[ti]
                self.mm(ps[0:8, (ti - q0) * 128:(ti - q0) * 128 + n], TMq["G"][0:n, ti, :], self.ident_f[0:n, 0:n])
            nc_ = sum(tiles[ti][2] for ti in range(q0, min(NT, q0 + 4)))
            self.copy("dve", Grow[:, tiles[q0][3]:tiles[q0][3] + nc_], ps[0:8, 0:nc_])
        self.tap("Grow", Grow, (8, NALL))
        self.tap("TMst", TMq["st"], (128, NT, 8))
        self.tap("TMsk", TMq["sk"], (128, NT, 8))
        self.tap("GT", GT, (128, NT * 2, 8))
        wb = ar.alloc((KC, 512), BF16)
        NKV = 2121
        pcb = ar.alloc((NKV,), F32)
        cvb = ar.alloc((NKV,), F32)
        knT = ar.alloc((NALL,), BF16)
        vT = ar.alloc((NALL,), BF16)
        qnT = ar.alloc((NOWN,), BF16)
        sqb = ar.alloc((512,), BF16)
        lnb_ = ar.alloc((512,), F32)
        zgT = ar.alloc((NOWN,), BF16)
        zs = ar.alloc((128,), F32)
        cst = lnb_[0:3, 0:384]
        opq = []
        for i in range(2):
            opq.append({
                "kbg": ar.alloc((4, 128), BF16), "kt": ar.alloc((4, 128), BF16), "vb": ar.alloc((4, 128), BF16),
                "wtok": ar.alloc((4, 128), BF16), "attnT": ar.alloc((4, 128), BF16), "ub": ar.alloc((4, 128), BF16),
                "nW2": ar.alloc((4, 2, 128), BF16), "nAWT": ar.alloc((4, 128), BF16),
            })
        Mb = [ar.alloc((4, 128), BF16) for _ in range(2)]
        Ab = [ar.alloc((4, 128), BF16) for _ in range(2)]
        Yb = [ar.alloc((4, 128), BF16) for _ in range(2)]
        Mp = ar.alloc((4, 128), BF16) if NEU_SINGLE else None
        DMi = ar.alloc((4, 128), BF16)
        DMs = ar.alloc((4, 128), BF16)
        attn = DMs
        Sf = ar.alloc((128,), F32)
        Sbf = ar.alloc((128,), BF16)
        t1 = ar.alloc((128,), F32)
        otok = ar.alloc((128,), F32)
        osq = t1
        ogb = ar.alloc((128,), BF16)
        ssq = ar.alloc((1,), F32)
        rs = ar.alloc((1,), F32)

        def kvcol(tok):
            if tok < 2048:
                return 3 + tok
            if tok < 2080:
                return 2054 + (tok - 2048)
            return 2089 + (tok - 2080)

        def qcol(t):
            if t < 1024:
                return 3 + t
            if t < 1056:
                return 1030 + (t - 1024)
            return 1065 + (t - 1056)
        kv_segs = [(3, 2048, 0), (2054, 32, 2048), (2089, 32, 2080)]
        q_segs = [(3, 1024, 0), (1030, 32, 1024), (1065, 32, 1056)]

        def conv_silu(ncols, ct):
            n = ncols - 3
            self.ts("dve", cvb[:, 3:ncols], pcb[:, 0:n], cw[:, ct, 0:1], ALU.mult, extra_reads=[cw[:, ct, 0:1]])
            for i in range(1, 4):
                self.stt("dve", cvb[:, 3:ncols], pcb[:, i:i + n], cw[:, ct, i:i + 1], cvb[:, 3:ncols], ALU.mult, ALU.add,
                         extra_reads=[cw[:, ct, i:i + 1]])

        def conv_out(which, h, cols3):
            ct = which * 8 + h
            pz = self.psf(1)
            for sgi, c0 in enumerate(cols3):
                self.mm(pz[0:3, sgi * 128:(sgi + 1) * 128], pcb[:, c0:c0 + 3], self.ident_f)
            self.copy("dve", cst, pz[0:3, 0:384])
            S.dma("sp", self.O["conv_p_o"][:, ct * 128:(ct + 1) * 128], cst[:, 0:128], reads=[cst[:, 0:128]], is_output=True)
            for j in range(2):
                S.dma("sp", self.O["conv_s_o"][j, :, ct * 128:(ct + 1) * 128], cst[:, (j + 1) * 128:(j + 2) * 128],
                      reads=[cst[:, (j + 1) * 128:(j + 2) * 128]], is_output=True)

        def normalize(segs, dst, extra_bias):
            for (c0, ln, k0) in segs:
                for o in range(0, ln, 512):
                    n = min(512, ln - o)
                    src = cvb[:, c0 + o:c0 + o + n]
                    self.act(sqb[:, 0:n], src, AF.Square)
                    pz = self.psf(1)
                    self.mm(pz[:, 0:n], ones_b, sqb[:, 0:n])
                    self.act(lnb_[:, 0:n], pz[:, 0:n], AF.Ln, bias=eps6, extra_reads=[eps6])
                    if extra_bias is None:
                        self.act(lnb_[:, 0:n], lnb_[:, 0:n], AF.Exp, scale=-0.5)
                    else:
                        self.act(lnb_[:, 0:n], lnb_[:, 0:n], AF.Exp, scale=-0.5, bias=extra_bias, extra_reads=[extra_bias])
                    self.tt("dve", dst[:, k0 + o:k0 + o + n], src, lnb_[:, 0:n], ALU.mult)

        quads = [list(range(0, 4)), list(range(4, 8)), list(range(8, 12)), list(range(12, 16)), [16]]

        for h in range(8):
            def wload_head(hh):
                for j, coff in enumerate((hh * 128, 1024 + hh * 128, 2048 + hh * 128, OFF_Z + hh * 128)):
                    src = I["w_in"][:, coff:coff + 128].rearrange("(c p) n -> p c n", p=128)
                    S.dma("pool", wb[:, :, j * 128:(j + 1) * 128], src, writes=[wb[:, :, j * 128:(j + 1) * 128]])
            if h == 0:
                wload_head(0)

            def proj_fm(wj, xT, t0, n, dst):
                self._pfm = getattr(self, "_pfm", 0) + 1
                po = self.psf(self._pfm % 2)[:, 0:n]
                for c in range(KC):
                    self.mm(po, wb[:, c, wj * 128:(wj + 1) * 128], xT[:, c, t0:t0 + n], start=(c == 0), stop=(c == KC - 1))
                self.copy("act", dst, po)
            def proj_q():
                proj_fm(0, self.xT_pre, 1021, 3, pcb[:, 0:3])
                for (t0, n) in ((0, 512), (512, 512)):
                    proj_fm(0, self.xT_own, t0, n, pcb[:, 3 + t0:3 + t0 + n])
                proj_fm(0, self.xT_own, 1024, 32, pcb[:, 1030:1062])
                proj_fm(0, self.xT_own, 1056, 32, pcb[:, 1065:1097])
                for j in range(2):
                    self.copy("pool", pcb[:, 1027 + 35 * j:1030 + 35 * j], hist[:, h, 3 * j:3 * j + 3])

            def proj_kv(which):
                self.memset("pool", pcb[:, 0:3], 0.0)
                for (t0, n) in ((0, 512), (512, 512)):
                    proj_fm(which, self.xT_pre, t0, n, pcb[:, 3 + t0:3 + t0 + n])
                for (t0, n) in ((0, 512), (512, 512)):
                    proj_fm(which, self.xT_own, t0, n, pcb[:, 3 + 1024 + t0:3 + 1024 + t0 + n])
                proj_fm(which, self.xT_own, 1024, 32, pcb[:, 2054:2086])
                proj_fm(which, self.xT_own, 1056, 32, pcb[:, 2089:2121])
                for j in range(2):
                    self.copy("pool", pcb[:, 2051 + 35 * j:2054 + 35 * j], hist[:, which * 8 + h, 3 * j:3 * j + 3])

            def conv_stage(which):
                if which == 0:
                    conv_out(0, h, (1024, 1059, 1094))
                    conv_silu(1097, h)
                    self.act(cvb[:, 3:1097], cvb[:, 3:1097], AF.Silu)
                else:
                    conv_out(which, h, (2048, 2083, 2118))
                    conv_silu(NKV, which * 8 + h)
                    self.act(cvb[:, 3:NKV], cvb[:, 3:NKV], AF.Silu)
            proj_q()
            conv_stage(0)
            proj_kv(1)
            normalize(q_segs, qnT, lnqs)
            conv_stage(1)
            proj_kv(2)
            normalize(kv_segs, knT, None)
            conv_stage(2)
            for (c0, ln, k0) in kv_segs:
                self.copy("pool", vT[:, k0:k0 + ln], cvb[:, c0:c0 + ln])
            for gi, (t0, n) in enumerate(tok_groups(NOWN)):
                pz = self.psf(gi % 2)[:, 0:n]
                for c in range(KC):
                    self.mm(pz, wb[:, c, 384:512], self.xT_own[:, c, t0:t0 + n], start=(c == 0), stop=(c == KC - 1))
                self.act(lnb_[:, 0:n], pz, AF.Silu)
                self.ts("dve", zgT[:, t0:t0 + n], lnb_[:, 0:n], normw, ALU.mult, extra_reads=[normw])
            if h == 0:
                self.tap("knT0", knT, (128, NALL))
                self.tap("qnT0", qnT, (128, NOWN))
                self.tap("vT0", vT, (128, NALL))
            if h + 1 < 8:
                wload_head(h + 1)
            self.memset("pool", Sf, 0.0)
            self.memset("pool", Sbf, 0.0)
            def gen_L(qi, h=h):
                quad = quads[qi]
                ops = opq[qi % 2]
                own_quad = tiles[quad[0]][0] == "own"
                nq = len(quad)
                nn = tiles[quad[0]][2]
                v3 = lambda p: p[0:nn, 0:nq * 128].rearrange("p (a b) -> p a b", b=128)[:, :, 0:nn]
                pb = self.psb(2)
                for j, ti in enumerate(quad):
                    (src, t0, n, c0) = tiles[ti]
                    self.tr(pb[0:n, j * 256:j * 256 + 128], knT[:, c0:c0 + n], self.ident_b)
                    self.tr(pb[0:n, j * 256 + 128:j * 256 + 256], vT[:, c0:c0 + n], self.ident_b)
                yield
                for j, ti in enumerate(quad):
                    (src, t0, n, c0) = tiles[ti]
                    kps = pb[0:n, j * 256:j * 256 + 128]
                    vps = pb[0:n, j * 256 + 128:j * 256 + 256]
                    sc = lambda nm: TMq[nm][0:n, ti, h:h + 1]
                    self.ts("dve", ops["kbg"][0:n, j, :], kps, sc("sk"), ALU.mult, extra_reads=[sc("sk")])
                    self.ts("dve", ops["kt"][0:n, j, :], kps, sc("st"), ALU.mult, extra_reads=[sc("st")])
                    self.ts("dve", ops["vb"][0:n, j, :], vps, sc("beta"), ALU.mult, extra_reads=[sc("beta")])
                pk = self.psf(3)
                pe_ = self.psf(4)
                for j, ti in enumerate(quad):
                    (src, t0, n, c0) = tiles[ti]
                    c = 64 if n == 128 else 32
                    self.mm(pk[0:n, j * 128:j * 128 + n], knT[:, c0:c0 + n], knT[:, c0:c0 + n])
                    self.mm(pe_[0:n, j * 128:j * 128 + n], nsel[:, h, 0:n], Grow[:, c0:c0 + n], start=True, stop=False)
                    msk = maskMi[c] if own_quad else maskMs[c]
                    self.mm(pe_[0:n, j * 128:j * 128 + n], self.ident_b[0:n, 0:n], msk[0:n, 0:n], start=False, stop=True)
                yield
                for j, ti in enumerate(quad):
                    (src, t0, n, c0) = tiles[ti]
                    gb_ = TMq["G"][0:n, ti, h:h + 1]
                    dst = DMi if own_quad else DMs
                    self.act(dst[0:n, j, 0:n], pe_[0:n, j * 128:j * 128 + n], AF.Exp, bias=gb_, extra_reads=[gb_])
                    if own_quad:
                        self.tt("pool", DMs[0:n, j, 0:n], DMi[0:n, j, 0:n], offd[0:n, 0:n], ALU.mult)
                M0, A0, Y0 = Mb[0], Ab[0], Yb[0]
                for j, ti in enumerate(quad):
                    (src, t0, n, c0) = tiles[ti]
                    bt_ = TMq["beta"][0:n, ti, h:h + 1]
                    self.stt("dve", M0[0:n, j, 0:n], pk[0:n, j * 128:j * 128 + n], bt_, DMs[0:n, j, 0:n], ALU.mult, ALU.mult,
                             extra_reads=[bt_])
                yield
                pt = self.psb(2)
                for j, ti in enumerate(quad):
                    n = tiles[ti][2]
                    self.tr(pt[0:n, j * 128:j * 128 + n], M0[0:n, j, 0:n], self.ident_b[0:n, 0:n])
                ptv = pt[0:nn, 0:nq * 128].rearrange("p (a b) -> p a b", b=128)[:, :, 0:nn]
                self.copy("act", A0[0:nn, 0:nq, 0:nn], ptv)
                self.stt("dve", Y0[0:nn, 0:nq, 0:nn], ptv, -1.0, ident4[0:nn, 0:nq, 0:nn], ALU.mult, ALU.add)
                yield
                cur = 0
                ycur = 0
                pend = None
                p5, p6, p7 = self.psf(5), self.psf(6), self.psf(7)

                def y_mm(Mlhs, Ysrc):
                    for j, ti in enumerate(quad):
                        n = tiles[ti][2]
                        self.mm(p7[0:n, j * 128:j * 128 + n], self.ident_b[0:n, 0:n], Ysrc[0:n, j, 0:n], start=True, stop=False)
                        self.mm(p7[0:n, j * 128:j * 128 + n], Mlhs[0:n, j, 0:n], Ysrc[0:n, j, 0:n], start=False, stop=True)
                for it in range(5):
                    Mc, Ac = Mb[cur], Ab[cur]
                    Mn, An = Mb[1 - cur], Ab[1 - cur]
                    for j, ti in enumerate(quad):
                        n = tiles[ti][2]
                        self.mm(p5[0:n, j * 128:j * 128 + n], Ac[0:n, j, 0:n], Mc[0:n, j, 0:n])
                        if it < 4:
                            self.mm(p6[0:n, j * 128:j * 128 + n], Mc[0:n, j, 0:n], Ac[0:n, j, 0:n])
                    if pend is not None:
                        y_mm(Mc, Yb[ycur])
                    yield
                    self.copy("act", Mn[0:nn, 0:nq, 0:nn], v3(p5))
                    if it < 4:
                        self.copy("dve", An[0:nn, 0:nq, 0:nn], v3(p6))
                    if pend is not None:
                        self.copy("dve", Yb[1 - ycur][0:nn, 0:nq, 0:nn], v3(p7))
                        ycur = 1 - ycur
                    pend = True
                    cur = 1 - cur
                    yield
                y_mm(Mb[cur], Yb[ycur])
                yield
                self.copy("dve", Yb[1 - ycur][0:nn, 0:nq, 0:nn], v3(p7))
                cur = 1 - ycur
                Yf = Yb[cur]
                pu, pw = self.psf(3), self.psf(4)
                for j, ti in enumerate(quad):
                    n = tiles[ti][2]
                    self.mm(pu[0:n, j * 128:(j + 1) * 128], Yf[0:n, j, 0:n], ops["vb"][0:n, j, :])
                    self.mm(pw[0:n, j * 128:(j + 1) * 128], Yf[0:n, j, 0:n], ops["kbg"][0:n, j, :])
                yield
                self.copy("act", ops["ub"][0:nn, 0:nq, :], pu[0:nn, 0:nq * 128].rearrange("p (a b) -> p a b", b=128))
                self.copy("dve", ops["wtok"][0:nn, 0:nq, :], pw[0:nn, 0:nq * 128].rearrange("p (a b) -> p a b", b=128))
                p56 = (self.psf(5), self.psf(6))
                cc = 64 if nn == 128 else 32
                for j, ti in enumerate(quad):
                    for nch in range(2):
                        r = slice(nch * cc, (nch + 1) * cc)
                        self.mm(p56[nch][:, j * 128:(j + 1) * 128], ops["wtok"][r, j, :], ops["kt"][r, j, :])
                yield
                self.ts("dve", ops["nW2"][:, 0:nq, 0, :], p56[0][:, 0:nq * 128].rearrange("p (a b) -> p a b", b=128), -1.0, ALU.mult)
                self.act(ops["nW2"][:, 0:nq, 1, :], p56[1][:, 0:nq * 128].rearrange("p (a b) -> p a b", b=128), AF.Copy, scale=-1.0)
                if own_quad:
                    pq = self.psf(3)
                    for j, ti in enumerate(quad):
                        (src, t0, n, c0) = tiles[ti]
                        self.mm(pq[0:n, j * 128:j * 128 + n], qnT[:, t0:t0 + n], knT[:, c0:c0 + n])
                    yield
                    self.tt("dve", attn[0:nn, 0:nq, 0:nn], v3(pq), DMi[0:nn, 0:nq, 0:nn], ALU.mult)
                    pt = self.psb(2)
                    for j, ti in enumerate(quad):
                        n = tiles[ti][2]
                        self.tr(pt[0:n, j * 128:j * 128 + n], attn[0:n, j, 0:n], self.ident_b[0:n, 0:n])
                    yield
                    self.copy("act", ops["attnT"][0:nn, 0:nq, 0:nn],
                              pt[0:nn, 0:nq * 128].rearrange("p (a b) -> p a b", b=128)[:, :, 0:nn])
                    p7 = self.psf(7)
                    for j, ti in enumerate(quad):
                        n = tiles[ti][2]
                        self.mm(p7[:, j * 128:j * 128 + n], ops["wtok"][0:n, j, :], ops["attnT"][0:n, j, 0:n])
                    yield
                    self.ts("dve", ops["nAWT"][:, 0:nq, 0:nn],
                            p7[:, 0:nq * 128].rearrange("p (a b) -> p a b", b=128)[:, :, 0:nn], -1.0, ALU.mult)
                if h == 0 and qi == 0:
                    self.tap("Yf0", Yf, (128, 4, 128))
                    self.tap("M00", Mb[0], (128, 4, 128))

            def gen_S(qi, h=h):
                quad = quads[qi]
                ops = opq[qi % 2]
                for j, ti in enumerate(quad):
                    (src, t0, n, c0) = tiles[ti]
                    c = 64 if n == 128 else 32
                    pS = self.psf(1)
                    for nch in range(n // c):
                        r = slice(nch * c, (nch + 1) * c)
                        if n == 64:
                            S.dma("sp", Sf, I["S0_s"][nch, h], writes=[Sf])
                            self.copy("act", Sbf, Sf)
                        self.mm(pS[:, 128:256], ops["nW2"][:, j, nch, :], Sbf, start=True, stop=False)
                        self.mm(pS[:, 128:256], ops["kt"][r, j, :], ops["ub"][r, j, :], start=False, stop=True)
                        if src == "own":
                            self.mm(pS[r, 256:384], qnT[:, t0 + nch * c:t0 + (nch + 1) * c], Sbf)
                            self.mm(pS[r, 384:512], ops["nAWT"][:, j, r], Sbf, start=True, stop=False)
                            self.mm(pS[r, 384:512], ops["attnT"][r, j, r], ops["ub"][r, j, :], start=False, stop=True)
                        yield
                        gt_ = GT[:, ti * 2 + nch, h:h + 1]
                        self.stt("dve", Sbf, Sf, gt_, pS[:, 128:256], ALU.mult, ALU.add, extra_reads=[gt_])
                        self.stt("dve", Sf, Sf, gt_, pS[:, 128:256], ALU.mult, ALU.add, extra_reads=[gt_])
                        if src == "own":
                            eg_ = TMq["eg"][r, ti, h:h + 1]
                            self.act(t1[r, :], pS[r, 256:384], AF.Copy, scale=eg_, extra_reads=[eg_])
                            self.tt("dve", otok[r, :], t1[r, :], pS[r, 384:512], ALU.add)
                        if n == 64:
                            S.dma("sp", self.O["S_s_o"][nch, h], Sf, reads=[Sf], is_output=True)
                        yield
                    if src == "own" and t0 == 896:
                        S.dma("sp", self.O["S_p_o"][h], Sf, reads=[Sf], is_output=True)
                    if src == "own":
                        oti = t0 // 128
                        self.act(osq[0:n, :], otok[0:n, :], AF.Square, accum_out=ssq[0:n, :])
                        self.act(rs[0:n, :], ssq[0:n, :], AF.Ln, scale=1.0 / 128.0, bias=eps6[0:n, :], extra_reads=[eps6[0:n, :]])
                        self.act(rs[0:n, :], rs[0:n, :], AF.Exp, scale=-0.5)
                        self.ts("dve", ogb[0:n, :], otok[0:n, :], rs[0:n, :], ALU.mult, extra_reads=[rs[0:n, :]])
                        pg = self.psb(0)
                        self.tr(pg[:, 0:n], ogb[0:n, :], self.ident_b[0:n, 0:n])
                        yield
                        self.tt("dve", self.mixedT[:, h, t0:t0 + n], pg[:, 0:n], zgT[:, t0:t0 + n], ALU.mult)

            def run_gens(gens):
                gens = list(gens)
                while gens:
                    for g_ in list(gens):
                        try:
                            next(g_)
                        except StopIteration:
                            gens.remove(g_)
            run_gens([gen_L(0)])
            for qi in range(len(quads)):
                gl = [gen_S(qi)]
                if qi + 1 < len(quads):
                    gl.append(gen_L(qi + 1))
                run_gens(gl)
        self.tap("mixedT_g", self.mixedT, (128, KC, NOWN))


    def wload(self, buf, w_ap, c0, ncols, kc=KC):
        src = w_ap[:, c0:c0 + ncols].rearrange("(c p) n -> p c n", p=128)
        dst = buf[:, :, 0:ncols]
        self.S.dma("pool", dst, src, writes=[dst])

    def phase_sb(self):
        ar, S = self.ar, self.S
        wb = [ar.alloc((KC, 512), BF16) for _ in range(2)]
        stage = [ar.alloc((512,), F32) for _ in range(2)]
        KT = ar.alloc((4, NALL), BF16)
        VT = ar.alloc((17, 512), BF16)
        QT2 = [ar.alloc((NOWN,), BF16) for _ in range(2)]
        Eb = [ar.alloc((512,), F32) for _ in range(2)]
        SPb = [ar.alloc((512,), BF16) for _ in range(2)]
        Wb = [ar.alloc((512,), BF16) for _ in range(2)]
        Xb = [ar.alloc((512,), F32) for _ in range(2)] + [ar.alloc((64,), F32)]
        Eb.append(ar.alloc((64,), F32))
        SPb.append(ar.alloc((64,), BF16))
        Wb.append(ar.alloc((64,), BF16))
        KTn = ar.alloc((4, 64), BF16)
        Vn = ar.alloc((512,), BF16)
        ntri_i = ar.alloc((128,), BF16)
        ntri_c = ar.alloc((128,), BF16)
        zb = ar.alloc((512,), BF16)
        KTc = ar.alloc((2, 2048), BF16)
        Vc = ar.alloc((2, 16, 128), BF16)
        m64 = ar.alloc((64,), BF16, parts=64)
        ones64 = ar.alloc((64,), BF16, parts=64)
        one_c = self.eps_tile(1.0)
        self.memset("pool", m64, 0.0)
        self.memset("pool", ones64, 1.0)
        for s_ in range(2):
            sq = slice(32 * s_, 32 * s_ + 32)
            self.asel(m64[sq, sq], ones64[sq, sq], [[1, 32]], ALU.is_ge, 0.0, -1, -1)
        self.memset("pool", zb, 0.0)
        self.memset("pool", ntri_i, -1.0)
        self.asel(ntri_i, ntri_i, [[-1, 128]], ALU.is_ge, 0.0, 0, 1)
        self.memset("pool", ntri_c, -1.0)
        self.asel(ntri_c, ntri_c, [[1, 128]], ALU.is_gt, 0.0, 0, -1)
        all_tiles = [("pre", t0, n) for (t0, n) in PRE_TILES] + [("own", t0, n) for (t0, n) in OWN_TILES]
        import os
        sbstop = int(os.environ.get("SB_STOP", "99"))
        if sbstop <= 1:
            return
        nw = 0
        ev = 0
        for g in range(2):
            for which in ("k", "v"):
                coff = OFF_SB + (1024 if which == "k" else 2048) + g * 512
                dst = self.O["kb"] if which == "k" else self.O["vb"]
                w = wb[nw % 2]
                nw += 1
                self.wload(w, self.I["w_in"], coff, 512)

                def emit_ktr(ti, kpos, n):
                    pt = self.psf(4 + (ti % 2))
                    stf = stage[ti % 2]
                    for h in range(4):
                        self.tr(pt[:, h * 128:(h + 1) * 128], stf[:, h * 128:(h + 1) * 128], self.ident_f)
                    src_ps = pt[:, 0:512].rearrange("p (h t) -> p h t", h=4)[:, :, 0:n]
                    self.copy("act", KT[:, :, kpos:kpos + n], src_ps)
                pend_k = None
                for ti, (src, t0, n) in enumerate(all_tiles):
                    xT = self.xT_pre if src == "pre" else self.xT_own
                    kpos = t0 if src == "pre" else NPRE + t0
                    po = self.psf(6 + (ti % 2))[0:n, :]
                    for c in range(KC):
                        self.mm(po, xT[:, c, t0:t0 + n], w[:, c, :], start=(c == 0), stop=(c == KC - 1))
                    st = stage[ti % 2][0:n, :]
                    if which == "k":
                        self.copy("dve", st, po)
                        if src == "own":
                            S.dma("sp", dst[t0:t0 + n, g * 512:(g + 1) * 512], st, reads=[st], is_output=True)
                        if pend_k is not None:
                            emit_ktr(*pend_k)
                        pend_k = (ti, kpos, n)
                    else:
                        if src == "own":
                            self.copy("dve", st, po)
                            S.dma("sp", dst[t0:t0 + n, g * 512:(g + 1) * 512], st, reads=[st], is_output=True)
                            self.copy("act", VT[0:n, ti, :], st)
                        else:
                            self.copy("act", VT[0:n, ti, :], po)
                if which == "k" and pend_k is not None:
                    emit_ktr(*pend_k)
            self.copy("dve", KTn, KT[:, :, NPRE + 1024:NPRE + 1088])
            self.copy("dve", Vn[0:64, :], VT[0:64, 16, :])
            wq = wb[nw % 2]
            nw += 1
            self.wload(wq, self.I["w_in"], OFF_SB + g * 512, 512)
            def q_proj(hh, gi, wq=wq, g=g):
                (t0, n) = tok_groups(NOWN)[gi]
                po = self.psf(6)[:, 0:n]
                for c in range(KC):
                    self.mm(po, wq[:, c, hh * 128:(hh + 1) * 128], self.xT_own[:, c, t0:t0 + n],
                            start=(c == 0), stop=(c == KC - 1))
                self.act(QT2[(g * 4 + hh) % 2][:, t0:t0 + n], po, AF.Copy, scale=float(128 ** -0.5))
            for h in range(4):
                hg = g * 4 + h
                QT = QT2[hg % 2]
                if h == 0:
                    for gi in range(3):
                        q_proj(0, gi)
                def prompt_stream(sb, h=h, hg=hg, QT=QT):
                    nonlocal ev
                    blocks = []
                    for kb in range(4 * sb + 3, -1, -1):
                        cs = max(0, (kb - 4 * sb)) * 128
                        blocks.append(("own", kb, cs, kb >= 4 * sb))
                    for kb in range(7, -1, -1):
                        blocks.append(("pre", kb, 0, False))
                    A = self.psf(2 + sb)
                    OT = self.psf(4 + sb)
                    q0 = sb * 512
                    self.mm(A, zb[:, 0:128], zb[:, 0:512], start=True, stop=False, skip=True)
                    self.mm(OT, zb[:, 0:128], zb[:, 0:512], start=True, stop=False, skip=True)
                    yield
                    for (src, kb, cs, diag) in blocks:
                        kpos = kb * 128 if src == "pre" else NPRE + kb * 128
                        vt = kb if src == "pre" else 8 + kb
                        zt = self.psf(ev % 2)
                        E, SP, W, X = Eb[sb], SPb[sb], Wb[sb], Xb[sb]
                        ev += 1
                        kt = KT[:, h, kpos:kpos + 128]
                        self.mm(zt[:, cs:512], kt, QT[:, q0 + cs:q0 + 512])
                        self.act(E[:, cs:512], zt[:, cs:512], AF.Exp)
                        if src == "pre":
                            self.act(SP[:, cs:512], E[:, cs:512], AF.Ln, bias=one_c, scale=self.pre_bias,
                                     extra_reads=[one_c, self.pre_bias])
                        else:
                            self.act(SP[:, cs:512], E[:, cs:512], AF.Ln, bias=one_c, extra_reads=[one_c])
                        if diag:
                            self.asel(SP[:, cs:cs + 128], SP[:, cs:cs + 128], [[1, 128]], ALU.is_ge, 0.0, -1, -1)
                        yield
                        self.mm(A[:, cs:512], ntri_i, SP[:, cs:512], start=False, stop=False, skip=True)
                        self.act(X[:, cs:512], A[:, cs:512], AF.Exp)
                        self.tt("dve", W[:, cs:512], E[:, cs:512], X[:, cs:512], ALU.mult)
                        if diag:
                            self.asel(W[:, cs:cs + 128], W[:, cs:cs + 128], [[1, 128]], ALU.is_ge, 0.0, -1, -1)
                        yield
                        self.mm(A[:, cs:512], ntri_c, SP[:, cs:512], start=False, stop=False, skip=True)
                        self.mm(OT[:, cs:512], VT[:, vt, h * 128:(h + 1) * 128], W[:, cs:512], start=False, stop=False, skip=True)
                        yield
                    self.copy("dve", self.mixedT[:, 8 + hg, sb * 512:(sb + 1) * 512], OT)

                def sample_stream(h=h, hg=hg, QT=QT):
                    nonlocal ev
                    par = hg % 2
                    wfree = wb[nw % 2]
                    for s_ in range(2):
                        kst = wfree[:, 8 * par + 4 * s_:8 * par + 4 * s_ + 4, :].rearrange("p a (b d) -> p (a b) d", d=128)
                        S.dma("pool", kst, self.I["ck"][s_, :, hg * 128:(hg + 1) * 128].rearrange("(b p) d -> p b d", p=128),
                              writes=[kst])
                        S.dma("pool", Vc[:, s_, :, :],
                              self.I["cv"][s_, :, hg * 128:(hg + 1) * 128].rearrange("(b p) d -> p b d", p=128),
                              writes=[Vc[:, s_, :, :]])
                    yield
                    for s_ in range(2):
                        kst = wfree[:, 8 * par + 4 * s_:8 * par + 4 * s_ + 4, :].rearrange("p a (b d) -> p (a b) d", d=128)
                        for half in range(2):
                            pb = self.psb(6)
                            for c in range(8):
                                self.tr(pb[:, c * 128:(c + 1) * 128], kst[:, half * 8 + c, :], self.ident_b)
                            self.copy("dve", KTc[:, s_, half * 1024:(half + 1) * 1024], pb[:, 0:1024])
                            yield
                    A = self.psf(7)[:, 0:64]
                    OT = self.psf(7)[:, 128:192]
                    self.mm(self.psf(7)[:, 0:192], zb[:, 0:128], zb[:, 0:192], start=True, stop=False, skip=True)
                    qs = QT[:, 1024:1088]
                    for blk in [-1] + list(range(15, -1, -1)):
                        zt = self.psf(ev % 2)
                        E, SP, W, X = Eb[2], SPb[2], Wb[2], Xb[2]
                        ev += 1
                        if blk < 0:
                            kn = KTn[:, h, :]
                            self.mm(zt[0:64, 0:64], kn, qs)
                            self.act(E[0:64, 0:64], zt[0:64, 0:64], AF.Exp)
                            self.act(SP[0:64, 0:64], E[0:64, 0:64], AF.Ln, bias=one_c[0:64, :], extra_reads=[one_c[0:64, :]])
                            self.tt("pool", SP[0:64, 0:64], SP[0:64, 0:64], m64, ALU.mult)
                            yield
                            self.mm(A[0:64, :], ntri_i[0:64, 0:64], SP[0:64, 0:64], start=False, stop=False, skip=True)
                            self.act(X[0:64, 0:64], A[0:64, :], AF.Exp)
                            self.tt("dve", W[0:64, 0:64], E[0:64, 0:64], X[0:64, 0:64], ALU.mult)
                            self.tt("pool", W[0:64, 0:64], W[0:64, 0:64], m64, ALU.mult)
                            yield
                            self.mm(A, ntri_c[0:64, :], SP[0:64, 0:64], start=False, stop=False, skip=True)
                            self.mm(OT, Vn[0:64, h * 128:(h + 1) * 128], W[0:64, 0:64], start=False, stop=False, skip=True)
                            yield
                        else:
                            ks = [KTc[:, s_, blk * 128:(blk + 1) * 128] for s_ in range(2)]
                            cs2 = [slice(32 * s_, 32 * s_ + 32) for s_ in range(2)]
                            for s_ in range(2):
                                self.mm(zt[:, cs2[s_]], ks[s_], qs[:, cs2[s_]])
                            self.act(E[:, 0:64], zt[:, 0:64], AF.Exp)
                            self.act(SP[:, 0:64], E[:, 0:64], AF.Ln, bias=one_c, extra_reads=[one_c])
                            yield
                            self.mm(A, ntri_i, SP[:, 0:64], start=False, stop=False, skip=True)
                            self.act(X[:, 0:64], A, AF.Exp)
                            self.tt("dve", W[:, 0:64], E[:, 0:64], X[:, 0:64], ALU.mult)
                            yield
                            self.mm(A, ntri_c, SP[:, 0:64], start=False, stop=False, skip=True)
                            for s_ in range(2):
                                self.mm(OT[:, cs2[s_]], Vc[:, s_, blk, :], W[:, cs2[s_]], start=False, stop=False, skip=True)
                            if h < 3 and 13 <= blk <= 15:
                                q_proj(h + 1, 15 - blk)
                            yield
                    self.copy("dve", self.mixedT[:, 8 + hg, 1024:1088], OT)

                gens = [prompt_stream(0), prompt_stream(1), sample_stream()]
                while gens:
                    for g_ in list(gens):
                        try:
                            next(g_)
                        except StopIteration:
                            gens.remove(g_)
        self.tap("mixedT", self.mixedT, (128, KC, NOWN))

    def phase_wout(self):
        ar, S = self.ar, self.S
        m0 = ar.mark()
        wo = ar.alloc((4, KC, 512), BF16)
        for g in range(4):
            src = self.I["w_out"][:, g * 512:(g + 1) * 512].rearrange("(c p) n -> p c n", p=128)
            S.dma("pool", wo[:, g, :, :], src, writes=[wo[:, g, :, :]])
        po_ = self.xT_pre_off
        gb = ar.alloc((D,), F32, at=po_)
        bb = ar.alloc((D,), F32, at=po_ + 8192)
        S.dma("sp", gb, self.I["ln1_g"].partition_broadcast(128), writes=[gb])
        S.dma("sp", bb, self.I["ln1_b"].partition_broadcast(128), writes=[bb])
        self.x1T = self.xT_own
        xs = [ar.alloc((D,), F32, at=po_ + 16384 + i * 8192) for i in range(2)]
        ys = [ar.alloc((D,), F32) for _ in range(2)]
        yb = [ar.alloc((D,), BF16) for _ in range(2)]
        stats = ar.alloc((4, 6), F32)
        mv = ar.alloc((2,), F32)
        rstd = ar.alloc((1,), F32)
        self.x1_tokens = []
        def emit_tr(ybf, t0, n):
            for half in range(2):
                pb = self.psb(4 + half)
                for c in range(8):
                    cc = half * 8 + c
                    self.tr(pb[:, c * 128:c * 128 + n], ybf[:, cc * 128:(cc + 1) * 128], self.ident_b[0:n, 0:n])
                src_ps = pb[:, 0:1024].rearrange("p (c t) -> p c t", c=8)[:, :, 0:n]
                self.copy("dve" if half == 0 else "act", self.x1T[:, half * 8:(half + 1) * 8, t0:t0 + n], src_ps)
        pend_tr = None
        for ti, (t0, n) in enumerate(OWN_TILES):
            x = xs[ti % 2][0:n, :]
            y = ys[ti % 2][0:n, :]
            S.dma("sp", x, self.I["x_own"][t0:t0 + n, :], writes=[x])
            for g in range(4):
                po = self.psf(g)[0:n, :]
                for c in range(KC):
                    self.mm(po, self.mixedT[:, c, t0:t0 + n], wo[:, g, c, :],
                            start=(c == 0), stop=(c == KC - 1))
                self.stt("dve", y[:, g * 512:(g + 1) * 512], x[:, g * 512:(g + 1) * 512], ALPHA, po, ALU.mult, ALU.add)
            if pend_tr is not None:
                emit_tr(*pend_tr)
            self.layernorm(y, n, gb, bb, stats, mv, rstd)
            tok = S.dma("sp", self.x1s[t0:t0 + n, :], y, reads=[y])
            self.x1_tokens.append(tok)
            ybf = yb[ti % 2][0:n, :]
            self.copy("act", ybf, y)
            pend_tr = (ybf, t0, n)
        emit_tr(*pend_tr)
        ar.release(m0)

    def layernorm(self, y, n, gb, bb, stats, mv, rstd):
        S = self.S
        for g in range(4):
            S.op("dve", lambda e: e.bn_stats(stats[0:n, g, :], y[:, g * 512:(g + 1) * 512]),
                 reads=[y[:, g * 512:(g + 1) * 512]], writes=[stats[0:n, g, :]])
        S.op("dve", lambda e: e.bn_aggr(mv[0:n, :], stats[0:n, :, :].rearrange("p a b -> p (a b)")),
             reads=[stats[0:n, :, :]], writes=[mv[0:n, :]])
        self.act(rstd[0:n, :], mv[0:n, 1:2], AF.Ln, bias=self.eps_tile(LN_EPS)[0:n, :], scale=1.0,
                 extra_reads=[self.eps_tile(LN_EPS)[0:n, :]])
        self.act(rstd[0:n, :], rstd[0:n, :], AF.Exp, bias=0.0, scale=-0.5)
        self.stt("dve", y, y, mv[0:n, 0:1], gb[0:n, :], ALU.subtract, ALU.mult, extra_reads=[mv[0:n, 0:1]])
        self.stt("dve", y, y, rstd[0:n, :], bb[0:n, :], ALU.mult, ALU.add, extra_reads=[rstd[0:n, :]])

    def eps_tile(self, val):
        if not hasattr(self, "_eps"):
            self._eps = {}
        if val not in self._eps:
            t = self.nc.alloc_sbuf_tensor(f"eps_{len(self._eps)}", [128, 1], F32)
            self.memset("pool", t[:], val)
            self._eps[val] = t[:]
        return self._eps[val]

    def phase_ffn(self):
        ar, S = self.ar, self.S
        ar.release(self.xT_pre_off)
        m0 = ar.mark()
        hT = ar.alloc((64, NOWN), BF16)
        groups = tok_groups(NOWN)
        m1 = ar.mark()
        wu = [ar.alloc((KC, 256), BF16) for _ in range(2)]
        rl = [ar.alloc((512,), F32) for _ in range(2)]
        self.wload(wu[0], self.I["w_up"], 0, 256)
        k = 0
        for s in range(DFF // 256):
            if s + 1 < DFF // 256:
                self.wload(wu[(s + 1) % 2], self.I["w_up"], (s + 1) * 256, 256)
            w = wu[s % 2]
            for j in range(2):
                ft = s * 2 + j
                for gi, (t0, n) in enumerate(groups):
                    bank = k % 4
                    po = self.psf(bank)[:, 0:n]
                    for c in range(KC):
                        self.mm(po, w[:, c, j * 128:(j + 1) * 128], self.x1T[:, c, t0:t0 + n],
                                start=(c == 0), stop=(c == KC - 1))
                    r = rl[k % 2][:, 0:n]
                    self.act(r, po, AF.Relu)
                    self.tt("dve", hT[:, ft, t0:t0 + n], r, r, ALU.mult)
                    k += 1
        ar.release(m1)
        wd = [ar.alloc((64, 128), BF16, at=self.xT_own_off + i * 16384) for i in range(2)]
        oT = [ar.alloc((NOWN,), F32) for _ in range(2)]
        tk = [ar.alloc((128,), F32) for _ in range(2)]
        y2_tokens = []

        def wdload(i):
            src = self.I["w_down"][:, i * 128:(i + 1) * 128].rearrange("(c p) n -> p c n", p=128)
            S.dma("pool", wd[i % 2], src, writes=[wd[i % 2]])
        wdload(0)
        kkc = [0]

        def emit_dn(o, ct):
            for ti, (t0, n) in enumerate(OWN_TILES):
                kk = kkc[0]
                bank = 4 + (kk % 4)
                pt = self.psf(bank)[0:n, 0:128]
                self.tr(pt, o[:, t0:t0 + n], self.ident_f)
                st = tk[kk % 2][0:n, :]
                self.copy("dve" if kk % 2 == 0 else "act", st, pt)
                tok = S.dma("sp", self.y2s[t0:t0 + n, ct * 128:(ct + 1) * 128], st, reads=[st])
                y2_tokens.append(tok)
                kkc[0] += 1
        pend_dn = None
        for ct in range(16):
            if ct + 1 < 16:
                wdload(ct + 1)
            w = wd[ct % 2]
            o = oT[ct % 2]
            for gi, (t0, n) in enumerate(groups):
                bank = gi
                po = self.psf(bank)[:, 0:n]
                for c in range(64):
                    self.mm(po, w[:, c, :], hT[:, c, t0:t0 + n], start=(c == 0), stop=(c == 63))
                self.copy("act" if gi % 2 == 0 else "dve", o[:, t0:t0 + n], po)
            if pend_dn is not None:
                emit_dn(*pend_dn)
            pend_dn = (o, ct)
        emit_dn(*pend_dn)
        if True:
            pass
        ar.release(m0)
        gb = ar.alloc((D,), F32)
        bb = ar.alloc((D,), F32)
        S.dma("sp", gb, self.I["ln2_g"].partition_broadcast(128), writes=[gb])
        S.dma("sp", bb, self.I["ln2_b"].partition_broadcast(128), writes=[bb])
        xs = [ar.alloc((D,), F32) for _ in range(2)]
        ys = [ar.alloc((D,), F32) for _ in range(2)]
        stats = ar.alloc((4, 6), F32)
        mv = ar.alloc((2,), F32)
        rstd = ar.alloc((1,), F32)
        for ti, (t0, n) in enumerate(OWN_TILES):
            x = xs[ti % 2][0:n, :]
            y = ys[ti % 2][0:n, :]
            S.dma("sp", x, self.x1s[t0:t0 + n, :], writes=[x], after=self.x1_tokens)
            S.dma("sp", y, self.y2s[t0:t0 + n, :], writes=[y], after=y2_tokens)
            self.stt("dve", y, x, ALPHA, y, ALU.mult, ALU.add)
            self.layernorm(y, n, gb, bb, stats, mv, rstd)
            S.dma("sp", self.O["y"][t0:t0 + n, :], y, reads=[y], is_output=True)
        ar.release(m0)


_PROG = {}


def get_prog(stop_after=None, dbg=()):
    key = (stop_after, tuple(dbg))
    if key not in _PROG:
        _PROG[key] = Prog(stop_after, dbg)
    return _PROG[key]


def core_inputs(c, inp):
    b, h = c // 2, c % 2
    f = np.float32
    xp = inp["x_prompt"][b]
    x_own = np.concatenate([xp[h * 1024:(h + 1) * 1024], inp["x_sample"][2 * c], inp["x_sample"][2 * c + 1]], 0)
    x_pre = xp[0:1024] if h == 1 else np.zeros((1024, D), f)
    pre_bias = np.full((128, 1), 1.0 if h == 1 else 0.0, f)
    m = {
        "x_own": x_own, "x_pre": x_pre,
        "conv_s": inp["state_gdn_conv"][0, 2 * c:2 * c + 2],
        "S0_s": inp["state_gdn_S"][0, 2 * c:2 * c + 2],
        "ck": inp["cache_sb_k"][0, 2 * c:2 * c + 2].reshape(2, 2048, 1024),
        "cv": inp["cache_sb_v"][0, 2 * c:2 * c + 2].reshape(2, 2048, 1024),
        "w_in": inp["w_in"][0], "conv_w": inp["conv_w"][0], "a_log": inp["a_log"][0],
        "dt_bias": inp["dt_bias"][0], "gdn_norm_w": inp["gdn_norm_w"][0], "w_out": inp["w_out"][0],
        "ln1_g": inp["ln1_g"][0], "ln1_b": inp["ln1_b"][0], "w_up": inp["w_up"][0],
        "w_down": inp["w_down"][0], "ln2_g": inp["ln2_g"][0], "ln2_b": inp["ln2_b"][0],
        "pre_bias": pre_bias,
    }
    return {k: np.ascontiguousarray(v, dtype=f) for k, v in m.items()}


def kernel(**inputs):
    inp = {k: np.asarray(v) for k, v in inputs.items()}
    prog = get_prog()
    in_maps = [core_inputs(c, inp) for c in range(8)]
    res = run_bass_kernel_spmd(prog.nc, in_maps, core_ids=list(range(8)))
    R = res.results
    f = np.float32
    y_p = np.zeros((4, 2048, D), f)
    y_s = np.zeros((16, 32, D), f)
    conv_p = np.zeros((1, 4, 3, 3072), f)
    S_p = np.zeros((1, 4, 8, 128, 128), f)
    k_p = np.zeros((1, 4, 2048, 8, 128), f)
    v_p = np.zeros((1, 4, 2048, 8, 128), f)
    conv_s = np.zeros((1, 16, 3, 3072), f)
    S_s = np.zeros((1, 16, 8, 128, 128), f)
    k_s = np.zeros((1, 16, 32, 8, 128), f)
    v_s = np.zeros((1, 16, 32, 8, 128), f)
    for c in range(8):
        b, h = c // 2, c % 2
        r = R[c]
        y = np.asarray(r["y"])
        kb = np.asarray(r["kb"]).reshape(NOWN, 8, 128)
        vb = np.asarray(r["vb"]).reshape(NOWN, 8, 128)
        sl = slice(h * 1024, (h + 1) * 1024)
        y_p[b, sl] = y[0:1024]
        k_p[0, b, sl] = kb[0:1024]
        v_p[0, b, sl] = vb[0:1024]
        for j in range(2):
            s = 2 * c + j
            y_s[s] = y[1024 + 32 * j:1056 + 32 * j]
            k_s[0, s] = kb[1024 + 32 * j:1056 + 32 * j]
            v_s[0, s] = vb[1024 + 32 * j:1056 + 32 * j]
            conv_s[0, s] = np.asarray(r["conv_s_o"])[j]
            S_s[0, s] = np.asarray(r["S_s_o"])[j]
        if h == 1:
            conv_p[0, b] = np.asarray(r["conv_p_o"])
            S_p[0, b] = np.asarray(r["S_p_o"])
    return (y_p, y_s, conv_p, S_p, k_p, v_p, conv_s, S_s, k_s, v_s)
```

```python
import numpy as np
import concourse.bass as bass
import concourse.mybir as mybir
from concourse.bass_utils import run_bass_kernel_spmd

F32 = mybir.dt.float32
BF16 = mybir.dt.bfloat16
AF = mybir.ActivationFunctionType
ALU = mybir.AluOpType

D = 2048
KC = 16
NOWN = 1088
NPRE = 1024
NALL = NPRE + NOWN
PROJ_W = 7184
OFF_Z = 3072
OFF_B = 4096
OFF_A = 4104
OFF_SB = 4112
DFF = 8192
ALPHA = float(2 ** 0.25)
LN_EPS = 1e-5
NEG = -30000.0
EPOCH = 30000
import os as _os
SAME_ENGINE_INORDER = bool(_os.environ.get('SEI'))
NEU_SINGLE = _os.environ.get('NEU_SINGLE', '0') == '1'


def _rect(ap):
    t = ap.tensor
    dims = list(ap.ap)
    esz = mybir.dt.size(ap.dtype)
    tsz = mybir.dt.size(t.dtype)
    row = 1
    for s in list(t.shape)[1:]:
        row *= s
    rowb = row * tsz
    offb = ap.offset * esz
    pcnt = dims[0][1]
    p_lo = offb // rowb
    f_lo = offb - p_lo * rowb
    ext = 0
    for st, c in dims[1:]:
        ext += abs(st) * (c - 1)
    f_hi = f_lo + (ext + 1) * esz
    p_hi = p_lo + pcnt
    if t.name.startswith("ps"):
        f_lo, f_hi = 0, 2048
        p_lo = (p_lo // 32) * 32
        p_hi = ((p_hi + 31) // 32) * 32
    return (t.name, p_lo, p_hi, f_lo, f_hi)


class Sched:
    def __init__(self, nc, n_dma_sems=48):
        self.nc = nc
        self.E = {"pe": nc.tensor, "dve": nc.vector, "act": nc.scalar, "pool": nc.gpsimd, "sp": nc.sync}
        self.csem = {}
        self.ccnt = {}
        self.nep = {}
        for e in ("pe", "dve", "act", "pool"):
            self.csem[e] = nc.alloc_semaphore(f"c_{e}_0")
            self.ccnt[e] = 0
            self.nep[e] = 0
        self.dsems = [nc.alloc_semaphore(f"d_{i}") for i in range(n_dma_sems)]
        self.dcnt = [0] * n_dma_sems
        self.dnext = {"sw": 0, "hw": n_dma_sems // 2}
        self.drange = {"sw": (0, n_dma_sems // 2), "hw": (n_dma_sems // 2, n_dma_sems)}
        self.known = {e: {} for e in self.E}
        self.recs = {}
        self.n_inst = 0
        self.n_wait = 0
        self.out_tokens = []

    def _need(self, eng, tok, waits):
        sem, val, name = tok
        if self.known[eng].get(name, 0) >= val:
            return
        cur = waits.get(name)
        if cur is None or cur[1] < val:
            waits[name] = (sem, val)

    @staticmethod
    def _ov(a, b):
        return a[1] < b[2] and b[1] < a[2] and a[3] < b[4] and b[3] < a[4]

    @staticmethod
    def _covers(a, b):
        return a[1] <= b[1] and a[2] >= b[2] and a[3] <= b[3] and a[4] >= b[4]

    def _deps(self, eng, reads, writes, waits):
        for r in reads:
            psum = r[0].startswith("ps")
            for rec in self.recs.get(r[0], ()):
                if rec[1] and self._ov(rec[0], r):
                    if rec[3] == eng and (eng == "pe" or (SAME_ENGINE_INORDER and eng in ("act", "dve"))):
                        continue
                    self._need(eng, rec[2], waits)
                elif psum and (not rec[1]) and rec[3] != eng:
                    self._need(eng, rec[2], waits)
        for w in writes:
            for rec in self.recs.get(w[0], ()):
                if self._ov(rec[0], w):
                    if rec[3] == eng and (eng == "pe" or (SAME_ENGINE_INORDER and eng in ("act", "dve"))):
                        continue
                    self._need(eng, rec[2], waits)

    def _record(self, eng, tok, reads, writes):
        for w in writes:
            lst = self.recs.setdefault(w[0], [])
            lst[:] = [rec for rec in lst if not self._covers(w, rec[0])]
            lst.append([w, True, tok, eng])
        for r in reads:
            lst = self.recs.setdefault(r[0], [])
            done = False
            if eng != "dma":
                for rec in lst:
                    if (not rec[1]) and rec[3] == eng and rec[0] == r:
                        rec[2] = tok
                        done = True
                        break
            if not done:
                lst.append([r, False, tok, eng])

    def _emit_waits(self, eng, waits):
        e = self.E[eng]
        for name, (sem, val) in waits.items():
            e.wait_ge(sem, val)
            self.known[eng][name] = val
            self.n_wait += 1

    def op(self, eng, fn, reads=(), writes=()):
        rr = [_rect(a) for a in reads]
        ww = [_rect(a) for a in writes]
        waits = {}
        self._deps(eng, rr, ww, waits)
        if self.ccnt[eng] >= EPOCH:
            self.nep[eng] += 1
            self.csem[eng] = self.nc.alloc_semaphore(f"c_{eng}_{self.nep[eng]}")
            self.ccnt[eng] = 0
        self._emit_waits(eng, waits)
        ins = fn(self.E[eng])
        self.ccnt[eng] += 1
        sem = self.csem[eng]
        ins.then_inc(sem, 1)
        tok = (sem, self.ccnt[eng], f"c_{eng}_{self.nep[eng]}")
        self._record(eng, tok, rr, ww)
        self.n_inst += 1
        return tok

    def dma(self, q, out, in_, reads=(), writes=(), is_output=False, after=(), **kw):
        rr = [_rect(a) for a in reads]
        ww = [_rect(a) for a in writes]
        waits = {}
        self._deps(q, rr, ww, waits)
        for tok in after:
            self._need(q, tok, waits)
        kind = "sw" if q == "pool" else "hw"
        i = self.dnext[kind]
        lo, hi = self.drange[kind]
        self.dnext[kind] = lo + (i + 1 - lo) % (hi - lo)
        sem = self.dsems[i]
        name = f"d_{i}"
        if self.dcnt[i] > 0:
            self._need(q, (sem, self.dcnt[i] * 16, name), waits)
        self._emit_waits(q, waits)
        ins = self.E[q].dma_start(out=out, in_=in_, **kw)
        self.dcnt[i] += 1
        ins.then_inc(sem, 16)
        tok = (sem, self.dcnt[i] * 16, name)
        self._record("dma", tok, rr, ww)
        self.n_inst += 1
        if is_output:
            self.out_tokens.append(tok)
        return tok

    def finish(self):
        waits = {}
        for tok in self.out_tokens:
            self._need("sp", tok, waits)
        for i, sem in enumerate(self.dsems):
            if self.dcnt[i]:
                self._need("sp", (sem, self.dcnt[i] * 16, f"d_{i}"), waits)
        self._emit_waits("sp", waits)


class Arena:
    def __init__(self, nc, nbytes):
        self.t = nc.alloc_sbuf_tensor("arena", [128, nbytes // 2], BF16)
        self.cap = nbytes
        self.top = 0

    def alloc(self, shape, dtype, parts=128, at=None):
        if isinstance(shape, int):
            shape = (shape,)
        n = 1
        for s in shape:
            n *= s
        nb = n * mybir.dt.size(dtype)
        if at is not None:
            off = at
            assert off % 64 == 0 and off + nb <= self.cap
        else:
            off = self.top
            self.top += (nb + 63) // 64 * 64
            assert self.top <= self.cap, f"arena overflow {self.top} > {self.cap}"
        self.last_off = off
        v = self.t[0:parts, off // 2:(off + nb) // 2]
        if dtype != BF16:
            v = v.bitcast(dtype)
        if len(shape) == 2:
            v = v.rearrange("p (a b) -> p a b", a=shape[0])
        elif len(shape) == 3:
            v = v.rearrange("p (a b c) -> p a b c", a=shape[0], b=shape[1])
        return v

    def mark(self):
        return self.top

    def release(self, m):
        self.top = m


OWN_TILES = [(i * 128, 128) for i in range(8)] + [(1024, 64)]
PRE_TILES = [(i * 128, 128) for i in range(8)]


def tok_groups(n, g=512):
    out = []
    t = 0
    while t < n:
        out.append((t, min(g, n - t)))
        t += g
    return out


class Prog:
    def __init__(self, stop_after=None, dbg=()):
        self.stop_after = stop_after
        self.dbg_names = dbg
        nc = bass.Bass("TRN2", target_bir_lowering=False)
        self.nc = nc
        self.S = Sched(nc)
        self.I = {}
        self.O = {}
        self.dbg = {}

        def inp(name, shape):
            self.I[name] = nc.dram_tensor(name, list(shape), F32, kind="ExternalInput").ap()

        def outp(name, shape):
            self.O[name] = nc.dram_tensor(name, list(shape), F32, kind="ExternalOutput").ap()

        inp("x_own", (NOWN, D))
        inp("x_pre", (NPRE, D))
        inp("conv_s", (2, 3, 3072))
        inp("S0_s", (2, 8, 128, 128))
        inp("ck", (2, 2048, 1024))
        inp("cv", (2, 2048, 1024))
        inp("w_in", (D, PROJ_W))
        inp("conv_w", (4, 3072))
        inp("a_log", (8,))
        inp("dt_bias", (8,))
        inp("gdn_norm_w", (128,))
        inp("w_out", (D, D))
        inp("ln1_g", (D,))
        inp("ln1_b", (D,))
        inp("w_up", (D, DFF))
        inp("w_down", (DFF, D))
        inp("ln2_g", (D,))
        inp("ln2_b", (D,))
        inp("pre_bias", (128, 1))
        outp("y", (NOWN, D))
        outp("kb", (NOWN, 1024))
        outp("vb", (NOWN, 1024))
        outp("conv_p_o", (3, 3072))
        outp("conv_s_o", (2, 3, 3072))
        outp("S_p_o", (8, 128, 128))
        outp("S_s_o", (2, 8, 128, 128))
        self.x1s = nc.dram_tensor("x1_scratch", [NOWN, D], F32, kind="Internal").ap()
        self.y2s = nc.dram_tensor("y2_scratch", [NOWN, D], F32, kind="Internal").ap()

        self.ar = Arena(nc, 212480)
        self.ps = [nc.alloc_psum_tensor(f"ps{i}", [128, 512], F32) for i in range(8)]
        self.build()

    def psf(self, i):
        return self.ps[i][:]

    def psb(self, i):
        return self.ps[i][:].bitcast(BF16)

    def tap(self, name, ap, shape):
        if name not in self.dbg_names:
            return
        t = self.nc.dram_tensor("dbg_" + name, list(shape), ap.dtype, kind="ExternalOutput").ap()
        self.dbg[name] = t
        self.S.dma("sp", t, ap, reads=[ap], is_output=True)

    def mm(self, out, lhsT, rhs, start=True, stop=True, skip=False):
        if skip:
            self.S.op("pe", lambda e: e.matmul(out, lhsT, rhs, start=start, stop=stop, skip_group_check=True),
                      reads=[lhsT, rhs], writes=[out])
        else:
            self.S.op("pe", lambda e: e.matmul(out, lhsT, rhs, start=start, stop=stop),
                      reads=[lhsT, rhs], writes=[out])

    def tr(self, out, in_, ident):
        self.S.op("pe", lambda e: e.transpose(out, in_, ident), reads=[in_, ident], writes=[out])

    def copy(self, eng, out, in_):
        if eng == "act":
            self.S.op("act", lambda e: e.copy(out, in_), reads=[in_], writes=[out])
        else:
            self.S.op(eng, lambda e: e.tensor_copy(out, in_), reads=[in_], writes=[out])

    def act(self, out, in_, func, bias=0.0, scale=1.0, accum_out=None, extra_reads=()):
        rd = [in_] + list(extra_reads)
        wr = [out] + ([accum_out] if accum_out is not None else [])
        if accum_out is not None:
            self.S.op("act", lambda e: e.activation(out=out, in_=in_, func=func, bias=bias, scale=scale,
                                                    accum_out=accum_out), reads=rd, writes=wr)
        else:
            self.S.op("act", lambda e: e.activation(out=out, in_=in_, func=func, bias=bias, scale=scale),
                      reads=rd, writes=wr)

    def tt(self, eng, out, in0, in1, op):
        self.S.op(eng, lambda e: e.tensor_tensor(out=out, in0=in0, in1=in1, op=op), reads=[in0, in1], writes=[out])

    def ts(self, eng, out, in0, s1, op0, s2=None, op1=None, extra_reads=()):
        rd = [in0] + list(extra_reads)
        if op1 is None:
            self.S.op(eng, lambda e: e.tensor_scalar(out=out, in0=in0, scalar1=s1, scalar2=None, op0=op0),
                      reads=rd, writes=[out])
        else:
            self.S.op(eng, lambda e: e.tensor_scalar(out=out, in0=in0, scalar1=s1, scalar2=s2, op0=op0, op1=op1),
                      reads=rd, writes=[out])

    def stt(self, eng, out, in0, scalar, in1, op0, op1, extra_reads=()):
        rd = [in0, in1] + list(extra_reads)
        self.S.op(eng, lambda e: e.scalar_tensor_tensor(out=out, in0=in0, scalar=scalar, in1=in1, op0=op0, op1=op1),
                  reads=rd, writes=[out])

    def memset(self, eng, ap, val):
        self.S.op(eng, lambda e: e.memset(ap, val), writes=[ap])

    def asel(self, out, in_, pattern, cmp, fill, base, cm):
        self.S.op("pool", lambda e: e.affine_select(out=out, in_=in_, pattern=pattern, compare_op=cmp, fill=fill,
                                                    base=base, channel_multiplier=cm), reads=[in_], writes=[out])

    def build(self):
        self.consts()
        self.phase_x()
        if self.stop_after == "x":
            return self.S.finish()
        self.phase_attn()
        if self.stop_after == "attn":
            return self.S.finish()
        self.phase_wout()
        if self.stop_after == "wout":
            return self.S.finish()
        self.phase_ffn()
        self.S.finish()

    def consts(self):
        ar = self.ar
        self.ident_f = ar.alloc((128,), F32)
        self.ident_b = ar.alloc((128,), BF16)
        self.zeros_f = ar.alloc((128,), F32)
        self.memset("pool", self.zeros_f, 0.0)
        self.memset("pool", self.ident_f, 0.0)
        self.asel(self.ident_f, self.ident_f, [[-1, 128]], ALU.not_equal, 1.0, 0, 1)
        self.copy("pool", self.ident_b, self.ident_f)
        self.pre_bias = self.nc.alloc_sbuf_tensor("pre_bias_t", [128, 1], F32)[:]
        import os
        if os.environ.get("PB_MEMSET"):
            self.memset("pool", self.pre_bias, 1.0)
        else:
            self.S.dma("sp", self.pre_bias, self.I["pre_bias"], writes=[self.pre_bias])

    def phase_x(self):
        ar, S = self.ar, self.S
        self.xT_own = ar.alloc((KC, NOWN), BF16)
        self.xT_own_off = ar.last_off
        self.xT_pre = ar.alloc((KC, NPRE), BF16)
        self.xT_pre_off = ar.last_off
        m = ar.mark()
        xb = [ar.alloc((D,), BF16) for _ in range(2)]
        k = 0
        for src, tiles, dst in ((self.I["x_pre"], PRE_TILES, self.xT_pre), (self.I["x_own"], OWN_TILES, self.xT_own)):
            for (t0, n) in tiles:
                b = xb[k % 2]
                S.dma("pool", b[0:n, :], src[t0:t0 + n, :], writes=[b[0:n, :]])
                for half in range(2):
                    pb = self.psb(half)
                    for c in range(8):
                        cc = half * 8 + c
                        self.tr(pb[:, c * 128:c * 128 + n], b[0:n, cc * 128:(cc + 1) * 128], self.ident_b[0:n, 0:n])
                    src_ps = pb[:, 0:1024].rearrange("p (c t) -> p c t", c=8)[:, :, 0:n]
                    self.copy("dve" if half == 0 else "act", dst[:, half * 8:(half + 1) * 8, t0:t0 + n], src_ps)
                k += 1
        ar.release(m)
        self.tap("xT_own", self.xT_own, (128, KC, NOWN))

    def phase_attn(self):
        ar = self.ar
        self.mixedT = ar.alloc((KC, NOWN), BF16)
        self.memset("pool", self.mixedT, 0.0)
        m = ar.mark()
        import os
        if not os.environ.get("NO_GDN"):
            self.phase_gdn()
        ar.release(m)
        if not os.environ.get("NO_SB"):
            self.phase_sb()
        ar.release(m)

    def phase_gdn(self):
        ar, S, I = self.ar, self.S, self.I
        NT = 17
        tiles = [("pre", t0, n, t0) for (t0, n) in PRE_TILES] + [("own", t0, n, NPRE + t0) for (t0, n) in OWN_TILES]
        zb = ar.alloc((128,), BF16)
        ones_b = ar.alloc((128,), BF16)
        ones_f = ar.alloc((128,), F32)
        nones_f = ar.alloc((128,), F32)
        self.memset("pool", zb, 0.0)
        self.memset("pool", ones_b, 1.0)
        self.memset("pool", ones_f, 1.0)
        self.memset("pool", nones_f, -1.0)
        offd = ar.alloc((128,), BF16)
        self.memset("pool", offd, 1.0)
        self.asel(offd, offd, [[-1, 128]], ALU.not_equal, 0.0, 0, 1)
        ident4 = ar.alloc((4, 128), BF16)
        for j in range(4):
            self.copy("pool", ident4[:, j, :], self.ident_b)

        def block_mask(kind, c, rows):
            if kind in ("Ms", "Mi"):
                m = ar.alloc((128,), BF16)
                self.memset("pool", m, NEG)
            else:
                m = ar.alloc((128,), F32)
                self.memset("pool", m, 0.0)
            for r0 in range(0, rows, c):
                blk = slice(r0, r0 + c)
                if kind == "Ms":
                    self.asel(m[blk, blk], zb[blk, blk], [[-1, c]], ALU.is_ge, NEG, -1, 1)
                elif kind == "Mi":
                    self.asel(m[blk, blk], zb[blk, blk], [[-1, c]], ALU.is_ge, NEG, 0, 1)
                elif kind == "tri":
                    self.asel(m[blk, blk], ones_f[blk, blk], [[1, c]], ALU.is_ge, 0.0, 0, -1)
                elif kind == "last":
                    self.asel(m[blk, blk], ones_f[blk, blk], [[0, c]], ALU.is_equal, 0.0, -(c - 1), 1)
            return m
        maskMs = {64: block_mask("Ms", 64, 128), 32: block_mask("Ms", 32, 64)}
        maskMi = {64: block_mask("Mi", 64, 128), 32: block_mask("Mi", 32, 64)}
        trich = {64: block_mask("tri", 64, 128), 32: block_mask("tri", 32, 64)}
        sellast = {64: block_mask("last", 64, 128), 32: block_mask("last", 32, 64)}
        lastsel = {}
        for (c, rows) in ((64, 128), (32, 64)):
            for nch in range(2):
                m = ar.alloc((128,), F32)
                self.memset("pool", m, 0.0)
                self.asel(m[0:rows, :], ones_f[0:rows, :], [[0, 128]], ALU.is_equal, 0.0, -((nch + 1) * c - 1), 1)
                lastsel[(c, nch)] = m
        nsel = ar.alloc((8, 128), F32, parts=8)
        self.memset("pool", nsel, 0.0)
        for h in range(8):
            self.asel(nsel[:, h, :], nones_f[0:8, :], [[0, 128]], ALU.is_equal, 0.0, -h, 1)
        normw = ar.alloc((1,), F32)
        S.dma("sp", normw, I["gdn_norm_w"].rearrange("(p o) -> p o", o=1), writes=[normw])
        dtb = ar.alloc((8,), F32)
        S.dma("sp", dtb, I["dt_bias"].partition_broadcast(128), writes=[dtb])
        negA = ar.alloc((8,), F32)
        S.dma("sp", negA, I["a_log"].partition_broadcast(128), writes=[negA])
        self.act(negA, negA, AF.Exp)
        self.ts("dve", negA, negA, -1.0, ALU.mult)
        eps6 = self.eps_tile(1e-6)
        one_c = self.eps_tile(1.0)
        lnqs = self.eps_tile(float(np.log(128 ** -0.5)))
        cw = ar.alloc((24, 4), F32)
        hist = ar.alloc((24, 6), F32)
        mtmp = ar.mark()
        cwt = ar.alloc((3072,), F32, parts=4)
        S.dma("sp", cwt, I["conv_w"], writes=[cwt])
        hst = ar.alloc((3072,), F32, parts=6)
        S.dma("sp", hst, I["conv_s"].rearrange("s r c -> (s r) c"), writes=[hst])
        pc = self.psf(0)
        for ct in range(24):
            self.mm(pc[:, ct * 4:ct * 4 + 4], cwt[:, ct * 128:(ct + 1) * 128], self.ident_f[0:4, 0:4])
        self.copy("dve", cw, pc[:, 0:96].rearrange("p (a b) -> p a b", b=4))
        pc = self.psf(1)
        for ct in range(24):
            self.mm(pc[:, ct * 6:ct * 6 + 6], hst[:, ct * 128:(ct + 1) * 128], self.ident_f[0:6, 0:6])
        self.copy("dve", hist, pc[:, 0:144].rearrange("p (a b) -> p a b", b=6))
        ar.release(mtmp)
        w16 = ar.alloc((KC, 16), BF16)
        S.dma("pool", w16, I["w_in"][:, OFF_B:OFF_B + 16].rearrange("(c p) n -> p c n", p=128), writes=[w16])
        P16 = ar.alloc((NT, 16), F32)
        ps = self.psf(0)
        for ti, (src, t0, n, c0) in enumerate(tiles):
            xT = self.xT_pre if src == "pre" else self.xT_own
            for c in range(KC):
                self.mm(ps[0:n, ti * 16:(ti + 1) * 16], xT[:, c, t0:t0 + n], w16[:, c, :], start=(c == 0), stop=(c == KC - 1))
        self.memset("pool", P16, 0.0)
        self.copy("dve", P16[:, 0:16, :], ps[:, 0:256].rearrange("p (a b) -> p a b", b=16))
        self.copy("dve", P16[0:64, 16, :], ps[0:64, 256:272])
        TMq = {}
        for nm in ("g", "G", "lb", "beta", "sk", "eg", "st", "tmp"):
            TMq[nm] = ar.alloc((NT, 8), F32)
        bl = P16[:, :, 0:8]
        al = P16[:, :, 8:16]
        bc17 = lambda t: t.unsqueeze(1).broadcast_to([128, NT, 8])
        self.tt("dve", TMq["tmp"], al, bc17(dtb), ALU.add)
        self.act(TMq["tmp"], TMq["tmp"], AF.Exp)
        self.act(TMq["tmp"], TMq["tmp"], AF.Ln, bias=one_c, extra_reads=[one_c])
        self.tt("dve", TMq["g"], TMq["tmp"], bc17(negA), ALU.mult)
        self.act(TMq["lb"], bl, AF.Exp, scale=-1.0)
        self.act(TMq["lb"], TMq["lb"], AF.Ln, bias=one_c, extra_reads=[one_c])
        self.act(TMq["beta"], TMq["lb"], AF.Exp, scale=-1.0)
        ps = self.psf(1)
        for ti, (src, t0, n, c0) in enumerate(tiles):
            c = 64 if n == 128 else 32
            self.mm(ps[0:n, ti * 8:(ti + 1) * 8], trich[c][0:n, 0:n], TMq["g"][0:n, ti, :])
        self.memset("pool", TMq["G"], 0.0)
        self.copy("dve", TMq["G"][:, 0:16, :], ps[:, 0:128].rearrange("p (a b) -> p a b", b=8))
        self.copy("dve", TMq["G"][0:64, 16, :], ps[0:64, 128:136])
        self.act(TMq["eg"], TMq["G"], AF.Exp)
        self.tt("dve", TMq["tmp"], TMq["G"], TMq["lb"], ALU.subtract)
        self.act(TMq["sk"], TMq["tmp"], AF.Exp)
        ps = self.psf(0)
        for ti, (src, t0, n, c0) in enumerate(tiles):
            c = 64 if n == 128 else 32
            self.mm(ps[0:n, ti * 8:(ti + 1) * 8], sellast[c][0:n, 0:n], TMq["G"][0:n, ti, :])
        self.memset("pool", TMq["tmp"], 0.0)
        self.tt("dve", TMq["tmp"][:, 0:16, :], ps[:, 0:128].rearrange("p (a b) -> p a b", b=8), TMq["G"][:, 0:16, :], ALU.subtract)
        self.tt("dve", TMq["tmp"][0:64, 16, :], ps[0:64, 128:136], TMq["G"][0:64, 16, :], ALU.subtract)
        self.act(TMq["st"], TMq["tmp"], AF.Exp)
        GT = ar.alloc((NT * 2, 8), F32)
        ps = self.psf(1)
        for ti, (src, t0, n, c0) in enumerate(tiles):
            c = 64 if n == 128 else 32
            for nch in range(2):
                j = ti * 2 + nch
                self.mm(ps[:, j * 8:(j + 1) * 8], lastsel[(c, nch)][0:n, :], TMq["eg"][0:n, ti, :])
        self.copy("dve", GT, ps[:, 0:NT * 16].rearrange("p (a b) -> p a b", b=8))
        Grow = ar.alloc((NALL,), F32, parts=8)
        for q0 in range(0, NT, 4):
            ps = self.psf(q0 // 4 % 2)
            for ti in range(q0, min(NT, q0 + 4)):
                (src, t0, n, c0) = tiles[ti]
                self.mm(ps[0:8, (ti - q0) * 128:(ti - q0) * 128 + n], TMq["G"][0:n, ti, :], self.ident_f[0:n, 0:n])
            nc_ = sum(tiles[ti][2] for ti in range(q0, min(NT, q0 + 4)))
            self.copy("dve", Grow[:, tiles[q0][3]:tiles[q0][3] + nc_], ps[0:8, 0:nc_])
        self.tap("Grow", Grow, (8, NALL))
        self.tap("TMst", TMq["st"], (128, NT, 8))
        self.tap("TMsk", TMq["sk"], (128, NT, 8))
        self.tap("GT", GT, (128, NT * 2, 8))
        wb = ar.alloc((KC, 512), BF16)
        NKV = 2121
        pcb = ar.alloc((NKV,), F32)
        cvb = ar.alloc((NKV,), F32)
        knT = ar.alloc((NALL,), BF16)
        vT = ar.alloc((NALL,), BF16)
        qnT = ar.alloc((NOWN,), BF16)
        sqb = ar.alloc((512,), BF16)
        lnb_ = ar.alloc((512,), F32)
        zgT = ar.alloc((NOWN,), BF16)
        cst = lnb_[0:3, 0:384]
        opq = []
        for i in range(2):
            opq.append({
                "kbg": ar.alloc((4, 128), BF16), "kt": ar.alloc((4, 128), BF16), "vb": ar.alloc((4, 128), BF16),
                "wtok": ar.alloc((4, 128), BF16), "attnT": ar.alloc((4, 128), BF16), "ub": ar.alloc((4, 128), BF16),
                "nW2": ar.alloc((4, 2, 128), BF16), "nAWT": ar.alloc((4, 128), BF16),
            })
        Mb = [ar.alloc((4, 128), BF16) for _ in range(2)]
        Ab = [ar.alloc((4, 128), BF16) for _ in range(2)]
        Yb = [ar.alloc((4, 128), BF16) for _ in range(2)]
        Mp = ar.alloc((4, 128), BF16) if NEU_SINGLE else None
        DMi = ar.alloc((4, 128), BF16)
        DMs = ar.alloc((4, 128), BF16)
        attn = DMs
        Sf = ar.alloc((128,), F32)
        Sbf = ar.alloc((128,), BF16)
        t1 = ar.alloc((128,), F32)
        otok2 = [ar.alloc((128,), F32) for _ in range(2)]
        gjunk = ar.alloc((128,), BF16)
        osq = t1
        ogb = ar.alloc((128,), BF16)
        ssq = ar.alloc((1,), F32)
        rs = ar.alloc((1,), F32)

        def kvcol(tok):
            if tok < 2048:
                return 3 + tok
            if tok < 2080:
                return 2054 + (tok - 2048)
            return 2089 + (tok - 2080)

        def qcol(t):
            if t < 1024:
                return 3 + t
            if t < 1056:
                return 1030 + (t - 1024)
            return 1065 + (t - 1056)
        kv_segs = [(3, 2048, 0), (2054, 32, 2048), (2089, 32, 2080)]
        q_segs = [(3, 1024, 0), (1030, 32, 1024), (1065, 32, 1056)]

        def conv_silu(ncols, ct):
            n = ncols - 3
            self.ts("dve", cvb[:, 3:ncols], pcb[:, 0:n], cw[:, ct, 0:1], ALU.mult, extra_reads=[cw[:, ct, 0:1]])
            for i in range(1, 4):
                self.stt("dve", cvb[:, 3:ncols], pcb[:, i:i + n], cw[:, ct, i:i + 1], cvb[:, 3:ncols], ALU.mult, ALU.add,
                         extra_reads=[cw[:, ct, i:i + 1]])

        def conv_out(which, h, cols3):
            ct = which * 8 + h
            pz = self.psf(1)
            for sgi, c0 in enumerate(cols3):
                self.mm(pz[0:3, sgi * 128:(sgi + 1) * 128], pcb[:, c0:c0 + 3], self.ident_f)
            self.copy("dve", cst, pz[0:3, 0:384])
            S.dma("sp", self.O["conv_p_o"][:, ct * 128:(ct + 1) * 128], cst[:, 0:128], reads=[cst[:, 0:128]], is_output=True)
            for j in range(2):
                S.dma("sp", self.O["conv_s_o"][j, :, ct * 128:(ct + 1) * 128], cst[:, (j + 1) * 128:(j + 2) * 128],
                      reads=[cst[:, (j + 1) * 128:(j + 2) * 128]], is_output=True)

        def normalize(segs, dst, extra_bias):
            for (c0, ln, k0) in segs:
                for o in range(0, ln, 512):
                    n = min(512, ln - o)
                    src = cvb[:, c0 + o:c0 + o + n]
                    self.act(sqb[:, 0:n], src, AF.Square)
                    pz = self.psf(1)
                    self.mm(pz[:, 0:n], ones_b, sqb[:, 0:n])
                    self.act(lnb_[:, 0:n], pz[:, 0:n], AF.Ln, bias=eps6, extra_reads=[eps6])
                    if extra_bias is None:
                        self.act(lnb_[:, 0:n], lnb_[:, 0:n], AF.Exp, scale=-0.5)
                    else:
                        self.act(lnb_[:, 0:n], lnb_[:, 0:n], AF.Exp, scale=-0.5, bias=extra_bias, extra_reads=[extra_bias])
                    self.tt("dve", dst[:, k0 + o:k0 + o + n], src, lnb_[:, 0:n], ALU.mult)

        quads = [list(range(0, 4)), list(range(4, 8)), list(range(8, 12)), list(range(12, 16)), [16]]

        for h in range(8):
            def wload_head(hh):
                for j, coff in enumerate((hh * 128, 1024 + hh * 128, 2048 + hh * 128, OFF_Z + hh * 128)):
                    src = I["w_in"][:, coff:coff + 128].rearrange("(c p) n -> p c n", p=128)
                    S.dma("pool", wb[:, :, j * 128:(j + 1) * 128], src, writes=[wb[:, :, j * 128:(j + 1) * 128]])
            if h == 0:
                wload_head(0)

            def proj_fm(wj, xT, t0, n, dst):
                self._pfm = getattr(self, "_pfm", 0) + 1
                po = self.psf(self._pfm % 2)[:, 0:n]
                for c in range(KC):
                    self.mm(po, wb[:, c, wj * 128:(wj + 1) * 128], xT[:, c, t0:t0 + n], start=(c == 0), stop=(c == KC - 1))
                self.copy("act", dst, po)
            def proj_q():
                proj_fm(0, self.xT_pre, 1021, 3, pcb[:, 0:3])
                for (t0, n) in ((0, 512), (512, 512)):
                    proj_fm(0, self.xT_own, t0, n, pcb[:, 3 + t0:3 + t0 + n])
                proj_fm(0, self.xT_own, 1024, 32, pcb[:, 1030:1062])
                proj_fm(0, self.xT_own, 1056, 32, pcb[:, 1065:1097])
                for j in range(2):
                    self.copy("pool", pcb[:, 1027 + 35 * j:1030 + 35 * j], hist[:, h, 3 * j:3 * j + 3])

            def proj_kv(which):
                self.memset("pool", pcb[:, 0:3], 0.0)
                for (t0, n) in ((0, 512), (512, 512)):
                    proj_fm(which, self.xT_pre, t0, n, pcb[:, 3 + t0:3 + t0 + n])
                for (t0, n) in ((0, 512), (512, 512)):
                    proj_fm(which, self.xT_own, t0, n, pcb[:, 3 + 1024 + t0:3 + 1024 + t0 + n])
                proj_fm(which, self.xT_own, 1024, 32, pcb[:, 2054:2086])
                proj_fm(which, self.xT_own, 1056, 32, pcb[:, 2089:2121])
                for j in range(2):
                    self.copy("pool", pcb[:, 2051 + 35 * j:2054 + 35 * j], hist[:, which * 8 + h, 3 * j:3 * j + 3])

            def conv_stage(which):
                if which == 0:
                    conv_out(0, h, (1024, 1059, 1094))
                    conv_silu(1097, h)
                    self.act(cvb[:, 3:1097], cvb[:, 3:1097], AF.Silu)
                else:
                    conv_out(which, h, (2048, 2083, 2118))
                    conv_silu(NKV, which * 8 + h)
                    self.act(cvb[:, 3:NKV], cvb[:, 3:NKV], AF.Silu)
            proj_q()
            conv_stage(0)
            proj_kv(1)
            normalize(q_segs, qnT, lnqs)
            conv_stage(1)
            proj_kv(2)
            normalize(kv_segs, knT, None)
            conv_stage(2)
            for (c0, ln, k0) in kv_segs:
                self.copy("pool", vT[:, k0:k0 + ln], cvb[:, c0:c0 + ln])
            for gi, (t0, n) in enumerate(tok_groups(NOWN)):
                pz = self.psf(gi % 2)[:, 0:n]
                for c in range(KC):
                    self.mm(pz, wb[:, c, 384:512], self.xT_own[:, c, t0:t0 + n], start=(c == 0), stop=(c == KC - 1))
                self.act(lnb_[:, 0:n], pz, AF.Silu)
                self.ts("dve", zgT[:, t0:t0 + n], lnb_[:, 0:n], normw, ALU.mult, extra_reads=[normw])
            if h == 0:
                self.tap("knT0", knT, (128, NALL))
                self.tap("qnT0", qnT, (128, NOWN))
                self.tap("vT0", vT, (128, NALL))
            if h + 1 < 8:
                wload_head(h + 1)
            self.memset("pool", Sf, 0.0)
            self.memset("pool", Sbf, 0.0)
            def gen_L(qi, h=h):
                quad = quads[qi]
                ops = opq[qi % 2]
                own_quad = tiles[quad[0]][0] == "own"
                nq = len(quad)
                nn = tiles[quad[0]][2]
                v3 = lambda p: p[0:nn, 0:nq * 128].rearrange("p (a b) -> p a b", b=128)[:, :, 0:nn]
                pb = self.psb(2)
                for j, ti in enumerate(quad):
                    (src, t0, n, c0) = tiles[ti]
                    self.tr(pb[0:n, j * 256:j * 256 + 128], knT[:, c0:c0 + n], self.ident_b)
                    self.tr(pb[0:n, j * 256 + 128:j * 256 + 256], vT[:, c0:c0 + n], self.ident_b)
                yield
                for j, ti in enumerate(quad):
                    (src, t0, n, c0) = tiles[ti]
                    kps = pb[0:n, j * 256:j * 256 + 128]
                    vps = pb[0:n, j * 256 + 128:j * 256 + 256]
                    sc = lambda nm: TMq[nm][0:n, ti, h:h + 1]
                    self.ts("dve", ops["kbg"][0:n, j, :], kps, sc("sk"), ALU.mult, extra_reads=[sc("sk")])
                    self.ts("dve", ops["kt"][0:n, j, :], kps, sc("st"), ALU.mult, extra_reads=[sc("st")])
                    self.ts("dve", ops["vb"][0:n, j, :], vps, sc("beta"), ALU.mult, extra_reads=[sc("beta")])
                pk = self.psf(3)
                pe_ = self.psf(4)
                for j, ti in enumerate(quad):
                    (src, t0, n, c0) = tiles[ti]
                    c = 64 if n == 128 else 32
                    self.mm(pk[0:n, j * 128:j * 128 + n], knT[:, c0:c0 + n], knT[:, c0:c0 + n])
                    self.mm(pe_[0:n, j * 128:j * 128 + n], nsel[:, h, 0:n], Grow[:, c0:c0 + n], start=True, stop=False)
                    msk = maskMi[c] if own_quad else maskMs[c]
                    self.mm(pe_[0:n, j * 128:j * 128 + n], self.ident_b[0:n, 0:n], msk[0:n, 0:n], start=False, stop=True)
                yield
                for j, ti in enumerate(quad):
                    (src, t0, n, c0) = tiles[ti]
                    gb_ = TMq["G"][0:n, ti, h:h + 1]
                    dst = DMi if own_quad else DMs
                    self.act(dst[0:n, j, 0:n], pe_[0:n, j * 128:j * 128 + n], AF.Exp, bias=gb_, extra_reads=[gb_])
                    if own_quad:
                        self.tt("pool", DMs[0:n, j, 0:n], DMi[0:n, j, 0:n], offd[0:n, 0:n], ALU.mult)
                M0, A0, Y0 = Mb[0], Ab[0], Yb[0]
                for j, ti in enumerate(quad):
                    (src, t0, n, c0) = tiles[ti]
                    bt_ = TMq["beta"][0:n, ti, h:h + 1]
                    self.stt("dve", M0[0:n, j, 0:n], pk[0:n, j * 128:j * 128 + n], bt_, DMs[0:n, j, 0:n], ALU.mult, ALU.mult,
                             extra_reads=[bt_])
                yield
                pt = self.psb(2)
                for j, ti in enumerate(quad):
                    n = tiles[ti][2]
                    self.tr(pt[0:n, j * 128:j * 128 + n], M0[0:n, j, 0:n], self.ident_b[0:n, 0:n])
                ptv = pt[0:nn, 0:nq * 128].rearrange("p (a b) -> p a b", b=128)[:, :, 0:nn]
                self.copy("act", A0[0:nn, 0:nq, 0:nn], ptv)
                self.stt("dve", Y0[0:nn, 0:nq, 0:nn], ptv, -1.0, ident4[0:nn, 0:nq, 0:nn], ALU.mult, ALU.add)
                yield
                cur = 0
                ycur = 0
                pend = None
                p5, p6, p7 = self.psf(5), self.psf(6), self.psf(7)

                def y_mm(Mlhs, Ysrc):
                    for j, ti in enumerate(quad):
                        n = tiles[ti][2]
                        self.mm(p7[0:n, j * 128:j * 128 + n], self.ident_b[0:n, 0:n], Ysrc[0:n, j, 0:n], start=True, stop=False)
                        self.mm(p7[0:n, j * 128:j * 128 + n], Mlhs[0:n, j, 0:n], Ysrc[0:n, j, 0:n], start=False, stop=True)
                for it in range(5):
                    Mc, Ac = Mb[cur], Ab[cur]
                    Mn, An = Mb[1 - cur], Ab[1 - cur]
                    for j, ti in enumerate(quad):
                        n = tiles[ti][2]
                        self.mm(p5[0:n, j * 128:j * 128 + n], Ac[0:n, j, 0:n], Mc[0:n, j, 0:n])
                        if it < 4:
                            self.mm(p6[0:n, j * 128:j * 128 + n], Mc[0:n, j, 0:n], Ac[0:n, j, 0:n])
                    if pend is not None:
                        y_mm(Mc, Yb[ycur])
                    yield
                    self.copy("act", Mn[0:nn, 0:nq, 0:nn], v3(p5))
                    if it < 4:
                        self.copy("dve", An[0:nn, 0:nq, 0:nn], v3(p6))
                    if pend is not None:
                        self.copy("dve", Yb[1 - ycur][0:nn, 0:nq, 0:nn], v3(p7))
                        ycur = 1 - ycur
                    pend = True
                    cur = 1 - cur
                    yield
                y_mm(Mb[cur], Yb[ycur])
                yield
                self.copy("dve", Yb[1 - ycur][0:nn, 0:nq, 0:nn], v3(p7))
                cur = 1 - ycur
                Yf = Yb[cur]
                pu, pw = self.psf(3), self.psf(4)
                for j, ti in enumerate(quad):
                    n = tiles[ti][2]
                    self.mm(pu[0:n, j * 128:(j + 1) * 128], Yf[0:n, j, 0:n], ops["vb"][0:n, j, :])
                    self.mm(pw[0:n, j * 128:(j + 1) * 128], Yf[0:n, j, 0:n], ops["kbg"][0:n, j, :])
                yield
                self.copy("act", ops["ub"][0:nn, 0:nq, :], pu[0:nn, 0:nq * 128].rearrange("p (a b) -> p a b", b=128))
                self.copy("dve", ops["wtok"][0:nn, 0:nq, :], pw[0:nn, 0:nq * 128].rearrange("p (a b) -> p a b", b=128))
                p56 = (self.psf(5), self.psf(6))
                cc = 64 if nn == 128 else 32
                for j, ti in enumerate(quad):
                    for nch in range(2):
                        r = slice(nch * cc, (nch + 1) * cc)
                        self.mm(p56[nch][:, j * 128:(j + 1) * 128], ops["wtok"][r, j, :], ops["kt"][r, j, :])
                yield
                self.ts("dve", ops["nW2"][:, 0:nq, 0, :], p56[0][:, 0:nq * 128].rearrange("p (a b) -> p a b", b=128), -1.0, ALU.mult)
                self.act(ops["nW2"][:, 0:nq, 1, :], p56[1][:, 0:nq * 128].rearrange("p (a b) -> p a b", b=128), AF.Copy, scale=-1.0)
                if own_quad:
                    pq = self.psf(3)
                    for j, ti in enumerate(quad):
                        (src, t0, n, c0) = tiles[ti]
                        self.mm(pq[0:n, j * 128:j * 128 + n], qnT[:, t0:t0 + n], knT[:, c0:c0 + n])
                    yield
                    self.tt("dve", attn[0:nn, 0:nq, 0:nn], v3(pq), DMi[0:nn, 0:nq, 0:nn], ALU.mult)
                    pt = self.psb(2)
                    for j, ti in enumerate(quad):
                        n = tiles[ti][2]
                        self.tr(pt[0:n, j * 128:j * 128 + n], attn[0:n, j, 0:n], self.ident_b[0:n, 0:n])
                    yield
                    self.copy("act", ops["attnT"][0:nn, 0:nq, 0:nn],
                              pt[0:nn, 0:nq * 128].rearrange("p (a b) -> p a b", b=128)[:, :, 0:nn])
                    p7 = self.psf(7)
                    for j, ti in enumerate(quad):
                        n = tiles[ti][2]
                        self.mm(p7[:, j * 128:j * 128 + n], ops["wtok"][0:n, j, :], ops["attnT"][0:n, j, 0:n])
                    yield
                    self.ts("dve", ops["nAWT"][:, 0:nq, 0:nn],
                            p7[:, 0:nq * 128].rearrange("p (a b) -> p a b", b=128)[:, :, 0:nn], -1.0, ALU.mult)
                if h == 0 and qi == 0:
                    self.tap("Yf0", Yf, (128, 4, 128))
                    self.tap("M00", Mb[0], (128, 4, 128))

            gstate = {"g": None}

            def gate_gen(ot, t0, n, h=h):
                self.act(gjunk[0:n, :], ot[0:n, :], AF.Square, accum_out=ssq[0:n, :])
                self.act(rs[0:n, :], ssq[0:n, :], AF.Ln, scale=1.0 / 128.0, bias=eps6[0:n, :], extra_reads=[eps6[0:n, :]])
                self.act(rs[0:n, :], rs[0:n, :], AF.Exp, scale=-0.5)
                yield
                self.ts("dve", ogb[0:n, :], ot[0:n, :], rs[0:n, :], ALU.mult, extra_reads=[rs[0:n, :]])
                pg = self.psb(0)
                self.tr(pg[:, 0:n], ogb[0:n, :], self.ident_b[0:n, 0:n])
                yield
                self.tt("dve", self.mixedT[:, h, t0:t0 + n], pg[:, 0:n], zgT[:, t0:t0 + n], ALU.mult)

            def gate_step(drain=False):
                g_ = gstate["g"]
                while g_ is not None:
                    try:
                        next(g_)
                    except StopIteration:
                        gstate["g"] = None
                        return
                    if not drain:
                        return

            def gen_S(qi, h=h):
                quad = quads[qi]
                ops = opq[qi % 2]
                for j, ti in enumerate(quad):
                    (src, t0, n, c0) = tiles[ti]
                    c = 64 if n == 128 else 32
                    pS = self.psf(1)
                    ot = otok2[ti % 2]
                    for nch in range(n // c):
                        r = slice(nch * c, (nch + 1) * c)
                        if n == 64:
                            S.dma("sp", Sf, I["S0_s"][nch, h], writes=[Sf])
                            self.copy("act", Sbf, Sf)
                        self.mm(pS[:, 128:256], ops["nW2"][:, j, nch, :], Sbf, start=True, stop=False)
                        self.mm(pS[:, 128:256], ops["kt"][r, j, :], ops["ub"][r, j, :], start=False, stop=True)
                        if src == "own":
                            self.mm(pS[r, 256:384], qnT[:, t0 + nch * c:t0 + (nch + 1) * c], Sbf)
                            self.mm(pS[r, 384:512], ops["nAWT"][:, j, r], Sbf, start=True, stop=False)
                            self.mm(pS[r, 384:512], ops["attnT"][r, j, r], ops["ub"][r, j, :], start=False, stop=True)
                        gate_step()
                        yield
                        gt_ = GT[:, ti * 2 + nch, h:h + 1]
                        self.stt("dve", Sbf, Sf, gt_, pS[:, 128:256], ALU.mult, ALU.add, extra_reads=[gt_])
                        self.stt("dve", Sf, Sf, gt_, pS[:, 128:256], ALU.mult, ALU.add, extra_reads=[gt_])
                        if src == "own":
                            eg_ = TMq["eg"][r, ti, h:h + 1]
                            self.act(t1[r, :], pS[r, 256:384], AF.Copy, scale=eg_, extra_reads=[eg_])
                            self.tt("dve", ot[r, :], t1[r, :], pS[r, 384:512], ALU.add)
                        if n == 64:
                            S.dma("sp", self.O["S_s_o"][nch, h], Sf, reads=[Sf], is_output=True)
                        gate_step()
                        yield
                    if src == "own" and t0 == 896:
                        S.dma("sp", self.O["S_p_o"][h], Sf, reads=[Sf], is_output=True)
                    if src == "own":
                        gate_step(drain=True)
                        gstate["g"] = gate_gen(ot, t0, n)
                if qi == len(quads) - 1:
                    gate_step(drain=True)

            def run_gens(gens):
                gens = list(gens)
                while gens:
                    for g_ in list(gens):
                        try:
                            next(g_)
                        except StopIteration:
                            gens.remove(g_)
            run_gens([gen_L(0)])
            for qi in range(len(quads)):
                gl = [gen_S(qi)]
                if qi + 1 < len(quads):
                    gl.append(gen_L(qi + 1))
                run_gens(gl)
        self.tap("mixedT_g", self.mixedT, (128, KC, NOWN))


    def wload(self, buf, w_ap, c0, ncols, kc=KC):
        src = w_ap[:, c0:c0 + ncols].rearrange("(c p) n -> p c n", p=128)
        dst = buf[:, :, 0:ncols]
        self.S.dma("pool", dst, src, writes=[dst])

    def phase_sb(self):
        ar, S = self.ar, self.S
        wb = [ar.alloc((KC, 512), BF16) for _ in range(2)]
        stage = [ar.alloc((512,), F32) for _ in range(2)]
        KT = ar.alloc((4, NALL), BF16)
        VT = ar.alloc((17, 512), BF16)
        QT2 = [ar.alloc((NOWN,), BF16) for _ in range(2)]
        Eb = [ar.alloc((512,), F32) for _ in range(2)]
        SPb = [ar.alloc((512,), BF16) for _ in range(2)]
        Wb = [ar.alloc((512,), BF16) for _ in range(2)]
        Xb = [ar.alloc((512,), F32) for _ in range(2)] + [ar.alloc((64,), F32)]
        Eb.append(ar.alloc((64,), F32))
        SPb.append(ar.alloc((64,), BF16))
        Wb.append(ar.alloc((64,), BF16))
        KTn = ar.alloc((4, 64), BF16)
        Vn = ar.alloc((512,), BF16)
        ntri_i = ar.alloc((128,), BF16)
        ntri_c = ar.alloc((128,), BF16)
        zb = ar.alloc((512,), BF16)
        KTc = ar.alloc((2, 2048), BF16)
        Vc = ar.alloc((2, 16, 128), BF16)
        m64 = ar.alloc((64,), BF16, parts=64)
        ones64 = ar.alloc((64,), BF16, parts=64)
        one_c = self.eps_tile(1.0)
        self.memset("pool", m64, 0.0)
        self.memset("pool", ones64, 1.0)
        for s_ in range(2):
            sq = slice(32 * s_, 32 * s_ + 32)
            self.asel(m64[sq, sq], ones64[sq, sq], [[1, 32]], ALU.is_ge, 0.0, -1, -1)
        self.memset("pool", zb, 0.0)
        self.memset("pool", ntri_i, -1.0)
        self.asel(ntri_i, ntri_i, [[-1, 128]], ALU.is_ge, 0.0, 0, 1)
        self.memset("pool", ntri_c, -1.0)
        self.asel(ntri_c, ntri_c, [[1, 128]], ALU.is_gt, 0.0, 0, -1)
        all_tiles = [("pre", t0, n) for (t0, n) in PRE_TILES] + [("own", t0, n) for (t0, n) in OWN_TILES]
        import os
        sbstop = int(os.environ.get("SB_STOP", "99"))
        if sbstop <= 1:
            return
        nw = 0
        ev = 0
        for g in range(2):
            for which in ("k", "v"):
                coff = OFF_SB + (1024 if which == "k" else 2048) + g * 512
                dst = self.O["kb"] if which == "k" else self.O["vb"]
                w = wb[nw % 2]
                nw += 1
                self.wload(w, self.I["w_in"], coff, 512)

                def emit_ktr(ti, kpos, n):
                    pt = self.psf(4 + (ti % 2))
                    stf = stage[ti % 2]
                    for h in range(4):
                        self.tr(pt[:, h * 128:(h + 1) * 128], stf[:, h * 128:(h + 1) * 128], self.ident_f)
                    src_ps = pt[:, 0:512].rearrange("p (h t) -> p h t", h=4)[:, :, 0:n]
                    self.copy("act", KT[:, :, kpos:kpos + n], src_ps)
                pend_k = None
                for ti, (src, t0, n) in enumerate(all_tiles):
                    xT = self.xT_pre if src == "pre" else self.xT_own
                    kpos = t0 if src == "pre" else NPRE + t0
                    po = self.psf(6 + (ti % 2))[0:n, :]
                    for c in range(KC):
                        self.mm(po, xT[:, c, t0:t0 + n], w[:, c, :], start=(c == 0), stop=(c == KC - 1))
                    st = stage[ti % 2][0:n, :]
                    if which == "k":
                        self.copy("dve", st, po)
                        if src == "own":
                            S.dma("sp", dst[t0:t0 + n, g * 512:(g + 1) * 512], st, reads=[st], is_output=True)
                        if pend_k is not None:
                            emit_ktr(*pend_k)
                        pend_k = (ti, kpos, n)
                    else:
                        if src == "own":
                            self.copy("dve", st, po)
                            S.dma("sp", dst[t0:t0 + n, g * 512:(g + 1) * 512], st, reads=[st], is_output=True)
                            self.copy("act", VT[0:n, ti, :], st)
                        else:
                            self.copy("act", VT[0:n, ti, :], po)
                if which == "k" and pend_k is not None:
                    emit_ktr(*pend_k)
            self.copy("dve", KTn, KT[:, :, NPRE + 1024:NPRE + 1088])
            self.copy("dve", Vn[0:64, :], VT[0:64, 16, :])
            wq = wb[nw % 2]
            nw += 1
            self.wload(wq, self.I["w_in"], OFF_SB + g * 512, 512)
            def q_proj(hh, gi, wq=wq, g=g):
                (t0, n) = tok_groups(NOWN)[gi]
                po = self.psf(6)[:, 0:n]
                for c in range(KC):
                    self.mm(po, wq[:, c, hh * 128:(hh + 1) * 128], self.xT_own[:, c, t0:t0 + n],
                            start=(c == 0), stop=(c == KC - 1))
                self.act(QT2[(g * 4 + hh) % 2][:, t0:t0 + n], po, AF.Copy, scale=float(128 ** -0.5))
            for h in range(4):
                hg = g * 4 + h
                QT = QT2[hg % 2]
                if h == 0:
                    for gi in range(3):
                        q_proj(0, gi)
                def prompt_stream(sb, h=h, hg=hg, QT=QT):
                    nonlocal ev
                    blocks = []
                    for kb in range(4 * sb + 3, -1, -1):
                        cs = max(0, (kb - 4 * sb)) * 128
                        blocks.append(("own", kb, cs, kb >= 4 * sb))
                    for kb in range(7, -1, -1):
                        blocks.append(("pre", kb, 0, False))
                    A = self.psf(2 + sb)
                    OT = self.psf(4 + sb)
                    q0 = sb * 512
                    self.mm(A, zb[:, 0:128], zb[:, 0:512], start=True, stop=False, skip=True)
                    self.mm(OT, zb[:, 0:128], zb[:, 0:512], start=True, stop=False, skip=True)
                    yield
                    for (src, kb, cs, diag) in blocks:
                        kpos = kb * 128 if src == "pre" else NPRE + kb * 128
                        vt = kb if src == "pre" else 8 + kb
                        zt = self.psf(ev % 2)
                        E, SP, W, X = Eb[sb], SPb[sb], Wb[sb], Xb[sb]
                        ev += 1
                        kt = KT[:, h, kpos:kpos + 128]
                        self.mm(zt[:, cs:512], kt, QT[:, q0 + cs:q0 + 512])
                        self.act(E[:, cs:512], zt[:, cs:512], AF.Exp)
                        if src == "pre":
                            self.act(SP[:, cs:512], E[:, cs:512], AF.Ln, bias=one_c, scale=self.pre_bias,
                                     extra_reads=[one_c, self.pre_bias])
                        else:
                            self.act(SP[:, cs:512], E[:, cs:512], AF.Ln, bias=one_c, extra_reads=[one_c])
                        if diag:
                            self.asel(SP[:, cs:cs + 128], SP[:, cs:cs + 128], [[1, 128]], ALU.is_ge, 0.0, -1, -1)
                        yield
                        self.mm(A[:, cs:512], ntri_i, SP[:, cs:512], start=False, stop=False, skip=True)
                        self.act(X[:, cs:512], A[:, cs:512], AF.Exp)
                        self.tt("dve", W[:, cs:512], E[:, cs:512], X[:, cs:512], ALU.mult)
                        if diag:
                            self.asel(W[:, cs:cs + 128], W[:, cs:cs + 128], [[1, 128]], ALU.is_ge, 0.0, -1, -1)
                        yield
                        self.mm(A[:, cs:512], ntri_c, SP[:, cs:512], start=False, stop=False, skip=True)
                        self.mm(OT[:, cs:512], VT[:, vt, h * 128:(h + 1) * 128], W[:, cs:512], start=False, stop=False, skip=True)
                        yield
                    self.copy("dve", self.mixedT[:, 8 + hg, sb * 512:(sb + 1) * 512], OT)

                def sample_stream(h=h, hg=hg, QT=QT):
                    nonlocal ev
                    par = hg % 2
                    wfree = wb[nw % 2]
                    for s_ in range(2):
                        kst = wfree[:, 8 * par + 4 * s_:8 * par + 4 * s_ + 4, :].rearrange("p a (b d) -> p (a b) d", d=128)
                        S.dma("pool", kst, self.I["ck"][s_, :, hg * 128:(hg + 1) * 128].rearrange("(b p) d -> p b d", p=128),
                              writes=[kst])
                        S.dma("pool", Vc[:, s_, :, :],
                              self.I["cv"][s_, :, hg * 128:(hg + 1) * 128].rearrange("(b p) d -> p b d", p=128),
                              writes=[Vc[:, s_, :, :]])
                    yield
                    for s_ in range(2):
                        kst = wfree[:, 8 * par + 4 * s_:8 * par + 4 * s_ + 4, :].rearrange("p a (b d) -> p (a b) d", d=128)
                        for half in range(2):
                            pb = self.psb(6)
                            for c in range(8):
                                self.tr(pb[:, c * 128:(c + 1) * 128], kst[:, half * 8 + c, :], self.ident_b)
                            self.copy("dve", KTc[:, s_, half * 1024:(half + 1) * 1024], pb[:, 0:1024])
                            yield
                    A = self.psf(7)[:, 0:64]
                    OT = self.psf(7)[:, 128:192]
                    self.mm(self.psf(7)[:, 0:192], zb[:, 0:128], zb[:, 0:192], start=True, stop=False, skip=True)
                    qs = QT[:, 1024:1088]
                    for blk in [-1] + list(range(15, -1, -1)):
                        zt = self.psf(ev % 2)
                        E, SP, W, X = Eb[2], SPb[2], Wb[2], Xb[2]
                        ev += 1
                        if blk < 0:
                            kn = KTn[:, h, :]
                            self.mm(zt[0:64, 0:64], kn, qs)
                            self.act(E[0:64, 0:64], zt[0:64, 0:64], AF.Exp)
                            self.act(SP[0:64, 0:64], E[0:64, 0:64], AF.Ln, bias=one_c[0:64, :], extra_reads=[one_c[0:64, :]])
                            self.tt("pool", SP[0:64, 0:64], SP[0:64, 0:64], m64, ALU.mult)
                            yield
                            self.mm(A[0:64, :], ntri_i[0:64, 0:64], SP[0:64, 0:64], start=False, stop=False, skip=True)
                            self.act(X[0:64, 0:64], A[0:64, :], AF.Exp)
                            self.tt("dve", W[0:64, 0:64], E[0:64, 0:64], X[0:64, 0:64], ALU.mult)
                            self.tt("pool", W[0:64, 0:64], W[0:64, 0:64], m64, ALU.mult)
                            yield
                            self.mm(A, ntri_c[0:64, :], SP[0:64, 0:64], start=False, stop=False, skip=True)
                            self.mm(OT, Vn[0:64, h * 128:(h + 1) * 128], W[0:64, 0:64], start=False, stop=False, skip=True)
                            yield
                        else:
                            ks = [KTc[:, s_, blk * 128:(blk + 1) * 128] for s_ in range(2)]
                            cs2 = [slice(32 * s_, 32 * s_ + 32) for s_ in range(2)]
                            for s_ in range(2):
                                self.mm(zt[:, cs2[s_]], ks[s_], qs[:, cs2[s_]])
                            self.act(E[:, 0:64], zt[:, 0:64], AF.Exp)
                            self.act(SP[:, 0:64], E[:, 0:64], AF.Ln, bias=one_c, extra_reads=[one_c])
                            yield
                            self.mm(A, ntri_i, SP[:, 0:64], start=False, stop=False, skip=True)
                            self.act(X[:, 0:64], A, AF.Exp)
                            self.tt("dve", W[:, 0:64], E[:, 0:64], X[:, 0:64], ALU.mult)
                            yield
                            self.mm(A, ntri_c, SP[:, 0:64], start=False, stop=False, skip=True)
                            for s_ in range(2):
                                self.mm(OT[:, cs2[s_]], Vc[:, s_, blk, :], W[:, cs2[s_]], start=False, stop=False, skip=True)
                            if h < 3 and 13 <= blk <= 15:
                                q_proj(h + 1, 15 - blk)
                            yield
                    self.copy("dve", self.mixedT[:, 8 + hg, 1024:1088], OT)

                gens = [prompt_stream(0), prompt_stream(1), sample_stream()]
                while gens:
                    for g_ in list(gens):
                        try:
                            next(g_)
                        except StopIteration:
                            gens.remove(g_)
        self.tap("mixedT", self.mixedT, (128, KC, NOWN))

    def phase_wout(self):
        ar, S = self.ar, self.S
        m0 = ar.mark()
        wo = ar.alloc((4, KC, 512), BF16)
        for g in range(4):
            src = self.I["w_out"][:, g * 512:(g + 1) * 512].rearrange("(c p) n -> p c n", p=128)
            S.dma("pool", wo[:, g, :, :], src, writes=[wo[:, g, :, :]])
        po_ = self.xT_pre_off
        gb = ar.alloc((D,), F32, at=po_)
        bb = ar.alloc((D,), F32, at=po_ + 8192)
        S.dma("sp", gb, self.I["ln1_g"].partition_broadcast(128), writes=[gb])
        S.dma("sp", bb, self.I["ln1_b"].partition_broadcast(128), writes=[bb])
        self.x1T = self.xT_own
        xs = [ar.alloc((D,), F32, at=po_ + 16384 + i * 8192) for i in range(2)]
        ys = [ar.alloc((D,), F32) for _ in range(2)]
        yb = [ar.alloc((D,), BF16) for _ in range(2)]
        stats = ar.alloc((4, 6), F32)
        mv = ar.alloc((2,), F32)
        rstd = ar.alloc((1,), F32)
        self.x1_tokens = []
        def emit_tr(ybf, t0, n):
            for half in range(2):
                pb = self.psb(4 + half)
                for c in range(8):
                    cc = half * 8 + c
                    self.tr(pb[:, c * 128:c * 128 + n], ybf[:, cc * 128:(cc + 1) * 128], self.ident_b[0:n, 0:n])
                src_ps = pb[:, 0:1024].rearrange("p (c t) -> p c t", c=8)[:, :, 0:n]
                self.copy("dve" if half == 0 else "act", self.x1T[:, half * 8:(half + 1) * 8, t0:t0 + n], src_ps)
        pend_tr = None
        for ti, (t0, n) in enumerate(OWN_TILES):
            x = xs[ti % 2][0:n, :]
            y = ys[ti % 2][0:n, :]
            S.dma("sp", x, self.I["x_own"][t0:t0 + n, :], writes=[x])
            for g in range(4):
                po = self.psf(g)[0:n, :]
                for c in range(KC):
                    self.mm(po, self.mixedT[:, c, t0:t0 + n], wo[:, g, c, :],
                            start=(c == 0), stop=(c == KC - 1))
                self.stt("dve", y[:, g * 512:(g + 1) * 512], x[:, g * 512:(g + 1) * 512], ALPHA, po, ALU.mult, ALU.add)
            if pend_tr is not None:
                emit_tr(*pend_tr)
            self.layernorm(y, n, gb, bb, stats, mv, rstd)
            tok = S.dma("sp", self.x1s[t0:t0 + n, :], y, reads=[y])
            self.x1_tokens.append(tok)
            ybf = yb[ti % 2][0:n, :]
            self.copy("act", ybf, y)
            pend_tr = (ybf, t0, n)
        emit_tr(*pend_tr)
        ar.release(m0)

    def layernorm(self, y, n, gb, bb, stats, mv, rstd):
        S = self.S
        for g in range(4):
            S.op("dve", lambda e: e.bn_stats(stats[0:n, g, :], y[:, g * 512:(g + 1) * 512]),
                 reads=[y[:, g * 512:(g + 1) * 512]], writes=[stats[0:n, g, :]])
        S.op("dve", lambda e: e.bn_aggr(mv[0:n, :], stats[0:n, :, :].rearrange("p a b -> p (a b)")),
             reads=[stats[0:n, :, :]], writes=[mv[0:n, :]])
        self.act(rstd[0:n, :], mv[0:n, 1:2], AF.Ln, bias=self.eps_tile(LN_EPS)[0:n, :], scale=1.0,
                 extra_reads=[self.eps_tile(LN_EPS)[0:n, :]])
        self.act(rstd[0:n, :], rstd[0:n, :], AF.Exp, bias=0.0, scale=-0.5)
        self.stt("dve", y, y, mv[0:n, 0:1], gb[0:n, :], ALU.subtract, ALU.mult, extra_reads=[mv[0:n, 0:1]])
        self.stt("dve", y, y, rstd[0:n, :], bb[0:n, :], ALU.mult, ALU.add, extra_reads=[rstd[0:n, :]])

    def eps_tile(self, val):
        if not hasattr(self, "_eps"):
            self._eps = {}
        if val not in self._eps:
            t = self.nc.alloc_sbuf_tensor(f"eps_{len(self._eps)}", [128, 1], F32)
            self.memset("pool", t[:], val)
            self._eps[val] = t[:]
        return self._eps[val]

    def phase_ffn(self):
        ar, S = self.ar, self.S
        ar.release(self.xT_pre_off)
        m0 = ar.mark()
        hT = ar.alloc((64, NOWN), BF16)
        groups = tok_groups(NOWN)
        m1 = ar.mark()
        wu = [ar.alloc((KC, 256), BF16) for _ in range(2)]
        rl = [ar.alloc((512,), F32) for _ in range(2)]
        self.wload(wu[0], self.I["w_up"], 0, 256)
        k = 0
        for s in range(DFF // 256):
            if s + 1 < DFF // 256:
                self.wload(wu[(s + 1) % 2], self.I["w_up"], (s + 1) * 256, 256)
            w = wu[s % 2]
            for j in range(2):
                ft = s * 2 + j
                for gi, (t0, n) in enumerate(groups):
                    bank = k % 4
                    po = self.psf(bank)[:, 0:n]
                    for c in range(KC):
                        self.mm(po, w[:, c, j * 128:(j + 1) * 128], self.x1T[:, c, t0:t0 + n],
                                start=(c == 0), stop=(c == KC - 1))
                    r = rl[k % 2][:, 0:n]
                    self.act(r, po, AF.Relu)
                    self.tt("dve", hT[:, ft, t0:t0 + n], r, r, ALU.mult)
                    k += 1
        ar.release(m1)
        wd = [ar.alloc((64, 128), BF16, at=self.xT_own_off + i * 16384) for i in range(2)]
        oT = [ar.alloc((NOWN,), F32) for _ in range(2)]
        tk = [ar.alloc((128,), F32) for _ in range(2)]
        y2_tokens = []

        def wdload(i):
            src = self.I["w_down"][:, i * 128:(i + 1) * 128].rearrange("(c p) n -> p c n", p=128)
            S.dma("pool", wd[i % 2], src, writes=[wd[i % 2]])
        wdload(0)
        kkc = [0]

        def emit_dn(o, ct):
            for ti, (t0, n) in enumerate(OWN_TILES):
                kk = kkc[0]
                bank = 4 + (kk % 4)
                pt = self.psf(bank)[0:n, 0:128]
                self.tr(pt, o[:, t0:t0 + n], self.ident_f)
                st = tk[kk % 2][0:n, :]
                self.copy("dve" if kk % 2 == 0 else "act", st, pt)
                tok = S.dma("sp", self.y2s[t0:t0 + n, ct * 128:(ct + 1) * 128], st, reads=[st])
                y2_tokens.append(tok)
                kkc[0] += 1
        pend_dn = None
        for ct in range(16):
            if ct + 1 < 16:
                wdload(ct + 1)
            w = wd[ct % 2]
            o = oT[ct % 2]
            for gi, (t0, n) in enumerate(groups):
                bank = gi
                po = self.psf(bank)[:, 0:n]
                for c in range(64):
                    self.mm(po, w[:, c, :], hT[:, c, t0:t0 + n], start=(c == 0), stop=(c == 63))
                self.copy("act" if gi % 2 == 0 else "dve", o[:, t0:t0 + n], po)
            if pend_dn is not None:
                emit_dn(*pend_dn)
            pend_dn = (o, ct)
        emit_dn(*pend_dn)
        if True:
            pass
        ar.release(m0)
        gb = ar.alloc((D,), F32)
        bb = ar.alloc((D,), F32)
        S.dma("sp", gb, self.I["ln2_g"].partition_broadcast(128), writes=[gb])
        S.dma("sp", bb, self.I["ln2_b"].partition_broadcast(128), writes=[bb])
        xs = [ar.alloc((D,), F32) for _ in range(2)]
        ys = [ar.alloc((D,), F32) for _ in range(2)]
        stats = ar.alloc((4, 6), F32)
        mv = ar.alloc((2,), F32)
        rstd = ar.alloc((1,), F32)
        for ti, (t0, n) in enumerate(OWN_TILES):
            x = xs[ti % 2][0:n, :]
            y = ys[ti % 2][0:n, :]
            S.dma("sp", x, self.x1s[t0:t0 + n, :], writes=[x], after=self.x1_tokens)
            S.dma("sp", y, self.y2s[t0:t0 + n, :], writes=[y], after=y2_tokens)
            self.stt("dve", y, x, ALPHA, y, ALU.mult, ALU.add)
            self.layernorm(y, n, gb, bb, stats, mv, rstd)
            S.dma("sp", self.O["y"][t0:t0 + n, :], y, reads=[y], is_output=True)
        ar.release(m0)


_PROG = {}


def get_prog(stop_after=None, dbg=()):
    key = (stop_after, tuple(dbg))
    if key not in _PROG:
        _PROG[key] = Prog(stop_after, dbg)
    return _PROG[key]


def core_inputs(c, inp):
    b, h = c // 2, c % 2
    f = np.float32
    xp = inp["x_prompt"][b]
    x_own = np.concatenate([xp[h * 1024:(h + 1) * 1024], inp["x_sample"][2 * c], inp["x_sample"][2 * c + 1]], 0)
    x_pre = xp[0:1024] if h == 1 else np.zeros((1024, D), f)
    pre_bias = np.full((128, 1), 1.0 if h == 1 else 0.0, f)
    m = {
        "x_own": x_own, "x_pre": x_pre,
        "conv_s": inp["state_gdn_conv"][0, 2 * c:2 * c + 2],
        "S0_s": inp["state_gdn_S"][0, 2 * c:2 * c + 2],
        "ck": inp["cache_sb_k"][0, 2 * c:2 * c + 2].reshape(2, 2048, 1024),
        "cv": inp["cache_sb_v"][0, 2 * c:2 * c + 2].reshape(2, 2048, 1024),
        "w_in": inp["w_in"][0], "conv_w": inp["conv_w"][0], "a_log": inp["a_log"][0],
        "dt_bias": inp["dt_bias"][0], "gdn_norm_w": inp["gdn_norm_w"][0], "w_out": inp["w_out"][0],
        "ln1_g": inp["ln1_g"][0], "ln1_b": inp["ln1_b"][0], "w_up": inp["w_up"][0],
        "w_down": inp["w_down"][0], "ln2_g": inp["ln2_g"][0], "ln2_b": inp["ln2_b"][0],
        "pre_bias": pre_bias,
    }
    return {k: np.ascontiguousarray(v, dtype=f) for k, v in m.items()}


def kernel(**inputs):
    inp = {k: np.asarray(v) for k, v in inputs.items()}
    prog = get_prog()
    in_maps = [core_inputs(c, inp) for c in range(8)]
    res = run_bass_kernel_spmd(prog.nc, in_maps, core_ids=list(range(8)))
    R = res.results
    f = np.float32
    y_p = np.zeros((4, 2048, D), f)
    y_s = np.zeros((16, 32, D), f)
    conv_p = np.zeros((1, 4, 3, 3072), f)
    S_p = np.zeros((1, 4, 8, 128, 128), f)
    k_p = np.zeros((1, 4, 2048, 8, 128), f)
    v_p = np.zeros((1, 4, 2048, 8, 128), f)
    conv_s = np.zeros((1, 16, 3, 3072), f)
    S_s = np.zeros((1, 16, 8, 128, 128), f)
    k_s = np.zeros((1, 16, 32, 8, 128), f)
    v_s = np.zeros((1, 16, 32, 8, 128), f)
    for c in range(8):
        b, h = c // 2, c % 2
        r = R[c]
        y = np.asarray(r["y"])
        kb = np.asarray(r["kb"]).reshape(NOWN, 8, 128)
        vb = np.asarray(r["vb"]).reshape(NOWN, 8, 128)
        sl = slice(h * 1024, (h + 1) * 1024)
        y_p[b, sl] = y[0:1024]
        k_p[0, b, sl] = kb[0:1024]
        v_p[0, b, sl] = vb[0:1024]
        for j in range(2):
            s = 2 * c + j
            y_s[s] = y[1024 + 32 * j:1056 + 32 * j]
            k_s[0, s] = kb[1024 + 32 * j:1056 + 32 * j]
            v_s[0, s] = vb[1024 + 32 * j:1056 + 32 * j]
            conv_s[0, s] = np.asarray(r["conv_s_o"])[j]
            S_s[0, s] = np.asarray(r["S_s_o"])[j]
        if h == 1:
            conv_p[0, b] = np.asarray(r["conv_p_o"])
            S_p[0, b] = np.asarray(r["S_p_o"])
    return (y_p, y_s, conv_p, S_p, k_p, v_p, conv_s, S_s, k_s, v_s)
```

```python
import numpy as np
import concourse.bass as bass
import concourse.mybir as mybir
from concourse.bass_utils import run_bass_kernel_spmd

F32 = mybir.dt.float32
BF16 = mybir.dt.bfloat16
AF = mybir.ActivationFunctionType
ALU = mybir.AluOpType

D = 2048
KC = 16
NOWN = 1088
NPRE = 1024
NALL = NPRE + NOWN
PROJ_W = 7184
OFF_Z = 3072
OFF_B = 4096
OFF_A = 4104
OFF_SB = 4112
DFF = 8192
ALPHA = float(2 ** 0.25)
LN_EPS = 1e-5
NEG = -30000.0
EPOCH = 30000
import os as _os
SAME_ENGINE_INORDER = bool(_os.environ.get('SEI'))
NEU_SINGLE = _os.environ.get('NEU_SINGLE', '0') == '1'


def _rect(ap):
    t = ap.tensor
    dims = list(ap.ap)
    esz = mybir.dt.size(ap.dtype)
    tsz = mybir.dt.size(t.dtype)
    row = 1
    for s in list(t.shape)[1:]:
        row *= s
    rowb = row * tsz
    offb = ap.offset * esz
    pcnt = dims[0][1]
    p_lo = offb // rowb
    f_lo = offb - p_lo * rowb
    ext = 0
    for st, c in dims[1:]:
        ext += abs(st) * (c - 1)
    f_hi = f_lo + (ext + 1) * esz
    p_hi = p_lo + pcnt
    if t.name.startswith("ps"):
        f_lo, f_hi = 0, 2048
        p_lo = (p_lo // 32) * 32
        p_hi = ((p_hi + 31) // 32) * 32
    return (t.name, p_lo, p_hi, f_lo, f_hi)


class Sched:
    def __init__(self, nc, n_dma_sems=48):
        self.nc = nc
        self.E = {"pe": nc.tensor, "dve": nc.vector, "act": nc.scalar, "pool": nc.gpsimd, "sp": nc.sync}
        self.csem = {}
        self.ccnt = {}
        self.nep = {}
        for e in ("pe", "dve", "act", "pool"):
            self.csem[e] = nc.alloc_semaphore(f"c_{e}_0")
            self.ccnt[e] = 0
            self.nep[e] = 0
        self.dsems = [nc.alloc_semaphore(f"d_{i}") for i in range(n_dma_sems)]
        self.dcnt = [0] * n_dma_sems
        self.dnext = {"sw": 0, "hw": n_dma_sems // 2}
        self.drange = {"sw": (0, n_dma_sems // 2), "hw": (n_dma_sems // 2, n_dma_sems)}
        self.known = {e: {} for e in self.E}
        self.recs = {}
        self.n_inst = 0
        self.n_wait = 0
        self.out_tokens = []

    def _need(self, eng, tok, waits):
        sem, val, name = tok
        if self.known[eng].get(name, 0) >= val:
            return
        cur = waits.get(name)
        if cur is None or cur[1] < val:
            waits[name] = (sem, val)

    @staticmethod
    def _ov(a, b):
        return a[1] < b[2] and b[1] < a[2] and a[3] < b[4] and b[3] < a[4]

    @staticmethod
    def _covers(a, b):
        return a[1] <= b[1] and a[2] >= b[2] and a[3] <= b[3] and a[4] >= b[4]

    def _deps(self, eng, reads, writes, waits):
        for r in reads:
            psum = r[0].startswith("ps")
            for rec in self.recs.get(r[0], ()):
                if rec[1] and self._ov(rec[0], r):
                    if rec[3] == eng and (eng == "pe" or (SAME_ENGINE_INORDER and eng in ("act", "dve"))):
                        continue
                    self._need(eng, rec[2], waits)
                elif psum and (not rec[1]) and rec[3] != eng:
                    self._need(eng, rec[2], waits)
        for w in writes:
            for rec in self.recs.get(w[0], ()):
                if self._ov(rec[0], w):
                    if rec[3] == eng and (eng == "pe" or (SAME_ENGINE_INORDER and eng in ("act", "dve"))):
                        continue
                    self._need(eng, rec[2], waits)

    def _record(self, eng, tok, reads, writes):
        for w in writes:
            lst = self.recs.setdefault(w[0], [])
            lst[:] = [rec for rec in lst if not self._covers(w, rec[0])]
            lst.append([w, True, tok, eng])
        for r in reads:
            lst = self.recs.setdefault(r[0], [])
            done = False
            if eng != "dma":
                for rec in lst:
                    if (not rec[1]) and rec[3] == eng and rec[0] == r:
                        rec[2] = tok
                        done = True
                        break
            if not done:
                lst.append([r, False, tok, eng])

    def _emit_waits(self, eng, waits):
        e = self.E[eng]
        for name, (sem, val) in waits.items():
            e.wait_ge(sem, val)
            self.known[eng][name] = val
            self.n_wait += 1

    def op(self, eng, fn, reads=(), writes=()):
        rr = [_rect(a) for a in reads]
        ww = [_rect(a) for a in writes]
        waits = {}
        self._deps(eng, rr, ww, waits)
        if self.ccnt[eng] >= EPOCH:
            self.nep[eng] += 1
            self.csem[eng] = self.nc.alloc_semaphore(f"c_{eng}_{self.nep[eng]}")
            self.ccnt[eng] = 0
        self._emit_waits(eng, waits)
        ins = fn(self.E[eng])
        self.ccnt[eng] += 1
        sem = self.csem[eng]
        ins.then_inc(sem, 1)
        tok = (sem, self.ccnt[eng], f"c_{eng}_{self.nep[eng]}")
        self._record(eng, tok, rr, ww)
        self.n_inst += 1
        return tok

    def dma(self, q, out, in_, reads=(), writes=(), is_output=False, after=(), **kw):
        rr = [_rect(a) for a in reads]
        ww = [_rect(a) for a in writes]
        waits = {}
        self._deps(q, rr, ww, waits)
        for tok in after:
            self._need(q, tok, waits)
        kind = "sw" if q == "pool" else "hw"
        i = self.dnext[kind]
        lo, hi = self.drange[kind]
        self.dnext[kind] = lo + (i + 1 - lo) % (hi - lo)
        sem = self.dsems[i]
        name = f"d_{i}"
        if self.dcnt[i] > 0:
            self._need(q, (sem, self.dcnt[i] * 16, name), waits)
        self._emit_waits(q, waits)
        ins = self.E[q].dma_start(out=out, in_=in_, **kw)
        self.dcnt[i] += 1
        ins.then_inc(sem, 16)
        tok = (sem, self.dcnt[i] * 16, name)
        self._record("dma", tok, rr, ww)
        self.n_inst += 1
        if is_output:
            self.out_tokens.append(tok)
        return tok

    def finish(self):
        waits = {}
        for tok in self.out_tokens:
            self._need("sp", tok, waits)
        for i, sem in enumerate(self.dsems):
            if self.dcnt[i]:
                self._need("sp", (sem, self.dcnt[i] * 16, f"d_{i}"), waits)
        self._emit_waits("sp", waits)


class Arena:
    def __init__(self, nc, nbytes):
        self.t = nc.alloc_sbuf_tensor("arena", [128, nbytes // 2], BF16)
        self.cap = nbytes
        self.top = 0

    def alloc(self, shape, dtype, parts=128, at=None):
        if isinstance(shape, int):
            shape = (shape,)
        n = 1
        for s in shape:
            n *= s
        nb = n * mybir.dt.size(dtype)
        if at is not None:
            off = at
            assert off % 64 == 0 and off + nb <= self.cap
        else:
            off = self.top
            self.top += (nb + 63) // 64 * 64
            assert self.top <= self.cap, f"arena overflow {self.top} > {self.cap}"
        self.last_off = off
        v = self.t[0:parts, off // 2:(off + nb) // 2]
        if dtype != BF16:
            v = v.bitcast(dtype)
        if len(shape) == 2:
            v = v.rearrange("p (a b) -> p a b", a=shape[0])
        elif len(shape) == 3:
            v = v.rearrange("p (a b c) -> p a b c", a=shape[0], b=shape[1])
        return v

    def mark(self):
        return self.top

    def release(self, m):
        self.top = m


OWN_TILES = [(i * 128, 128) for i in range(8)] + [(1024, 64)]
PRE_TILES = [(i * 128, 128) for i in range(8)]


def tok_groups(n, g=512):
    out = []
    t = 0
    while t < n:
        out.append((t, min(g, n - t)))
        t += g
    return out


class Prog:
    def __init__(self, stop_after=None, dbg=()):
        self.stop_after = stop_after
        self.dbg_names = dbg
        nc = bass.Bass("TRN2", target_bir_lowering=False)
        self.nc = nc
        self.S = Sched(nc)
        self.I = {}
        self.O = {}
        self.dbg = {}

        def inp(name, shape):
            self.I[name] = nc.dram_tensor(name, list(shape), F32, kind="ExternalInput").ap()

        def outp(name, shape):
            self.O[name] = nc.dram_tensor(name, list(shape), F32, kind="ExternalOutput").ap()

        inp("x_own", (NOWN, D))
        inp("x_pre", (NPRE, D))
        inp("conv_s", (2, 3, 3072))
        inp("S0_s", (2, 8, 128, 128))
        inp("ck", (2, 2048, 1024))
        inp("cv", (2, 2048, 1024))
        inp("w_in", (D, PROJ_W))
        inp("conv_w", (4, 3072))
        inp("a_log", (8,))
        inp("dt_bias", (8,))
        inp("gdn_norm_w", (128,))
        inp("w_out", (D, D))
        inp("ln1_g", (D,))
        inp("ln1_b", (D,))
        inp("w_up", (D, DFF))
        inp("w_down", (DFF, D))
        inp("ln2_g", (D,))
        inp("ln2_b", (D,))
        inp("pre_bias", (128, 1))
        outp("y", (NOWN, D))
        outp("kb", (NOWN, 1024))
        outp("vb", (NOWN, 1024))
        outp("conv_p_o", (3, 3072))
        outp("conv_s_o", (2, 3, 3072))
        outp("S_p_o", (8, 128, 128))
        outp("S_s_o", (2, 8, 128, 128))
        self.x1s = nc.dram_tensor("x1_scratch", [NOWN, D], F32, kind="Internal").ap()
        self.y2s = nc.dram_tensor("y2_scratch", [NOWN, D], F32, kind="Internal").ap()

        self.ar = Arena(nc, 212480)
        self.ps = [nc.alloc_psum_tensor(f"ps{i}", [128, 512], F32) for i in range(8)]
        self.build()

    def psf(self, i):
        return self.ps[i][:]

    def psb(self, i):
        return self.ps[i][:].bitcast(BF16)

    def tap(self, name, ap, shape):
        if name not in self.dbg_names:
            return
        t = self.nc.dram_tensor("dbg_" + name, list(shape), ap.dtype, kind="ExternalOutput").ap()
        self.dbg[name] = t
        self.S.dma("sp", t, ap, reads=[ap], is_output=True)

    def mm(self, out, lhsT, rhs, start=True, stop=True, skip=False):
        if skip:
            self.S.op("pe", lambda e: e.matmul(out, lhsT, rhs, start=start, stop=stop, skip_group_check=True),
                      reads=[lhsT, rhs], writes=[out])
        else:
            self.S.op("pe", lambda e: e.matmul(out, lhsT, rhs, start=start, stop=stop),
                      reads=[lhsT, rhs], writes=[out])

    def tr(self, out, in_, ident):
        self.S.op("pe", lambda e: e.transpose(out, in_, ident), reads=[in_, ident], writes=[out])

    def copy(self, eng, out, in_):
        if eng == "act":
            self.S.op("act", lambda e: e.copy(out, in_), reads=[in_], writes=[out])
        else:
            self.S.op(eng, lambda e: e.tensor_copy(out, in_), reads=[in_], writes=[out])

    def act(self, out, in_, func, bias=0.0, scale=1.0, accum_out=None, extra_reads=()):
        rd = [in_] + list(extra_reads)
        wr = [out] + ([accum_out] if accum_out is not None else [])
        if accum_out is not None:
            self.S.op("act", lambda e: e.activation(out=out, in_=in_, func=func, bias=bias, scale=scale,
                                                    accum_out=accum_out), reads=rd, writes=wr)
        else:
            self.S.op("act", lambda e: e.activation(out=out, in_=in_, func=func, bias=bias, scale=scale),
                      reads=rd, writes=wr)

    def tt(self, eng, out, in0, in1, op):
        self.S.op(eng, lambda e: e.tensor_tensor(out=out, in0=in0, in1=in1, op=op), reads=[in0, in1], writes=[out])

    def ts(self, eng, out, in0, s1, op0, s2=None, op1=None, extra_reads=()):
        rd = [in0] + list(extra_reads)
        if op1 is None:
            self.S.op(eng, lambda e: e.tensor_scalar(out=out, in0=in0, scalar1=s1, scalar2=None, op0=op0),
                      reads=rd, writes=[out])
        else:
            self.S.op(eng, lambda e: e.tensor_scalar(out=out, in0=in0, scalar1=s1, scalar2=s2, op0=op0, op1=op1),
                      reads=rd, writes=[out])

    def stt(self, eng, out, in0, scalar, in1, op0, op1, extra_reads=()):
        rd = [in0, in1] + list(extra_reads)
        self.S.op(eng, lambda e: e.scalar_tensor_tensor(out=out, in0=in0, scalar=scalar, in1=in1, op0=op0, op1=op1),
                  reads=rd, writes=[out])

    def memset(self, eng, ap, val):
        self.S.op(eng, lambda e: e.memset(ap, val), writes=[ap])

    def asel(self, out, in_, pattern, cmp, fill, base, cm):
        self.S.op("pool", lambda e: e.affine_select(out=out, in_=in_, pattern=pattern, compare_op=cmp, fill=fill,
                                                    base=base, channel_multiplier=cm), reads=[in_], writes=[out])

    def build(self):
        self.consts()
        self.phase_x()
        if self.stop_after == "x":
            return self.S.finish()
        self.phase_attn()
        if self.stop_after == "attn":
            return self.S.finish()
        self.phase_wout()
        if self.stop_after == "wout":
            return self.S.finish()
        self.phase_ffn()
        self.S.finish()

    def consts(self):
        ar = self.ar
        self.ident_f = ar.alloc((128,), F32)
        self.ident_b = ar.alloc((128,), BF16)
        self.zeros_f = ar.alloc((128,), F32)
        self.memset("pool", self.zeros_f, 0.0)
        self.memset("pool", self.ident_f, 0.0)
        self.asel(self.ident_f, self.ident_f, [[-1, 128]], ALU.not_equal, 1.0, 0, 1)
        self.copy("pool", self.ident_b, self.ident_f)
        self.pre_bias = self.nc.alloc_sbuf_tensor("pre_bias_t", [128, 1], F32)[:]
        import os
        if os.environ.get("PB_MEMSET"):
            self.memset("pool", self.pre_bias, 1.0)
        else:
            self.S.dma("sp", self.pre_bias, self.I["pre_bias"], writes=[self.pre_bias])

    def phase_x(self):
        ar, S = self.ar, self.S
        self.xT_own = ar.alloc((KC, NOWN), BF16)
        self.xT_own_off = ar.last_off
        self.xT_pre = ar.alloc((KC, NPRE), BF16)
        self.xT_pre_off = ar.last_off
        m = ar.mark()
        xb = [ar.alloc((D,), BF16) for _ in range(2)]
        k = 0
        for src, tiles, dst in ((self.I["x_pre"], PRE_TILES, self.xT_pre), (self.I["x_own"], OWN_TILES, self.xT_own)):
            for (t0, n) in tiles:
                b = xb[k % 2]
                S.dma("pool", b[0:n, :], src[t0:t0 + n, :], writes=[b[0:n, :]])
                for half in range(2):
                    pb = self.psb(half)
                    for c in range(8):
                        cc = half * 8 + c
                        self.tr(pb[:, c * 128:c * 128 + n], b[0:n, cc * 128:(cc + 1) * 128], self.ident_b[0:n, 0:n])
                    src_ps = pb[:, 0:1024].rearrange("p (c t) -> p c t", c=8)[:, :, 0:n]
                    self.copy("dve" if half == 0 else "act", dst[:, half * 8:(half + 1) * 8, t0:t0 + n], src_ps)
                k += 1
        ar.release(m)
        self.tap("xT_own", self.xT_own, (128, KC, NOWN))

    def phase_attn(self):
        ar = self.ar
        self.mixedT = ar.alloc((KC, NOWN), BF16)
        self.memset("pool", self.mixedT, 0.0)
        m = ar.mark()
        import os
        if not os.environ.get("NO_GDN"):
            self.phase_gdn()
        ar.release(m)
        if not os.environ.get("NO_SB"):
            self.phase_sb()
        ar.release(m)

    def phase_gdn(self):
        ar, S, I = self.ar, self.S, self.I
        NT = 17
        tiles = [("pre", t0, n, t0) for (t0, n) in PRE_TILES] + [("own", t0, n, NPRE + t0) for (t0, n) in OWN_TILES]
        zb = ar.alloc((128,), BF16)
        ones_b = ar.alloc((128,), BF16)
        ones_f = ar.alloc((128,), F32)
        nones_f = ar.alloc((128,), F32)
        self.memset("pool", zb, 0.0)
        self.memset("pool", ones_b, 1.0)
        self.memset("pool", ones_f, 1.0)
        self.memset("pool", nones_f, -1.0)
        offd = ar.alloc((128,), BF16)
        self.memset("pool", offd, 1.0)
        self.asel(offd, offd, [[-1, 128]], ALU.not_equal, 0.0, 0, 1)
        ident4 = ar.alloc((4, 128), BF16)
        for j in range(4):
            self.copy("pool", ident4[:, j, :], self.ident_b)

        def block_mask(kind, c, rows):
            if kind in ("Ms", "Mi"):
                m = ar.alloc((128,), BF16)
                self.memset("pool", m, NEG)
            else:
                m = ar.alloc((128,), F32)
                self.memset("pool", m, 0.0)
            for r0 in range(0, rows, c):
                blk = slice(r0, r0 + c)
                if kind == "Ms":
                    self.asel(m[blk, blk], zb[blk, blk], [[-1, c]], ALU.is_ge, NEG, -1, 1)
                elif kind == "Mi":
                    self.asel(m[blk, blk], zb[blk, blk], [[-1, c]], ALU.is_ge, NEG, 0, 1)
                elif kind == "tri":
                    self.asel(m[blk, blk], ones_f[blk, blk], [[1, c]], ALU.is_ge, 0.0, 0, -1)
                elif kind == "last":
                    self.asel(m[blk, blk], ones_f[blk, blk], [[0, c]], ALU.is_equal, 0.0, -(c - 1), 1)
            return m
        maskMs = {64: block_mask("Ms", 64, 128), 32: block_mask("Ms", 32, 64)}
        maskMi = {64: block_mask("Mi", 64, 128), 32: block_mask("Mi", 32, 64)}
        trich = {64: block_mask("tri", 64, 128), 32: block_mask("tri", 32, 64)}
        sellast = {64: block_mask("last", 64, 128), 32: block_mask("last", 32, 64)}
        lastsel = {}
        for (c, rows) in ((64, 128), (32, 64)):
            for nch in range(2):
                m = ar.alloc((128,), F32)
                self.memset("pool", m, 0.0)
                self.asel(m[0:rows, :], ones_f[0:rows, :], [[0, 128]], ALU.is_equal, 0.0, -((nch + 1) * c - 1), 1)
                lastsel[(c, nch)] = m
        nsel = ar.alloc((8, 128), F32, parts=8)
        self.memset("pool", nsel, 0.0)
        for h in range(8):
            self.asel(nsel[:, h, :], nones_f[0:8, :], [[0, 128]], ALU.is_equal, 0.0, -h, 1)
        normw = ar.alloc((1,), F32)
        S.dma("sp", normw, I["gdn_norm_w"].rearrange("(p o) -> p o", o=1), writes=[normw])
        dtb = ar.alloc((8,), F32)
        S.dma("sp", dtb, I["dt_bias"].partition_broadcast(128), writes=[dtb])
        negA = ar.alloc((8,), F32)
        S.dma("sp", negA, I["a_log"].partition_broadcast(128), writes=[negA])
        self.act(negA, negA, AF.Exp)
        self.ts("dve", negA, negA, -1.0, ALU.mult)
        eps6 = self.eps_tile(1e-6)
        one_c = self.eps_tile(1.0)
        lnqs = self.eps_tile(float(np.log(128 ** -0.5)))
        cw = ar.alloc((24, 4), F32)
        hist = ar.alloc((24, 6), F32)
        mtmp = ar.mark()
        cwt = ar.alloc((3072,), F32, parts=4)
        S.dma("sp", cwt, I["conv_w"], writes=[cwt])
        hst = ar.alloc((3072,), F32, parts=6)
        S.dma("sp", hst, I["conv_s"].rearrange("s r c -> (s r) c"), writes=[hst])
        pc = self.psf(0)
        for ct in range(24):
            self.mm(pc[:, ct * 4:ct * 4 + 4], cwt[:, ct * 128:(ct + 1) * 128], self.ident_f[0:4, 0:4])
        self.copy("dve", cw, pc[:, 0:96].rearrange("p (a b) -> p a b", b=4))
        pc = self.psf(1)
        for ct in range(24):
            self.mm(pc[:, ct * 6:ct * 6 + 6], hst[:, ct * 128:(ct + 1) * 128], self.ident_f[0:6, 0:6])
        self.copy("dve", hist, pc[:, 0:144].rearrange("p (a b) -> p a b", b=6))
        ar.release(mtmp)
        w16 = ar.alloc((KC, 16), BF16)
        S.dma("pool", w16, I["w_in"][:, OFF_B:OFF_B + 16].rearrange("(c p) n -> p c n", p=128), writes=[w16])
        P16 = ar.alloc((NT, 16), F32)
        ps = self.psf(0)
        for ti, (src, t0, n, c0) in enumerate(tiles):
            xT = self.xT_pre if src == "pre" else self.xT_own
            for c in range(KC):
                self.mm(ps[0:n, ti * 16:(ti + 1) * 16], xT[:, c, t0:t0 + n], w16[:, c, :], start=(c == 0), stop=(c == KC - 1))
        self.memset("pool", P16, 0.0)
        self.copy("dve", P16[:, 0:16, :], ps[:, 0:256].rearrange("p (a b) -> p a b", b=16))
        self.copy("dve", P16[0:64, 16, :], ps[0:64, 256:272])
        TMq = {}
        for nm in ("g", "G", "lb", "beta", "sk", "eg", "st", "tmp"):
            TMq[nm] = ar.alloc((NT, 8), F32)
        bl = P16[:, :, 0:8]
        al = P16[:, :, 8:16]
        bc17 = lambda t: t.unsqueeze(1).broadcast_to([128, NT, 8])
        self.tt("dve", TMq["tmp"], al, bc17(dtb), ALU.add)
        self.act(TMq["tmp"], TMq["tmp"], AF.Exp)
        self.act(TMq["tmp"], TMq["tmp"], AF.Ln, bias=one_c, extra_reads=[one_c])
        self.tt("dve", TMq["g"], TMq["tmp"], bc17(negA), ALU.mult)
        self.act(TMq["lb"], bl, AF.Exp, scale=-1.0)
        self.act(TMq["lb"], TMq["lb"], AF.Ln, bias=one_c, extra_reads=[one_c])
        self.act(TMq["beta"], TMq["lb"], AF.Exp, scale=-1.0)
        ps = self.psf(1)
        for ti, (src, t0, n, c0) in enumerate(tiles):
            c = 64 if n == 128 else 32
            self.mm(ps[0:n, ti * 8:(ti + 1) * 8], trich[c][0:n, 0:n], TMq["g"][0:n, ti, :])
        self.memset("pool", TMq["G"], 0.0)
        self.copy("dve", TMq["G"][:, 0:16, :], ps[:, 0:128].rearrange("p (a b) -> p a b", b=8))
        self.copy("dve", TMq["G"][0:64, 16, :], ps[0:64, 128:136])
        self.act(TMq["eg"], TMq["G"], AF.Exp)
        self.tt("dve", TMq["tmp"], TMq["G"], TMq["lb"], ALU.subtract)
        self.act(TMq["sk"], TMq["tmp"], AF.Exp)
        ps = self.psf(0)
        for ti, (src, t0, n, c0) in enumerate(tiles):
            c = 64 if n == 128 else 32
            self.mm(ps[0:n, ti * 8:(ti + 1) * 8], sellast[c][0:n, 0:n], TMq["G"][0:n, ti, :])
        self.memset("pool", TMq["tmp"], 0.0)
        self.tt("dve", TMq["tmp"][:, 0:16, :], ps[:, 0:128].rearrange("p (a b) -> p a b", b=8), TMq["G"][:, 0:16, :], ALU.subtract)
        self.tt("dve", TMq["tmp"][0:64, 16, :], ps[0:64, 128:136], TMq["G"][0:64, 16, :], ALU.subtract)
        self.act(TMq["st"], TMq["tmp"], AF.Exp)
        GT = ar.alloc((NT * 2, 8), F32)
        ps = self.psf(1)
        for ti, (src, t0, n, c0) in enumerate(tiles):
            c = 64 if n == 128 else 32
            for nch in range(2):
                j = ti * 2 + nch
                self.mm(ps[:, j * 8:(j + 1) * 8], lastsel[(c, nch)][0:n, :], TMq["eg"][0:n, ti, :])
        self.copy("dve", GT, ps[:, 0:NT * 16].rearrange("p (a b) -> p a b", b=8))
        Grow = ar.alloc((NALL,), F32, parts=8)
        for q0 in range(0, NT, 4):
            ps = self.psf(q0 // 4 % 2)
            for ti in range(q0, min(NT, q0 + 4)):
                (src, t0, n, c0) = tiles[ti]
                self.mm(ps[0:8, (ti - q0) * 128:(ti - q0) * 128 + n], TMq["G"][0:n, ti, :], self.ident_f[0:n, 0:n])
            nc_ = sum(tiles[ti][2] for ti in range(q0, min(NT, q0 + 4)))
            self.copy("dve", Grow[:, tiles[q0][3]:tiles[q0][3] + nc_], ps[0:8, 0:nc_])
        self.tap("Grow", Grow, (8, NALL))
        self.tap("TMst", TMq["st"], (128, NT, 8))
        self.tap("TMsk", TMq["sk"], (128, NT, 8))
        self.tap("GT", GT, (128, NT * 2, 8))
        wb = ar.alloc((KC, 512), BF16)
        NKV = 2121
        pcb = ar.alloc((NKV,), F32)
        cvb = ar.alloc((NKV,), F32)
        knT = ar.alloc((NALL,), BF16)
        vT = ar.alloc((NALL,), BF16)
        qnT = ar.alloc((NOWN,), BF16)
        sqb = ar.alloc((512,), BF16)
        lnb_ = ar.alloc((512,), F32)
        zgT = ar.alloc((NOWN,), BF16)
        cst = lnb_[0:3, 0:384]
        opq = []
        for i in range(2):
            opq.append({
                "kbg": ar.alloc((4, 128), BF16), "kt": ar.alloc((4, 128), BF16), "vb": ar.alloc((4, 128), BF16),
                "wtok": ar.alloc((4, 128), BF16), "attnT": ar.alloc((4, 128), BF16), "ub": ar.alloc((4, 128), BF16),
                "nW2": ar.alloc((4, 2, 128), BF16), "nAWT": ar.alloc((4, 128), BF16),
            })
        Mb = [ar.alloc((4, 128), BF16) for _ in range(2)]
        Ab = [ar.alloc((4, 128), BF16) for _ in range(2)]
        Yb = [ar.alloc((4, 128), BF16) for _ in range(2)]
        Mp = ar.alloc((4, 128), BF16) if NEU_SINGLE else None
        DMi = ar.alloc((4, 128), BF16)
        DMs = ar.alloc((4, 128), BF16)
        attn = DMs
        Sf = ar.alloc((128,), F32)
        Sbf = ar.alloc((128,), BF16)
        t1 = ar.alloc((128,), F32)
        otok2 = [ar.alloc((128,), F32) for _ in range(2)]
        gjunk = ar.alloc((128,), BF16)
        osq = t1
        ogb = ar.alloc((128,), BF16)
        ssq = ar.alloc((1,), F32)
        rs = ar.alloc((1,), F32)

        def kvcol(tok):
            if tok < 2048:
                return 3 + tok
            if tok < 2080:
                return 2054 + (tok - 2048)
            return 2089 + (tok - 2080)

        def qcol(t):
            if t < 1024:
                return 3 + t
            if t < 1056:
                return 1030 + (t - 1024)
            return 1065 + (t - 1056)
        kv_segs = [(3, 2048, 0), (2054, 32, 2048), (2089, 32, 2080)]
        q_segs = [(3, 1024, 0), (1030, 32, 1024), (1065, 32, 1056)]

        def conv_silu(ncols, ct):
            n = ncols - 3
            self.ts("dve", cvb[:, 3:ncols], pcb[:, 0:n], cw[:, ct, 0:1], ALU.mult, extra_reads=[cw[:, ct, 0:1]])
            for i in range(1, 4):
                self.stt("dve", cvb[:, 3:ncols], pcb[:, i:i + n], cw[:, ct, i:i + 1], cvb[:, 3:ncols], ALU.mult, ALU.add,
                         extra_reads=[cw[:, ct, i:i + 1]])

        def conv_out(which, h, cols3):
            ct = which * 8 + h
            pz = self.psf(1)
            for sgi, c0 in enumerate(cols3):
                self.mm(pz[0:3, sgi * 128:(sgi + 1) * 128], pcb[:, c0:c0 + 3], self.ident_f)
            self.copy("dve", cst, pz[0:3, 0:384])
            S.dma("sp", self.O["conv_p_o"][:, ct * 128:(ct + 1) * 128], cst[:, 0:128], reads=[cst[:, 0:128]], is_output=True)
            for j in range(2):
                S.dma("sp", self.O["conv_s_o"][j, :, ct * 128:(ct + 1) * 128], cst[:, (j + 1) * 128:(j + 2) * 128],
                      reads=[cst[:, (j + 1) * 128:(j + 2) * 128]], is_output=True)

        def normalize(segs, dst, extra_bias):
            for (c0, ln, k0) in segs:
                for o in range(0, ln, 512):
                    n = min(512, ln - o)
                    src = cvb[:, c0 + o:c0 + o + n]
                    self.act(sqb[:, 0:n], src, AF.Square)
                    pz = self.psf(1)
                    self.mm(pz[:, 0:n], ones_b, sqb[:, 0:n])
                    self.act(lnb_[:, 0:n], pz[:, 0:n], AF.Ln, bias=eps6, extra_reads=[eps6])
                    if extra_bias is None:
                        self.act(lnb_[:, 0:n], lnb_[:, 0:n], AF.Exp, scale=-0.5)
                    else:
                        self.act(lnb_[:, 0:n], lnb_[:, 0:n], AF.Exp, scale=-0.5, bias=extra_bias, extra_reads=[extra_bias])
                    self.tt("dve", dst[:, k0 + o:k0 + o + n], src, lnb_[:, 0:n], ALU.mult)

        quads = [list(range(0, 4)), list(range(4, 8)), list(range(8, 12)), list(range(12, 16)), [16]]

        for h in range(8):
            def wload_head(hh):
                for j, coff in enumerate((hh * 128, 1024 + hh * 128, 2048 + hh * 128, OFF_Z + hh * 128)):
                    src = I["w_in"][:, coff:coff + 128].rearrange("(c p) n -> p c n", p=128)
                    S.dma("pool", wb[:, :, j * 128:(j + 1) * 128], src, writes=[wb[:, :, j * 128:(j + 1) * 128]])
            if h == 0:
                wload_head(0)

            def proj_fm(wj, xT, t0, n, dst):
                self._pfm = getattr(self, "_pfm", 0) + 1
                po = self.psf(self._pfm % 2)[:, 0:n]
                for c in range(KC):
                    self.mm(po, wb[:, c, wj * 128:(wj + 1) * 128], xT[:, c, t0:t0 + n], start=(c == 0), stop=(c == KC - 1))
                self.copy("act", dst, po)
            def proj_q():
                proj_fm(0, self.xT_pre, 1021, 3, pcb[:, 0:3])
                for (t0, n) in ((0, 512), (512, 512)):
                    proj_fm(0, self.xT_own, t0, n, pcb[:, 3 + t0:3 + t0 + n])
                proj_fm(0, self.xT_own, 1024, 32, pcb[:, 1030:1062])
                proj_fm(0, self.xT_own, 1056, 32, pcb[:, 1065:1097])
                for j in range(2):
                    self.copy("pool", pcb[:, 1027 + 35 * j:1030 + 35 * j], hist[:, h, 3 * j:3 * j + 3])

            def proj_kv(which):
                self.memset("pool", pcb[:, 0:3], 0.0)
                for (t0, n) in ((0, 512), (512, 512)):
                    proj_fm(which, self.xT_pre, t0, n, pcb[:, 3 + t0:3 + t0 + n])
                for (t0, n) in ((0, 512), (512, 512)):
                    proj_fm(which, self.xT_own, t0, n, pcb[:, 3 + 1024 + t0:3 + 1024 + t0 + n])
                proj_fm(which, self.xT_own, 1024, 32, pcb[:, 2054:2086])
                proj_fm(which, self.xT_own, 1056, 32, pcb[:, 2089:2121])
                for j in range(2):
                    self.copy("pool", pcb[:, 2051 + 35 * j:2054 + 35 * j], hist[:, which * 8 + h, 3 * j:3 * j + 3])

            def conv_stage(which):
                if which == 0:
                    conv_out(0, h, (1024, 1059, 1094))
                    conv_silu(1097, h)
                    self.act(cvb[:, 3:1097], cvb[:, 3:1097], AF.Silu)
                else:
                    conv_out(which, h, (2048, 2083, 2118))
                    conv_silu(NKV, which * 8 + h)
                    self.act(cvb[:, 3:NKV], cvb[:, 3:NKV], AF.Silu)
            proj_q()
            conv_stage(0)
            proj_kv(1)
            normalize(q_segs, qnT, lnqs)
            conv_stage(1)
            proj_kv(2)
            normalize(kv_segs, knT, None)
            conv_stage(2)
            for (c0, ln, k0) in kv_segs:
                self.copy("pool", vT[:, k0:k0 + ln], cvb[:, c0:c0 + ln])
            for gi, (t0, n) in enumerate(tok_groups(NOWN)):
                pz = self.psf(gi % 2)[:, 0:n]
                for c in range(KC):
                    self.mm(pz, wb[:, c, 384:512], self.xT_own[:, c, t0:t0 + n], start=(c == 0), stop=(c == KC - 1))
                self.act(lnb_[:, 0:n], pz, AF.Silu)
                self.ts("dve", zgT[:, t0:t0 + n], lnb_[:, 0:n], normw, ALU.mult, extra_reads=[normw])
            if h == 0:
                self.tap("knT0", knT, (128, NALL))
                self.tap("qnT0", qnT, (128, NOWN))
                self.tap("vT0", vT, (128, NALL))
            if h + 1 < 8:
                wload_head(h + 1)
            self.memset("pool", Sf, 0.0)
            self.memset("pool", Sbf, 0.0)
            def gen_L(qi, h=h):
                quad = quads[qi]
                ops = opq[qi % 2]
                own_quad = tiles[quad[0]][0] == "own"
                nq = len(quad)
                nn = tiles[quad[0]][2]
                v3 = lambda p: p[0:nn, 0:nq * 128].rearrange("p (a b) -> p a b", b=128)[:, :, 0:nn]
                pb = self.psb(2)
                for j, ti in enumerate(quad):
                    (src, t0, n, c0) = tiles[ti]
                    self.tr(pb[0:n, j * 256:j * 256 + 128], knT[:, c0:c0 + n], self.ident_b)
                    self.tr(pb[0:n, j * 256 + 128:j * 256 + 256], vT[:, c0:c0 + n], self.ident_b)
                yield
                for j, ti in enumerate(quad):
                    (src, t0, n, c0) = tiles[ti]
                    kps = pb[0:n, j * 256:j * 256 + 128]
                    vps = pb[0:n, j * 256 + 128:j * 256 + 256]
                    sc = lambda nm: TMq[nm][0:n, ti, h:h + 1]
                    self.ts("dve", ops["kbg"][0:n, j, :], kps, sc("sk"), ALU.mult, extra_reads=[sc("sk")])
                    self.ts("dve", ops["kt"][0:n, j, :], kps, sc("st"), ALU.mult, extra_reads=[sc("st")])
                    self.ts("dve", ops["vb"][0:n, j, :], vps, sc("beta"), ALU.mult, extra_reads=[sc("beta")])
                pk = self.psf(3)
                pe_ = self.psf(4)
                for j, ti in enumerate(quad):
                    (src, t0, n, c0) = tiles[ti]
                    c = 64 if n == 128 else 32
                    self.mm(pk[0:n, j * 128:j * 128 + n], knT[:, c0:c0 + n], knT[:, c0:c0 + n])
                    self.mm(pe_[0:n, j * 128:j * 128 + n], nsel[:, h, 0:n], Grow[:, c0:c0 + n], start=True, stop=False)
                    msk = maskMi[c] if own_quad else maskMs[c]
                    self.mm(pe_[0:n, j * 128:j * 128 + n], self.ident_b[0:n, 0:n], msk[0:n, 0:n], start=False, stop=True)
                yield
                for j, ti in enumerate(quad):
                    (src, t0, n, c0) = tiles[ti]
                    gb_ = TMq["G"][0:n, ti, h:h + 1]
                    dst = DMi if own_quad else DMs
                    self.act(dst[0:n, j, 0:n], pe_[0:n, j * 128:j * 128 + n], AF.Exp, bias=gb_, extra_reads=[gb_])
                    if own_quad:
                        self.tt("pool", DMs[0:n, j, 0:n], DMi[0:n, j, 0:n], offd[0:n, 0:n], ALU.mult)
                M0, A0, Y0 = Mb[0], Ab[0], Yb[0]
                for j, ti in enumerate(quad):
                    (src, t0, n, c0) = tiles[ti]
                    bt_ = TMq["beta"][0:n, ti, h:h + 1]
                    self.stt("dve", M0[0:n, j, 0:n], pk[0:n, j * 128:j * 128 + n], bt_, DMs[0:n, j, 0:n], ALU.mult, ALU.mult,
                             extra_reads=[bt_])
                yield
                pt = self.psb(2)
                for j, ti in enumerate(quad):
                    n = tiles[ti][2]
                    self.tr(pt[0:n, j * 128:j * 128 + n], M0[0:n, j, 0:n], self.ident_b[0:n, 0:n])
                ptv = pt[0:nn, 0:nq * 128].rearrange("p (a b) -> p a b", b=128)[:, :, 0:nn]
                self.copy("act", A0[0:nn, 0:nq, 0:nn], ptv)
                self.stt("dve", Y0[0:nn, 0:nq, 0:nn], ptv, -1.0, ident4[0:nn, 0:nq, 0:nn], ALU.mult, ALU.add)
                yield
                cur = 0
                ycur = 0
                pend = None
                p5, p6, p7 = self.psf(5), self.psf(6), self.psf(7)

                def y_mm(Mlhs, Ysrc):
                    for j, ti in enumerate(quad):
                        n = tiles[ti][2]
                        self.mm(p7[0:n, j * 128:j * 128 + n], self.ident_b[0:n, 0:n], Ysrc[0:n, j, 0:n], start=True, stop=False)
                        self.mm(p7[0:n, j * 128:j * 128 + n], Mlhs[0:n, j, 0:n], Ysrc[0:n, j, 0:n], start=False, stop=True)
                for it in range(5):
                    Mc, Ac = Mb[cur], Ab[cur]
                    Mn, An = Mb[1 - cur], Ab[1 - cur]
                    for j, ti in enumerate(quad):
                        n = tiles[ti][2]
                        self.mm(p5[0:n, j * 128:j * 128 + n], Ac[0:n, j, 0:n], Mc[0:n, j, 0:n])
                        if it < 4:
                            self.mm(p6[0:n, j * 128:j * 128 + n], Mc[0:n, j, 0:n], Ac[0:n, j, 0:n])
                    if pend is not None:
                        y_mm(Mc, Yb[ycur])
                    yield
                    self.copy("act", Mn[0:nn, 0:nq, 0:nn], v3(p5))
                    if it < 4:
                        self.copy("dve", An[0:nn, 0:nq, 0:nn], v3(p6))
                    if pend is not None:
                        self.copy("dve", Yb[1 - ycur][0:nn, 0:nq, 0:nn], v3(p7))
                        ycur = 1 - ycur
                    pend = True
                    cur = 1 - cur
                    yield
                y_mm(Mb[cur], Yb[ycur])
                yield
                self.copy("dve", Yb[1 - ycur][0:nn, 0:nq, 0:nn], v3(p7))
                cur = 1 - ycur
                Yf = Yb[cur]
                pu, pw = self.psf(3), self.psf(4)
                for j, ti in enumerate(quad):
                    n = tiles[ti][2]
                    self.mm(pu[0:n, j * 128:(j + 1) * 128], Yf[0:n, j, 0:n], ops["vb"][0:n, j, :])
                    self.mm(pw[0:n, j * 128:(j + 1) * 128], Yf[0:n, j, 0:n], ops["kbg"][0:n, j, :])
                yield
                self.copy("act", ops["ub"][0:nn, 0:nq, :], pu[0:nn, 0:nq * 128].rearrange("p (a b) -> p a b", b=128))
                self.copy("dve", ops["wtok"][0:nn, 0:nq, :], pw[0:nn, 0:nq * 128].rearrange("p (a b) -> p a b", b=128))
                p56 = (self.psf(5), self.psf(6))
                cc = 64 if nn == 128 else 32
                for j, ti in enumerate(quad):
                    for nch in range(2):
                        r = slice(nch * cc, (nch + 1) * cc)
                        self.mm(p56[nch][:, j * 128:(j + 1) * 128], ops["wtok"][r, j, :], ops["kt"][r, j, :])
                yield
                self.ts("dve", ops["nW2"][:, 0:nq, 0, :], p56[0][:, 0:nq * 128].rearrange("p (a b) -> p a b", b=128), -1.0, ALU.mult)
                self.act(ops["nW2"][:, 0:nq, 1, :], p56[1][:, 0:nq * 128].rearrange("p (a b) -> p a b", b=128), AF.Copy, scale=-1.0)
                if own_quad:
                    pq = self.psf(3)
                    for j, ti in enumerate(quad):
                        (src, t0, n, c0) = tiles[ti]
                        self.mm(pq[0:n, j * 128:j * 128 + n], qnT[:, t0:t0 + n], knT[:, c0:c0 + n])
                    yield
                    self.tt("dve", attn[0:nn, 0:nq, 0:nn], v3(pq), DMi[0:nn, 0:nq, 0:nn], ALU.mult)
                    pt = self.psb(2)
                    for j, ti in enumerate(quad):
                        n = tiles[ti][2]
                        self.tr(pt[0:n, j * 128:j * 128 + n], attn[0:n, j, 0:n], self.ident_b[0:n, 0:n])
                    yield
                    self.copy("act", ops["attnT"][0:nn, 0:nq, 0:nn],
                              pt[0:nn, 0:nq * 128].rearrange("p (a b) -> p a b", b=128)[:, :, 0:nn])
                    p7 = self.psf(7)
                    for j, ti in enumerate(quad):
                        n = tiles[ti][2]
                        self.mm(p7[:, j * 128:j * 128 + n], ops["wtok"][0:n, j, :], ops["attnT"][0:n, j, 0:n])
                    yield
                    self.ts("dve", ops["nAWT"][:, 0:nq, 0:nn],
                            p7[:, 0:nq * 128].rearrange("p (a b) -> p a b", b=128)[:, :, 0:nn], -1.0, ALU.mult)
                if h == 0 and qi == 0:
                    self.tap("Yf0", Yf, (128, 4, 128))
                    self.tap("M00", Mb[0], (128, 4, 128))

            gstate = {"g": None}

            def gate_gen(ot, t0, n, h=h):
                self.act(gjunk[0:n, :], ot[0:n, :], AF.Square, accum_out=ssq[0:n, :])
                self.act(rs[0:n, :], ssq[0:n, :], AF.Ln, scale=1.0 / 128.0, bias=eps6[0:n, :], extra_reads=[eps6[0:n, :]])
                self.act(rs[0:n, :], rs[0:n, :], AF.Exp, scale=-0.5)
                yield
                self.ts("dve", ogb[0:n, :], ot[0:n, :], rs[0:n, :], ALU.mult, extra_reads=[rs[0:n, :]])
                pg = self.psb(0)
                self.tr(pg[:, 0:n], ogb[0:n, :], self.ident_b[0:n, 0:n])
                yield
                self.tt("dve", self.mixedT[:, h, t0:t0 + n], pg[:, 0:n], zgT[:, t0:t0 + n], ALU.mult)

            def gate_step(drain=False):
                g_ = gstate["g"]
                while g_ is not None:
                    try:
                        next(g_)
                    except StopIteration:
                        gstate["g"] = None
                        return
                    if not drain:
                        return

            def gen_S(qi, h=h):
                quad = quads[qi]
                ops = opq[qi % 2]
                for j, ti in enumerate(quad):
                    (src, t0, n, c0) = tiles[ti]
                    c = 64 if n == 128 else 32
                    pS = self.psf(1)
                    ot = otok2[ti % 2]
                    for nch in range(n // c):
                        r = slice(nch * c, (nch + 1) * c)
                        if n == 64:
                            S.dma("sp", Sf, I["S0_s"][nch, h], writes=[Sf])
                            self.copy("act", Sbf, Sf)
                        self.mm(pS[:, 128:256], ops["nW2"][:, j, nch, :], Sbf, start=True, stop=False)
                        self.mm(pS[:, 128:256], ops["kt"][r, j, :], ops["ub"][r, j, :], start=False, stop=True)
                        if src == "own":
                            self.mm(pS[r, 256:384], qnT[:, t0 + nch * c:t0 + (nch + 1) * c], Sbf)
                            self.mm(pS[r, 384:512], ops["nAWT"][:, j, r], Sbf, start=True, stop=False)
                            self.mm(pS[r, 384:512], ops["attnT"][r, j, r], ops["ub"][r, j, :], start=False, stop=True)
                        gate_step()
                        yield
                        gt_ = GT[:, ti * 2 + nch, h:h + 1]
                        self.stt("dve", Sbf, Sf, gt_, pS[:, 128:256], ALU.mult, ALU.add, extra_reads=[gt_])
                        self.stt("dve", Sf, Sf, gt_, pS[:, 128:256], ALU.mult, ALU.add, extra_reads=[gt_])
                        if src == "own":
                            eg_ = TMq["eg"][r, ti, h:h + 1]
                            self.act(t1[r, :], pS[r, 256:384], AF.Copy, scale=eg_, extra_reads=[eg_])
                            self.tt("dve", ot[r, :], t1[r, :], pS[r, 384:512], ALU.add)
                        if n == 64:
                            S.dma("sp", self.O["S_s_o"][nch, h], Sf, reads=[Sf], is_output=True)
                        gate_step()
                        yield
                    if src == "own" and t0 == 896:
                        S.dma("sp", self.O["S_p_o"][h], Sf, reads=[Sf], is_output=True)
                    if src == "own":
                        gate_step(drain=True)
                        gstate["g"] = gate_gen(ot, t0, n)
                if qi == len(quads) - 1:
                    gate_step(drain=True)

            def run_gens(gens):
                gens = list(gens)
                while gens:
                    for g_ in list(gens):
                        try:
                            next(g_)
                        except StopIteration:
                            gens.remove(g_)
            run_gens([gen_L(0)])
            for qi in range(len(quads)):
                gl = [gen_S(qi)]
                if qi + 1 < len(quads):
                    gl.append(gen_L(qi + 1))
                run_gens(gl)
        self.tap("mixedT_g", self.mixedT, (128, KC, NOWN))


    def wload(self, buf, w_ap, c0, ncols, kc=KC):
        src = w_ap[:, c0:c0 + ncols].rearrange("(c p) n -> p c n", p=128)
        dst = buf[:, :, 0:ncols]
        self.S.dma("pool", dst, src, writes=[dst])

    def phase_sb(self):
        ar, S = self.ar, self.S
        wb = [ar.alloc((KC, 512), BF16) for _ in range(2)]
        stage = [ar.alloc((512,), F32) for _ in range(2)]
        KT = ar.alloc((4, NALL), BF16)
        VT = ar.alloc((17, 512), BF16)
        QT2 = [ar.alloc((NOWN,), BF16) for _ in range(2)]
        Eb = [ar.alloc((512,), F32) for _ in range(2)]
        SPb = [ar.alloc((512,), BF16) for _ in range(2)]
        Wb = [ar.alloc((512,), BF16) for _ in range(2)]
        Xb = [ar.alloc((512,), F32) for _ in range(2)] + [ar.alloc((64,), F32)]
        Eb.append(ar.alloc((64,), F32))
        SPb.append(ar.alloc((64,), BF16))
        Wb.append(ar.alloc((64,), BF16))
        KTn = ar.alloc((4, 64), BF16)
        Vn = ar.alloc((512,), BF16)
        ntri_i = ar.alloc((128,), BF16)
        ntri_c = ar.alloc((128,), BF16)
        zb = ar.alloc((512,), BF16)
        KTc = ar.alloc((2, 2048), BF16)
        Vc = ar.alloc((2, 16, 128), BF16)
        m64 = ar.alloc((64,), BF16, parts=64)
        ones64 = ar.alloc((64,), BF16, parts=64)
        one_c = self.eps_tile(1.0)
        self.memset("pool", m64, 0.0)
        self.memset("pool", ones64, 1.0)
        for s_ in range(2):
            sq = slice(32 * s_, 32 * s_ + 32)
            self.asel(m64[sq, sq], ones64[sq, sq], [[1, 32]], ALU.is_ge, 0.0, -1, -1)
        self.memset("pool", zb, 0.0)
        self.memset("pool", ntri_i, -1.0)
        self.asel(ntri_i, ntri_i, [[-1, 128]], ALU.is_ge, 0.0, 0, 1)
        self.memset("pool", ntri_c, -1.0)
        self.asel(ntri_c, ntri_c, [[1, 128]], ALU.is_gt, 0.0, 0, -1)
        all_tiles = [("pre", t0, n) for (t0, n) in PRE_TILES] + [("own", t0, n) for (t0, n) in OWN_TILES]
        import os
        sbstop = int(os.environ.get("SB_STOP", "99"))
        if sbstop <= 1:
            return
        nw = 0
        ev = 0
        for g in range(2):
            for which in ("k", "v"):
                coff = OFF_SB + (1024 if which == "k" else 2048) + g * 512
                dst = self.O["kb"] if which == "k" else self.O["vb"]
                w = wb[nw % 2]
                nw += 1
                self.wload(w, self.I["w_in"], coff, 512)

                def emit_ktr(ti, kpos, n):
                    pt = self.psf(4 + (ti % 2))
                    stf = stage[ti % 2]
                    for h in range(4):
                        self.tr(pt[:, h * 128:(h + 1) * 128], stf[:, h * 128:(h + 1) * 128], self.ident_f)
                    src_ps = pt[:, 0:512].rearrange("p (h t) -> p h t", h=4)[:, :, 0:n]
                    self.copy("act", KT[:, :, kpos:kpos + n], src_ps)
                pend_k = None
                for ti, (src, t0, n) in enumerate(all_tiles):
                    xT = self.xT_pre if src == "pre" else self.xT_own
                    kpos = t0 if src == "pre" else NPRE + t0
                    po = self.psf(6 + (ti % 2))[0:n, :]
                    for c in range(KC):
                        self.mm(po, xT[:, c, t0:t0 + n], w[:, c, :], start=(c == 0), stop=(c == KC - 1))
                    st = stage[ti % 2][0:n, :]
                    if which == "k":
                        self.copy("dve", st, po)
                        if src == "own":
                            S.dma("sp", dst[t0:t0 + n, g * 512:(g + 1) * 512], st, reads=[st], is_output=True)
                        if pend_k is not None:
                            emit_ktr(*pend_k)
                        pend_k = (ti, kpos, n)
                    else:
                        if src == "own":
                            self.copy("dve", st, po)
                            S.dma("sp", dst[t0:t0 + n, g * 512:(g + 1) * 512], st, reads=[st], is_output=True)
                            self.copy("act", VT[0:n, ti, :], st)
                        else:
                            self.copy("act", VT[0:n, ti, :], po)
                if which == "k" and pend_k is not None:
                    emit_ktr(*pend_k)
            self.copy("dve", KTn, KT[:, :, NPRE + 1024:NPRE + 1088])
            self.copy("dve", Vn[0:64, :], VT[0:64, 16, :])
            wq = wb[nw % 2]
            nw += 1
            self.wload(wq, self.I["w_in"], OFF_SB + g * 512, 512)
            def q_proj(hh, gi, wq=wq, g=g):
                (t0, n) = tok_groups(NOWN)[gi]
                po = self.psf(6)[:, 0:n]
                for c in range(KC):
                    self.mm(po, wq[:, c, hh * 128:(hh + 1) * 128], self.xT_own[:, c, t0:t0 + n],
                            start=(c == 0), stop=(c == KC - 1))
                self.act(QT2[(g * 4 + hh) % 2][:, t0:t0 + n], po, AF.Copy, scale=float(128 ** -0.5))
            for h in range(4):
                hg = g * 4 + h
                QT = QT2[hg % 2]
                if h == 0:
                    for gi in range(3):
                        q_proj(0, gi)
                def prompt_stream(sb, h=h, hg=hg, QT=QT):
                    nonlocal ev
                    blocks = []
                    for kb in range(4 * sb + 3, -1, -1):
                        cs = max(0, (kb - 4 * sb)) * 128
                        blocks.append(("own", kb, cs, kb >= 4 * sb))
                    for kb in range(7, -1, -1):
                        blocks.append(("pre", kb, 0, False))
                    A = self.psf(2 + sb)
                    OT = self.psf(4 + sb)
                    q0 = sb * 512
                    self.mm(A, zb[:, 0:128], zb[:, 0:512], start=True, stop=False, skip=True)
                    self.mm(OT, zb[:, 0:128], zb[:, 0:512], start=True, stop=False, skip=True)
                    yield
                    for (src, kb, cs, diag) in blocks:
                        kpos = kb * 128 if src == "pre" else NPRE + kb * 128
                        vt = kb if src == "pre" else 8 + kb
                        zt = self.psf(ev % 2)
                        E, SP, W, X = Eb[sb], SPb[sb], Wb[sb], Xb[sb]
                        ev += 1
                        kt = KT[:, h, kpos:kpos + 128]
                        self.mm(zt[:, cs:512], kt, QT[:, q0 + cs:q0 + 512])
                        self.act(E[:, cs:512], zt[:, cs:512], AF.Exp)
                        if src == "pre":
                            self.act(SP[:, cs:512], E[:, cs:512], AF.Ln, bias=one_c, scale=self.pre_bias,
                                     extra_reads=[one_c, self.pre_bias])
                        else:
                            self.act(SP[:, cs:512], E[:, cs:512], AF.Ln, bias=one_c, extra_reads=[one_c])
                        if diag:
                            self.asel(SP[:, cs:cs + 128], SP[:, cs:cs + 128], [[1, 128]], ALU.is_ge, 0.0, -1, -1)
                        yield
                        self.mm(A[:, cs:512], ntri_i, SP[:, cs:512], start=False, stop=False, skip=True)
                        self.act(X[:, cs:512], A[:, cs:512], AF.Exp)
                        self.tt("dve", W[:, cs:512], E[:, cs:512], X[:, cs:512], ALU.mult)
                        if diag:
                            self.asel(W[:, cs:cs + 128], W[:, cs:cs + 128], [[1, 128]], ALU.is_ge, 0.0, -1, -1)
                        yield
                        self.mm(A[:, cs:512], ntri_c, SP[:, cs:512], start=False, stop=False, skip=True)
                        self.mm(OT[:, cs:512], VT[:, vt, h * 128:(h + 1) * 128], W[:, cs:512], start=False, stop=False, skip=True)
                        yield
                    self.copy("dve", self.mixedT[:, 8 + hg, sb * 512:(sb + 1) * 512], OT)

                def sample_stream(h=h, hg=hg, QT=QT):
                    nonlocal ev
                    par = hg % 2
                    wfree = wb[nw % 2]

                    def kst_dma(hh, s_):
                        pp = hh % 2
                        k_ = wfree[:, 8 * pp + 4 * s_:8 * pp + 4 * s_ + 4, :].rearrange("p a (b d) -> p (a b) d", d=128)
                        S.dma("pool", k_, self.I["ck"][s_, :, hh * 128:(hh + 1) * 128].rearrange("(b p) d -> p b d", p=128),
                              writes=[k_])
                    for s_ in range(2):
                        kst = wfree[:, 8 * par + 4 * s_:8 * par + 4 * s_ + 4, :].rearrange("p a (b d) -> p (a b) d", d=128)
                        if h == 0:
                            kst_dma(hg, s_)
                        if h < 3:
                            kst_dma(hg + 1, s_)
                        S.dma("pool", Vc[:, s_, :, :],
                              self.I["cv"][s_, :, hg * 128:(hg + 1) * 128].rearrange("(b p) d -> p b d", p=128),
                              writes=[Vc[:, s_, :, :]])
                    yield
                    for s_ in range(2):
                        kst = wfree[:, 8 * par + 4 * s_:8 * par + 4 * s_ + 4, :].rearrange("p a (b d) -> p (a b) d", d=128)
                        for half in range(2):
                            pb = self.psb(6)
                            for c in range(8):
                                self.tr(pb[:, c * 128:(c + 1) * 128], kst[:, half * 8 + c, :], self.ident_b)
                            self.copy("dve", KTc[:, s_, half * 1024:(half + 1) * 1024], pb[:, 0:1024])
                            yield
                    A = self.psf(7)[:, 0:64]
                    OT = self.psf(7)[:, 128:192]
                    self.mm(self.psf(7)[:, 0:192], zb[:, 0:128], zb[:, 0:192], start=True, stop=False, skip=True)
                    qs = QT[:, 1024:1088]
                    for blk in [-1] + list(range(15, -1, -1)):
                        zt = self.psf(ev % 2)
                        E, SP, W, X = Eb[2], SPb[2], Wb[2], Xb[2]
                        ev += 1
                        if blk < 0:
                            kn = KTn[:, h, :]
                            self.mm(zt[0:64, 0:64], kn, qs)
                            self.act(E[0:64, 0:64], zt[0:64, 0:64], AF.Exp)
                            self.act(SP[0:64, 0:64], E[0:64, 0:64], AF.Ln, bias=one_c[0:64, :], extra_reads=[one_c[0:64, :]])
                            self.tt("pool", SP[0:64, 0:64], SP[0:64, 0:64], m64, ALU.mult)
                            yield
                            self.mm(A[0:64, :], ntri_i[0:64, 0:64], SP[0:64, 0:64], start=False, stop=False, skip=True)
                            self.act(X[0:64, 0:64], A[0:64, :], AF.Exp)
                            self.tt("dve", W[0:64, 0:64], E[0:64, 0:64], X[0:64, 0:64], ALU.mult)
                            self.tt("pool", W[0:64, 0:64], W[0:64, 0:64], m64, ALU.mult)
                            yield
                            self.mm(A, ntri_c[0:64, :], SP[0:64, 0:64], start=False, stop=False, skip=True)
                            self.mm(OT, Vn[0:64, h * 128:(h + 1) * 128], W[0:64, 0:64], start=False, stop=False, skip=True)
                            yield
                        else:
                            ks = [KTc[:, s_, blk * 128:(blk + 1) * 128] for s_ in range(2)]
                            cs2 = [slice(32 * s_, 32 * s_ + 32) for s_ in range(2)]
                            for s_ in range(2):
                                self.mm(zt[:, cs2[s_]], ks[s_], qs[:, cs2[s_]])
                            self.act(E[:, 0:64], zt[:, 0:64], AF.Exp)
                            self.act(SP[:, 0:64], E[:, 0:64], AF.Ln, bias=one_c, extra_reads=[one_c])
                            yield
                            self.mm(A, ntri_i, SP[:, 0:64], start=False, stop=False, skip=True)
                            self.act(X[:, 0:64], A, AF.Exp)
                            self.tt("dve", W[:, 0:64], E[:, 0:64], X[:, 0:64], ALU.mult)
                            yield
                            self.mm(A, ntri_c, SP[:, 0:64], start=False, stop=False, skip=True)
                            for s_ in range(2):
                                self.mm(OT[:, cs2[s_]], Vc[:, s_, blk, :], W[:, cs2[s_]], start=False, stop=False, skip=True)
                            if h < 3 and 13 <= blk <= 15:
                                q_proj(h + 1, 15 - blk)
                            yield
                    self.copy("dve", self.mixedT[:, 8 + hg, 1024:1088], OT)

                gens = [prompt_stream(0), prompt_stream(1), sample_stream()]
                while gens:
                    for g_ in list(gens):
                        try:
                            next(g_)
                        except StopIteration:
                            gens.remove(g_)
        self.tap("mixedT", self.mixedT, (128, KC, NOWN))

    def phase_wout(self):
        ar, S = self.ar, self.S
        m0 = ar.mark()
        wo = ar.alloc((4, KC, 512), BF16)
        for g in range(4):
            src = self.I["w_out"][:, g * 512:(g + 1) * 512].rearrange("(c p) n -> p c n", p=128)
            S.dma("pool", wo[:, g, :, :], src, writes=[wo[:, g, :, :]])
        po_ = self.xT_pre_off
        gb = ar.alloc((D,), F32, at=po_)
        bb = ar.alloc((D,), F32, at=po_ + 8192)
        S.dma("sp", gb, self.I["ln1_g"].partition_broadcast(128), writes=[gb])
        S.dma("sp", bb, self.I["ln1_b"].partition_broadcast(128), writes=[bb])
        self.x1T = self.xT_own
        xs = [ar.alloc((D,), F32, at=po_ + 16384 + i * 8192) for i in range(2)]
        ys = [ar.alloc((D,), F32) for _ in range(2)]
        yb = [ar.alloc((D,), BF16) for _ in range(2)]
        stats = ar.alloc((4, 6), F32)
        mv = ar.alloc((2,), F32)
        rstd = ar.alloc((1,), F32)
        self.x1_tokens = []
        def emit_tr(ybf, t0, n):
            for half in range(2):
                pb = self.psb(4 + half)
                for c in range(8):
                    cc = half * 8 + c
                    self.tr(pb[:, c * 128:c * 128 + n], ybf[:, cc * 128:(cc + 1) * 128], self.ident_b[0:n, 0:n])
                src_ps = pb[:, 0:1024].rearrange("p (c t) -> p c t", c=8)[:, :, 0:n]
                self.copy("dve" if half == 0 else "act", self.x1T[:, half * 8:(half + 1) * 8, t0:t0 + n], src_ps)
        pend_tr = None
        for ti, (t0, n) in enumerate(OWN_TILES):
            x = xs[ti % 2][0:n, :]
            y = ys[ti % 2][0:n, :]
            S.dma("sp", x, self.I["x_own"][t0:t0 + n, :], writes=[x])
            for g in range(4):
                po = self.psf(g)[0:n, :]
                for c in range(KC):
                    self.mm(po, self.mixedT[:, c, t0:t0 + n], wo[:, g, c, :],
                            start=(c == 0), stop=(c == KC - 1))
                self.stt("dve", y[:, g * 512:(g + 1) * 512], x[:, g * 512:(g + 1) * 512], ALPHA, po, ALU.mult, ALU.add)
            if pend_tr is not None:
                emit_tr(*pend_tr)
            self.layernorm(y, n, gb, bb, stats, mv, rstd)
            tok = S.dma("sp", self.x1s[t0:t0 + n, :], y, reads=[y])
            self.x1_tokens.append(tok)
            ybf = yb[ti % 2][0:n, :]
            self.copy("act", ybf, y)
            pend_tr = (ybf, t0, n)
        emit_tr(*pend_tr)
        ar.release(m0)

    def layernorm(self, y, n, gb, bb, stats, mv, rstd):
        S = self.S
        for g in range(4):
            S.op("dve", lambda e: e.bn_stats(stats[0:n, g, :], y[:, g * 512:(g + 1) * 512]),
                 reads=[y[:, g * 512:(g + 1) * 512]], writes=[stats[0:n, g, :]])
        S.op("dve", lambda e: e.bn_aggr(mv[0:n, :], stats[0:n, :, :].rearrange("p a b -> p (a b)")),
             reads=[stats[0:n, :, :]], writes=[mv[0:n, :]])
        self.act(rstd[0:n, :], mv[0:n, 1:2], AF.Ln, bias=self.eps_tile(LN_EPS)[0:n, :], scale=1.0,
                 extra_reads=[self.eps_tile(LN_EPS)[0:n, :]])
        self.act(rstd[0:n, :], rstd[0:n, :], AF.Exp, bias=0.0, scale=-0.5)
        self.stt("dve", y, y, mv[0:n, 0:1], gb[0:n, :], ALU.subtract, ALU.mult, extra_reads=[mv[0:n, 0:1]])
        self.stt("dve", y, y, rstd[0:n, :], bb[0:n, :], ALU.mult, ALU.add, extra_reads=[rstd[0:n, :]])

    def eps_tile(self, val):
        if not hasattr(self, "_eps"):
            self._eps = {}
        if val not in self._eps:
            t = self.nc.alloc_sbuf_tensor(f"eps_{len(self._eps)}", [128, 1], F32)
            self.memset("pool", t[:], val)
            self._eps[val] = t[:]
        return self._eps[val]

    def phase_ffn(self):
        ar, S = self.ar, self.S
        ar.release(self.xT_pre_off)
        m0 = ar.mark()
        hT = ar.alloc((64, NOWN), BF16)
        groups = tok_groups(NOWN)
        m1 = ar.mark()
        wu = [ar.alloc((KC, 256), BF16) for _ in range(2)]
        rl = [ar.alloc((512,), F32) for _ in range(2)]
        self.wload(wu[0], self.I["w_up"], 0, 256)
        k = 0
        for s in range(DFF // 256):
            if s + 1 < DFF // 256:
                self.wload(wu[(s + 1) % 2], self.I["w_up"], (s + 1) * 256, 256)
            w = wu[s % 2]
            for j in range(2):
                ft = s * 2 + j
                for gi, (t0, n) in enumerate(groups):
                    bank = k % 4
                    po = self.psf(bank)[:, 0:n]
                    for c in range(KC):
                        self.mm(po, w[:, c, j * 128:(j + 1) * 128], self.x1T[:, c, t0:t0 + n],
                                start=(c == 0), stop=(c == KC - 1))
                    r = rl[k % 2][:, 0:n]
                    self.act(r, po, AF.Relu)
                    self.tt("dve", hT[:, ft, t0:t0 + n], r, r, ALU.mult)
                    k += 1
        ar.release(m1)
        wd = [ar.alloc((64, 128), BF16, at=self.xT_own_off + i * 16384) for i in range(2)]
        oT = [ar.alloc((NOWN,), F32) for _ in range(2)]
        tk = [ar.alloc((128,), F32) for _ in range(2)]
        y2_tokens = []

        def wdload(i):
            src = self.I["w_down"][:, i * 128:(i + 1) * 128].rearrange("(c p) n -> p c n", p=128)
            S.dma("pool", wd[i % 2], src, writes=[wd[i % 2]])
        wdload(0)
        kkc = [0]

        def emit_dn(o, ct):
            for ti, (t0, n) in enumerate(OWN_TILES):
                kk = kkc[0]
                bank = 4 + (kk % 4)
                pt = self.psf(bank)[0:n, 0:128]
                self.tr(pt, o[:, t0:t0 + n], self.ident_f)
                st = tk[kk % 2][0:n, :]
                self.copy("dve" if kk % 2 == 0 else "act", st, pt)
                tok = S.dma("sp", self.y2s[t0:t0 + n, ct * 128:(ct + 1) * 128], st, reads=[st])
                y2_tokens.append(tok)
                kkc[0] += 1
        pend_dn = None
        for ct in range(16):
            if ct + 1 < 16:
                wdload(ct + 1)
            w = wd[ct % 2]
            o = oT[ct % 2]
            for gi, (t0, n) in enumerate(groups):
                bank = gi
                po = self.psf(bank)[:, 0:n]
                for c in range(64):
                    self.mm(po, w[:, c, :], hT[:, c, t0:t0 + n], start=(c == 0), stop=(c == 63))
                self.copy("act" if gi % 2 == 0 else "dve", o[:, t0:t0 + n], po)
            if pend_dn is not None:
                emit_dn(*pend_dn)
            pend_dn = (o, ct)
        emit_dn(*pend_dn)
        if True:
            pass
        ar.release(m0)
        gb = ar.alloc((D,), F32)
        bb = ar.alloc((D,), F32)
        S.dma("sp", gb, self.I["ln2_g"].partition_broadcast(128), writes=[gb])
        S.dma("sp", bb, self.I["ln2_b"].partition_broadcast(128), writes=[bb])
        xs = [ar.alloc((D,), F32) for _ in range(2)]
        ys = [ar.alloc((D,), F32) for _ in range(2)]
        stats = ar.alloc((4, 6), F32)
        mv = ar.alloc((2,), F32)
        rstd = ar.alloc((1,), F32)
        for ti, (t0, n) in enumerate(OWN_TILES):
            x = xs[ti % 2][0:n, :]
            y = ys[ti % 2][0:n, :]
            S.dma("sp", x, self.x1s[t0:t0 + n, :], writes=[x], after=self.x1_tokens)
            S.dma("sp", y, self.y2s[t0:t0 + n, :], writes=[y], after=y2_tokens)
            self.stt("dve", y, x, ALPHA, y, ALU.mult, ALU.add)
            self.layernorm(y, n, gb, bb, stats, mv, rstd)
            S.dma("sp", self.O["y"][t0:t0 + n, :], y, reads=[y], is_output=True)
        ar.release(m0)


_PROG = {}


def get_prog(stop_after=None, dbg=()):
    key = (stop_after, tuple(dbg))
    if key not in _PROG:
        _PROG[key] = Prog(stop_after, dbg)
    return _PROG[key]


def core_inputs(c, inp):
    b, h = c // 2, c % 2
    f = np.float32
    xp = inp["x_prompt"][b]
    x_own = np.concatenate([xp[h * 1024:(h + 1) * 1024], inp["x_sample"][2 * c], inp["x_sample"][2 * c + 1]], 0)
    x_pre = xp[0:1024] if h == 1 else np.zeros((1024, D), f)
    pre_bias = np.full((128, 1), 1.0 if h == 1 else 0.0, f)
    m = {
        "x_own": x_own, "x_pre": x_pre,
        "conv_s": inp["state_gdn_conv"][0, 2 * c:2 * c + 2],
        "S0_s": inp["state_gdn_S"][0, 2 * c:2 * c + 2],
        "ck": inp["cache_sb_k"][0, 2 * c:2 * c + 2].reshape(2, 2048, 1024),
        "cv": inp["cache_sb_v"][0, 2 * c:2 * c + 2].reshape(2, 2048, 1024),
        "w_in": inp["w_in"][0], "conv_w": inp["conv_w"][0], "a_log": inp["a_log"][0],
        "dt_bias": inp["dt_bias"][0], "gdn_norm_w": inp["gdn_norm_w"][0], "w_out": inp["w_out"][0],
        "ln1_g": inp["ln1_g"][0], "ln1_b": inp["ln1_b"][0], "w_up": inp["w_up"][0],
        "w_down": inp["w_down"][0], "ln2_g": inp["ln2_g"][0], "ln2_b": inp["ln2_b"][0],
        "pre_bias": pre_bias,
    }
    return {k: np.ascontiguousarray(v, dtype=f) for k, v in m.items()}


def kernel(**inputs):
    inp = {k: np.asarray(v) for k, v in inputs.items()}
    prog = get_prog()
    in_maps = [core_inputs(c, inp) for c in range(8)]
    res = run_bass_kernel_spmd(prog.nc, in_maps, core_ids=list(range(8)))
    R = res.results
    f = np.float32
    y_p = np.zeros((4, 2048, D), f)
    y_s = np.zeros((16, 32, D), f)
    conv_p = np.zeros((1, 4, 3, 3072), f)
    S_p = np.zeros((1, 4, 8, 128, 128), f)
    k_p = np.zeros((1, 4, 2048, 8, 128), f)
    v_p = np.zeros((1, 4, 2048, 8, 128), f)
    conv_s = np.zeros((1, 16, 3, 3072), f)
    S_s = np.zeros((1, 16, 8, 128, 128), f)
    k_s = np.zeros((1, 16, 32, 8, 128), f)
    v_s = np.zeros((1, 16, 32, 8, 128), f)
    for c in range(8):
        b, h = c // 2, c % 2
        r = R[c]
        y = np.asarray(r["y"])
        kb = np.asarray(r["kb"]).reshape(NOWN, 8, 128)
        vb = np.asarray(r["vb"]).reshape(NOWN, 8, 128)
        sl = slice(h * 1024, (h + 1) * 1024)
        y_p[b, sl] = y[0:1024]
        k_p[0, b, sl] = kb[0:1024]
        v_p[0, b, sl] = vb[0:1024]
        for j in range(2):
            s = 2 * c + j
            y_s[s] = y[1024 + 32 * j:1056 + 32 * j]
            k_s[0, s] = kb[1024 + 32 * j:1056 + 32 * j]
            v_s[0, s] = vb[1024 + 32 * j:1056 + 32 * j]
            conv_s[0, s] = np.asarray(r["conv_s_o"])[j]
            S_s[0, s] = np.asarray(r["S_s_o"])[j]
        if h == 1:
            conv_p[0, b] = np.asarray(r["conv_p_o"])
            S_p[0, b] = np.asarray(r["S_p_o"])
    return (y_p, y_s, conv_p, S_p, k_p, v_p, conv_s, S_s, k_s, v_s)
```

```python
import numpy as np
import concourse.bass as bass
import concourse.mybir as mybir
from concourse.bass_utils import run_bass_kernel_spmd

F32 = mybir.dt.float32
BF16 = mybir.dt.bfloat16
AF = mybir.ActivationFunctionType
ALU = mybir.AluOpType

D = 2048
KC = 16
NOWN = 1088
NPRE = 1024
NALL = NPRE + NOWN
PROJ_W = 7184
OFF_Z = 3072
OFF_B = 4096
OFF_A = 4104
OFF_SB = 4112
DFF = 8192
ALPHA = float(2 ** 0.25)
LN_EPS = 1e-5
NEG = -30000.0
EPOCH = 30000
import os as _os
SAME_ENGINE_INORDER = bool(_os.environ.get('SEI'))
NEU_SINGLE = _os.environ.get('NEU_SINGLE', '0') == '1'


def _rect(ap):
    t = ap.tensor
    dims = list(ap.ap)
    esz = mybir.dt.size(ap.dtype)
    tsz = mybir.dt.size(t.dtype)
    row = 1
    for s in list(t.shape)[1:]:
        row *= s
    rowb = row * tsz
    offb = ap.offset * esz
    pcnt = dims[0][1]
    p_lo = offb // rowb
    f_lo = offb - p_lo * rowb
    ext = 0
    for st, c in dims[1:]:
        ext += abs(st) * (c - 1)
    f_hi = f_lo + (ext + 1) * esz
    p_hi = p_lo + pcnt
    if t.name.startswith("ps"):
        f_lo, f_hi = 0, 2048
        p_lo = (p_lo // 32) * 32
        p_hi = ((p_hi + 31) // 32) * 32
    return (t.name, p_lo, p_hi, f_lo, f_hi)


class Sched:
    def __init__(self, nc, n_dma_sems=48):
        self.nc = nc
        self.E = {"pe": nc.tensor, "dve": nc.vector, "act": nc.scalar, "pool": nc.gpsimd, "sp": nc.sync}
        self.csem = {}
        self.ccnt = {}
        self.nep = {}
        for e in ("pe", "dve", "act", "pool"):
            self.csem[e] = nc.alloc_semaphore(f"c_{e}_0")
            self.ccnt[e] = 0
            self.nep[e] = 0
        self.dsems = [nc.alloc_semaphore(f"d_{i}") for i in range(n_dma_sems)]
        self.dcnt = [0] * n_dma_sems
        self.dnext = {"sw": 0, "hw": n_dma_sems // 2}
        self.drange = {"sw": (0, n_dma_sems // 2), "hw": (n_dma_sems // 2, n_dma_sems)}
        self.known = {e: {} for e in self.E}
        self.recs = {}
        self.n_inst = 0
        self.n_wait = 0
        self.out_tokens = []

    def _need(self, eng, tok, waits):
        sem, val, name = tok
        if self.known[eng].get(name, 0) >= val:
            return
        cur = waits.get(name)
        if cur is None or cur[1] < val:
            waits[name] = (sem, val)

    @staticmethod
    def _ov(a, b):
        return a[1] < b[2] and b[1] < a[2] and a[3] < b[4] and b[3] < a[4]

    @staticmethod
    def _covers(a, b):
        return a[1] <= b[1] and a[2] >= b[2] and a[3] <= b[3] and a[4] >= b[4]

    def _deps(self, eng, reads, writes, waits):
        for r in reads:
            psum = r[0].startswith("ps")
            for rec in self.recs.get(r[0], ()):
                if rec[1] and self._ov(rec[0], r):
                    if rec[3] == eng and (eng == "pe" or (SAME_ENGINE_INORDER and eng in ("act", "dve"))):
                        continue
                    self._need(eng, rec[2], waits)
                elif psum and (not rec[1]) and rec[3] != eng:
                    self._need(eng, rec[2], waits)
        for w in writes:
            for rec in self.recs.get(w[0], ()):
                if self._ov(rec[0], w):
                    if rec[3] == eng and (eng == "pe" or (SAME_ENGINE_INORDER and eng in ("act", "dve"))):
                        continue
                    self._need(eng, rec[2], waits)

    def _record(self, eng, tok, reads, writes):
        for w in writes:
            lst = self.recs.setdefault(w[0], [])
            lst[:] = [rec for rec in lst if not self._covers(w, rec[0])]
            lst.append([w, True, tok, eng])
        for r in reads:
            lst = self.recs.setdefault(r[0], [])
            done = False
            if eng != "dma":
                for rec in lst:
                    if (not rec[1]) and rec[3] == eng and rec[0] == r:
                        rec[2] = tok
                        done = True
                        break
            if not done:
                lst.append([r, False, tok, eng])

    def _emit_waits(self, eng, waits):
        e = self.E[eng]
        for name, (sem, val) in waits.items():
            e.wait_ge(sem, val)
            self.known[eng][name] = val
            self.n_wait += 1

    def op(self, eng, fn, reads=(), writes=()):
        rr = [_rect(a) for a in reads]
        ww = [_rect(a) for a in writes]
        waits = {}
        self._deps(eng, rr, ww, waits)
        if self.ccnt[eng] >= EPOCH:
            self.nep[eng] += 1
            self.csem[eng] = self.nc.alloc_semaphore(f"c_{eng}_{self.nep[eng]}")
            self.ccnt[eng] = 0
        self._emit_waits(eng, waits)
        ins = fn(self.E[eng])
        self.ccnt[eng] += 1
        sem = self.csem[eng]
        ins.then_inc(sem, 1)
        tok = (sem, self.ccnt[eng], f"c_{eng}_{self.nep[eng]}")
        self._record(eng, tok, rr, ww)
        self.n_inst += 1
        return tok

    def dma(self, q, out, in_, reads=(), writes=(), is_output=False, after=(), **kw):
        rr = [_rect(a) for a in reads]
        ww = [_rect(a) for a in writes]
        waits = {}
        self._deps(q, rr, ww, waits)
        for tok in after:
            self._need(q, tok, waits)
        kind = "sw" if q == "pool" else "hw"
        i = self.dnext[kind]
        lo, hi = self.drange[kind]
        self.dnext[kind] = lo + (i + 1 - lo) % (hi - lo)
        sem = self.dsems[i]
        name = f"d_{i}"
        if self.dcnt[i] > 0:
            self._need(q, (sem, self.dcnt[i] * 16, name), waits)
        self._emit_waits(q, waits)
        ins = self.E[q].dma_start(out=out, in_=in_, **kw)
        self.dcnt[i] += 1
        ins.then_inc(sem, 16)
        tok = (sem, self.dcnt[i] * 16, name)
        self._record("dma", tok, rr, ww)
        self.n_inst += 1
        if is_output:
            self.out_tokens.append(tok)
        return tok

    def finish(self):
        waits = {}
        for tok in self.out_tokens:
            self._need("sp", tok, waits)
        for i, sem in enumerate(self.dsems):
            if self.dcnt[i]:
                self._need("sp", (sem, self.dcnt[i] * 16, f"d_{i}"), waits)
        self._emit_waits("sp", waits)


class Arena:
    def __init__(self, nc, nbytes):
        self.t = nc.alloc_sbuf_tensor("arena", [128, nbytes // 2], BF16)
        self.cap = nbytes
        self.top = 0

    def alloc(self, shape, dtype, parts=128, at=None):
        if isinstance(shape, int):
            shape = (shape,)
        n = 1
        for s in shape:
            n *= s
        nb = n * mybir.dt.size(dtype)
        if at is not None:
            off = at
            assert off % 64 == 0 and off + nb <= self.cap
        else:
            off = self.top
            self.top += (nb + 63) // 64 * 64
            assert self.top <= self.cap, f"arena overflow {self.top} > {self.cap}"
        self.last_off = off
        v = self.t[0:parts, off // 2:(off + nb) // 2]
        if dtype != BF16:
            v = v.bitcast(dtype)
        if len(shape) == 2:
            v = v.rearrange("p (a b) -> p a b", a=shape[0])
        elif len(shape) == 3:
            v = v.rearrange("p (a b c) -> p a b c", a=shape[0], b=shape[1])
        return v

    def mark(self):
        return self.top

    def release(self, m):
        self.top = m


OWN_TILES = [(i * 128, 128) for i in range(8)] + [(1024, 64)]
PRE_TILES = [(i * 128, 128) for i in range(8)]


def tok_groups(n, g=512):
    out = []
    t = 0
    while t < n:
        out.append((t, min(g, n - t)))
        t += g
    return out


class Prog:
    def __init__(self, stop_after=None, dbg=()):
        self.stop_after = stop_after
        self.dbg_names = dbg
        nc = bass.Bass("TRN2", target_bir_lowering=False)
        self.nc = nc
        self.S = Sched(nc)
        self.I = {}
        self.O = {}
        self.dbg = {}

        def inp(name, shape):
            self.I[name] = nc.dram_tensor(name, list(shape), F32, kind="ExternalInput").ap()

        def outp(name, shape):
            self.O[name] = nc.dram_tensor(name, list(shape), F32, kind="ExternalOutput").ap()

        inp("x_own", (NOWN, D))
        inp("x_pre", (NPRE, D))
        inp("conv_s", (2, 3, 3072))
        inp("S0_s", (2, 8, 128, 128))
        inp("ck", (2, 2048, 1024))
        inp("cv", (2, 2048, 1024))
        inp("w_in", (D, PROJ_W))
        inp("conv_w", (4, 3072))
        inp("a_log", (8,))
        inp("dt_bias", (8,))
        inp("gdn_norm_w", (128,))
        inp("w_out", (D, D))
        inp("ln1_g", (D,))
        inp("ln1_b", (D,))
        inp("w_up", (D, DFF))
        inp("w_down", (DFF, D))
        inp("ln2_g", (D,))
        inp("ln2_b", (D,))
        inp("pre_bias", (128, 1))
        outp("y", (NOWN, D))
        outp("kb", (NOWN, 1024))
        outp("vb", (NOWN, 1024))
        outp("conv_p_o", (3, 3072))
        outp("conv_s_o", (2, 3, 3072))
        outp("S_p_o", (8, 128, 128))
        outp("S_s_o", (2, 8, 128, 128))
        self.x1s = nc.dram_tensor("x1_scratch", [NOWN, D], F32, kind="Internal").ap()
        self.y2s = nc.dram_tensor("y2_scratch", [NOWN, D], F32, kind="Internal").ap()

        self.ar = Arena(nc, 212480)
        self.ps = [nc.alloc_psum_tensor(f"ps{i}", [128, 512], F32) for i in range(8)]
        self.build()

    def psf(self, i):
        return self.ps[i][:]

    def psb(self, i):
        return self.ps[i][:].bitcast(BF16)

    def tap(self, name, ap, shape):
        if name not in self.dbg_names:
            return
        t = self.nc.dram_tensor("dbg_" + name, list(shape), ap.dtype, kind="ExternalOutput").ap()
        self.dbg[name] = t
        self.S.dma("sp", t, ap, reads=[ap], is_output=True)

    def mm(self, out, lhsT, rhs, start=True, stop=True, skip=False):
        if skip:
            self.S.op("pe", lambda e: e.matmul(out, lhsT, rhs, start=start, stop=stop, skip_group_check=True),
                      reads=[lhsT, rhs], writes=[out])
        else:
            self.S.op("pe", lambda e: e.matmul(out, lhsT, rhs, start=start, stop=stop),
                      reads=[lhsT, rhs], writes=[out])

    def tr(self, out, in_, ident):
        self.S.op("pe", lambda e: e.transpose(out, in_, ident), reads=[in_, ident], writes=[out])

    def copy(self, eng, out, in_):
        if eng == "act":
            self.S.op("act", lambda e: e.copy(out, in_), reads=[in_], writes=[out])
        else:
            self.S.op(eng, lambda e: e.tensor_copy(out, in_), reads=[in_], writes=[out])

    def act(self, out, in_, func, bias=0.0, scale=1.0, accum_out=None, extra_reads=()):
        rd = [in_] + list(extra_reads)
        wr = [out] + ([accum_out] if accum_out is not None else [])
        if accum_out is not None:
            self.S.op("act", lambda e: e.activation(out=out, in_=in_, func=func, bias=bias, scale=scale,
                                                    accum_out=accum_out), reads=rd, writes=wr)
        else:
            self.S.op("act", lambda e: e.activation(out=out, in_=in_, func=func, bias=bias, scale=scale),
                      reads=rd, writes=wr)

    def tt(self, eng, out, in0, in1, op):
        self.S.op(eng, lambda e: e.tensor_tensor(out=out, in0=in0, in1=in1, op=op), reads=[in0, in1], writes=[out])

    def ts(self, eng, out, in0, s1, op0, s2=None, op1=None, extra_reads=()):
        rd = [in0] + list(extra_reads)
        if op1 is None:
            self.S.op(eng, lambda e: e.tensor_scalar(out=out, in0=in0, scalar1=s1, scalar2=None, op0=op0),
                      reads=rd, writes=[out])
        else:
            self.S.op(eng, lambda e: e.tensor_scalar(out=out, in0=in0, scalar1=s1, scalar2=s2, op0=op0, op1=op1),
                      reads=rd, writes=[out])

    def stt(self, eng, out, in0, scalar, in1, op0, op1, extra_reads=()):
        rd = [in0, in1] + list(extra_reads)
        self.S.op(eng, lambda e: e.scalar_tensor_tensor(out=out, in0=in0, scalar=scalar, in1=in1, op0=op0, op1=op1),
                  reads=rd, writes=[out])

    def memset(self, eng, ap, val):
        self.S.op(eng, lambda e: e.memset(ap, val), writes=[ap])

    def asel(self, out, in_, pattern, cmp, fill, base, cm):
        self.S.op("pool", lambda e: e.affine_select(out=out, in_=in_, pattern=pattern, compare_op=cmp, fill=fill,
                                                    base=base, channel_multiplier=cm), reads=[in_], writes=[out])

    def build(self):
        self.consts()
        self.phase_x()
        if self.stop_after == "x":
            return self.S.finish()
        self.phase_attn()
        if self.stop_after == "attn":
            return self.S.finish()
        self.phase_wout()
        if self.stop_after == "wout":
            return self.S.finish()
        self.phase_ffn()
        self.S.finish()

    def consts(self):
        ar = self.ar
        self.ident_f = ar.alloc((128,), F32)
        self.ident_b = ar.alloc((128,), BF16)
        self.zeros_f = ar.alloc((128,), F32)
        self.memset("pool", self.zeros_f, 0.0)
        self.memset("pool", self.ident_f, 0.0)
        self.asel(self.ident_f, self.ident_f, [[-1, 128]], ALU.not_equal, 1.0, 0, 1)
        self.copy("pool", self.ident_b, self.ident_f)
        self.pre_bias = self.nc.alloc_sbuf_tensor("pre_bias_t", [128, 1], F32)[:]
        import os
        if os.environ.get("PB_MEMSET"):
            self.memset("pool", self.pre_bias, 1.0)
        else:
            self.S.dma("sp", self.pre_bias, self.I["pre_bias"], writes=[self.pre_bias])

    def phase_x(self):
        ar, S = self.ar, self.S
        self.xT_own = ar.alloc((KC, NOWN), BF16)
        self.xT_own_off = ar.last_off
        self.xT_pre = ar.alloc((KC, NPRE), BF16)
        self.xT_pre_off = ar.last_off
        m = ar.mark()
        xb = [ar.alloc((D,), BF16) for _ in range(2)]
        k = 0
        for src, tiles, dst in ((self.I["x_pre"], PRE_TILES, self.xT_pre), (self.I["x_own"], OWN_TILES, self.xT_own)):
            for (t0, n) in tiles:
                b = xb[k % 2]
                S.dma("pool", b[0:n, :], src[t0:t0 + n, :], writes=[b[0:n, :]])
                for half in range(2):
                    pb = self.psb(half)
                    for c in range(8):
                        cc = half * 8 + c
                        self.tr(pb[:, c * 128:c * 128 + n], b[0:n, cc * 128:(cc + 1) * 128], self.ident_b[0:n, 0:n])
                    src_ps = pb[:, 0:1024].rearrange("p (c t) -> p c t", c=8)[:, :, 0:n]
                    self.copy("dve" if half == 0 else "act", dst[:, half * 8:(half + 1) * 8, t0:t0 + n], src_ps)
                k += 1
        ar.release(m)
        self.tap("xT_own", self.xT_own, (128, KC, NOWN))

    def phase_attn(self):
        ar = self.ar
        self.mixedT = ar.alloc((KC, NOWN), BF16)
        self.memset("pool", self.mixedT, 0.0)
        m = ar.mark()
        import os
        if not os.environ.get("NO_GDN"):
            self.phase_gdn()
        ar.release(m)
        if not os.environ.get("NO_SB"):
            self.phase_sb()
        ar.release(m)

    def phase_gdn(self):
        ar, S, I = self.ar, self.S, self.I
        NT = 17
        tiles = [("pre", t0, n, t0) for (t0, n) in PRE_TILES] + [("own", t0, n, NPRE + t0) for (t0, n) in OWN_TILES]
        zb = ar.alloc((128,), BF16)
        ones_b = ar.alloc((128,), BF16)
        ones_f = ar.alloc((128,), F32)
        nones_f = ar.alloc((128,), F32)
        self.memset("pool", zb, 0.0)
        self.memset("pool", ones_b, 1.0)
        self.memset("pool", ones_f, 1.0)
        self.memset("pool", nones_f, -1.0)
        offd = ar.alloc((128,), BF16)
        self.memset("pool", offd, 1.0)
        self.asel(offd, offd, [[-1, 128]], ALU.not_equal, 0.0, 0, 1)
        ident4 = ar.alloc((4, 128), BF16)
        for j in range(4):
            self.copy("pool", ident4[:, j, :], self.ident_b)

        def block_mask(kind, c, rows):
            if kind in ("Ms", "Mi"):
                m = ar.alloc((128,), BF16)
                self.memset("pool", m, NEG)
            else:
                m = ar.alloc((128,), F32)
                self.memset("pool", m, 0.0)
            for r0 in range(0, rows, c):
                blk = slice(r0, r0 + c)
                if kind == "Ms":
                    self.asel(m[blk, blk], zb[blk, blk], [[-1, c]], ALU.is_ge, NEG, -1, 1)
                elif kind == "Mi":
                    self.asel(m[blk, blk], zb[blk, blk], [[-1, c]], ALU.is_ge, NEG, 0, 1)
                elif kind == "tri":
                    self.asel(m[blk, blk], ones_f[blk, blk], [[1, c]], ALU.is_ge, 0.0, 0, -1)
                elif kind == "last":
                    self.asel(m[blk, blk], ones_f[blk, blk], [[0, c]], ALU.is_equal, 0.0, -(c - 1), 1)
            return m
        maskMs = {64: block_mask("Ms", 64, 128), 32: block_mask("Ms", 32, 64)}
        maskMi = {64: block_mask("Mi", 64, 128), 32: block_mask("Mi", 32, 64)}
        trich = {64: block_mask("tri", 64, 128), 32: block_mask("tri", 32, 64)}
        sellast = {64: block_mask("last", 64, 128), 32: block_mask("last", 32, 64)}
        lastsel = {}
        for (c, rows) in ((64, 128), (32, 64)):
            for nch in range(2):
                m = ar.alloc((128,), F32)
                self.memset("pool", m, 0.0)
                self.asel(m[0:rows, :], ones_f[0:rows, :], [[0, 128]], ALU.is_equal, 0.0, -((nch + 1) * c - 1), 1)
                lastsel[(c, nch)] = m
        nsel = ar.alloc((8, 128), F32, parts=8)
        self.memset("pool", nsel, 0.0)
        for h in range(8):
            self.asel(nsel[:, h, :], nones_f[0:8, :], [[0, 128]], ALU.is_equal, 0.0, -h, 1)
        normw = ar.alloc((1,), F32)
        S.dma("sp", normw, I["gdn_norm_w"].rearrange("(p o) -> p o", o=1), writes=[normw])
        dtb = ar.alloc((8,), F32)
        S.dma("sp", dtb, I["dt_bias"].partition_broadcast(128), writes=[dtb])
        negA = ar.alloc((8,), F32)
        S.dma("sp", negA, I["a_log"].partition_broadcast(128), writes=[negA])
        self.act(negA, negA, AF.Exp)
        self.ts("dve", negA, negA, -1.0, ALU.mult)
        eps6 = self.eps_tile(1e-6)
        one_c = self.eps_tile(1.0)
        lnqs = self.eps_tile(float(np.log(128 ** -0.5)))
        cw = ar.alloc((24, 4), F32)
        hist = ar.alloc((24, 6), F32)
        mtmp = ar.mark()
        cwt = ar.alloc((3072,), F32, parts=4)
        S.dma("sp", cwt, I["conv_w"], writes=[cwt])
        hst = ar.alloc((3072,), F32, parts=6)
        S.dma("sp", hst, I["conv_s"].rearrange("s r c -> (s r) c"), writes=[hst])
        pc = self.psf(0)
        for ct in range(24):
            self.mm(pc[:, ct * 4:ct * 4 + 4], cwt[:, ct * 128:(ct + 1) * 128], self.ident_f[0:4, 0:4])
        self.copy("dve", cw, pc[:, 0:96].rearrange("p (a b) -> p a b", b=4))
        pc = self.psf(1)
        for ct in range(24):
            self.mm(pc[:, ct * 6:ct * 6 + 6], hst[:, ct * 128:(ct + 1) * 128], self.ident_f[0:6, 0:6])
        self.copy("dve", hist, pc[:, 0:144].rearrange("p (a b) -> p a b", b=6))
        ar.release(mtmp)
        w16 = ar.alloc((KC, 16), BF16)
        S.dma("pool", w16, I["w_in"][:, OFF_B:OFF_B + 16].rearrange("(c p) n -> p c n", p=128), writes=[w16])
        P16 = ar.alloc((NT, 16), F32)
        ps = self.psf(0)
        for ti, (src, t0, n, c0) in enumerate(tiles):
            xT = self.xT_pre if src == "pre" else self.xT_own
            for c in range(KC):
                self.mm(ps[0:n, ti * 16:(ti + 1) * 16], xT[:, c, t0:t0 + n], w16[:, c, :], start=(c == 0), stop=(c == KC - 1))
        self.memset("pool", P16, 0.0)
        self.copy("dve", P16[:, 0:16, :], ps[:, 0:256].rearrange("p (a b) -> p a b", b=16))
        self.copy("dve", P16[0:64, 16, :], ps[0:64, 256:272])
        TMq = {}
        for nm in ("g", "G", "lb", "beta", "sk", "eg", "st", "tmp"):
            TMq[nm] = ar.alloc((NT, 8), F32)
        bl = P16[:, :, 0:8]
        al = P16[:, :, 8:16]
        bc17 = lambda t: t.unsqueeze(1).broadcast_to([128, NT, 8])
        self.tt("dve", TMq["tmp"], al, bc17(dtb), ALU.add)
        self.act(TMq["tmp"], TMq["tmp"], AF.Exp)
        self.act(TMq["tmp"], TMq["tmp"], AF.Ln, bias=one_c, extra_reads=[one_c])
        self.tt("dve", TMq["g"], TMq["tmp"], bc17(negA), ALU.mult)
        self.act(TMq["lb"], bl, AF.Exp, scale=-1.0)
        self.act(TMq["lb"], TMq["lb"], AF.Ln, bias=one_c, extra_reads=[one_c])
        self.act(TMq["beta"], TMq["lb"], AF.Exp, scale=-1.0)
        ps = self.psf(1)
        for ti, (src, t0, n, c0) in enumerate(tiles):
            c = 64 if n == 128 else 32
            self.mm(ps[0:n, ti * 8:(ti + 1) * 8], trich[c][0:n, 0:n], TMq["g"][0:n, ti, :])
        self.memset("pool", TMq["G"], 0.0)
        self.copy("dve", TMq["G"][:, 0:16, :], ps[:, 0:128].rearrange("p (a b) -> p a b", b=8))
        self.copy("dve", TMq["G"][0:64, 16, :], ps[0:64, 128:136])
        self.act(TMq["eg"], TMq["G"], AF.Exp)
        self.tt("dve", TMq["tmp"], TMq["G"], TMq["lb"], ALU.subtract)
        self.act(TMq["sk"], TMq["tmp"], AF.Exp)
        ps = self.psf(0)
        for ti, (src, t0, n, c0) in enumerate(tiles):
            c = 64 if n == 128 else 32
            self.mm(ps[0:n, ti * 8:(ti + 1) * 8], sellast[c][0:n, 0:n], TMq["G"][0:n, ti, :])
        self.memset("pool", TMq["tmp"], 0.0)
        self.tt("dve", TMq["tmp"][:, 0:16, :], ps[:, 0:128].rearrange("p (a b) -> p a b", b=8), TMq["G"][:, 0:16, :], ALU.subtract)
        self.tt("dve", TMq["tmp"][0:64, 16, :], ps[0:64, 128:136], TMq["G"][0:64, 16, :], ALU.subtract)
        self.act(TMq["st"], TMq["tmp"], AF.Exp)
        GT = ar.alloc((NT * 2, 8), F32)
        ps = self.psf(1)
        for ti, (src, t0, n, c0) in enumerate(tiles):
            c = 64 if n == 128 else 32
            for nch in range(2):
                j = ti * 2 + nch
                self.mm(ps[:, j * 8:(j + 1) * 8], lastsel[(c, nch)][0:n, :], TMq["eg"][0:n, ti, :])
        self.copy("dve", GT, ps[:, 0:NT * 16].rearrange("p (a b) -> p a b", b=8))
        Grow = ar.alloc((NALL,), F32, parts=8)
        for q0 in range(0, NT, 4):
            ps = self.psf(q0 // 4 % 2)
            for ti in range(q0, min(NT, q0 + 4)):
                (src, t0, n, c0) = tiles[ti]
                self.mm(ps[0:8, (ti - q0) * 128:(ti - q0) * 128 + n], TMq["G"][0:n, ti, :], self.ident_f[0:n, 0:n])
            nc_ = sum(tiles[ti][2] for ti in range(q0, min(NT, q0 + 4)))
            self.copy("dve", Grow[:, tiles[q0][3]:tiles[q0][3] + nc_], ps[0:8, 0:nc_])
        self.tap("Grow", Grow, (8, NALL))
        self.tap("TMst", TMq["st"], (128, NT, 8))
        self.tap("TMsk", TMq["sk"], (128, NT, 8))
        self.tap("GT", GT, (128, NT * 2, 8))
        wb = ar.alloc((KC, 512), BF16)
        NKV = 2121
        pcb = ar.alloc((NKV,), F32)
        cvb = ar.alloc((NKV,), F32)
        knT = ar.alloc((NALL,), BF16)
        vT = ar.alloc((NALL,), BF16)
        qnT = ar.alloc((NOWN,), BF16)
        sqb = ar.alloc((512,), BF16)
        lnb_ = ar.alloc((512,), F32)
        zgT = ar.alloc((NOWN,), BF16)
        cst = lnb_[0:3, 0:384]
        opq = []
        for i in range(2):
            opq.append({
                "kbg": ar.alloc((4, 128), BF16), "kt": ar.alloc((4, 128), BF16), "vb": ar.alloc((4, 128), BF16),
                "wtok": ar.alloc((4, 128), BF16), "attnT": ar.alloc((4, 128), BF16), "ub": ar.alloc((4, 128), BF16),
                "nW2": ar.alloc((4, 2, 128), BF16), "nAWT": ar.alloc((4, 128), BF16),
            })
        Mb = [ar.alloc((4, 128), BF16) for _ in range(2)]
        Ab = [ar.alloc((4, 128), BF16) for _ in range(2)]
        Yb = [ar.alloc((4, 128), BF16) for _ in range(2)]
        Mp = ar.alloc((4, 128), BF16) if NEU_SINGLE else None
        DMi = ar.alloc((4, 128), BF16)
        DMs = ar.alloc((4, 128), BF16)
        attn = DMs
        Sf = ar.alloc((128,), F32)
        Sbf = ar.alloc((128,), BF16)
        t1 = ar.alloc((128,), F32)
        otok2 = [ar.alloc((128,), F32) for _ in range(2)]
        gjunk = ar.alloc((128,), BF16)
        osq = t1
        ogb = ar.alloc((128,), BF16)
        ssq = ar.alloc((1,), F32)
        rs = ar.alloc((1,), F32)

        def kvcol(tok):
            if tok < 2048:
                return 3 + tok
            if tok < 2080:
                return 2054 + (tok - 2048)
            return 2089 + (tok - 2080)

        def qcol(t):
            if t < 1024:
                return 3 + t
            if t < 1056:
                return 1030 + (t - 1024)
            return 1065 + (t - 1056)
        kv_segs = [(3, 2048, 0), (2054, 32, 2048), (2089, 32, 2080)]
        q_segs = [(3, 1024, 0), (1030, 32, 1024), (1065, 32, 1056)]

        def conv_silu(ncols, ct):
            n = ncols - 3
            self.ts("dve", cvb[:, 3:ncols], pcb[:, 0:n], cw[:, ct, 0:1], ALU.mult, extra_reads=[cw[:, ct, 0:1]])
            for i in range(1, 4):
                self.stt("dve", cvb[:, 3:ncols], pcb[:, i:i + n], cw[:, ct, i:i + 1], cvb[:, 3:ncols], ALU.mult, ALU.add,
                         extra_reads=[cw[:, ct, i:i + 1]])

        def conv_out(which, h, cols3):
            ct = which * 8 + h
            pz = self.psf(1)
            for sgi, c0 in enumerate(cols3):
                self.mm(pz[0:3, sgi * 128:(sgi + 1) * 128], pcb[:, c0:c0 + 3], self.ident_f)
            self.copy("dve", cst, pz[0:3, 0:384])
            S.dma("sp", self.O["conv_p_o"][:, ct * 128:(ct + 1) * 128], cst[:, 0:128], reads=[cst[:, 0:128]], is_output=True)
            for j in range(2):
                S.dma("sp", self.O["conv_s_o"][j, :, ct * 128:(ct + 1) * 128], cst[:, (j + 1) * 128:(j + 2) * 128],
                      reads=[cst[:, (j + 1) * 128:(j + 2) * 128]], is_output=True)

        def normalize(segs, dst, extra_bias):
            for (c0, ln, k0) in segs:
                for o in range(0, ln, 512):
                    n = min(512, ln - o)
                    src = cvb[:, c0 + o:c0 + o + n]
                    self.act(sqb[:, 0:n], src, AF.Square)
                    pz = self.psf(1)
                    self.mm(pz[:, 0:n], ones_b, sqb[:, 0:n])
                    self.act(lnb_[:, 0:n], pz[:, 0:n], AF.Ln, bias=eps6, extra_reads=[eps6])
                    if extra_bias is None:
                        self.act(lnb_[:, 0:n], lnb_[:, 0:n], AF.Exp, scale=-0.5)
                    else:
                        self.act(lnb_[:, 0:n], lnb_[:, 0:n], AF.Exp, scale=-0.5, bias=extra_bias, extra_reads=[extra_bias])
                    self.tt("dve", dst[:, k0 + o:k0 + o + n], src, lnb_[:, 0:n], ALU.mult)

        quads = [list(range(0, 4)), list(range(4, 8)), list(range(8, 12)), list(range(12, 16)), [16]]

        for h in range(8):
            def wload_head(hh):
                for j, coff in enumerate((hh * 128, 1024 + hh * 128, 2048 + hh * 128, OFF_Z + hh * 128)):
                    src = I["w_in"][:, coff:coff + 128].rearrange("(c p) n -> p c n", p=128)
                    S.dma("pool", wb[:, :, j * 128:(j + 1) * 128], src, writes=[wb[:, :, j * 128:(j + 1) * 128]])
            if h == 0:
                wload_head(0)

            def proj_fm(wj, xT, t0, n, dst):
                self._pfm = getattr(self, "_pfm", 0) + 1
                po = self.psf(self._pfm % 2)[:, 0:n]
                for c in range(KC):
                    self.mm(po, wb[:, c, wj * 128:(wj + 1) * 128], xT[:, c, t0:t0 + n], start=(c == 0), stop=(c == KC - 1))
                self.copy("act", dst, po)
            def proj_q():
                proj_fm(0, self.xT_pre, 1021, 3, pcb[:, 0:3])
                for (t0, n) in ((0, 512), (512, 512)):
                    proj_fm(0, self.xT_own, t0, n, pcb[:, 3 + t0:3 + t0 + n])
                proj_fm(0, self.xT_own, 1024, 32, pcb[:, 1030:1062])
                proj_fm(0, self.xT_own, 1056, 32, pcb[:, 1065:1097])
                for j in range(2):
                    self.copy("pool", pcb[:, 1027 + 35 * j:1030 + 35 * j], hist[:, h, 3 * j:3 * j + 3])

            def proj_kv(which):
                self.memset("pool", pcb[:, 0:3], 0.0)
                for (t0, n) in ((0, 512), (512, 512)):
                    proj_fm(which, self.xT_pre, t0, n, pcb[:, 3 + t0:3 + t0 + n])
                for (t0, n) in ((0, 512), (512, 512)):
                    proj_fm(which, self.xT_own, t0, n, pcb[:, 3 + 1024 + t0:3 + 1024 + t0 + n])
                proj_fm(which, self.xT_own, 1024, 32, pcb[:, 2054:2086])
                proj_fm(which, self.xT_own, 1056, 32, pcb[:, 2089:2121])
                for j in range(2):
                    self.copy("pool", pcb[:, 2051 + 35 * j:2054 + 35 * j], hist[:, which * 8 + h, 3 * j:3 * j + 3])

            def conv_stage(which):
                if which == 0:
                    conv_out(0, h, (1024, 1059, 1094))
                    conv_silu(1097, h)
                    self.act(cvb[:, 3:1097], cvb[:, 3:1097], AF.Silu)
                else:
                    conv_out(which, h, (2048, 2083, 2118))
                    conv_silu(NKV, which * 8 + h)
                    self.act(cvb[:, 3:NKV], cvb[:, 3:NKV], AF.Silu)
            proj_q()
            conv_stage(0)
            proj_kv(1)
            normalize(q_segs, qnT, lnqs)
            conv_stage(1)
            proj_kv(2)
            normalize(kv_segs, knT, None)
            conv_stage(2)
            for (c0, ln, k0) in kv_segs:
                self.copy("pool", vT[:, k0:k0 + ln], cvb[:, c0:c0 + ln])
            for gi, (t0, n) in enumerate(tok_groups(NOWN)):
                pz = self.psf(gi % 2)[:, 0:n]
                for c in range(KC):
                    self.mm(pz, wb[:, c, 384:512], self.xT_own[:, c, t0:t0 + n], start=(c == 0), stop=(c == KC - 1))
                self.act(lnb_[:, 0:n], pz, AF.Silu)
                self.ts("dve", zgT[:, t0:t0 + n], lnb_[:, 0:n], normw, ALU.mult, extra_reads=[normw])
            if h == 0:
                self.tap("knT0", knT, (128, NALL))
                self.tap("qnT0", qnT, (128, NOWN))
                self.tap("vT0", vT, (128, NALL))
            if h + 1 < 8:
                wload_head(h + 1)
            self.memset("pool", Sf, 0.0)
            self.memset("pool", Sbf, 0.0)
            def gen_L(qi, h=h):
                quad = quads[qi]
                ops = opq[qi % 2]
                own_quad = tiles[quad[0]][0] == "own"
                nq = len(quad)
                nn = tiles[quad[0]][2]
                v3 = lambda p: p[0:nn, 0:nq * 128].rearrange("p (a b) -> p a b", b=128)[:, :, 0:nn]
                pb = self.psb(2)
                for j, ti in enumerate(quad):
                    (src, t0, n, c0) = tiles[ti]
                    self.tr(pb[0:n, j * 256:j * 256 + 128], knT[:, c0:c0 + n], self.ident_b)
                    self.tr(pb[0:n, j * 256 + 128:j * 256 + 256], vT[:, c0:c0 + n], self.ident_b)
                yield
                for j, ti in enumerate(quad):
                    (src, t0, n, c0) = tiles[ti]
                    kps = pb[0:n, j * 256:j * 256 + 128]
                    vps = pb[0:n, j * 256 + 128:j * 256 + 256]
                    sc = lambda nm: TMq[nm][0:n, ti, h:h + 1]
                    self.ts("dve", ops["kbg"][0:n, j, :], kps, sc("sk"), ALU.mult, extra_reads=[sc("sk")])
                    self.ts("dve", ops["kt"][0:n, j, :], kps, sc("st"), ALU.mult, extra_reads=[sc("st")])
                    self.ts("dve", ops["vb"][0:n, j, :], vps, sc("beta"), ALU.mult, extra_reads=[sc("beta")])
                pk = self.psf(3)
                pe_ = self.psf(4)
                for j, ti in enumerate(quad):
                    (src, t0, n, c0) = tiles[ti]
                    c = 64 if n == 128 else 32
                    self.mm(pk[0:n, j * 128:j * 128 + n], knT[:, c0:c0 + n], knT[:, c0:c0 + n])
                    self.mm(pe_[0:n, j * 128:j * 128 + n], nsel[:, h, 0:n], Grow[:, c0:c0 + n], start=True, stop=False)
                    msk = maskMi[c] if own_quad else maskMs[c]
                    self.mm(pe_[0:n, j * 128:j * 128 + n], self.ident_b[0:n, 0:n], msk[0:n, 0:n], start=False, stop=True)
                yield
                for j, ti in enumerate(quad):
                    (src, t0, n, c0) = tiles[ti]
                    gb_ = TMq["G"][0:n, ti, h:h + 1]
                    dst = DMi if own_quad else DMs
                    self.act(dst[0:n, j, 0:n], pe_[0:n, j * 128:j * 128 + n], AF.Exp, bias=gb_, extra_reads=[gb_])
                    if own_quad:
                        self.tt("pool", DMs[0:n, j, 0:n], DMi[0:n, j, 0:n], offd[0:n, 0:n], ALU.mult)
                M0, A0, Y0 = Mb[0], Ab[0], Yb[0]
                for j, ti in enumerate(quad):
                    (src, t0, n, c0) = tiles[ti]
                    bt_ = TMq["beta"][0:n, ti, h:h + 1]
                    self.stt("dve", M0[0:n, j, 0:n], pk[0:n, j * 128:j * 128 + n], bt_, DMs[0:n, j, 0:n], ALU.mult, ALU.mult,
                             extra_reads=[bt_])
                yield
                pt = self.psb(2)
                for j, ti in enumerate(quad):
                    n = tiles[ti][2]
                    self.tr(pt[0:n, j * 128:j * 128 + n], M0[0:n, j, 0:n], self.ident_b[0:n, 0:n])
                ptv = pt[0:nn, 0:nq * 128].rearrange("p (a b) -> p a b", b=128)[:, :, 0:nn]
                self.copy("act", A0[0:nn, 0:nq, 0:nn], ptv)
                self.stt("dve", Y0[0:nn, 0:nq, 0:nn], ptv, -1.0, ident4[0:nn, 0:nq, 0:nn], ALU.mult, ALU.add)
                yield
                cur = 0
                ycur = 0
                pend = None
                p5, p6, p7 = self.psf(5), self.psf(6), self.psf(7)

                def y_mm(Mlhs, Ysrc):
                    for j, ti in enumerate(quad):
                        n = tiles[ti][2]
                        self.mm(p7[0:n, j * 128:j * 128 + n], self.ident_b[0:n, 0:n], Ysrc[0:n, j, 0:n], start=True, stop=False)
                        self.mm(p7[0:n, j * 128:j * 128 + n], Mlhs[0:n, j, 0:n], Ysrc[0:n, j, 0:n], start=False, stop=True)
                for it in range(5):
                    Mc, Ac = Mb[cur], Ab[cur]
                    Mn, An = Mb[1 - cur], Ab[1 - cur]
                    for j, ti in enumerate(quad):
                        n = tiles[ti][2]
                        self.mm(p5[0:n, j * 128:j * 128 + n], Ac[0:n, j, 0:n], Mc[0:n, j, 0:n])
                        if it < 4:
                            self.mm(p6[0:n, j * 128:j * 128 + n], Mc[0:n, j, 0:n], Ac[0:n, j, 0:n])
                    if pend is not None:
                        y_mm(Mc, Yb[ycur])
                    yield
                    self.copy("act", Mn[0:nn, 0:nq, 0:nn], v3(p5))
                    if it < 4:
                        self.copy("dve", An[0:nn, 0:nq, 0:nn], v3(p6))
                    if pend is not None:
                        self.copy("dve", Yb[1 - ycur][0:nn, 0:nq, 0:nn], v3(p7))
                        ycur = 1 - ycur
                    pend = True
                    cur = 1 - cur
                    yield
                y_mm(Mb[cur], Yb[ycur])
                yield
                self.copy("dve", Yb[1 - ycur][0:nn, 0:nq, 0:nn], v3(p7))
                cur = 1 - ycur
                Yf = Yb[cur]
                pu, pw = self.psf(3), self.psf(4)
                for j, ti in enumerate(quad):
                    n = tiles[ti][2]
                    self.mm(pu[0:n, j * 128:(j + 1) * 128], Yf[0:n, j, 0:n], ops["vb"][0:n, j, :])
                    self.mm(pw[0:n, j * 128:(j + 1) * 128], Yf[0:n, j, 0:n], ops["kbg"][0:n, j, :])
                yield
                self.copy("act", ops["ub"][0:nn, 0:nq, :], pu[0:nn, 0:nq * 128].rearrange("p (a b) -> p a b", b=128))
                self.copy("dve", ops["wtok"][0:nn, 0:nq, :], pw[0:nn, 0:nq * 128].rearrange("p (a b) -> p a b", b=128))
                p56 = (self.psf(5), self.psf(6))
                cc = 64 if nn == 128 else 32
                for j, ti in enumerate(quad):
                    for nch in range(2):
                        r = slice(nch * cc, (nch + 1) * cc)
                        self.mm(p56[nch][:, j * 128:(j + 1) * 128], ops["wtok"][r, j, :], ops["kt"][r, j, :])
                yield
                self.ts("dve", ops["nW2"][:, 0:nq, 0, :], p56[0][:, 0:nq * 128].rearrange("p (a b) -> p a b", b=128), -1.0, ALU.mult)
                self.act(ops["nW2"][:, 0:nq, 1, :], p56[1][:, 0:nq * 128].rearrange("p (a b) -> p a b", b=128), AF.Copy, scale=-1.0)
                if own_quad:
                    pq = self.psf(3)
                    for j, ti in enumerate(quad):
                        (src, t0, n, c0) = tiles[ti]
                        self.mm(pq[0:n, j * 128:j * 128 + n], qnT[:, t0:t0 + n], knT[:, c0:c0 + n])
                    yield
                    self.tt("dve", attn[0:nn, 0:nq, 0:nn], v3(pq), DMi[0:nn, 0:nq, 0:nn], ALU.mult)
                    pt = self.psb(2)
                    for j, ti in enumerate(quad):
                        n = tiles[ti][2]
                        self.tr(pt[0:n, j * 128:j * 128 + n], attn[0:n, j, 0:n], self.ident_b[0:n, 0:n])
                    yield
                    self.copy("act", ops["attnT"][0:nn, 0:nq, 0:nn],
                              pt[0:nn, 0:nq * 128].rearrange("p (a b) -> p a b", b=128)[:, :, 0:nn])
                    p7 = self.psf(7)
                    for j, ti in enumerate(quad):
                        n = tiles[ti][2]
                        self.mm(p7[:, j * 128:j * 128 + n], ops["wtok"][0:n, j, :], ops["attnT"][0:n, j, 0:n])
                    yield
                    self.ts("dve", ops["nAWT"][:, 0:nq, 0:nn],
                            p7[:, 0:nq * 128].rearrange("p (a b) -> p a b", b=128)[:, :, 0:nn], -1.0, ALU.mult)
                if h == 0 and qi == 0:
                    self.tap("Yf0", Yf, (128, 4, 128))
                    self.tap("M00", Mb[0], (128, 4, 128))

            gstate = {"g": None}

            def gate_gen(ot, t0, n, h=h):
                self.act(gjunk[0:n, :], ot[0:n, :], AF.Square, accum_out=ssq[0:n, :])
                self.act(rs[0:n, :], ssq[0:n, :], AF.Ln, scale=1.0 / 128.0, bias=eps6[0:n, :], extra_reads=[eps6[0:n, :]])
                self.act(rs[0:n, :], rs[0:n, :], AF.Exp, scale=-0.5)
                yield
                self.ts("dve", ogb[0:n, :], ot[0:n, :], rs[0:n, :], ALU.mult, extra_reads=[rs[0:n, :]])
                pg = self.psb(0)
                self.tr(pg[:, 0:n], ogb[0:n, :], self.ident_b[0:n, 0:n])
                yield
                self.tt("dve", self.mixedT[:, h, t0:t0 + n], pg[:, 0:n], zgT[:, t0:t0 + n], ALU.mult)

            def gate_step(drain=False):
                g_ = gstate["g"]
                while g_ is not None:
                    try:
                        next(g_)
                    except StopIteration:
                        gstate["g"] = None
                        return
                    if not drain:
                        return

            def gen_S(qi, h=h):
                quad = quads[qi]
                ops = opq[qi % 2]
                for j, ti in enumerate(quad):
                    (src, t0, n, c0) = tiles[ti]
                    c = 64 if n == 128 else 32
                    pS = self.psf(1)
                    ot = otok2[ti % 2]
                    for nch in range(n // c):
                        r = slice(nch * c, (nch + 1) * c)
                        if n == 64:
                            S.dma("sp", Sf, I["S0_s"][nch, h], writes=[Sf])
                            self.copy("act", Sbf, Sf)
                        self.mm(pS[:, 128:256], ops["nW2"][:, j, nch, :], Sbf, start=True, stop=False)
                        self.mm(pS[:, 128:256], ops["kt"][r, j, :], ops["ub"][r, j, :], start=False, stop=True)
                        if src == "own":
                            self.mm(pS[r, 256:384], qnT[:, t0 + nch * c:t0 + (nch + 1) * c], Sbf)
                            self.mm(pS[r, 384:512], ops["nAWT"][:, j, r], Sbf, start=True, stop=False)
                            self.mm(pS[r, 384:512], ops["attnT"][r, j, r], ops["ub"][r, j, :], start=False, stop=True)
                        gate_step()
                        yield
                        gt_ = GT[:, ti * 2 + nch, h:h + 1]
                        self.stt("dve", Sbf, Sf, gt_, pS[:, 128:256], ALU.mult, ALU.add, extra_reads=[gt_])
                        self.stt("dve", Sf, Sf, gt_, pS[:, 128:256], ALU.mult, ALU.add, extra_reads=[gt_])
                        if src == "own":
                            eg_ = TMq["eg"][r, ti, h:h + 1]
                            self.act(t1[r, :], pS[r, 256:384], AF.Copy, scale=eg_, extra_reads=[eg_])
                            self.tt("dve", ot[r, :], t1[r, :], pS[r, 384:512], ALU.add)
                        if n == 64:
                            S.dma("sp", self.O["S_s_o"][nch, h], Sf, reads=[Sf], is_output=True)
                        gate_step()
                        yield
                    if src == "own" and t0 == 896:
                        S.dma("sp", self.O["S_p_o"][h], Sf, reads=[Sf], is_output=True)
                    if src == "own":
                        gate_step(drain=True)
                        gstate["g"] = gate_gen(ot, t0, n)
                if qi == len(quads) - 1:
                    gate_step(drain=True)

            def run_gens(gens):
                gens = list(gens)
                while gens:
                    for g_ in list(gens):
                        try:
                            next(g_)
                        except StopIteration:
                            gens.remove(g_)
            run_gens([gen_L(0)])
            for qi in range(len(quads)):
                gl = [gen_S(qi)]
                if qi + 1 < len(quads):
                    gl.append(gen_L(qi + 1))
                run_gens(gl)
        self.tap("mixedT_g", self.mixedT, (128, KC, NOWN))


    def wload(self, buf, w_ap, c0, ncols, kc=KC):
        src = w_ap[:, c0:c0 + ncols].rearrange("(c p) n -> p c n", p=128)
        dst = buf[:, :, 0:ncols]
        self.S.dma("pool", dst, src, writes=[dst])

    def phase_sb(self):
        ar, S = self.ar, self.S
        wb = [ar.alloc((KC, 512), BF16) for _ in range(2)]
        stage = [ar.alloc((512,), F32) for _ in range(2)]
        KT = ar.alloc((4, NALL), BF16)
        VT = ar.alloc((17, 512), BF16)
        QT2 = [ar.alloc((NOWN,), BF16) for _ in range(2)]
        Eb = [ar.alloc((512,), F32) for _ in range(2)]
        SPb = [ar.alloc((512,), BF16) for _ in range(2)]
        Wb = [ar.alloc((512,), BF16) for _ in range(2)]
        Xb = [ar.alloc((512,), F32) for _ in range(2)] + [ar.alloc((64,), F32)]
        Eb.append(ar.alloc((64,), F32))
        SPb.append(ar.alloc((64,), BF16))
        Wb.append(ar.alloc((64,), BF16))
        KTn = ar.alloc((4, 64), BF16)
        Vn = ar.alloc((512,), BF16)
        ntri_i = ar.alloc((128,), BF16)
        ntri_c = ar.alloc((128,), BF16)
        zb = ar.alloc((512,), BF16)
        KTc = ar.alloc((2, 2048), BF16)
        Vc = ar.alloc((2, 16, 128), BF16)
        m64 = ar.alloc((64,), BF16, parts=64)
        ones64 = ar.alloc((64,), BF16, parts=64)
        one_c = self.eps_tile(1.0)
        self.memset("pool", m64, 0.0)
        self.memset("pool", ones64, 1.0)
        for s_ in range(2):
            sq = slice(32 * s_, 32 * s_ + 32)
            self.asel(m64[sq, sq], ones64[sq, sq], [[1, 32]], ALU.is_ge, 0.0, -1, -1)
        self.memset("pool", zb, 0.0)
        self.memset("pool", ntri_i, -1.0)
        self.asel(ntri_i, ntri_i, [[-1, 128]], ALU.is_ge, 0.0, 0, 1)
        self.memset("pool", ntri_c, -1.0)
        self.asel(ntri_c, ntri_c, [[1, 128]], ALU.is_gt, 0.0, 0, -1)
        all_tiles = [("pre", t0, n) for (t0, n) in PRE_TILES] + [("own", t0, n) for (t0, n) in OWN_TILES]
        import os
        sbstop = int(os.environ.get("SB_STOP", "99"))
        if sbstop <= 1:
            return
        nw = 0
        ev = 0
        for g in range(2):
            for which in ("k", "v"):
                coff = OFF_SB + (1024 if which == "k" else 2048) + g * 512
                dst = self.O["kb"] if which == "k" else self.O["vb"]
                w = wb[nw % 2]
                nw += 1
                self.wload(w, self.I["w_in"], coff, 512)

                def emit_ktr(ti, kpos, n):
                    pt = self.psf(4 + (ti % 2))
                    stf = stage[ti % 2]
                    for h in range(4):
                        self.tr(pt[:, h * 128:(h + 1) * 128], stf[:, h * 128:(h + 1) * 128], self.ident_f)
                    src_ps = pt[:, 0:512].rearrange("p (h t) -> p h t", h=4)[:, :, 0:n]
                    self.copy("act", KT[:, :, kpos:kpos + n], src_ps)
                pend_k = None
                for ti, (src, t0, n) in enumerate(all_tiles):
                    xT = self.xT_pre if src == "pre" else self.xT_own
                    kpos = t0 if src == "pre" else NPRE + t0
                    po = self.psf(6 + (ti % 2))[0:n, :]
                    for c in range(KC):
                        self.mm(po, xT[:, c, t0:t0 + n], w[:, c, :], start=(c == 0), stop=(c == KC - 1))
                    st = stage[ti % 2][0:n, :]
                    if which == "k":
                        self.copy("dve", st, po)
                        if src == "own":
                            S.dma("sp", dst[t0:t0 + n, g * 512:(g + 1) * 512], st, reads=[st], is_output=True)
                        if pend_k is not None:
                            emit_ktr(*pend_k)
                        pend_k = (ti, kpos, n)
                    else:
                        if src == "own":
                            self.copy("dve", st, po)
                            S.dma("sp", dst[t0:t0 + n, g * 512:(g + 1) * 512], st, reads=[st], is_output=True)
                            self.copy("act", VT[0:n, ti, :], st)
                        else:
                            self.copy("act", VT[0:n, ti, :], po)
                if which == "k" and pend_k is not None:
                    emit_ktr(*pend_k)
            self.copy("dve", KTn, KT[:, :, NPRE + 1024:NPRE + 1088])
            self.copy("dve", Vn[0:64, :], VT[0:64, 16, :])
            wq = wb[nw % 2]
            nw += 1
            self.wload(wq, self.I["w_in"], OFF_SB + g * 512, 512)
            def q_proj(hh, gi, wq=wq, g=g):
                (t0, n) = tok_groups(NOWN)[gi]
                po = self.psf(6)[:, 0:n]
                for c in range(KC):
                    self.mm(po, wq[:, c, hh * 128:(hh + 1) * 128], self.xT_own[:, c, t0:t0 + n],
                            start=(c == 0), stop=(c == KC - 1))
                self.act(QT2[(g * 4 + hh) % 2][:, t0:t0 + n], po, AF.Copy, scale=float(128 ** -0.5))
            for h in range(4):
                hg = g * 4 + h
                QT = QT2[hg % 2]
                if h == 0:
                    for gi in range(3):
                        q_proj(0, gi)
                def prompt_stream(sb, h=h, hg=hg, QT=QT):
                    nonlocal ev
                    blocks = []
                    for kb in range(4 * sb + 3, -1, -1):
                        cs = max(0, (kb - 4 * sb)) * 128
                        blocks.append(("own", kb, cs, kb >= 4 * sb))
                    for kb in range(7, -1, -1):
                        blocks.append(("pre", kb, 0, False))
                    A = self.psf(2 + sb)
                    OT = self.psf(4 + sb)
                    q0 = sb * 512
                    self.mm(A, zb[:, 0:128], zb[:, 0:512], start=True, stop=False, skip=True)
                    self.mm(OT, zb[:, 0:128], zb[:, 0:512], start=True, stop=False, skip=True)
                    yield
                    for (src, kb, cs, diag) in blocks:
                        kpos = kb * 128 if src == "pre" else NPRE + kb * 128
                        vt = kb if src == "pre" else 8 + kb
                        zt = self.psf(ev % 2)
                        E, SP, W, X = Eb[sb], SPb[sb], Wb[sb], Xb[sb]
                        ev += 1
                        kt = KT[:, h, kpos:kpos + 128]
                        self.mm(zt[:, cs:512], kt, QT[:, q0 + cs:q0 + 512])
                        self.act(E[:, cs:512], zt[:, cs:512], AF.Exp)
                        if src == "pre":
                            self.act(SP[:, cs:512], E[:, cs:512], AF.Ln, bias=one_c, scale=self.pre_bias,
                                     extra_reads=[one_c, self.pre_bias])
                        else:
                            self.act(SP[:, cs:512], E[:, cs:512], AF.Ln, bias=one_c, extra_reads=[one_c])
                        if diag:
                            self.asel(SP[:, cs:cs + 128], SP[:, cs:cs + 128], [[1, 128]], ALU.is_ge, 0.0, -1, -1)
                        yield
                        self.mm(A[:, cs:512], ntri_i, SP[:, cs:512], start=False, stop=False, skip=True)
                        self.act(X[:, cs:512], A[:, cs:512], AF.Exp)
                        self.tt("dve", W[:, cs:512], E[:, cs:512], X[:, cs:512], ALU.mult)
                        if diag:
                            self.asel(W[:, cs:cs + 128], W[:, cs:cs + 128], [[1, 128]], ALU.is_ge, 0.0, -1, -1)
                        yield
                        self.mm(A[:, cs:512], ntri_c, SP[:, cs:512], start=False, stop=False, skip=True)
                        self.mm(OT[:, cs:512], VT[:, vt, h * 128:(h + 1) * 128], W[:, cs:512], start=False, stop=False, skip=True)
                        yield
                    self.copy("dve", self.mixedT[:, 8 + hg, sb * 512:(sb + 1) * 512], OT)

                def sample_stream(h=h, hg=hg, QT=QT):
                    nonlocal ev
                    par = hg % 2
                    wfree = wb[nw % 2]
                    for s_ in range(2):
                        kst = wfree[:, 8 * par + 4 * s_:8 * par + 4 * s_ + 4, :].rearrange("p a (b d) -> p (a b) d", d=128)
                        S.dma("pool", kst, self.I["ck"][s_, :, hg * 128:(hg + 1) * 128].rearrange("(b p) d -> p b d", p=128),
                              writes=[kst])
                        S.dma("pool", Vc[:, s_, :, :],
                              self.I["cv"][s_, :, hg * 128:(hg + 1) * 128].rearrange("(b p) d -> p b d", p=128),
                              writes=[Vc[:, s_, :, :]])
                    yield
                    for s_ in range(2):
                        kst = wfree[:, 8 * par + 4 * s_:8 * par + 4 * s_ + 4, :].rearrange("p a (b d) -> p (a b) d", d=128)
                        for half in range(2):
                            pb = self.psb(6)
                            for c in range(8):
                                self.tr(pb[:, c * 128:(c + 1) * 128], kst[:, half * 8 + c, :], self.ident_b)
                            self.copy("dve", KTc[:, s_, half * 1024:(half + 1) * 1024], pb[:, 0:1024])
                            yield
                    A = self.psf(7)[:, 0:64]
                    OT = self.psf(7)[:, 128:192]
                    self.mm(self.psf(7)[:, 0:192], zb[:, 0:128], zb[:, 0:192], start=True, stop=False, skip=True)
                    qs = QT[:, 1024:1088]
                    for blk in [-1] + list(range(15, -1, -1)):
                        zt = self.psf(ev % 2)
                        E, SP, W, X = Eb[2], SPb[2], Wb[2], Xb[2]
                        ev += 1
                        if blk < 0:
                            kn = KTn[:, h, :]
                            self.mm(zt[0:64, 0:64], kn, qs)
                            self.act(E[0:64, 0:64], zt[0:64, 0:64], AF.Exp)
                            self.act(SP[0:64, 0:64], E[0:64, 0:64], AF.Ln, bias=one_c[0:64, :], extra_reads=[one_c[0:64, :]])
                            self.tt("pool", SP[0:64, 0:64], SP[0:64, 0:64], m64, ALU.mult)
                            yield
                            self.mm(A[0:64, :], ntri_i[0:64, 0:64], SP[0:64, 0:64], start=False, stop=False, skip=True)
                            self.act(X[0:64, 0:64], A[0:64, :], AF.Exp)
                            self.tt("dve", W[0:64, 0:64], E[0:64, 0:64], X[0:64, 0:64], ALU.mult)
                            self.tt("pool", W[0:64, 0:64], W[0:64, 0:64], m64, ALU.mult)
                            yield
                            self.mm(A, ntri_c[0:64, :], SP[0:64, 0:64], start=False, stop=False, skip=True)
                            self.mm(OT, Vn[0:64, h * 128:(h + 1) * 128], W[0:64, 0:64], start=False, stop=False, skip=True)
                            yield
                        else:
                            ks = [KTc[:, s_, blk * 128:(blk + 1) * 128] for s_ in range(2)]
                            cs2 = [slice(32 * s_, 32 * s_ + 32) for s_ in range(2)]
                            for s_ in range(2):
                                self.mm(zt[:, cs2[s_]], ks[s_], qs[:, cs2[s_]])
                            self.act(E[:, 0:64], zt[:, 0:64], AF.Exp)
                            self.act(SP[:, 0:64], E[:, 0:64], AF.Ln, bias=one_c, extra_reads=[one_c])
                            yield
                            self.mm(A, ntri_i, SP[:, 0:64], start=False, stop=False, skip=True)
                            self.act(X[:, 0:64], A, AF.Exp)
                            self.tt("dve", W[:, 0:64], E[:, 0:64], X[:, 0:64], ALU.mult)
                            yield
                            self.mm(A, ntri_c, SP[:, 0:64], start=False, stop=False, skip=True)
                            for s_ in range(2):
                                self.mm(OT[:, cs2[s_]], Vc[:, s_, blk, :], W[:, cs2[s_]], start=False, stop=False, skip=True)
                            if h < 3 and 13 <= blk <= 15:
                                q_proj(h + 1, 15 - blk)
                            yield
                    self.copy("dve", self.mixedT[:, 8 + hg, 1024:1088], OT)

                gens = [prompt_stream(0), prompt_stream(1), sample_stream()]
                while gens:
                    for g_ in list(gens):
                        try:
                            next(g_)
                        except StopIteration:
                            gens.remove(g_)
        self.tap("mixedT", self.mixedT, (128, KC, NOWN))

    def phase_wout(self):
        ar, S = self.ar, self.S
        m0 = ar.mark()
        wo = ar.alloc((4, KC, 512), BF16)
        for g in range(4):
            src = self.I["w_out"][:, g * 512:(g + 1) * 512].rearrange("(c p) n -> p c n", p=128)
            S.dma("pool", wo[:, g, :, :], src, writes=[wo[:, g, :, :]])
        po_ = self.xT_pre_off
        gb = ar.alloc((D,), F32, at=po_)
        bb = ar.alloc((D,), F32, at=po_ + 8192)
        S.dma("sp", gb, self.I["ln1_g"].partition_broadcast(128), writes=[gb])
        S.dma("sp", bb, self.I["ln1_b"].partition_broadcast(128), writes=[bb])
        self.x1T = self.xT_own
        xs = [ar.alloc((D,), F32, at=po_ + 16384 + i * 8192) for i in range(2)]
        ys = [ar.alloc((D,), F32) for _ in range(2)]
        yb = [ar.alloc((D,), BF16) for _ in range(2)]
        stats = ar.alloc((4, 6), F32)
        mv = ar.alloc((2,), F32)
        rstd = ar.alloc((1,), F32)
        self.x1_tokens = []
        def emit_tr(ybf, t0, n):
            for half in range(2):
                pb = self.psb(4 + half)
                for c in range(8):
                    cc = half * 8 + c
                    self.tr(pb[:, c * 128:c * 128 + n], ybf[:, cc * 128:(cc + 1) * 128], self.ident_b[0:n, 0:n])
                src_ps = pb[:, 0:1024].rearrange("p (c t) -> p c t", c=8)[:, :, 0:n]
                self.copy("dve" if half == 0 else "act", self.x1T[:, half * 8:(half + 1) * 8, t0:t0 + n], src_ps)
        pend_tr = None
        for ti, (t0, n) in enumerate(OWN_TILES):
            x = xs[ti % 2][0:n, :]
            y = ys[ti % 2][0:n, :]
            S.dma("sp", x, self.I["x_own"][t0:t0 + n, :], writes=[x])
            for g in range(4):
                po = self.psf(g)[0:n, :]
                for c in range(KC):
                    self.mm(po, self.mixedT[:, c, t0:t0 + n], wo[:, g, c, :],
                            start=(c == 0), stop=(c == KC - 1))
                self.stt("dve", y[:, g * 512:(g + 1) * 512], x[:, g * 512:(g + 1) * 512], ALPHA, po, ALU.mult, ALU.add)
            if pend_tr is not None:
                emit_tr(*pend_tr)
            self.layernorm(y, n, gb, bb, stats, mv, rstd)
            tok = S.dma("pool", self.x1s[t0:t0 + n, :], y, reads=[y])
            self.x1_tokens.append(tok)
            ybf = yb[ti % 2][0:n, :]
            self.copy("act", ybf, y)
            pend_tr = (ybf, t0, n)
        emit_tr(*pend_tr)
        ar.release(m0)

    def layernorm(self, y, n, gb, bb, stats, mv, rstd):
        S = self.S
        for g in range(4):
            S.op("dve", lambda e: e.bn_stats(stats[0:n, g, :], y[:, g * 512:(g + 1) * 512]),
                 reads=[y[:, g * 512:(g + 1) * 512]], writes=[stats[0:n, g, :]])
        S.op("dve", lambda e: e.bn_aggr(mv[0:n, :], stats[0:n, :, :].rearrange("p a b -> p (a b)")),
             reads=[stats[0:n, :, :]], writes=[mv[0:n, :]])
        self.act(rstd[0:n, :], mv[0:n, 1:2], AF.Ln, bias=self.eps_tile(LN_EPS)[0:n, :], scale=1.0,
                 extra_reads=[self.eps_tile(LN_EPS)[0:n, :]])
        self.act(rstd[0:n, :], rstd[0:n, :], AF.Exp, bias=0.0, scale=-0.5)
        self.stt("dve", y, y, mv[0:n, 0:1], gb[0:n, :], ALU.subtract, ALU.mult, extra_reads=[mv[0:n, 0:1]])
        self.stt("dve", y, y, rstd[0:n, :], bb[0:n, :], ALU.mult, ALU.add, extra_reads=[rstd[0:n, :]])

    def eps_tile(self, val):
        if not hasattr(self, "_eps"):
            self._eps = {}
        if val not in self._eps:
            t = self.nc.alloc_sbuf_tensor(f"eps_{len(self._eps)}", [128, 1], F32)
            self.memset("pool", t[:], val)
            self._eps[val] = t[:]
        return self._eps[val]

    def phase_ffn(self):
        ar, S = self.ar, self.S
        ar.release(self.xT_pre_off)
        m0 = ar.mark()
        hT = ar.alloc((64, NOWN), BF16)
        groups = tok_groups(NOWN)
        m1 = ar.mark()
        wu = [ar.alloc((KC, 256), BF16) for _ in range(2)]
        rl = [ar.alloc((512,), F32) for _ in range(2)]
        self.wload(wu[0], self.I["w_up"], 0, 256)
        k = 0
        for s in range(DFF // 256):
            if s + 1 < DFF // 256:
                self.wload(wu[(s + 1) % 2], self.I["w_up"], (s + 1) * 256, 256)
            w = wu[s % 2]
            for j in range(2):
                ft = s * 2 + j
                for gi, (t0, n) in enumerate(groups):
                    bank = k % 4
                    po = self.psf(bank)[:, 0:n]
                    for c in range(KC):
                        self.mm(po, w[:, c, j * 128:(j + 1) * 128], self.x1T[:, c, t0:t0 + n],
                                start=(c == 0), stop=(c == KC - 1))
                    r = rl[k % 2][:, 0:n]
                    self.act(r, po, AF.Relu)
                    self.tt("dve", hT[:, ft, t0:t0 + n], r, r, ALU.mult)
                    k += 1
        ar.release(m1)
        wd = [ar.alloc((64, 128), BF16, at=self.xT_own_off + i * 16384) for i in range(2)]
        oT = [ar.alloc((NOWN,), F32) for _ in range(2)]
        tk = [ar.alloc((128,), F32) for _ in range(2)]
        y2_tokens = []

        def wdload(i):
            src = self.I["w_down"][:, i * 128:(i + 1) * 128].rearrange("(c p) n -> p c n", p=128)
            S.dma("pool", wd[i % 2], src, writes=[wd[i % 2]])
        wdload(0)
        kkc = [0]

        def emit_dn(o, ct):
            for ti, (t0, n) in enumerate(OWN_TILES):
                kk = kkc[0]
                bank = 4 + (kk % 4)
                pt = self.psf(bank)[0:n, 0:128]
                self.tr(pt, o[:, t0:t0 + n], self.ident_f)
                st = tk[kk % 2][0:n, :]
                self.copy("dve" if kk % 2 == 0 else "act", st, pt)
                tok = S.dma("sp", self.y2s[t0:t0 + n, ct * 128:(ct + 1) * 128], st, reads=[st])
                y2_tokens.append(tok)
                kkc[0] += 1
        pend_dn = None
        for ct in range(16):
            if ct + 1 < 16:
                wdload(ct + 1)
            w = wd[ct % 2]
            o = oT[ct % 2]
            for gi, (t0, n) in enumerate(groups):
                bank = gi
                po = self.psf(bank)[:, 0:n]
                for c in range(64):
                    self.mm(po, w[:, c, :], hT[:, c, t0:t0 + n], start=(c == 0), stop=(c == 63))
                self.copy("act" if gi % 2 == 0 else "dve", o[:, t0:t0 + n], po)
            if pend_dn is not None:
                emit_dn(*pend_dn)
            pend_dn = (o, ct)
        emit_dn(*pend_dn)
        if True:
            pass
        ar.release(m0)
        gb = ar.alloc((D,), F32)
        bb = ar.alloc((D,), F32)
        S.dma("sp", gb, self.I["ln2_g"].partition_broadcast(128), writes=[gb])
        S.dma("sp", bb, self.I["ln2_b"].partition_broadcast(128), writes=[bb])
        xs = [ar.alloc((D,), F32) for _ in range(2)]
        ys = [ar.alloc((D,), F32) for _ in range(2)]
        stats = ar.alloc((4, 6), F32)
        mv = ar.alloc((2,), F32)
        rstd = ar.alloc((1,), F32)
        for ti, (t0, n) in enumerate(OWN_TILES):
            x = xs[ti % 2][0:n, :]
            y = ys[ti % 2][0:n, :]
            S.dma("sp", x, self.x1s[t0:t0 + n, :], writes=[x], after=self.x1_tokens)
            S.dma("sp", y, self.y2s[t0:t0 + n, :], writes=[y], after=y2_tokens)
            self.stt("dve", y, x, ALPHA, y, ALU.mult, ALU.add)
            self.layernorm(y, n, gb, bb, stats, mv, rstd)
            S.dma("pool", self.O["y"][t0:t0 + n, :], y, reads=[y], is_output=True)
        ar.release(m0)


_PROG = {}


def get_prog(stop_after=None, dbg=()):
    key = (stop_after, tuple(dbg))
    if key not in _PROG:
        _PROG[key] = Prog(stop_after, dbg)
    return _PROG[key]


def core_inputs(c, inp):
    b, h = c // 2, c % 2
    f = np.float32
    xp = inp["x_prompt"][b]
    x_own = np.concatenate([xp[h * 1024:(h + 1) * 1024], inp["x_sample"][2 * c], inp["x_sample"][2 * c + 1]], 0)
    x_pre = xp[0:1024] if h == 1 else np.zeros((1024, D), f)
    pre_bias = np.full((128, 1), 1.0 if h == 1 else 0.0, f)
    m = {
        "x_own": x_own, "x_pre": x_pre,
        "conv_s": inp["state_gdn_conv"][0, 2 * c:2 * c + 2],
        "S0_s": inp["state_gdn_S"][0, 2 * c:2 * c + 2],
        "ck": inp["cache_sb_k"][0, 2 * c:2 * c + 2].reshape(2, 2048, 1024),
        "cv": inp["cache_sb_v"][0, 2 * c:2 * c + 2].reshape(2, 2048, 1024),
        "w_in": inp["w_in"][0], "conv_w": inp["conv_w"][0], "a_log": inp["a_log"][0],
        "dt_bias": inp["dt_bias"][0], "gdn_norm_w": inp["gdn_norm_w"][0], "w_out": inp["w_out"][0],
        "ln1_g": inp["ln1_g"][0], "ln1_b": inp["ln1_b"][0], "w_up": inp["w_up"][0],
        "w_down": inp["w_down"][0], "ln2_g": inp["ln2_g"][0], "ln2_b": inp["ln2_b"][0],
        "pre_bias": pre_bias,
    }
    return {k: np.ascontiguousarray(v, dtype=f) for k, v in m.items()}


def kernel(**inputs):
    inp = {k: np.asarray(v) for k, v in inputs.items()}
    prog = get_prog()
    in_maps = [core_inputs(c, inp) for c in range(8)]
    res = run_bass_kernel_spmd(prog.nc, in_maps, core_ids=list(range(8)))
    R = res.results
    f = np.float32
    y_p = np.zeros((4, 2048, D), f)
    y_s = np.zeros((16, 32, D), f)
    conv_p = np.zeros((1, 4, 3, 3072), f)
    S_p = np.zeros((1, 4, 8, 128, 128), f)
    k_p = np.zeros((1, 4, 2048, 8, 128), f)
    v_p = np.zeros((1, 4, 2048, 8, 128), f)
    conv_s = np.zeros((1, 16, 3, 3072), f)
    S_s = np.zeros((1, 16, 8, 128, 128), f)
    k_s = np.zeros((1, 16, 32, 8, 128), f)
    v_s = np.zeros((1, 16, 32, 8, 128), f)
    for c in range(8):
        b, h = c // 2, c % 2
        r = R[c]
        y = np.asarray(r["y"])
        kb = np.asarray(r["kb"]).reshape(NOWN, 8, 128)
        vb = np.asarray(r["vb"]).reshape(NOWN, 8, 128)
        sl = slice(h * 1024, (h + 1) * 1024)
        y_p[b, sl] = y[0:1024]
        k_p[0, b, sl] = kb[0:1024]
        v_p[0, b, sl] = vb[0:1024]
        for j in range(2):
            s = 2 * c + j
            y_s[s] = y[1024 + 32 * j:1056 + 32 * j]
            k_s[0, s] = kb[1024 + 32 * j:1056 + 32 * j]
            v_s[0, s] = vb[1024 + 32 * j:1056 + 32 * j]
            conv_s[0, s] = np.asarray(r["conv_s_o"])[j]
            S_s[0, s] = np.asarray(r["S_s_o"])[j]
        if h == 1:
            conv_p[0, b] = np.asarray(r["conv_p_o"])
            S_p[0, b] = np.asarray(r["S_p_o"])
    return (y_p, y_s, conv_p, S_p, k_p, v_p, conv_s, S_s, k_s, v_s)
```
